# Optimizing a Trainium2 kernel written in Bass

```python
import jax, jax.numpy as jnp
from jax import lax
import numpy as np


D_MODEL = 1024
BATCH = 16
SEQ = 256
DEPTH = 2
DEC_BATCH = 2
DEC_SEQ = 2048
PAST_LEN = 512

GRID_W = 64
EXPAND = 2
D_MIX = EXPAND * D_MODEL
D_SSD = D_MIX // 2
D_CONF = D_MIX - D_SSD
SSD_HEAD_DIM = 64
SSD_HEADS = D_SSD // SSD_HEAD_DIM
SSD_GROUPS = 2
HEADS_PER_GROUP = SSD_HEADS // SSD_GROUPS
D_STATE = 128
SSD_CONV_W = 5
SSD_CONV_CH = D_SSD + 2 * SSD_GROUPS * D_STATE
CHUNK = 128
N_DIR = 2
CONF_CONV_W = 31
EPS = 1e-6

I_Z = D_SSD
I_XBC = I_Z + SSD_CONV_CH
I_DT = I_XBC + N_DIR * SSD_HEADS
I_GA = I_DT + D_CONF
I_GB = I_GA + D_CONF
IN_COLS = I_GB + D_CONF

kernel_name = "hybrid_ssd_conformer_diffusion_step"


def rms_norm(x, g):
    xf = x.astype(jnp.float32)
    y = xf * lax.rsqrt(jnp.mean(xf * xf, axis=-1, keepdims=True) + EPS)
    return (y * g.astype(jnp.float32)).astype(x.dtype)


def layer_norm(x, g, b):
    xf = x.astype(jnp.float32)
    mu = jnp.mean(xf, axis=-1, keepdims=True)
    var = jnp.mean(jnp.square(xf - mu), axis=-1, keepdims=True)
    y = (xf - mu) * lax.rsqrt(var + EPS)
    return (y * g.astype(jnp.float32) + b.astype(jnp.float32)).astype(x.dtype)


def depthwise_conv(x, w, b):
    k, ch = w.shape
    out = lax.conv_general_dilated(
        x, w.reshape(k, 1, 1, ch).astype(x.dtype), window_strides=(1, 1),
        padding=((k // 2, k // 2), (0, 0)),
        dimension_numbers=('NHWC', 'HWIO', 'NHWC'), feature_group_count=ch)
    return out + b.astype(x.dtype)


def segsum(a):
    t = a.shape[-1]
    idx = jnp.arange(t)
    xr = jnp.where(idx[:, None] > idx[None, :], a[..., :, None], 0.0)
    s = jnp.cumsum(xr, axis=-2)
    return jnp.where(idx[:, None] >= idx[None, :], s, -jnp.inf)


def ssd_scan(x, dt, a, b_in, c_in, h0):
    bsz, seqlen = x.shape[:2]
    nc = seqlen // CHUNK
    g, r, p, n = SSD_GROUPS, HEADS_PER_GROUP, SSD_HEAD_DIM, D_STATE
    xg = (x * dt[..., None]).reshape(bsz, nc, CHUNK, g, r, p)
    la = (dt * a).reshape(bsz, nc, CHUNK, g, r).transpose(0, 3, 4, 1, 2)
    bc = b_in.reshape(bsz, nc, CHUNK, g, n)
    cc = c_in.reshape(bsz, nc, CHUNK, g, n)
    la_cum = jnp.cumsum(la, axis=-1)
    decay_in = jnp.exp(segsum(la))
    cb = jnp.einsum('bclgn,bcsgn->bgcls', cc, bc)
    y_diag = jnp.einsum('bgcls,bgrcls,bcsgrp->bclgrp', cb, decay_in, xg)
    decay_to_end = jnp.exp(la_cum[..., -1:] - la_cum)
    states = jnp.einsum('bclgn,bgrcl,bclgrp->bcgrpn', bc, decay_to_end, xg)
    h0g = h0.astype(jnp.float32).reshape(bsz, 1, g, r, p, n)
    states = jnp.concatenate([h0g, states], axis=1)
    chunk_tot = jnp.pad(la_cum[..., -1], ((0, 0), (0, 0), (0, 0), (1, 0)))
    decay_chunk = jnp.exp(segsum(chunk_tot))
    states = jnp.einsum('bgrzc,bcgrpn->bzgrpn', decay_chunk, states)
    prev, final = states[:, :-1], states[:, -1]
    y_off = jnp.einsum('bclgn,bcgrpn,bgrcl->bclgrp', cc, prev, jnp.exp(la_cum))
    y = (y_diag + y_off).reshape(bsz, seqlen, SSD_HEADS, p)
    return y, final.reshape(bsz, SSD_HEADS, p, n)


def ssd_branch(z, xbc, dt_raw, conv_w, conv_b, a_log, dt_bias, d_skip, norm_g, h0):
    bsz, seqlen, _ = xbc.shape
    xbc = jax.nn.silu(depthwise_conv(xbc[:, :, None, :], conv_w, conv_b)[:, :, 0, :])
    xs, b_in, c_in = jnp.split(xbc.astype(jnp.float32), [D_SSD, D_SSD + SSD_GROUPS * D_STATE], axis=-1)
    xs = xs.reshape(bsz, seqlen, SSD_HEADS, SSD_HEAD_DIM)
    b_in = b_in.reshape(bsz, seqlen, SSD_GROUPS, D_STATE)
    c_in = c_in.reshape(bsz, seqlen, SSD_GROUPS, D_STATE)
    dt = jax.nn.softplus(dt_raw.astype(jnp.float32).reshape(bsz, seqlen, N_DIR, SSD_HEADS)
                         + dt_bias.astype(jnp.float32))
    a = -jnp.exp(a_log.astype(jnp.float32))
    y_f, h_f = ssd_scan(xs, dt[:, :, 0], a[0], b_in, c_in, h0[:, 0])
    y_b, h_b = ssd_scan(jnp.flip(xs, 1), jnp.flip(dt[:, :, 1], 1), a[1],
                        jnp.flip(b_in, 1), jnp.flip(c_in, 1), h0[:, 1])
    y = y_f + jnp.flip(y_b, 1) + d_skip.astype(jnp.float32)[:, None] * xs
    y = y.reshape(bsz, seqlen, D_SSD) * jax.nn.silu(z.astype(jnp.float32))
    y = rms_norm(y, norm_g)
    return y.astype(z.dtype), jnp.stack([h_f, h_b], axis=1)


def conformer_branch(ga, gb, conv_w, conv_b, ln_g, ln_b):
    h = ga * jax.nn.sigmoid(gb)
    h = depthwise_conv(h, conv_w, conv_b)
    h = layer_norm(h, ln_g, ln_b)
    return jax.nn.silu(h)


def trunk_layer(x, mod, grid, h0, g_pre, g_post, w_in, ssd_conv_w, ssd_conv_b, a_log, dt_bias,
                d_skip, ssd_norm_g, conf_conv_w, conf_conv_b, conf_ln_g, conf_ln_b, w_out):
    bsz, seqlen, _ = x.shape
    shift, scale, gate = jnp.split(mod[:, None, :].astype(x.dtype), 3, axis=-1)
    h = rms_norm(x, g_pre) * (1 + scale) + shift
    u = h @ w_in
    z, xbc, dt_raw, ga, gb, gsil = jnp.split(u, [I_Z, I_XBC, I_DT, I_GA, I_GB], axis=-1)
    y_ssd, h_fin = ssd_branch(z, xbc, dt_raw, ssd_conv_w, ssd_conv_b, a_log, dt_bias, d_skip,
                              ssd_norm_g, h0)
    rows, cols = grid
    y_conf = conformer_branch(ga.reshape(bsz, rows, cols, D_CONF), gb.reshape(bsz, rows, cols, D_CONF),
                              conf_conv_w, conf_conv_b, conf_ln_g, conf_ln_b)
    y_conf = y_conf.reshape(bsz, seqlen, D_CONF) * jax.nn.silu(gsil)
    out = jnp.concatenate([y_ssd.astype(x.dtype), y_conf.astype(x.dtype)], axis=-1) @ w_out
    return x + gate * rms_norm(out, g_post), h_fin


def setup_inputs(seed: int = 0) -> dict:
    key = jax.random.key(seed)
    ks = jax.random.split(key, 24)
    f32 = jnp.float32
    nrm = lambda k, shape, s: jax.random.normal(k, shape, f32) * s
    dt0 = jnp.exp(jax.random.uniform(ks[12], (DEPTH, N_DIR, SSD_HEADS), f32,
                                     np.log(1e-3).astype(np.float32), np.log(1e-1).astype(np.float32)))
    return {
        "x_prompt": nrm(ks[0], (BATCH, SEQ, D_MODEL), 1.0),
        "x_sample": nrm(ks[1], (DEC_BATCH, DEC_SEQ, D_MODEL), 1.0),
        "state_ssd": nrm(ks[2], (DEC_BATCH, DEPTH, N_DIR, SSD_HEADS, SSD_HEAD_DIM, D_STATE), 0.1),
        "c": nrm(ks[3], (DEC_BATCH, D_MODEL), 1.0),
        "c_ctx": nrm(ks[4], (D_MODEL,), 1.0),
        "w_mod": nrm(ks[5], (DEPTH, D_MODEL, 3 * D_MODEL), 0.2 * D_MODEL ** -0.5),
        "b_mod": nrm(ks[6], (DEPTH, 3 * D_MODEL), 0.02),
        "g_pre": 1.0 + nrm(ks[7], (DEPTH, D_MODEL), 0.1),
        "g_post": 1.0 + nrm(ks[8], (DEPTH, D_MODEL), 0.1),
        "w_in": nrm(ks[9], (DEPTH, D_MODEL, IN_COLS), D_MODEL ** -0.5),
        "ssd_conv_w": nrm(ks[10], (DEPTH, SSD_CONV_W, SSD_CONV_CH), SSD_CONV_W ** -0.5),
        "ssd_conv_b": nrm(ks[11], (DEPTH, SSD_CONV_CH), 0.01),
        "ssd_a_log": jnp.log(jax.random.uniform(ks[13], (DEPTH, N_DIR, SSD_HEADS), f32, 1.0, 16.0)),
        "ssd_dt_bias": dt0 + jnp.log(-jnp.expm1(-dt0)),
        "ssd_d": 1.0 + nrm(ks[14], (DEPTH, SSD_HEADS), 0.1),
        "ssd_norm_g": 1.0 + nrm(ks[15], (DEPTH, D_SSD), 0.1),
        "conf_conv_w": nrm(ks[16], (DEPTH, CONF_CONV_W, D_CONF), CONF_CONV_W ** -0.5),
        "conf_conv_b": nrm(ks[17], (DEPTH, D_CONF), 0.01),
        "conf_ln_g": 1.0 + nrm(ks[18], (DEPTH, D_CONF), 0.1),
        "conf_ln_b": nrm(ks[19], (DEPTH, D_CONF), 0.01),
        "w_out": nrm(ks[20], (DEPTH, D_MIX, D_MODEL), D_MIX ** -0.5),
    }


def reference(x_prompt, x_sample, state_ssd, c, c_ctx, w_mod, b_mod, g_pre, g_post, w_in,
              ssd_conv_w, ssd_conv_b, ssd_a_log, ssd_dt_bias, ssd_d, ssd_norm_g,
              conf_conv_w, conf_conv_b, conf_ln_g, conf_ln_b, w_out):
    ctx_b, ctx_len = x_prompt.shape[0], x_prompt.shape[1]
    rows = x_sample.shape[1] // GRID_W
    ctx_grid = (ctx_len, 1)
    lat_grid = (rows, GRID_W)
    silu_ctx = jax.nn.silu(c_ctx)[None, :]
    silu_c = jax.nn.silu(c)
    h0_ctx = jnp.zeros((ctx_b, N_DIR, SSD_HEADS, SSD_HEAD_DIM, D_STATE), jnp.float32)
    xp, xs = x_prompt, x_sample
    ctx_states = []
    for l in range(DEPTH):
        lp = (g_pre[l], g_post[l], w_in[l], ssd_conv_w[l], ssd_conv_b[l], ssd_a_log[l], ssd_dt_bias[l],
              ssd_d[l], ssd_norm_g[l], conf_conv_w[l], conf_conv_b[l], conf_ln_g[l], conf_ln_b[l], w_out[l])
        mod_ctx = silu_ctx @ w_mod[l] + b_mod[l]
        xp, st = trunk_layer(xp, mod_ctx, ctx_grid, h0_ctx, *lp)
        ctx_states.append(st.astype(x_prompt.dtype))
        mod_lat = silu_c @ w_mod[l] + b_mod[l]
        xs, _ = trunk_layer(xs, mod_lat, lat_grid, state_ssd[:, l], *lp)
    new_state_ssd = jnp.stack(ctx_states, axis=1)
    return (xp, xs, new_state_ssd)
```

```python
import numpy as np
from contextlib import ExitStack
import concourse.bass as bass
import concourse.mybir as mybir
from concourse.bass_utils import run_bass_kernel_spmd

F32 = mybir.dt.float32
BF16 = mybir.dt.bfloat16
AF = mybir.ActivationFunctionType
ALU = mybir.AluOpType

D = 1024
DEPTH = 2
NCORES = 8
EPS = 1e-6
I_Z, I_X, I_B, I_C, I_DT, I_GA, I_GB, I_GS = 0, 1024, 2048, 2304, 2560, 2592, 3616, 4640
IN_COLS = 5664
HP = 4
WP = HP * 64
TMAX = 2048
NPS = 7
CONV_ND = 8
CONV_NP = 0
NROT = 3
OFF_ENG = "pool"


class Prog:
    SEM_LIMIT = 4000
    WINDOW = 128
    SEM_LAT = 64.0

    def __init__(self, nc, stack, same_engine_sync=True, schedule=True):
        self.nc = nc
        self.stack = stack
        self.engs = {"pe": nc.tensor, "act": nc.scalar, "dve": nc.vector, "pool": nc.gpsimd, "sp": nc.sync}
        self.ins = []
        self.last_w = {}
        self.readers = {}
        self.same_engine_sync = same_engine_sync
        self.schedule = schedule
        self.n_dma_sems = {"sp": 16, "pool": 8, "act": 4, "dve": 4, "pe": 4}
        self.out_dmas = []
        self.w_rdeps = {}
        self.phase = 0

    def barrier(self):
        self.phase += 1

    def op(self, eng, fn, reads=(), writes=(), dma=False, final=False, cost=300.0, lat=0.0, group=None):
        deps = set()
        for r in reads:
            if r in self.last_w:
                deps |= set(self.last_w[r][1])
        i = len(self.ins)
        for w in writes:
            same = False
            if w in self.last_w:
                gid, members = self.last_w[w]
                same = group is not None and gid == group
                if not same:
                    deps |= set(members)
            if same:
                deps |= self.w_rdeps.get(w, set())
            else:
                rd = set(self.readers.get(w, set()))
                deps |= rd
                self.w_rdeps[w] = (set(self.last_w[w][1]) if w in self.last_w else set()) | rd
        deps.discard(i)
        self.ins.append(dict(eng=eng, fn=fn, deps=deps, dma=dma, cost=cost, lat=lat, phase=self.phase))
        for r in reads:
            self.readers.setdefault(r, set()).add(i)
        for w in writes:
            if w in self.last_w and group is not None and self.last_w[w][0] == group:
                self.last_w[w][1].append(i)
            else:
                self.last_w[w] = (group, [i])
                self.readers[w] = set()
        if final:
            self.out_dmas.append(i)
        return i

    def _order(self):
        ins = self.ins
        n = len(ins)
        per_eng = {e: [] for e in self.engs}
        for i, it in enumerate(ins):
            per_eng[it["eng"]].append(i)
        if not self.schedule:
            return per_eng
        users = [[] for _ in range(n)]
        nun = [0] * n
        for i, it in enumerate(ins):
            nun[i] = len(it["deps"])
            for d in it["deps"]:
                users[d].append(i)
        blev = [0.0] * n
        for i in range(n - 1, -1, -1):
            it = ins[i]
            m = 0.0
            for u in users[i]:
                if ins[u]["phase"] == it["phase"] and blev[u] > m:
                    m = blev[u]
            blev[i] = it["cost"] + it["lat"] + m
        rdy = [0.0] * n
        self.t_start = [0.0] * n
        self.t_fin = [0.0] * n
        nphase = self.phase + 1
        left = [0] * nphase
        for it in ins:
            left[it["phase"]] += 1
        cur = 0
        while cur < nphase and left[cur] == 0:
            cur += 1
        phase_t = 0.0
        tmax = 0.0
        eng_free = {e: 0.0 for e in self.engs}
        pend = {e: list(v) for e, v in per_eng.items()}
        order = {e: [] for e in self.engs}
        remaining = n
        while remaining:
            best = None
            for e, lst in pend.items():
                cand = None
                ef = eng_free[e]
                for i in lst[:self.WINDOW]:
                    it = ins[i]
                    if it["phase"] != cur:
                        break
                    if nun[i]:
                        continue
                    stt = max(rdy[i], ef, phase_t)
                    key = (stt, -blev[i]) if stt > ef + 1e-9 else (ef, -blev[i])
                    if cand is None or key < cand[2]:
                        cand = (key[0], i, key)
                if cand is not None and (best is None or cand[0] < best[0] - 1e-9 or
                                         (abs(cand[0] - best[0]) <= 1e-9 and cand[1] < best[1])):
                    best = (cand[0], cand[1], e)
            assert best is not None, "scheduler stuck"
            stt, i, e = best
            it = ins[i]
            eng_free[e] = stt + it["cost"]
            f = stt + it["cost"] + it["lat"]
            self.t_start[i] = stt
            self.t_fin[i] = f
            tmax = max(tmax, f)
            for u in users[i]:
                nun[u] -= 1
                fl = f if (ins[u]["eng"] == e and e == "pe" and not it["dma"]) else f + self.SEM_LAT
                if fl > rdy[u]:
                    rdy[u] = fl
            pend[e].remove(i)
            order[e].append(i)
            remaining -= 1
            left[cur] -= 1
            if left[cur] == 0:
                while cur < nphase and left[cur] == 0:
                    cur += 1
                phase_t = tmax + 200.0
        self.sim_time = tmax
        return order

    def emit(self):
        nc = self.nc
        ins = self.ins
        n = len(ins)
        order = self._order()
        pos = [0] * n
        for e, lst in order.items():
            for k, i in enumerate(lst):
                pos[i] = k
        last_before = {}
        for e, lst in order.items():
            cuts = {}
            for k, i in enumerate(lst):
                cuts.setdefault(ins[i]["phase"], k)
            last_before[e] = (lst, cuts)
        for e, lst in order.items():
            seen = -1
            for i in lst:
                p = ins[i]["phase"]
                if p == seen:
                    continue
                seen = p
                if p == 0:
                    continue
                extra = set()
                for e2, (lst2, cuts2) in last_before.items():
                    ks = [k for ph, k in cuts2.items() if ph >= p]
                    endk = min(ks) if ks else len(lst2)
                    if endk == 0:
                        continue
                    extra.add(lst2[endk - 1])
                    nd = self.n_dma_sems[e2]
                    cnt = 0
                    for k in range(endk - 1, -1, -1):
                        if ins[lst2[k]]["dma"]:
                            extra.add(lst2[k])
                            cnt += 1
                            if cnt >= nd:
                                break
                extra.discard(i)
                ins[i]["deps"] = set(ins[i]["deps"]) | extra
        pruned = [None] * n
        for i, it in enumerate(ins):
            e = it["eng"]
            keep = {}
            dmas = []
            for d in it["deps"]:
                p = ins[d]
                if p["dma"]:
                    dmas.append(d)
                    continue
                if p["eng"] == e and (e == "pe" or not self.same_engine_sync):
                    continue
                pe_ = p["eng"]
                if pe_ not in keep or pos[d] > pos[keep[pe_]]:
                    keep[pe_] = d
            pruned[i] = list(keep.values()) + dmas
        needed = [False] * n
        for i in range(n):
            for d in pruned[i]:
                needed[d] = True
        for i in self.out_dmas:
            needed[i] = True
        sem_of = [None] * n
        dma_prev = [None] * n
        for e, lst in order.items():
            nd = self.n_dma_sems[e]
            dsems = None
            dcnt = None
            rr = 0
            cur = None
            ccnt = 0
            k = 0
            for i in lst:
                it = ins[i]
                if it["dma"]:
                    if dsems is None:
                        dsems = [self.stack.enter_context(nc.semaphore(f"dq_{e}_{j}")) for j in range(nd)]
                        dcnt = [0] * nd
                    j = rr
                    rr = (rr + 1) % nd
                    if dcnt[j] > 0:
                        dma_prev[i] = (dsems[j], dcnt[j])
                    dcnt[j] += 16
                    sem_of[i] = (dsems[j], dcnt[j])
                elif needed[i]:
                    if cur is None or ccnt >= self.SEM_LIMIT:
                        cur = self.stack.enter_context(nc.semaphore(f"s_{e}_{k}"))
                        k += 1
                        ccnt = 0
                    ccnt += 1
                    sem_of[i] = (cur, ccnt)
        for e, lst in order.items():
            eng = self.engs[e]
            waited = {}

            def do_wait(sem, cnt):
                key = id(sem)
                if waited.get(key, 0) >= cnt:
                    return
                eng.wait_ge(sem, cnt)
                waited[key] = cnt

            for i in lst:
                it = ins[i]
                ws = [sem_of[d] for d in pruned[i]]
                ws.sort(key=lambda sc: -sc[1])
                for sem, c in ws:
                    do_wait(sem, c)
                if it["dma"] and dma_prev[i] is not None:
                    do_wait(*dma_prev[i])
                inst = it["fn"](eng)
                if sem_of[i] is not None:
                    inst.then_inc(sem_of[i][0], 16 if it["dma"] else 1)
            if e == "sp":
                for i in self.out_dmas:
                    do_wait(*sem_of[i])


def build_program(debug=False, only=None):
    nc = bass.Bass("TRN2", target_bir_lowering=False)
    dt_in = lambda name, shape: nc.dram_tensor(name, shape, F32, kind="ExternalInput").ap()
    dt_out = lambda name, shape: nc.dram_tensor(name, shape, F32, kind="ExternalOutput").ap()
    xp_d = dt_in("xp", [D, 512])
    xs_d = dt_in("xs", [D, 2048])
    h0_d = dt_in("h0", [DEPTH, 2, 1024, 128])
    cvec_d = dt_in("cvec", [128, 8, 2])
    wmod_d = dt_in("w_mod", [DEPTH, D, 3 * D])
    bmod_d = dt_in("b_mod", [128, DEPTH, 24])
    gpre_d = dt_in("g_pre", [128, DEPTH, 8])
    gpost_d = dt_in("g_post", [128, DEPTH, 8])
    win_d = dt_in("w_in", [DEPTH, D, IN_COLS])
    cw5_d = dt_in("cw5", [128, DEPTH, 12, 5])
    cb5_d = dt_in("cb5", [128, DEPTH, 12])
    cb5row_d = dt_in("cb5row", [1, DEPTH * 1536])
    alog_d = dt_in("alog", [128, DEPTH, 32])
    dtb_d = dt_in("dtb", [128, DEPTH, 32])
    dsk_d = dt_in("dsk", [128, DEPTH, 16])
    sng_d = dt_in("sng", [128, DEPTH, 8])
    cw31_d = dt_in("cw31", [128, DEPTH, 8, 31])
    cb31_d = dt_in("cb31", [128, DEPTH, 8])
    lng_d = dt_in("lng", [128, DEPTH, 8])
    lnb_d = dt_in("lnb", [128, DEPTH, 8])
    wout_d = dt_in("w_out", [DEPTH, 2 * D, D])
    yp_d = dt_out("yp", [D, 512])
    ys_d = dt_out("ys", [D, 2048])
    ns_d = dt_out("ns", [2, DEPTH, 2, 1024, 128])
    dbg = {}
    if debug:
        dbg["d_modA"] = dt_out("d_modA", [128, DEPTH, 8, 2])
        dbg["d_modB"] = dt_out("d_modB", [128, DEPTH, 8, 2])
        dbg["d_modG"] = dt_out("d_modG", [128, DEPTH, 8, 2])
        for nm, T_ in (("P", 512), ("S", 2048)):
            dbg[f"d_hT_{nm}"] = nc.dram_tensor(f"d_hT_{nm}", [128, 8, T_], BF16, kind="ExternalOutput").ap()
            dbg[f"d_yA_{nm}"] = nc.dram_tensor(f"d_yA_{nm}", [128, 16, T_], BF16, kind="ExternalOutput").ap()
            dbg[f"d_yB_{nm}"] = nc.dram_tensor(f"d_yB_{nm}", [128, 16, T_], BF16, kind="ExternalOutput").ap()
            dbg[f"d_yC_{nm}"] = nc.dram_tensor(f"d_yC_{nm}", [128, 16, T_], BF16, kind="ExternalOutput").ap()
    x1p_d = nc.dram_tensor("x1p", [D, 512], F32, kind="ExternalOutput" if debug else "Internal").ap()
    x1s_d = nc.dram_tensor("x1s", [D, 2048], F32, kind="ExternalOutput" if debug else "Internal").ap()

    with ExitStack() as st:
        P = Prog(nc, st)
        cnt = [0]

        def T(shape, dt, name=None, stack=None):
            cnt[0] += 1
            return (stack or st).enter_context(nc.sbuf_tensor(f"sb{cnt[0]}_{name or 't'}", shape, dt))

        def nfree(ap):
            r = 1
            for d in ap.shape[1:]:
                r *= d
            return r

        def DMA(out, in_, reads=(), writes=(), eng="sp", final=False):
            nbytes = nfree(out) * out.shape[0] * 4
            P.op(eng, lambda e: e.dma_start(out=out, in_=in_), reads, writes, dma=True, final=final,
                 cost=(150.0 if eng == "sp" else 1200.0), lat=2000.0 + nbytes / 120.0)

        def MM(out, lhsT, rhs, start, stop, reads, writes):
            passes = 4 if lhsT.dtype == F32 else 1
            P.op("pe", lambda e: e.matmul(out, lhsT=lhsT, rhs=rhs, start=start, stop=stop), reads, writes,
                 cost=30.0 + passes * max(nfree(rhs), 64) / 2.4, lat=120.0)

        def TR(out, in_, ident, reads, writes):
            P.op("pe", lambda e: e.transpose(out=out, in_=in_, identity=ident), reads, writes,
                 cost=(4 if in_.dtype == F32 else 1) * 60.0 + 30.0, lat=120.0)

        def ACT(out, in_, func, reads, writes, bias=None, scale=None, group=None):
            kw = {}
            if bias is not None:
                kw["bias"] = bias
            if scale is not None:
                kw["scale"] = scale
            P.op("act", lambda e: e.activation(out=out, in_=in_, func=func, **kw), reads, writes, cost=220.0 + nfree(out) / 1.4,
                 group=group)

        def TT(out, in0, in1, op, reads, writes, eng="dve"):
            c = 120.0 + nfree(out) / 0.96 if eng != "pool" else 200.0 + nfree(out) / 0.55
            P.op(eng, lambda e: e.tensor_tensor(out=out, in0=in0, in1=in1, op=op), reads, writes, cost=c)

        def TS(out, in0, s1, op0, reads, writes, s2=None, op1=None, eng="dve"):
            if op1 is None:
                P.op(eng, lambda e: e.tensor_scalar(out=out, in0=in0, scalar1=s1, scalar2=None, op0=op0), reads, writes,
                     cost=120.0 + nfree(out) / 0.96)
            else:
                P.op(eng, lambda e: e.tensor_scalar(out=out, in0=in0, scalar1=s1, scalar2=s2, op0=op0, op1=op1), reads, writes,
                     cost=120.0 + nfree(out) / 0.96)

        def STT(out, in0, scalar, in1, op0, op1, reads, writes, eng="dve"):
            P.op(eng, lambda e: e.scalar_tensor_tensor(out=out, in0=in0, scalar=scalar, in1=in1, op0=op0, op1=op1), reads, writes,
                 cost=120.0 + nfree(out) / 0.96)

        def CP(out, in_, reads, writes, eng="dve"):
            P.op(eng, lambda e: e.tensor_copy(out=out, in_=in_), reads, writes, cost=120.0 + nfree(out) / 0.96)

        def RECIP(out, in_, reads, writes):
            P.op("dve", lambda e: e.reciprocal(out=out, in_=in_), reads, writes, cost=120.0 + nfree(out) * 6.5)

        def MEMSET(ap, val, writes, eng="pool"):
            P.op(eng, lambda e: e.memset(ap, val), (), writes, cost=150.0 + nfree(ap) / 1.0)

        class Rot:
            def __init__(self, name, shape, dt, n, stack):
                self.t = [T(shape, dt, f"{name}{i}", stack) for i in range(n)]
                self.name = name
                self.i = 0

            def next(self):
                k = self.i % len(self.t)
                self.i += 1
                return self.t[k], (self.name, k)

        ps_t = [st.enter_context(nc.psum_tensor(f"ps{i}", [128, 512], F32)) for i in range(NPS)]
        psb_t = st.enter_context(nc.psum_tensor("psb", [128, 1024], BF16))
        ps_i = [0]

        def PS():
            k = ps_i[0] % NPS
            ps_i[0] += 1
            return ps_t[k], ("ps", k)

        PSH = PS

        ident = T([128, 128], F32, "ident")
        identb = T([128, 128], BF16, "identb")
        onesb = T([128, 128], BF16, "onesb")
        onesf = T([128, 128], F32, "onesf")
        Uf = T([128, 128], F32, "Uf")
        SLf = T([128, 128], F32, "SLf")
        Ub = T([128, 128], F32, "Ub")
        SLb = T([128, 128], F32, "SLb")
        MEMSET(onesf[:], 1.0, ["onesf"])
        MEMSET(onesb[:], 1.0, ["onesb"])

        def SEL(t, key, cm, pat, op):
            MEMSET(t[:], 1.0, [key])
            P.op("pool", lambda e: e.affine_select(out=t[:], in_=t[:], pattern=[[pat, 128]], compare_op=op,
                                                   fill=0.0, base=0, channel_multiplier=cm), [key], [key])
        SEL(Uf, "Uf", -1, 1, ALU.is_ge)
        SEL(SLf, "SLf", 1, -1, ALU.is_gt)
        SEL(Ub, "Ub", 1, -1, ALU.is_ge)
        SEL(SLb, "SLb", -1, 1, ALU.is_gt)
        MEMSET(ident[:], 0.0, ["ident"])
        P.op("pool", lambda e: e.affine_select(out=ident[:], in_=ident[:], pattern=[[-1, 128]], compare_op=ALU.not_equal,
                                               fill=1.0, base=0, channel_multiplier=1), ["ident"], ["ident"])
        CP(identb[:], ident[:], ["ident"], ["identb"])

        def LOADP(dram, shape, name):
            t = T(shape, F32, name)
            DMA(t[:], dram, (), [name])
            return t
        cvec = LOADP(cvec_d, [128, 8, 2], "cvec")
        bmod = LOADP(bmod_d, [128, DEPTH, 24], "bmod")
        gpre = LOADP(gpre_d, [128, DEPTH, 8], "gpre")
        gpost = LOADP(gpost_d, [128, DEPTH, 8], "gpost")
        cw5 = LOADP(cw5_d, [128, DEPTH, 12, 5], "cw5")
        cb5 = LOADP(cb5_d, [128, DEPTH, 12], "cb5")
        alog = LOADP(alog_d, [128, DEPTH, 32], "alog")
        dtb = LOADP(dtb_d, [128, DEPTH, 32], "dtb")
        dsk = LOADP(dsk_d, [128, DEPTH, 16], "dsk")
        sng = LOADP(sng_d, [128, DEPTH, 8], "sng")
        cw31 = LOADP(cw31_d, [128, DEPTH, 8, 31], "cw31")
        cb31 = LOADP(cb31_d, [128, DEPTH, 8], "cb31")
        lng = LOADP(lng_d, [128, DEPTH, 8], "lng")
        lnb = LOADP(lnb_d, [128, DEPTH, 8], "lnb")
        cb5row = T([1, DEPTH * 1536], BF16, "cb5row")
        for l_ in range(DEPTH):
            DMA(cb5row[:, l_ * 1536:(l_ + 1) * 1536], cb5row_d[:, l_ * 1536:(l_ + 1) * 1536], (), ["cb5row"], eng="pool")
        aneg = T([128, DEPTH, 32], F32, "aneg")
        ACT(aneg[:], alog[:], AF.Exp, ["alog"], ["aneg"])
        TS(aneg[:], aneg[:], -1.0, ALU.mult, ["aneg"], ["aneg"])

        silc = T([128, 8, 2], F32, "silc")
        ACT(silc[:], cvec[:], AF.Silu, ["cvec"], ["silc"])
        modA = T([128, DEPTH, 8, 2], F32, "modA")
        modB = T([128, DEPTH, 8, 2], F32, "modB")
        modG = T([128, DEPTH, 8, 2], F32, "modG")
        hT = T([128, 8, TMAX], BF16, "hT")
        yT = T([128, 16, TMAX], BF16, "yT")
        ms = ExitStack()
        st.callback(ms.close)
        if True:
            wm = Rot("wm", [128, 8, 512], F32, 2, ms)
            modsb = T([128, 24, 2], F32, "modsb", ms)
            modrow = T([2, 3 * D], F32, "modrow", ms)
            for l in range(DEPTH):
                for cb in range(6):
                    wt, wk = wm.next()
                    DMA(wt[:], wmod_d[l].rearrange("(kc p) c -> p kc c", p=128)[:, :, cb * 512:(cb + 1) * 512], (), [wk])
                    pr, prk = PS()
                    for kc in range(8):
                        MM(pr[0:2, :], silc[:, kc, :], wt[:, kc, :], kc == 0, kc == 7, [wk, "silc"], [prk])
                    CP(modrow[:, cb * 512:(cb + 1) * 512], pr[0:2, :], [prk], [("modrow", cb)])
                pm, pmk = PSH()
                for f in range(24):
                    TR(pm[:, f * 2:f * 2 + 2], modrow[0:2, f * 128:(f + 1) * 128], ident[0:2, 0:2], [("modrow", f // 4), "ident"], [pmk])
                TT(modsb[:], pm[:, 0:48].rearrange("p (f w) -> p f w", w=2),
                   bmod[:, l, :].unsqueeze(2).to_broadcast([128, 24, 2]), ALU.add, [pmk, "bmod"], ["modsb"])
                TS(modA[:, l], modsb[:, 8:16, :], 1.0, ALU.add, ["modsb"], ["modA"])
                TT(modA[:, l], modA[:, l], gpre[:, l, :].unsqueeze(2).to_broadcast([128, 8, 2]), ALU.mult, ["modA", "gpre"], ["modA"])
                CP(modB[:, l], modsb[:, 0:8, :], ["modsb"], ["modB"])
                TT(modG[:, l], modsb[:, 16:24, :], gpost[:, l, :].unsqueeze(2).to_broadcast([128, 8, 2]), ALU.mult,
                   ["modsb", "gpost"], ["modG"])

        if debug:
            DMA(dbg["d_modA"], modA[:], ["modA"], (), final=True)
            DMA(dbg["d_modB"], modB[:], ["modB"], (), final=True)
            DMA(dbg["d_modG"], modG[:], ["modG"], (), final=True)

        def stats_rs(src_sq_fn, nk, rs, rsk, extra_reads, eps=EPS):
            pst, pstk = PS()
            for k in range(nk):
                ap, rd = src_sq_fn(k)
                MM(pst[:], onesb[:], ap, k == 0, k == nk - 1, ["onesb"] + rd, [pstk])
            ACT(rs[:], pst[:], AF.Ln, [pstk], [rsk], bias=eps, scale=1.0 / 1024.0)
            ACT(rs[:], rs[:], AF.Exp, [rsk], [rsk], scale=-0.5)

        ms_holder = [ms]

        def run_block(l, x_src, x_dst, nseq, L, stride, wsel, h0, ns_out, final_out, do_front, fuse_next):
            Ttok = nseq * L
            NT = Ttok // 512
            nch = L // 128
            nblk = nseq * nch
            xsrc_v = x_src.rearrange("(kc p) t -> p kc t", p=128)
            xdst_v = x_dst.rearrange("(kc p) t -> p kc t", p=128)
            hk = lambda t: ("hT", t)
            yk = lambda k, b: ("yT", k, b)
            ytile = lambda ks, t: [yk(k, b) for k in ks for b in range(4 * t, 4 * t + 4)]

            def segs(t):
                if L >= 512:
                    per = L // 512
                    return [(t // per, (t % per) * 512, 512, 0)]
                n = 512 // L
                return [(t * n + i, 0, L, i * L) for i in range(n)]

            def win_load(dst, c0, w, key):
                DMA(dst, win_d[l].rearrange("(kc p) c -> p kc c", p=128)[:, :, c0:c0 + w], (), [key], eng="pool")

            with ExitStack() as s0:
                xt_r = Rot("xt", [128, 8, 512], F32, 2, s0)
                sq_r = Rot("sq0", [128, 8, 512], BF16, 2, s0)
                rs_r = Rot("rs0", [128, 512], F32, 2, s0)
                for t in range(NT if do_front else 0):
                    xt, xtk = xt_r.next()
                    sq, sqk = sq_r.next()
                    rs, rsk = rs_r.next()
                    DMA(xt[:], xsrc_v[:, :, t * 512:(t + 1) * 512], [("xd", id(x_src), t)], [xtk])
                    ACT(sq[:], xt[:], AF.Square, [xtk], [sqk])
                    stats_rs(lambda k: (sq[:, k, :], [sqk]), 8, rs, rsk, [])
                    TT(xt[:], xt[:], rs[:].unsqueeze(1).to_broadcast([128, 8, 512]), ALU.mult, [xtk, rsk], [xtk])
                    for kc in range(8):
                        ACT(hT[:, kc, t * 512:(t + 1) * 512], xt[:, kc, :], AF.Identity, [xtk, "modA", "modB"], [hk(t)],
                            bias=modB[:, l, kc, wsel:wsel + 1], scale=modA[:, l, kc, wsel:wsel + 1], group=("hTf", l, t, nseq))

            if ms_holder:
                ms_holder.pop().close()
            P.barrier()
            nm = "P" if nseq == 2 else "S"
            allk = [yk(k, b) for k in range(16) for b in range(nblk)]
            if debug and l == 0:
                DMA(dbg[f"d_hT_{nm}"], hT[:, :, 0:Ttok], [hk(t) for t in range(NT)], (), final=True)
            with ExitStack() as sa:
                wx = T([128, 8, WP], BF16, "wx", sa)
                wB = T([128, 8, 128], BF16, "wB", sa)
                wC = T([128, 8, 128], BF16, "wC", sa)
                wz = T([128, 8, WP], BF16, "wz", sa)
                wdt = T([128, 8, 32], BF16, "wdt", sa)
                upad_r = Rot("upad", [128, nseq, L + 4], BF16, 2, sa)
                diag5_r = Rot("diag5", [128, 5, 128], BF16, 2, sa)
                xg = T([128, nblk, WP], BF16, "xg", sa)
                Btm = T([128, nblk, 128], BF16, "Btm", sa)
                Bfm = T([128, Ttok], BF16, "Bfm", sa)
                Cfm = T([128, Ttok], BF16, "Cfm", sa)
                dt_all = T([128, nblk, 32], F32, "dt_all", sa)
                la_all = T([128, nblk, 32], F32, "la_all", sa)
                v_all = T([128, nblk, 32], F32, "v_all", sa)
                cum_sb = [T([128, nblk, HP], F32, f"cum{d}", sa) for d in range(2)]
                cum_hi = [T([128, nblk, HP], BF16, f"cumhi{d}", sa) for d in range(2)]
                cum_lo = [T([128, nblk, HP], BF16, f"cumlo{d}", sa) for d in range(2)]
                decs_all = [T([128, 3, nblk, HP], F32, f"decs{d}", sa) for d in range(2)]
                diagD = T([128, HP, 128], BF16, "diagD", sa)
                ypark = T([128, nblk, WP], F32, "ypark", sa)
                Sf = [T([128, WP], F32, f"Sf{d}", sa) for d in range(2)]
                Sb = [T([128, WP], BF16, f"Sb{d}", sa) for d in range(2)]
                cbm_r = Rot("cbm", [128, 128], BF16, NROT, sa)
                xdt_r = Rot("xdt", [128, HP, 64], BF16, NROT, sa)
                xs2_r = Rot("xs2", [128, HP, 64], BF16, NROT, sa)
                Lh_r = Rot("Lh", [128, HP, 128], BF16, NROT, sa)
                La_r = Rot("La", [128, HP, 128], F32, 2, sa)
                Mh_r = Rot("Mh", [128, HP, 128], BF16, NROT, sa)
                t1_r = Rot("t1", [128, HP, 64], F32, NROT, sa)
                zs_r = Rot("zs", [128, WP], F32, 2, sa)
                yg_r = Rot("yg", [128, WP], BF16, 2, sa)
                stg_r = Rot("stg", [128, 128], F32, 2, sa)
                for r in upad_r.t:
                    MEMSET(r[:], 0.0, [("upad", upad_r.t.index(r))])

                win_load(wdt[:], I_DT, 32, "wdt")
                for bi in range(nblk):
                    t = bi // 4
                    pd, pdk = PSH()
                    for kc in range(8):
                        MM(pd[:, 0:32], hT[:, kc, bi * 128:(bi + 1) * 128], wdt[:, kc, :], kc == 0, kc == 7, [hk(t), "wdt"], [pdk])
                    TT(v_all[:, bi, :], pd[:, 0:32], dtb[:, l, :], ALU.add, [pdk, "dtb"], ["v_all"])
                TS(dt_all[:], v_all[:], 30.0, ALU.min, ["v_all"], ["dt_all"])
                ACT(dt_all[:], dt_all[:], AF.Exp, ["dt_all"], ["dt_all"])
                ACT(dt_all[:], dt_all[:], AF.Ln, ["dt_all"], ["dt_all"], bias=1.0)
                TT(dt_all[:], dt_all[:], v_all[:], ALU.max, ["dt_all", "v_all"], ["dt_all"])
                TT(la_all[:], dt_all[:], aneg[:, l, :].unsqueeze(1).to_broadcast([128, nblk, 32]), ALU.mult, ["dt_all", "aneg"], ["la_all"])
                for q in range(16 // HP):
                    g = (q * HP) // 8
                    nb4 = nblk * HP
                    win_load(wx[:], I_X + q * WP, WP, "wx")
                    if (q * HP) % 8 == 0:
                        win_load(wB[:], I_B + g * 128, 128, "wB")
                        win_load(wC[:], I_C + g * 128, 128, "wC")
                    win_load(wz[:], I_Z + q * WP, WP, "wz")
                    nxc = WP // 128
                    chunks = [("x", a, q * nxc + a, wx, a * 128, "wx") for a in range(nxc)]
                    if (q * HP) % 8 == 0:
                        chunks += [("B", 0, 8 + g, wB, 0, "wB"), ("C", 0, 10 + g, wC, 0, "wC")]
                    for kind, a, cidx, wt, wc0, wk in chunks:
                        upad, upk = upad_r.next()
                        dg, dgk = diag5_r.next()
                        TT(dg[:], ident[:].unsqueeze(1).to_broadcast([128, 5, 128]),
                           cw5[:, l, cidx, :].unsqueeze(2).to_broadcast([128, 5, 128]), ALU.mult, ["ident", "cw5"], [dgk])
                        for t in range(NT):
                            pu, puk = PS()
                            for kc in range(8):
                                MM(pu[:], wt[:, kc, wc0:wc0 + 128], hT[:, kc, t * 512:(t + 1) * 512], kc == 0, kc == 7,
                                   [wk, hk(t)], [puk])
                            if L >= 512:
                                s_, off, n_, c0 = segs(t)[0]
                                P.op("act", (lambda e, o=upad[:, s_, 2 + off:2 + off + 512], i=pu[:]: e.copy(out=o, in_=i)),
                                     [puk], [upk], cost=590.0)
                            else:
                                n = 512 // L
                                P.op("act", (lambda e, o=upad[:, t * n:(t + 1) * n, 2:2 + L],
                                             i=pu[:].rearrange("p (s x) -> p s x", s=n): e.copy(out=o, in_=i)), [puk], [upk], cost=590.0)
                        if kind in ("x", "B"):
                            for s_ in range(nseq):
                                for j in range(nch):
                                    bi = s_ * nch + j
                                    pc, pck = PSH()
                                    for k in range(5):
                                        MM(pc[:, 0:128], upad[:, s_, j * 128 + k:j * 128 + k + 128], dg[:, k, :], k == 0, False,
                                           [upk, dgk], [pck])
                                    MM(pc[:, 0:128], onesb[0:1, 0:128], cb5row[0:1, l * 1536 + cidx * 128:l * 1536 + (cidx + 1) * 128],
                                       False, True, ["onesb", "cb5row"], [pck])
                                    if kind == "x":
                                        ACT(xg[:, bi, a * 128:(a + 1) * 128], pc[:, 0:128], AF.Silu, [pck], [("xg", bi)], group=("xg", l, nseq, q, bi))
                                    else:
                                        ACT(Btm[:, bi, :], pc[:, 0:128], AF.Silu, [pck], [("Btm", bi)])
                        if kind in ("B", "C"):
                            dstT, dkey = (Bfm, "Bfm") if kind == "B" else (Cfm, "Cfm")
                            for t in range(NT):
                                pc, pck = PS()
                                for (s_, off, n_, c0) in segs(t):
                                    for k in range(5):
                                        MM(pc[:, c0:c0 + n_], dg[:, k, :], upad[:, s_, off + k:off + k + n_], k == 0, k == 4,
                                           [upk, dgk], [pck])
                                ACT(dstT[:, t * 512:(t + 1) * 512], pc[:], AF.Silu, [pck, "cb5"], [(dkey, t)],
                                    bias=cb5[:, l, cidx:cidx + 1])
                    TT(diagD[:], ident[:].unsqueeze(1).to_broadcast([128, HP, 128]),
                       dsk[:, l, q * HP:(q + 1) * HP].unsqueeze(2).to_broadcast([128, HP, 128]), ALU.mult, ["ident", "dsk"], ["diagD"])
                    for d in (1, 0):
                        Uin, UinK = (Uf, "Uf") if d == 0 else (Ub, "Ub")
                        SLo, SLoK = (SLf, "SLf") if d == 0 else (SLb, "SLb")
                        pdc, pdck = PSH()
                        la_d = la_all[:, :, d * 16 + q * HP:d * 16 + (q + 1) * HP]
                        for ci, (mt, mk) in enumerate(((Uin, UinK), (SLo, SLoK), (onesf, "onesf"))):
                            MM(pdc[:, ci * nb4:(ci + 1) * nb4].rearrange("p (b h) -> p b h", h=HP), mt[:], la_d, True, True,
                               [mk, "la_all"], [pdck])
                        dcs = decs_all[d]
                        dcsk = ("decs", d)
                        ACT(dcs[:].rearrange("p c b h -> p (c b h)"), pdc[:, 0:3 * nb4], AF.Exp, [pdck], [dcsk])
                        cum = cum_sb[d]
                        cumk = ("cum", d)
                        CP(cum[:].rearrange("p b h -> p (b h)"), pdc[:, 0:nb4], [pdck, dcsk], [cumk])
                        chi, clo = cum_hi[d], cum_lo[d]
                        CP(chi[:], cum[:], [cumk], [("chi", d)])
                        TT(clo[:], cum[:], chi[:], ALU.subtract, [cumk, ("chi", d)], [("clo", d)])
                        TT(cum[:], chi[:], clo[:], ALU.add, [("chi", d), ("clo", d), cumk], [cumk])
                        for s_ in range(nseq):
                            if h0 is None:
                                MEMSET(Sf[d][:], 0.0, [("Sf", d)], eng="dve")
                                MEMSET(Sb[d][:], 0.0, [("Sb", d)], eng="dve")
                            else:
                                for a in range(WP // 128):
                                    sg, sgk = stg_r.next()
                                    DMA(sg[:], h0[l, d, q * WP + a * 128:q * WP + (a + 1) * 128, :], (), [sgk])
                                    pt, ptk = PSH()
                                    TR(pt[:, 0:128], sg[:], ident[:], [sgk, "ident"], [ptk])
                                    CP(Sf[d][:, a * 128:(a + 1) * 128], pt[:, 0:128], [ptk], [("Sf", d)])
                                CP(Sb[d][:], Sf[d][:], [("Sf", d)], [("Sb", d)])
                            order = range(nch) if d == 0 else range(nch - 1, -1, -1)
                            for j in order:
                                bi = s_ * nch + j
                                t = bi // 4
                                tok = slice(bi * 128, (bi + 1) * 128)
                                la_b = la_all[:, bi, d * 16 + q * HP:d * 16 + (q + 1) * HP]
                                dt_b = dt_all[:, bi, d * 16 + q * HP:d * 16 + (q + 1) * HP]
                                pcb, pcbk = PSH()
                                MM(pcb[:, 0:128], Bfm[:, tok], Cfm[:, tok], True, True, [("Bfm", t), ("Cfm", t)], [pcbk])
                                cbm, cbmk = cbm_r.next()
                                TT(cbm[:], pcb[:, 0:128], Uin[:], ALU.mult, [pcbk, UinK], [cbmk])
                                xdt, xdtk = xdt_r.next()
                                xs2, xs2k = xs2_r.next()
                                xg_b = xg[:, bi, :].rearrange("p (h c) -> p h c", h=HP)
                                TT(xdt[:], xg_b, dt_b.unsqueeze(2).to_broadcast([128, HP, 64]), ALU.mult, [("xg", bi), "dt_all"], [xdtk], eng=OFF_ENG)
                                TT(xs2[:], xdt[:], dcs[:, 1, bi, :].unsqueeze(2).to_broadcast([128, HP, 64]), ALU.mult,
                                   [xdtk, dcsk], [xs2k], eng=OFF_ENG)
                                parg, pargk = PS()
                                for h in range(HP):
                                    po_ = parg[:, h * 128:(h + 1) * 128]
                                    MM(po_, chi[:, bi, h:h + 1].to_broadcast([128, 128]), identb[:], True, False, [("chi", d), "identb"], [pargk])
                                    MM(po_, clo[:, bi, h:h + 1].to_broadcast([128, 128]), identb[:], False, True, [("clo", d), "identb"], [pargk])
                                La, Lak = La_r.next()
                                for h in range(HP):
                                    ACT(La[:, h, :], parg[:, h * 128:(h + 1) * 128], AF.Relu, [pargk, cumk], [Lak],
                                        bias=cum[:, bi, h:h + 1], scale=-1.0, group=("relu", l, nseq, q, d, bi))
                                Lh, Lhk = Lh_r.next()
                                ACT(Lh[:].rearrange("p h c -> p (h c)"), La[:].rearrange("p h c -> p (h c)"), AF.Exp, [Lak], [Lhk], scale=-1.0)
                                Mh, Mhk = Mh_r.next()
                                TT(Mh[:], Lh[:], cbm[:].unsqueeze(1).to_broadcast([128, HP, 128]), ALU.mult, [Lhk, cbmk], [Mhk])
                                py, pyk = PSH()
                                for h in range(HP):
                                    MM(py[:, h * 64:(h + 1) * 64], Mh[:, h, :], xdt[:, h, :], True, d == 1, [Mhk, xdtk], [pyk])
                                    if d == 0:
                                        MM(py[:, h * 64:(h + 1) * 64], diagD[:, h, :], xg[:, bi, h * 64:(h + 1) * 64], False, True,
                                           ["diagD", ("xg", bi)], [pyk])
                                po, pok = PSH()
                                MM(po[:, 0:WP], Cfm[:, tok], Sb[d][:], True, True, [("Cfm", t), ("Sb", d)], [pok])
                                t1, t1k = t1_r.next()
                                TT(t1[:], po[:, 0:WP].rearrange("p (h c) -> p h c", h=HP),
                                   dcs[:, 0, bi, :].unsqueeze(2).to_broadcast([128, HP, 64]), ALU.mult, [pok, dcsk], [t1k])
                                t1f = t1[:].rearrange("p h c -> p (h c)")
                                if d == 1:
                                    TT(ypark[:, bi, :], t1f, py[:, 0:WP], ALU.add, [t1k, pyk], [("ypark", bi)])
                                else:
                                    TT(t1f, t1f, py[:, 0:WP], ALU.add, [t1k, pyk], [t1k])
                                    TT(t1f, t1f, ypark[:, bi, :], ALU.add, [t1k, ("ypark", bi)], [t1k], eng=OFF_ENG)
                                    yg, ygk = yg_r.next()
                                    zs, zsk = zs_r.next()
                                    pz, pzk = PSH()
                                    for kc in range(8):
                                        MM(pz[:, 0:WP], hT[:, kc, tok], wz[:, kc, :], kc == 0, kc == 7, [hk(t), "wz"], [pzk])
                                    ACT(zs[:], pz[:, 0:WP], AF.Tanh, [pzk], [zsk], scale=0.5)
                                    STT(zs[:], zs[:], 1.0, pz[:, 0:WP], ALU.add, ALU.mult, [zsk, pzk], [zsk])
                                    TT(yg[:], t1f, zs[:], ALU.mult, [t1k, zsk], [ygk])
                                    for a in range(WP // 128):
                                        TR(psb_t[:, a * 128:(a + 1) * 128], yg[:, a * 128:(a + 1) * 128], identb[:], [ygk, "identb"], ["psb"])
                                    kc0 = q * (WP // 128)
                                    P.op("act", (lambda e, o=yT[:, kc0:kc0 + WP // 128, tok],
                                                 i=psb_t[:, 0:WP].rearrange("p (a c) -> p a c", c=128): e.copy(out=o, in_=i)),
                                         ["psb"], [yk(kc0 + a, bi) for a in range(WP // 128)], cost=400.0)
                                pds, pdsk = PSH()
                                MM(pds[:, 0:WP], Btm[:, bi, :], xs2[:].rearrange("p h c -> p (h c)"), True, True, [("Btm", bi), xs2k], [pdsk])
                                Sf3 = Sf[d][:].rearrange("p (h c) -> p h c", h=HP)
                                TT(Sf3, Sf3, dcs[:, 2, bi, :].unsqueeze(2).to_broadcast([128, HP, 64]), ALU.mult,
                                   [("Sf", d), dcsk], [("Sf", d)], eng=OFF_ENG)
                                TT(Sf[d][:], Sf[d][:], pds[:, 0:WP], ALU.add, [("Sf", d), pdsk], [("Sf", d)])
                                P.op("act", (lambda e, o=Sb[d][:], i=Sf[d][:]: e.copy(out=o, in_=i)), [("Sf", d)], [("Sb", d)], cost=400.0)
                            if ns_out is not None:
                                for a in range(WP // 128):
                                    pt, ptk = PSH()
                                    TR(pt[:, 0:128], Sf[d][:, a * 128:(a + 1) * 128], ident[:], [("Sf", d), "ident"], [ptk])
                                    sg, sgk = stg_r.next()
                                    CP(sg[:], pt[:, 0:128], [ptk], [sgk])
                                    DMA(ns_out[s_, l, d, q * WP + a * 128:q * WP + (a + 1) * 128, :], sg[:], [sgk], (), final=True)

            P.barrier()
            if debug and l == 0:
                DMA(dbg[f"d_yA_{nm}"], yT[:, :, 0:Ttok], allk, (), final=True)
            def ssd_norm(sn):
                sq_r = Rot("sqn", [128, 8, 512], BF16, 1, sn)
                rs_r = Rot("rsn", [128, 512], F32, 2, sn)
                for t in range(NT):
                    sq, sqk = sq_r.next()
                    rs, rsk = rs_r.next()
                    tl = slice(t * 512, (t + 1) * 512)
                    ACT(sq[:], yT[:, 0:8, tl], AF.Square, ytile(range(8), t), [sqk])
                    stats_rs(lambda k: (sq[:, k, :], [sqk]), 8, rs, rsk, [], eps=4.0 * EPS)
                    for k in range(8):
                        STT(yT[:, k, tl], yT[:, k, tl], sng[:, l, k:k + 1], rs[:], ALU.mult, ALU.mult,
                            ytile([k], t) + ["sng", rsk], ytile([k], t))

            pad = 15 * stride
            with ExitStack() as sb_:
                ssd_norm(sb_)
                if debug and l == 0:
                    DMA(dbg[f"d_yB_{nm}"], yT[:, :, 0:Ttok], allk, (), final=True)
                wga = T([128, 8, 1024], BF16, "wga", sb_)
                wgb = T([128, 8, 1024], BF16, "wgb", sb_)
                for j in range(8):
                    win_load(wga[:, :, j * 128:(j + 1) * 128], I_GA + j * 128, 128, ("wga", j))
                    win_load(wgb[:, :, j * 128:(j + 1) * 128], I_GB + j * 128, 128, ("wgb", j))
                hc_r = Rot("hc", [128, nseq, L + 2 * pad], BF16, 2, sb_)
                d31_r = Rot("d31", [128, 31, 128], BF16, 2, sb_)
                sig_r = Rot("sig", [128, 512], F32, 2, sb_)
                accd_r = Rot("accd", [128, 512], F32, 2, sb_)
                accp_r = Rot("accp", [128, 512], F32, 2, sb_)
                accbd_r = Rot("accbd", [128, 512], BF16, 2, sb_)
                accbp_r = Rot("accbp", [128, 512], BF16, 2, sb_)
                for r in hc_r.t:
                    MEMSET(r[:], 0.0, [("hc", hc_r.t.index(r))])
                for j in range(8):
                    hc, hck = hc_r.next()
                    dg, dgk = d31_r.next()
                    TT(dg[:], ident[:].unsqueeze(1).to_broadcast([128, 31, 128]),
                       cw31[:, l, j, :].unsqueeze(2).to_broadcast([128, 31, 128]), ALU.mult, ["ident", "cw31"], [dgk])
                    for t in range(NT):
                        pa, pak = PS()
                        pb, pbk = PS()
                        for kc in range(8):
                            MM(pa[:], wga[:, kc, j * 128:(j + 1) * 128], hT[:, kc, t * 512:(t + 1) * 512], kc == 0, kc == 7, [("wga", j), hk(t)], [pak])
                        for kc in range(8):
                            MM(pb[:], wgb[:, kc, j * 128:(j + 1) * 128], hT[:, kc, t * 512:(t + 1) * 512], kc == 0, kc == 7, [("wgb", j), hk(t)], [pbk])
                        sig, sigk = sig_r.next()
                        ACT(sig[:], pb[:], AF.Sigmoid, [pbk], [sigk])
                        if L >= 512:
                            s_, off, n_, c0 = segs(t)[0]
                            TT(hc[:, s_, pad + off:pad + off + 512], pa[:], sig[:], ALU.mult, [pak, sigk], [hck])
                        else:
                            n = 512 // L
                            TT(hc[:, t * n:(t + 1) * n, pad:pad + L], pa[:].rearrange("p (s x) -> p s x", s=n),
                               sig[:].rearrange("p (s x) -> p s x", s=n), ALU.mult, [pak, sigk], [hck])
                    for t in range(NT):
                        pc, pck = PS()
                        for (s_, off, n_, c0) in segs(t):
                            taps = [k for k in range(31) if off + (k - 15) * stride + n_ > 0 and off + (k - 15) * stride < L]
                            win = lambda k: hc[:, s_, pad + off + (k - 15) * stride:pad + off + (k - 15) * stride + n_]
                            wk_ = lambda k: cw31[:, l, j, k:k + 1]
                            extra = []
                            rest = list(taps)
                            for eng_, ntap, acc_r, accb_r in (("dve", CONV_ND, accd_r, accbd_r), ("pool", CONV_NP, accp_r, accbp_r)):
                                if ntap == 0 or len(rest) - ntap < 4:
                                    continue
                                mine, rest = rest[:ntap], rest[ntap:]
                                acc, acck = acc_r.next()
                                accb, accbk = accb_r.next()
                                c_ = (120.0 + n_ / 0.96) if eng_ == "dve" else (200.0 + n_ / 0.55)
                                for ii, k in enumerate(mine):
                                    last_ = ii == len(mine) - 1
                                    dst, dstk = (accb, accbk) if last_ else (acc, acck)
                                    if ii == 0:
                                        P.op(eng_, (lambda e, o=dst[:, 0:n_], i0=win(k), sc=wk_(k): e.tensor_scalar(
                                            out=o, in0=i0, scalar1=sc, scalar2=None, op0=ALU.mult)), [hck, "cw31"], [dstk], cost=c_)
                                    else:
                                        P.op(eng_, (lambda e, o=dst[:, 0:n_], i0=win(k), sc=wk_(k), i1=acc[:, 0:n_]: e.scalar_tensor_tensor(
                                            out=o, in0=i0, scalar=sc, in1=i1, op0=ALU.mult, op1=ALU.add)), [hck, "cw31", acck], [dstk], cost=c_)
                                extra.append((accb, accbk))
                            nmm = len(rest) + len(extra)
                            im = 0
                            for k in rest:
                                MM(pc[:, c0:c0 + n_], dg[:, k, :], win(k), im == 0, im == nmm - 1, [hck, dgk], [pck])
                                im += 1
                            for accb, accbk in extra:
                                MM(pc[:, c0:c0 + n_], identb[:], accb[:, 0:n_], im == 0, im == nmm - 1, ["identb", accbk], [pck])
                                im += 1
                        ACT(yT[:, 8 + j, t * 512:(t + 1) * 512], pc[:], AF.Identity, [pck, "cb31"], ytile([8 + j], t),
                            bias=cb31[:, l, j:j + 1])
            P.barrier()
            with ExitStack() as sb_:
                wgs = T([128, 8, 1024], BF16, "wgs", sb_)
                for j in range(8):
                    win_load(wgs[:, :, j * 128:(j + 1) * 128], I_GS + j * 128, 128, ("wgs", j))
                sq_r = Rot("sqc", [128, 8, 512], BF16, 2, sb_)
                mean_r = Rot("mean", [128, 512], F32, 2, sb_)
                rs_r = Rot("rsc", [128, 512], F32, 2, sb_)
                tmp_r = Rot("tmpc", [128, 512], F32, 2, sb_)
                s1_r = Rot("s1c", [128, 512], F32, 2, sb_)
                for t in range(NT):
                    tl = slice(t * 512, (t + 1) * 512)
                    sq, sqk = sq_r.next()
                    mean, meank = mean_r.next()
                    rs, rsk = rs_r.next()
                    p1, p1k = PS()
                    for k in range(8):
                        MM(p1[:], onesb[:], yT[:, 8 + k, tl], k == 0, k == 7, ["onesb"] + ytile([8 + k], t), [p1k])
                    ACT(sq[:], yT[:, 8:16, tl], AF.Square, ytile(range(8, 16), t), [sqk])
                    p2, p2k = PS()
                    for k in range(8):
                        MM(p2[:], onesb[:], sq[:, k, :], k == 0, k == 7, ["onesb", sqk], [p2k])
                    TS(mean[:], p1[:], 1.0 / 1024.0, ALU.mult, [p1k], [meank])
                    tmp, tmpk = tmp_r.next()
                    TT(tmp[:], mean[:], mean[:], ALU.mult, [meank], [tmpk])
                    STT(tmp[:], p2[:], 1.0 / 1024.0, tmp[:], ALU.mult, ALU.subtract, [p2k, tmpk], [tmpk])
                    ACT(rs[:], tmp[:], AF.Ln, [tmpk], [rsk], bias=EPS)
                    ACT(rs[:], rs[:], AF.Exp, [rsk], [rsk], scale=-0.5)
                    for j in range(8):
                        tmp, tmpk = tmp_r.next()
                        s1, s1k = s1_r.next()
                        TT(tmp[:], yT[:, 8 + j, tl], mean[:], ALU.subtract, ytile([8 + j], t) + [meank], [tmpk])
                        TT(tmp[:], tmp[:], rs[:], ALU.mult, [tmpk, rsk], [tmpk])
                        ACT(s1[:], tmp[:], AF.Silu, [tmpk, "lng", "lnb"], [s1k], bias=lnb[:, l, j:j + 1], scale=lng[:, l, j:j + 1])
                        pg, pgk = PS()
                        for kc in range(8):
                            MM(pg[:], wgs[:, kc, j * 128:(j + 1) * 128], hT[:, kc, tl], kc == 0, kc == 7, [("wgs", j), hk(t)], [pgk])
                        ACT(tmp[:], pg[:], AF.Silu, [pgk], [tmpk])
                        TT(yT[:, 8 + j, tl], s1[:], tmp[:], ALU.mult, [s1k, tmpk], ytile([8 + j], t))

            P.barrier()
            if debug and l == 0:
                DMA(dbg[f"d_yC_{nm}"], yT[:, :, 0:Ttok], allk, (), final=True)
            with ExitStack() as sc:
                wo = T([128, 16, 1024], BF16, "wo", sc)
                for fo in range(8):
                    DMA(wo[:, :, fo * 128:(fo + 1) * 128], wout_d[l].rearrange("(kc p) c -> p kc c", p=128)[:, :, fo * 128:(fo + 1) * 128],
                        (), [("wo", fo)], eng="pool")
                osb_r = Rot("osb", [128, 8, 512], F32, 2, sc)
                sq_r = Rot("sqo", [128, 8, 512], BF16, 1, sc)
                rs_r = Rot("rso", [128, 512], F32, 2, sc)
                xt_r = Rot("xto", [128, 8, 512], F32, 1, sc)
                for t in range(NT):
                    tl = slice(t * 512, (t + 1) * 512)
                    osb, osbk = osb_r.next()
                    sq, sqk = sq_r.next()
                    rs, rsk = rs_r.next()
                    xt, xtk = xt_r.next()
                    DMA(xt[:], xsrc_v[:, :, tl], [("xd", id(x_src), t)], [xtk])
                    for fo in range(8):
                        po, pok = PS()
                        for kc in range(16):
                            MM(po[:], wo[:, kc, fo * 128:(fo + 1) * 128], yT[:, kc, tl], kc == 0, kc == 15,
                               [("wo", fo)] + ytile([kc], t), [pok])
                        P.op("act", (lambda e, o=osb[:, fo, :], i=po[:]: e.copy(out=o, in_=i)), [pok], [(osbk, fo)], cost=590.0)
                        ACT(sq[:, fo, :], po[:], AF.Square, [pok], [(sqk, fo)])
                    stats_rs(lambda k: (sq[:, k, :], [(sqk, k)]), 8, rs, rsk, [])
                    for fo in range(8):
                        TT(osb[:, fo, :], osb[:, fo, :], rs[:], ALU.mult, [(osbk, fo), rsk], [(osbk, fo)])
                        STT(osb[:, fo, :], osb[:, fo, :], modG[:, l, fo, wsel:wsel + 1], xt[:, fo, :], ALU.mult, ALU.add,
                            [(osbk, fo), "modG", xtk], [(osbk, fo)])
                    allosb = [(osbk, k) for k in range(8)]
                    DMA(xdst_v[:, :, tl], osb[:], allosb, [("xd", id(x_dst), t)], final=final_out)
                    if fuse_next:
                        allsq = [(sqk, k) for k in range(8)]
                        ACT(sq[:], osb[:], AF.Square, allosb, allsq)
                        rs2, rs2k = rs_r.next()
                        stats_rs(lambda k: (sq[:, k, :], [(sqk, k)]), 8, rs2, rs2k, [])
                        TT(xt[:], osb[:], rs2[:].unsqueeze(1).to_broadcast([128, 8, 512]), ALU.mult, allosb + [rs2k], [xtk])
                        for kc in range(8):
                            ACT(hT[:, kc, tl], xt[:, kc, :], AF.Identity, [xtk, "modA", "modB"], [hk(t)],
                                bias=modB[:, l + 1, kc, wsel:wsel + 1], scale=modA[:, l + 1, kc, wsel:wsel + 1],
                                group=("hTn", l, t, nseq))
            P.barrier()

        for nm_ in ("P", "S"):
            for l in range(DEPTH):
                last = (l == DEPTH - 1)
                if only is not None and (l, nm_) not in only:
                    continue
                if nm_ == "P":
                    run_block(l, xp_d if l == 0 else x1p_d, yp_d if last else x1p_d, 2, 256, 1, 0, None, ns_d, last or debug,
                              l == 0 or debug, (not last) and not debug)
                else:
                    run_block(l, xs_d if l == 0 else x1s_d, ys_d if last else x1s_d, 1, 2048, 64, 1, h0_d, None, last or debug,
                              l == 0 or debug, (not last) and not debug)
        P.emit()
        n_ins = len(P.ins)
    return nc, n_ins


_CACHE = {}


def _fm(v):
    v = np.asarray(v, np.float32)
    lead = v.shape[:-1]
    nchunk = v.shape[-1] // 128
    r = v.reshape(lead + (nchunk, 128))
    return np.ascontiguousarray(np.moveaxis(r, -1, 0))


def kernel(x_prompt, x_sample, state_ssd, c, c_ctx, w_mod, b_mod, g_pre, g_post, w_in,
           ssd_conv_w, ssd_conv_b, ssd_a_log, ssd_dt_bias, ssd_d, ssd_norm_g,
           conf_conv_w, conf_conv_b, conf_ln_g, conf_ln_b, w_out):
    f = lambda a: np.ascontiguousarray(np.asarray(a, np.float32))
    x_prompt, x_sample, state_ssd = f(x_prompt), f(x_sample), f(state_ssd)
    if "nc" not in _CACHE:
        _CACHE["nc"] = build_program()[0]
    nc = _CACHE["nc"]
    rep = lambda a: np.ascontiguousarray(np.broadcast_to(f(a).reshape(1, DEPTH, -1), (128, DEPTH, f(a).reshape(DEPTH, -1).shape[1])))
    shared = {
        "w_mod": f(w_mod), "b_mod": _fm(b_mod), "g_pre": _fm(g_pre), "g_post": _fm(g_post), "w_in": f(w_in),
        "cw5": np.ascontiguousarray(np.transpose(f(ssd_conv_w).reshape(DEPTH, 5, 12, 128), (3, 0, 2, 1))),
        "cb5": _fm(ssd_conv_b), "cb5row": f(ssd_conv_b).reshape(1, DEPTH * 1536),
        "alog": rep(ssd_a_log), "dtb": rep(ssd_dt_bias), "dsk": rep(ssd_d), "sng": _fm(ssd_norm_g),
        "cw31": np.ascontiguousarray(np.transpose(f(conf_conv_w).reshape(DEPTH, 31, 8, 128), (3, 0, 2, 1))),
        "cb31": _fm(conf_conv_b), "lng": _fm(conf_ln_g), "lnb": _fm(conf_ln_b), "w_out": f(w_out),
    }
    in_maps = []
    for core in range(NCORES):
        b = core // 4
        m = dict(shared)
        m["xp"] = np.ascontiguousarray(x_prompt[2 * core:2 * core + 2].reshape(512, D).T)
        m["xs"] = np.ascontiguousarray(x_sample[b].T)
        m["h0"] = np.ascontiguousarray(state_ssd[b].reshape(DEPTH, 2, 1024, 128))
        cv = np.stack([f(c_ctx), f(c)[b]], axis=-1)
        m["cvec"] = np.ascontiguousarray(np.transpose(cv.reshape(8, 128, 2), (1, 0, 2)))
        in_maps.append(m)
    res = run_bass_kernel_spmd(nc, in_maps, core_ids=list(range(NCORES)))
    r = res.results
    y_prompt = np.stack([r[core]["yp"].T.reshape(2, 256, D) for core in range(NCORES)], 0).reshape(16, 256, D)
    y_sample = np.stack([r[0]["ys"].T, r[4]["ys"].T], 0)
    new_state = np.concatenate([r[core]["ns"] for core in range(NCORES)], 0).reshape(16, DEPTH, 2, 16, 64, 128)
    return (np.ascontiguousarray(y_prompt, dtype=np.float32), np.ascontiguousarray(y_sample, dtype=np.float32),
            np.ascontiguousarray(new_state, dtype=np.float32))
```

```python
import numpy as np
from contextlib import ExitStack
import concourse.bass as bass
import concourse.mybir as mybir
from concourse.bass_utils import run_bass_kernel_spmd

F32 = mybir.dt.float32
BF16 = mybir.dt.bfloat16
AF = mybir.ActivationFunctionType
ALU = mybir.AluOpType

D = 1024
DEPTH = 2
NCORES = 8
EPS = 1e-6
I_Z, I_X, I_B, I_C, I_DT, I_GA, I_GB, I_GS = 0, 1024, 2048, 2304, 2560, 2592, 3616, 4640
IN_COLS = 5664
HP = 4
WP = HP * 64
TMAX = 2048
NPS = 7
CONV_ND = 8
CONV_NP = 0
NROT = 3
OFF_ENG = "pool"


class Prog:
    SEM_LIMIT = 4000
    WINDOW = 128
    SEM_LAT = 160.0

    def __init__(self, nc, stack, same_engine_sync=True, schedule=True):
        self.nc = nc
        self.stack = stack
        self.engs = {"pe": nc.tensor, "act": nc.scalar, "dve": nc.vector, "pool": nc.gpsimd, "sp": nc.sync}
        self.ins = []
        self.last_w = {}
        self.readers = {}
        self.same_engine_sync = same_engine_sync
        self.schedule = schedule
        self.n_dma_sems = {"sp": 16, "pool": 8, "act": 4, "dve": 4, "pe": 4}
        self.out_dmas = []
        self.w_rdeps = {}
        self.phase = 0

    def barrier(self):
        self.phase += 1

    def op(self, eng, fn, reads=(), writes=(), dma=False, final=False, cost=300.0, lat=0.0, group=None):
        deps = set()
        for r in reads:
            if r in self.last_w:
                deps |= set(self.last_w[r][1])
        i = len(self.ins)
        for w in writes:
            same = False
            if w in self.last_w:
                gid, members = self.last_w[w]
                same = group is not None and gid == group
                if not same:
                    deps |= set(members)
            if same:
                deps |= self.w_rdeps.get(w, set())
            else:
                rd = set(self.readers.get(w, set()))
                deps |= rd
                self.w_rdeps[w] = (set(self.last_w[w][1]) if w in self.last_w else set()) | rd
        deps.discard(i)
        self.ins.append(dict(eng=eng, fn=fn, deps=deps, dma=dma, cost=cost, lat=lat, phase=self.phase))
        for r in reads:
            self.readers.setdefault(r, set()).add(i)
        for w in writes:
            if w in self.last_w and group is not None and self.last_w[w][0] == group:
                self.last_w[w][1].append(i)
            else:
                self.last_w[w] = (group, [i])
                self.readers[w] = set()
        if final:
            self.out_dmas.append(i)
        return i

    def _order(self):
        ins = self.ins
        n = len(ins)
        per_eng = {e: [] for e in self.engs}
        for i, it in enumerate(ins):
            per_eng[it["eng"]].append(i)
        if not self.schedule:
            return per_eng
        users = [[] for _ in range(n)]
        nun = [0] * n
        for i, it in enumerate(ins):
            nun[i] = len(it["deps"])
            for d in it["deps"]:
                users[d].append(i)
        blev = [0.0] * n
        for i in range(n - 1, -1, -1):
            it = ins[i]
            m = 0.0
            for u in users[i]:
                if ins[u]["phase"] == it["phase"] and blev[u] > m:
                    m = blev[u]
            blev[i] = it["cost"] + it["lat"] + m
        rdy = [0.0] * n
        self.t_start = [0.0] * n
        self.t_fin = [0.0] * n
        nphase = self.phase + 1
        left = [0] * nphase
        for it in ins:
            left[it["phase"]] += 1
        cur = 0
        while cur < nphase and left[cur] == 0:
            cur += 1
        phase_t = 0.0
        tmax = 0.0
        eng_free = {e: 0.0 for e in self.engs}
        pend = {e: list(v) for e, v in per_eng.items()}
        order = {e: [] for e in self.engs}
        remaining = n
        while remaining:
            best = None
            for e, lst in pend.items():
                cand = None
                ef = eng_free[e]
                for i in lst[:self.WINDOW]:
                    it = ins[i]
                    if it["phase"] != cur:
                        break
                    if nun[i]:
                        continue
                    stt = max(rdy[i], ef, phase_t)
                    key = (stt, -blev[i]) if stt > ef + 1e-9 else (ef, -blev[i])
                    if cand is None or key < cand[2]:
                        cand = (key[0], i, key)
                if cand is not None and (best is None or cand[0] < best[0] - 1e-9 or
                                         (abs(cand[0] - best[0]) <= 1e-9 and cand[1] < best[1])):
                    best = (cand[0], cand[1], e)
            assert best is not None, "scheduler stuck"
            stt, i, e = best
            it = ins[i]
            eng_free[e] = stt + it["cost"]
            f = stt + it["cost"] + it["lat"]
            self.t_start[i] = stt
            self.t_fin[i] = f
            tmax = max(tmax, f)
            for u in users[i]:
                nun[u] -= 1
                fl = f if (ins[u]["eng"] == e and e == "pe" and not it["dma"]) else f + self.SEM_LAT
                if fl > rdy[u]:
                    rdy[u] = fl
            pend[e].remove(i)
            order[e].append(i)
            remaining -= 1
            left[cur] -= 1
            if left[cur] == 0:
                while cur < nphase and left[cur] == 0:
                    cur += 1
                phase_t = tmax + 200.0
        self.sim_time = tmax
        return order

    def emit(self):
        nc = self.nc
        ins = self.ins
        n = len(ins)
        order = self._order()
        pos = [0] * n
        for e, lst in order.items():
            for k, i in enumerate(lst):
                pos[i] = k
        last_before = {}
        for e, lst in order.items():
            cuts = {}
            for k, i in enumerate(lst):
                cuts.setdefault(ins[i]["phase"], k)
            last_before[e] = (lst, cuts)
        for e, lst in order.items():
            seen = -1
            for i in lst:
                p = ins[i]["phase"]
                if p == seen:
                    continue
                seen = p
                if p == 0:
                    continue
                extra = set()
                for e2, (lst2, cuts2) in last_before.items():
                    ks = [k for ph, k in cuts2.items() if ph >= p]
                    endk = min(ks) if ks else len(lst2)
                    if endk == 0:
                        continue
                    extra.add(lst2[endk - 1])
                    nd = self.n_dma_sems[e2]
                    cnt = 0
                    for k in range(endk - 1, -1, -1):
                        if ins[lst2[k]]["dma"]:
                            extra.add(lst2[k])
                            cnt += 1
                            if cnt >= nd:
                                break
                extra.discard(i)
                ins[i]["deps"] = set(ins[i]["deps"]) | extra
        pruned = [None] * n
        for i, it in enumerate(ins):
            e = it["eng"]
            keep = {}
            dmas = []
            for d in it["deps"]:
                p = ins[d]
                if p["dma"]:
                    dmas.append(d)
                    continue
                if p["eng"] == e and (e == "pe" or not self.same_engine_sync):
                    continue
                pe_ = p["eng"]
                if pe_ not in keep or pos[d] > pos[keep[pe_]]:
                    keep[pe_] = d
            pruned[i] = list(keep.values()) + dmas
        needed = [False] * n
        for i in range(n):
            for d in pruned[i]:
                needed[d] = True
        for i in self.out_dmas:
            needed[i] = True
        sem_of = [None] * n
        dma_prev = [None] * n
        for e, lst in order.items():
            nd = self.n_dma_sems[e]
            dsems = None
            dcnt = None
            rr = 0
            cur = None
            ccnt = 0
            k = 0
            for i in lst:
                it = ins[i]
                if it["dma"]:
                    if dsems is None:
                        dsems = [self.stack.enter_context(nc.semaphore(f"dq_{e}_{j}")) for j in range(nd)]
                        dcnt = [0] * nd
                    j = rr
                    rr = (rr + 1) % nd
                    if dcnt[j] > 0:
                        dma_prev[i] = (dsems[j], dcnt[j])
                    dcnt[j] += 16
                    sem_of[i] = (dsems[j], dcnt[j])
                elif needed[i]:
                    if cur is None or ccnt >= self.SEM_LIMIT:
                        cur = self.stack.enter_context(nc.semaphore(f"s_{e}_{k}"))
                        k += 1
                        ccnt = 0
                    ccnt += 1
                    sem_of[i] = (cur, ccnt)
        for e, lst in order.items():
            eng = self.engs[e]
            waited = {}

            def do_wait(sem, cnt):
                key = id(sem)
                if waited.get(key, 0) >= cnt:
                    return
                eng.wait_ge(sem, cnt)
                waited[key] = cnt

            for i in lst:
                it = ins[i]
                ws = [sem_of[d] for d in pruned[i]]
                ws.sort(key=lambda sc: -sc[1])
                for sem, c in ws:
                    do_wait(sem, c)
                if it["dma"] and dma_prev[i] is not None:
                    do_wait(*dma_prev[i])
                inst = it["fn"](eng)
                if sem_of[i] is not None:
                    inst.then_inc(sem_of[i][0], 16 if it["dma"] else 1)
            if e == "sp":
                for i in self.out_dmas:
                    do_wait(*sem_of[i])


def build_program(debug=False, only=None):
    nc = bass.Bass("TRN2", target_bir_lowering=False)
    dt_in = lambda name, shape: nc.dram_tensor(name, shape, F32, kind="ExternalInput").ap()
    dt_out = lambda name, shape: nc.dram_tensor(name, shape, F32, kind="ExternalOutput").ap()
    xp_d = dt_in("xp", [D, 512])
    xs_d = dt_in("xs", [D, 2048])
    h0_d = dt_in("h0", [DEPTH, 2, 1024, 128])
    cvec_d = dt_in("cvec", [128, 8, 2])
    wmod_d = dt_in("w_mod", [DEPTH, D, 3 * D])
    bmod_d = dt_in("b_mod", [128, DEPTH, 24])
    gpre_d = dt_in("g_pre", [128, DEPTH, 8])
    gpost_d = dt_in("g_post", [128, DEPTH, 8])
    win_d = dt_in("w_in", [DEPTH, D, IN_COLS])
    cw5_d = dt_in("cw5", [128, DEPTH, 12, 5])
    cb5_d = dt_in("cb5", [128, DEPTH, 12])
    cb5row_d = dt_in("cb5row", [1, DEPTH * 1536])
    alog_d = dt_in("alog", [128, DEPTH, 32])
    dtb_d = dt_in("dtb", [128, DEPTH, 32])
    dsk_d = dt_in("dsk", [128, DEPTH, 16])
    sng_d = dt_in("sng", [128, DEPTH, 8])
    cw31_d = dt_in("cw31", [128, DEPTH, 8, 31])
    cb31_d = dt_in("cb31", [128, DEPTH, 8])
    lng_d = dt_in("lng", [128, DEPTH, 8])
    lnb_d = dt_in("lnb", [128, DEPTH, 8])
    wout_d = dt_in("w_out", [DEPTH, 2 * D, D])
    yp_d = dt_out("yp", [D, 512])
    ys_d = dt_out("ys", [D, 2048])
    ns_d = dt_out("ns", [2, DEPTH, 2, 1024, 128])
    dbg = {}
    if debug:
        dbg["d_modA"] = dt_out("d_modA", [128, DEPTH, 8, 2])
        dbg["d_modB"] = dt_out("d_modB", [128, DEPTH, 8, 2])
        dbg["d_modG"] = dt_out("d_modG", [128, DEPTH, 8, 2])
        for nm, T_ in (("P", 512), ("S", 2048)):
            dbg[f"d_hT_{nm}"] = nc.dram_tensor(f"d_hT_{nm}", [128, 8, T_], BF16, kind="ExternalOutput").ap()
            dbg[f"d_yA_{nm}"] = nc.dram_tensor(f"d_yA_{nm}", [128, 16, T_], BF16, kind="ExternalOutput").ap()
            dbg[f"d_yB_{nm}"] = nc.dram_tensor(f"d_yB_{nm}", [128, 16, T_], BF16, kind="ExternalOutput").ap()
            dbg[f"d_yC_{nm}"] = nc.dram_tensor(f"d_yC_{nm}", [128, 16, T_], BF16, kind="ExternalOutput").ap()
    x1p_d = nc.dram_tensor("x1p", [D, 512], F32, kind="ExternalOutput" if debug else "Internal").ap()
    x1s_d = nc.dram_tensor("x1s", [D, 2048], F32, kind="ExternalOutput" if debug else "Internal").ap()

    with ExitStack() as st:
        P = Prog(nc, st)
        cnt = [0]

        def T(shape, dt, name=None, stack=None):
            cnt[0] += 1
            return (stack or st).enter_context(nc.sbuf_tensor(f"sb{cnt[0]}_{name or 't'}", shape, dt))

        def nfree(ap):
            r = 1
            for d in ap.shape[1:]:
                r *= d
            return r

        def DMA(out, in_, reads=(), writes=(), eng="sp", final=False):
            nbytes = nfree(out) * out.shape[0] * 4
            P.op(eng, lambda e: e.dma_start(out=out, in_=in_), reads, writes, dma=True, final=final,
                 cost=(150.0 if eng == "sp" else 1200.0), lat=2000.0 + nbytes / 120.0)

        def MM(out, lhsT, rhs, start, stop, reads, writes):
            passes = 4 if lhsT.dtype == F32 else 1
            P.op("pe", lambda e: e.matmul(out, lhsT=lhsT, rhs=rhs, start=start, stop=stop), reads, writes,
                 cost=30.0 + passes * max(nfree(rhs), 64) / 2.4, lat=120.0)

        def TR(out, in_, ident, reads, writes):
            P.op("pe", lambda e: e.transpose(out=out, in_=in_, identity=ident), reads, writes,
                 cost=(4 if in_.dtype == F32 else 1) * 60.0 + 30.0, lat=120.0)

        def ACT(out, in_, func, reads, writes, bias=None, scale=None, group=None):
            kw = {}
            if bias is not None:
                kw["bias"] = bias
            if scale is not None:
                kw["scale"] = scale
            P.op("act", lambda e: e.activation(out=out, in_=in_, func=func, **kw), reads, writes, cost=220.0 + nfree(out) / 1.4,
                 group=group)

        def TT(out, in0, in1, op, reads, writes, eng="dve"):
            c = 120.0 + nfree(out) / 0.96 if eng != "pool" else 200.0 + nfree(out) / 0.55
            P.op(eng, lambda e: e.tensor_tensor(out=out, in0=in0, in1=in1, op=op), reads, writes, cost=c)

        def TS(out, in0, s1, op0, reads, writes, s2=None, op1=None, eng="dve"):
            if op1 is None:
                P.op(eng, lambda e: e.tensor_scalar(out=out, in0=in0, scalar1=s1, scalar2=None, op0=op0), reads, writes,
                     cost=120.0 + nfree(out) / 0.96)
            else:
                P.op(eng, lambda e: e.tensor_scalar(out=out, in0=in0, scalar1=s1, scalar2=s2, op0=op0, op1=op1), reads, writes,
                     cost=120.0 + nfree(out) / 0.96)

        def STT(out, in0, scalar, in1, op0, op1, reads, writes, eng="dve"):
            P.op(eng, lambda e: e.scalar_tensor_tensor(out=out, in0=in0, scalar=scalar, in1=in1, op0=op0, op1=op1), reads, writes,
                 cost=120.0 + nfree(out) / 0.96)

        def CP(out, in_, reads, writes, eng="dve"):
            P.op(eng, lambda e: e.tensor_copy(out=out, in_=in_), reads, writes, cost=120.0 + nfree(out) / 0.96)

        def RECIP(out, in_, reads, writes):
            P.op("dve", lambda e: e.reciprocal(out=out, in_=in_), reads, writes, cost=120.0 + nfree(out) * 6.5)

        def MEMSET(ap, val, writes, eng="pool"):
            P.op(eng, lambda e: e.memset(ap, val), (), writes, cost=150.0 + nfree(ap) / 1.0)

        class Rot:
            def __init__(self, name, shape, dt, n, stack):
                self.t = [T(shape, dt, f"{name}{i}", stack) for i in range(n)]
                self.name = name
                self.i = 0

            def next(self):
                k = self.i % len(self.t)
                self.i += 1
                return self.t[k], (self.name, k)

        ps_t = [st.enter_context(nc.psum_tensor(f"ps{i}", [128, 512], F32)) for i in range(NPS)]
        psb_t = st.enter_context(nc.psum_tensor("psb", [128, 1024], BF16))
        ps_i = [0]

        def PS():
            k = ps_i[0] % NPS
            ps_i[0] += 1
            return ps_t[k], ("ps", k)

        PSH = PS

        ident = T([128, 128], F32, "ident")
        identb = T([128, 128], BF16, "identb")
        onesb = T([128, 128], BF16, "onesb")
        onesf = T([128, 128], F32, "onesf")
        Uf = T([128, 128], F32, "Uf")
        SLf = T([128, 128], F32, "SLf")
        Ub = T([128, 128], F32, "Ub")
        SLb = T([128, 128], F32, "SLb")
        MEMSET(onesf[:], 1.0, ["onesf"])
        MEMSET(onesb[:], 1.0, ["onesb"])

        def SEL(t, key, cm, pat, op):
            MEMSET(t[:], 1.0, [key])
            P.op("pool", lambda e: e.affine_select(out=t[:], in_=t[:], pattern=[[pat, 128]], compare_op=op,
                                                   fill=0.0, base=0, channel_multiplier=cm), [key], [key])
        SEL(Uf, "Uf", -1, 1, ALU.is_ge)
        SEL(SLf, "SLf", 1, -1, ALU.is_gt)
        SEL(Ub, "Ub", 1, -1, ALU.is_ge)
        SEL(SLb, "SLb", -1, 1, ALU.is_gt)
        MEMSET(ident[:], 0.0, ["ident"])
        P.op("pool", lambda e: e.affine_select(out=ident[:], in_=ident[:], pattern=[[-1, 128]], compare_op=ALU.not_equal,
                                               fill=1.0, base=0, channel_multiplier=1), ["ident"], ["ident"])
        CP(identb[:], ident[:], ["ident"], ["identb"])

        def LOADP(dram, shape, name):
            t = T(shape, F32, name)
            DMA(t[:], dram, (), [name])
            return t
        cvec = LOADP(cvec_d, [128, 8, 2], "cvec")
        bmod = LOADP(bmod_d, [128, DEPTH, 24], "bmod")
        gpre = LOADP(gpre_d, [128, DEPTH, 8], "gpre")
        gpost = LOADP(gpost_d, [128, DEPTH, 8], "gpost")
        cw5 = LOADP(cw5_d, [128, DEPTH, 12, 5], "cw5")
        cb5 = LOADP(cb5_d, [128, DEPTH, 12], "cb5")
        alog = LOADP(alog_d, [128, DEPTH, 32], "alog")
        dtb = LOADP(dtb_d, [128, DEPTH, 32], "dtb")
        dsk = LOADP(dsk_d, [128, DEPTH, 16], "dsk")
        sng = LOADP(sng_d, [128, DEPTH, 8], "sng")
        cw31 = LOADP(cw31_d, [128, DEPTH, 8, 31], "cw31")
        cb31 = LOADP(cb31_d, [128, DEPTH, 8], "cb31")
        lng = LOADP(lng_d, [128, DEPTH, 8], "lng")
        lnb = LOADP(lnb_d, [128, DEPTH, 8], "lnb")
        cb5row = T([1, DEPTH * 1536], BF16, "cb5row")
        for l_ in range(DEPTH):
            DMA(cb5row[:, l_ * 1536:(l_ + 1) * 1536], cb5row_d[:, l_ * 1536:(l_ + 1) * 1536], (), ["cb5row"], eng="pool")
        aneg = T([128, DEPTH, 32], F32, "aneg")
        ACT(aneg[:], alog[:], AF.Exp, ["alog"], ["aneg"])
        TS(aneg[:], aneg[:], -1.0, ALU.mult, ["aneg"], ["aneg"])

        silc = T([128, 8, 2], F32, "silc")
        ACT(silc[:], cvec[:], AF.Silu, ["cvec"], ["silc"])
        modA = T([128, DEPTH, 8, 2], F32, "modA")
        modB = T([128, DEPTH, 8, 2], F32, "modB")
        modG = T([128, DEPTH, 8, 2], F32, "modG")
        hT = T([128, 8, TMAX], BF16, "hT")
        yT = T([128, 16, TMAX], BF16, "yT")
        ms = ExitStack()
        st.callback(ms.close)
        if True:
            wm = Rot("wm", [128, 8, 512], F32, 2, ms)
            modsb = T([128, 24, 2], F32, "modsb", ms)
            modrow = T([2, 3 * D], F32, "modrow", ms)
            for l in range(DEPTH):
                for cb in range(6):
                    wt, wk = wm.next()
                    DMA(wt[:], wmod_d[l].rearrange("(kc p) c -> p kc c", p=128)[:, :, cb * 512:(cb + 1) * 512], (), [wk])
                    pr, prk = PS()
                    for kc in range(8):
                        MM(pr[0:2, :], silc[:, kc, :], wt[:, kc, :], kc == 0, kc == 7, [wk, "silc"], [prk])
                    CP(modrow[:, cb * 512:(cb + 1) * 512], pr[0:2, :], [prk], [("modrow", cb)])
                pm, pmk = PSH()
                for f in range(24):
                    TR(pm[:, f * 2:f * 2 + 2], modrow[0:2, f * 128:(f + 1) * 128], ident[0:2, 0:2], [("modrow", f // 4), "ident"], [pmk])
                TT(modsb[:], pm[:, 0:48].rearrange("p (f w) -> p f w", w=2),
                   bmod[:, l, :].unsqueeze(2).to_broadcast([128, 24, 2]), ALU.add, [pmk, "bmod"], ["modsb"])
                TS(modA[:, l], modsb[:, 8:16, :], 1.0, ALU.add, ["modsb"], ["modA"])
                TT(modA[:, l], modA[:, l], gpre[:, l, :].unsqueeze(2).to_broadcast([128, 8, 2]), ALU.mult, ["modA", "gpre"], ["modA"])
                CP(modB[:, l], modsb[:, 0:8, :], ["modsb"], ["modB"])
                TT(modG[:, l], modsb[:, 16:24, :], gpost[:, l, :].unsqueeze(2).to_broadcast([128, 8, 2]), ALU.mult,
                   ["modsb", "gpost"], ["modG"])

        if debug:
            DMA(dbg["d_modA"], modA[:], ["modA"], (), final=True)
            DMA(dbg["d_modB"], modB[:], ["modB"], (), final=True)
            DMA(dbg["d_modG"], modG[:], ["modG"], (), final=True)

        def stats_rs(src_sq_fn, nk, rs, rsk, extra_reads, eps=EPS):
            pst, pstk = PS()
            for k in range(nk):
                ap, rd = src_sq_fn(k)
                MM(pst[:], onesb[:], ap, k == 0, k == nk - 1, ["onesb"] + rd, [pstk])
            ACT(rs[:], pst[:], AF.Ln, [pstk], [rsk], bias=eps, scale=1.0 / 1024.0)
            ACT(rs[:], rs[:], AF.Exp, [rsk], [rsk], scale=-0.5)

        ms_holder = [ms]

        def run_block(l, x_src, x_dst, nseq, L, stride, wsel, h0, ns_out, final_out, do_front, fuse_next):
            Ttok = nseq * L
            NT = Ttok // 512
            nch = L // 128
            nblk = nseq * nch
            xsrc_v = x_src.rearrange("(kc p) t -> p kc t", p=128)
            xdst_v = x_dst.rearrange("(kc p) t -> p kc t", p=128)
            hk = lambda t: ("hT", t)
            yk = lambda k, b: ("yT", k, b)
            ytile = lambda ks, t: [yk(k, b) for k in ks for b in range(4 * t, 4 * t + 4)]

            def segs(t):
                if L >= 512:
                    per = L // 512
                    return [(t // per, (t % per) * 512, 512, 0)]
                n = 512 // L
                return [(t * n + i, 0, L, i * L) for i in range(n)]

            def win_load(dst, c0, w, key):
                DMA(dst, win_d[l].rearrange("(kc p) c -> p kc c", p=128)[:, :, c0:c0 + w], (), [key], eng="pool")

            with ExitStack() as s0:
                xt_r = Rot("xt", [128, 8, 512], F32, 2, s0)
                sq_r = Rot("sq0", [128, 8, 512], BF16, 2, s0)
                rs_r = Rot("rs0", [128, 512], F32, 2, s0)
                for t in range(NT if do_front else 0):
                    xt, xtk = xt_r.next()
                    sq, sqk = sq_r.next()
                    rs, rsk = rs_r.next()
                    DMA(xt[:], xsrc_v[:, :, t * 512:(t + 1) * 512], [("xd", id(x_src), t)], [xtk])
                    ACT(sq[:], xt[:], AF.Square, [xtk], [sqk])
                    stats_rs(lambda k: (sq[:, k, :], [sqk]), 8, rs, rsk, [])
                    TT(xt[:], xt[:], rs[:].unsqueeze(1).to_broadcast([128, 8, 512]), ALU.mult, [xtk, rsk], [xtk])
                    for kc in range(8):
                        ACT(hT[:, kc, t * 512:(t + 1) * 512], xt[:, kc, :], AF.Identity, [xtk, "modA", "modB"], [hk(t)],
                            bias=modB[:, l, kc, wsel:wsel + 1], scale=modA[:, l, kc, wsel:wsel + 1], group=("hTf", l, t, nseq))

            if ms_holder:
                ms_holder.pop().close()
            P.barrier()
            nm = "P" if nseq == 2 else "S"
            allk = [yk(k, b) for k in range(16) for b in range(nblk)]
            if debug and l == 0:
                DMA(dbg[f"d_hT_{nm}"], hT[:, :, 0:Ttok], [hk(t) for t in range(NT)], (), final=True)
            with ExitStack() as sa:
                wx = T([128, 8, WP], BF16, "wx", sa)
                wB = T([128, 8, 128], BF16, "wB", sa)
                wC = T([128, 8, 128], BF16, "wC", sa)
                wz = T([128, 8, WP], BF16, "wz", sa)
                wdt = T([128, 8, 32], BF16, "wdt", sa)
                upad_r = Rot("upad", [128, nseq, L + 4], BF16, 2, sa)
                diag5_r = Rot("diag5", [128, 5, 128], BF16, 2, sa)
                xg = T([128, nblk, WP], BF16, "xg", sa)
                Btm = T([128, nblk, 128], BF16, "Btm", sa)
                Bfm = T([128, Ttok], BF16, "Bfm", sa)
                Cfm = T([128, Ttok], BF16, "Cfm", sa)
                dt_all = T([128, nblk, 32], F32, "dt_all", sa)
                la_all = T([128, nblk, 32], F32, "la_all", sa)
                v_all = T([128, nblk, 32], F32, "v_all", sa)
                cum_sb = [T([128, nblk, HP], F32, f"cum{d}", sa) for d in range(2)]
                cum_hi = [T([128, nblk, HP], BF16, f"cumhi{d}", sa) for d in range(2)]
                cum_lo = [T([128, nblk, HP], BF16, f"cumlo{d}", sa) for d in range(2)]
                decs_all = [T([128, 3, nblk, HP], F32, f"decs{d}", sa) for d in range(2)]
                diagD = T([128, HP, 128], BF16, "diagD", sa)
                ypark = T([128, nblk, WP], F32, "ypark", sa)
                Sf = [T([128, WP], F32, f"Sf{d}", sa) for d in range(2)]
                Sb = [T([128, WP], BF16, f"Sb{d}", sa) for d in range(2)]
                cbm_r = Rot("cbm", [128, 128], BF16, NROT, sa)
                xdt_r = Rot("xdt", [128, HP, 64], BF16, NROT, sa)
                xs2_r = Rot("xs2", [128, HP, 64], BF16, NROT, sa)
                Lh_r = Rot("Lh", [128, HP, 128], BF16, NROT, sa)
                La_r = Rot("La", [128, HP, 128], F32, 2, sa)
                Mh_r = Rot("Mh", [128, HP, 128], BF16, NROT, sa)
                t1_r = Rot("t1", [128, HP, 64], F32, NROT, sa)
                zs_r = Rot("zs", [128, WP], F32, 2, sa)
                yg_r = Rot("yg", [128, WP], BF16, 2, sa)
                stg_r = Rot("stg", [128, 128], F32, 2, sa)
                for r in upad_r.t:
                    MEMSET(r[:], 0.0, [("upad", upad_r.t.index(r))])

                win_load(wdt[:], I_DT, 32, "wdt")
                for bi in range(nblk):
                    t = bi // 4
                    pd, pdk = PSH()
                    for kc in range(8):
                        MM(pd[:, 0:32], hT[:, kc, bi * 128:(bi + 1) * 128], wdt[:, kc, :], kc == 0, kc == 7, [hk(t), "wdt"], [pdk])
                    TT(v_all[:, bi, :], pd[:, 0:32], dtb[:, l, :], ALU.add, [pdk, "dtb"], ["v_all"])
                TS(dt_all[:], v_all[:], 30.0, ALU.min, ["v_all"], ["dt_all"])
                ACT(dt_all[:], dt_all[:], AF.Exp, ["dt_all"], ["dt_all"])
                ACT(dt_all[:], dt_all[:], AF.Ln, ["dt_all"], ["dt_all"], bias=1.0)
                TT(dt_all[:], dt_all[:], v_all[:], ALU.max, ["dt_all", "v_all"], ["dt_all"])
                TT(la_all[:], dt_all[:], aneg[:, l, :].unsqueeze(1).to_broadcast([128, nblk, 32]), ALU.mult, ["dt_all", "aneg"], ["la_all"])
                for q in range(16 // HP):
                    g = (q * HP) // 8
                    nb4 = nblk * HP
                    win_load(wx[:], I_X + q * WP, WP, "wx")
                    if (q * HP) % 8 == 0:
                        win_load(wB[:], I_B + g * 128, 128, "wB")
                        win_load(wC[:], I_C + g * 128, 128, "wC")
                    win_load(wz[:], I_Z + q * WP, WP, "wz")
                    nxc = WP // 128
                    chunks = [("x", a, q * nxc + a, wx, a * 128, "wx") for a in range(nxc)]
                    if (q * HP) % 8 == 0:
                        chunks += [("B", 0, 8 + g, wB, 0, "wB"), ("C", 0, 10 + g, wC, 0, "wC")]
                    for kind, a, cidx, wt, wc0, wk in chunks:
                        upad, upk = upad_r.next()
                        dg, dgk = diag5_r.next()
                        TT(dg[:], ident[:].unsqueeze(1).to_broadcast([128, 5, 128]),
                           cw5[:, l, cidx, :].unsqueeze(2).to_broadcast([128, 5, 128]), ALU.mult, ["ident", "cw5"], [dgk])
                        for t in range(NT):
                            pu, puk = PS()
                            for kc in range(8):
                                MM(pu[:], wt[:, kc, wc0:wc0 + 128], hT[:, kc, t * 512:(t + 1) * 512], kc == 0, kc == 7,
                                   [wk, hk(t)], [puk])
                            if L >= 512:
                                s_, off, n_, c0 = segs(t)[0]
                                P.op("act", (lambda e, o=upad[:, s_, 2 + off:2 + off + 512], i=pu[:]: e.copy(out=o, in_=i)),
                                     [puk], [upk], cost=590.0)
                            else:
                                n = 512 // L
                                P.op("act", (lambda e, o=upad[:, t * n:(t + 1) * n, 2:2 + L],
                                             i=pu[:].rearrange("p (s x) -> p s x", s=n): e.copy(out=o, in_=i)), [puk], [upk], cost=590.0)
                        if kind in ("x", "B"):
                            for s_ in range(nseq):
                                for j in range(nch):
                                    bi = s_ * nch + j
                                    pc, pck = PSH()
                                    for k in range(5):
                                        MM(pc[:, 0:128], upad[:, s_, j * 128 + k:j * 128 + k + 128], dg[:, k, :], k == 0, False,
                                           [upk, dgk], [pck])
                                    MM(pc[:, 0:128], onesb[0:1, 0:128], cb5row[0:1, l * 1536 + cidx * 128:l * 1536 + (cidx + 1) * 128],
                                       False, True, ["onesb", "cb5row"], [pck])
                                    if kind == "x":
                                        ACT(xg[:, bi, a * 128:(a + 1) * 128], pc[:, 0:128], AF.Silu, [pck], [("xg", bi)], group=("xg", l, nseq, q, bi))
                                    else:
                                        ACT(Btm[:, bi, :], pc[:, 0:128], AF.Silu, [pck], [("Btm", bi)])
                        if kind in ("B", "C"):
                            dstT, dkey = (Bfm, "Bfm") if kind == "B" else (Cfm, "Cfm")
                            for t in range(NT):
                                pc, pck = PS()
                                for (s_, off, n_, c0) in segs(t):
                                    for k in range(5):
                                        MM(pc[:, c0:c0 + n_], dg[:, k, :], upad[:, s_, off + k:off + k + n_], k == 0, k == 4,
                                           [upk, dgk], [pck])
                                ACT(dstT[:, t * 512:(t + 1) * 512], pc[:], AF.Silu, [pck, "cb5"], [(dkey, t)],
                                    bias=cb5[:, l, cidx:cidx + 1])
                    TT(diagD[:], ident[:].unsqueeze(1).to_broadcast([128, HP, 128]),
                       dsk[:, l, q * HP:(q + 1) * HP].unsqueeze(2).to_broadcast([128, HP, 128]), ALU.mult, ["ident", "dsk"], ["diagD"])
                    for d in (1, 0):
                        Uin, UinK = (Uf, "Uf") if d == 0 else (Ub, "Ub")
                        SLo, SLoK = (SLf, "SLf") if d == 0 else (SLb, "SLb")
                        pdc, pdck = PSH()
                        la_d = la_all[:, :, d * 16 + q * HP:d * 16 + (q + 1) * HP]
                        for ci, (mt, mk) in enumerate(((Uin, UinK), (SLo, SLoK), (onesf, "onesf"))):
                            MM(pdc[:, ci * nb4:(ci + 1) * nb4].rearrange("p (b h) -> p b h", h=HP), mt[:], la_d, True, True,
                               [mk, "la_all"], [pdck])
                        dcs = decs_all[d]
                        dcsk = ("decs", d)
                        ACT(dcs[:].rearrange("p c b h -> p (c b h)"), pdc[:, 0:3 * nb4], AF.Exp, [pdck], [dcsk])
                        cum = cum_sb[d]
                        cumk = ("cum", d)
                        CP(cum[:].rearrange("p b h -> p (b h)"), pdc[:, 0:nb4], [pdck, dcsk], [cumk])
                        chi, clo = cum_hi[d], cum_lo[d]
                        CP(chi[:], cum[:], [cumk], [("chi", d)])
                        TT(clo[:], cum[:], chi[:], ALU.subtract, [cumk, ("chi", d)], [("clo", d)])
                        TT(cum[:], chi[:], clo[:], ALU.add, [("chi", d), ("clo", d), cumk], [cumk])
                        for s_ in range(nseq):
                            if h0 is None:
                                MEMSET(Sf[d][:], 0.0, [("Sf", d)], eng="dve")
                                MEMSET(Sb[d][:], 0.0, [("Sb", d)], eng="dve")
                            else:
                                for a in range(WP // 128):
                                    sg, sgk = stg_r.next()
                                    DMA(sg[:], h0[l, d, q * WP + a * 128:q * WP + (a + 1) * 128, :], (), [sgk])
                                    pt, ptk = PSH()
                                    TR(pt[:, 0:128], sg[:], ident[:], [sgk, "ident"], [ptk])
                                    CP(Sf[d][:, a * 128:(a + 1) * 128], pt[:, 0:128], [ptk], [("Sf", d)])
                                CP(Sb[d][:], Sf[d][:], [("Sf", d)], [("Sb", d)])
                            order = range(nch) if d == 0 else range(nch - 1, -1, -1)
                            for j in order:
                                bi = s_ * nch + j
                                t = bi // 4
                                tok = slice(bi * 128, (bi + 1) * 128)
                                la_b = la_all[:, bi, d * 16 + q * HP:d * 16 + (q + 1) * HP]
                                dt_b = dt_all[:, bi, d * 16 + q * HP:d * 16 + (q + 1) * HP]
                                pcb, pcbk = PSH()
                                MM(pcb[:, 0:128], Bfm[:, tok], Cfm[:, tok], True, True, [("Bfm", t), ("Cfm", t)], [pcbk])
                                cbm, cbmk = cbm_r.next()
                                TT(cbm[:], pcb[:, 0:128], Uin[:], ALU.mult, [pcbk, UinK], [cbmk])
                                xdt, xdtk = xdt_r.next()
                                xs2, xs2k = xs2_r.next()
                                xg_b = xg[:, bi, :].rearrange("p (h c) -> p h c", h=HP)
                                TT(xdt[:], xg_b, dt_b.unsqueeze(2).to_broadcast([128, HP, 64]), ALU.mult, [("xg", bi), "dt_all"], [xdtk], eng=OFF_ENG)
                                TT(xs2[:], xdt[:], dcs[:, 1, bi, :].unsqueeze(2).to_broadcast([128, HP, 64]), ALU.mult,
                                   [xdtk, dcsk], [xs2k], eng=OFF_ENG)
                                parg, pargk = PS()
                                for h in range(HP):
                                    po_ = parg[:, h * 128:(h + 1) * 128]
                                    MM(po_, chi[:, bi, h:h + 1].to_broadcast([128, 128]), identb[:], True, False, [("chi", d), "identb"], [pargk])
                                    MM(po_, clo[:, bi, h:h + 1].to_broadcast([128, 128]), identb[:], False, True, [("clo", d), "identb"], [pargk])
                                La, Lak = La_r.next()
                                for h in range(HP):
                                    ACT(La[:, h, :], parg[:, h * 128:(h + 1) * 128], AF.Relu, [pargk, cumk], [Lak],
                                        bias=cum[:, bi, h:h + 1], scale=-1.0, group=("relu", l, nseq, q, d, bi))
                                Lh, Lhk = Lh_r.next()
                                ACT(Lh[:].rearrange("p h c -> p (h c)"), La[:].rearrange("p h c -> p (h c)"), AF.Exp, [Lak], [Lhk], scale=-1.0)
                                Mh, Mhk = Mh_r.next()
                                TT(Mh[:], Lh[:], cbm[:].unsqueeze(1).to_broadcast([128, HP, 128]), ALU.mult, [Lhk, cbmk], [Mhk])
                                py, pyk = PSH()
                                for h in range(HP):
                                    MM(py[:, h * 64:(h + 1) * 64], Mh[:, h, :], xdt[:, h, :], True, d == 1, [Mhk, xdtk], [pyk])
                                    if d == 0:
                                        MM(py[:, h * 64:(h + 1) * 64], diagD[:, h, :], xg[:, bi, h * 64:(h + 1) * 64], False, True,
                                           ["diagD", ("xg", bi)], [pyk])
                                po, pok = PSH()
                                MM(po[:, 0:WP], Cfm[:, tok], Sb[d][:], True, True, [("Cfm", t), ("Sb", d)], [pok])
                                t1, t1k = t1_r.next()
                                TT(t1[:], po[:, 0:WP].rearrange("p (h c) -> p h c", h=HP),
                                   dcs[:, 0, bi, :].unsqueeze(2).to_broadcast([128, HP, 64]), ALU.mult, [pok, dcsk], [t1k])
                                t1f = t1[:].rearrange("p h c -> p (h c)")
                                if d == 1:
                                    TT(ypark[:, bi, :], t1f, py[:, 0:WP], ALU.add, [t1k, pyk], [("ypark", bi)])
                                else:
                                    TT(t1f, t1f, py[:, 0:WP], ALU.add, [t1k, pyk], [t1k])
                                    TT(t1f, t1f, ypark[:, bi, :], ALU.add, [t1k, ("ypark", bi)], [t1k], eng=OFF_ENG)
                                    yg, ygk = yg_r.next()
                                    zs, zsk = zs_r.next()
                                    pz, pzk = PSH()
                                    for kc in range(8):
                                        MM(pz[:, 0:WP], hT[:, kc, tok], wz[:, kc, :], kc == 0, kc == 7, [hk(t), "wz"], [pzk])
                                    ACT(zs[:], pz[:, 0:WP], AF.Tanh, [pzk], [zsk], scale=0.5)
                                    STT(zs[:], zs[:], 1.0, pz[:, 0:WP], ALU.add, ALU.mult, [zsk, pzk], [zsk])
                                    TT(yg[:], t1f, zs[:], ALU.mult, [t1k, zsk], [ygk])
                                    for a in range(WP // 128):
                                        TR(psb_t[:, a * 128:(a + 1) * 128], yg[:, a * 128:(a + 1) * 128], identb[:], [ygk, "identb"], ["psb"])
                                    kc0 = q * (WP // 128)
                                    P.op("act", (lambda e, o=yT[:, kc0:kc0 + WP // 128, tok],
                                                 i=psb_t[:, 0:WP].rearrange("p (a c) -> p a c", c=128): e.copy(out=o, in_=i)),
                                         ["psb"], [yk(kc0 + a, bi) for a in range(WP // 128)], cost=400.0)
                                pds, pdsk = PSH()
                                MM(pds[:, 0:WP], Btm[:, bi, :], xs2[:].rearrange("p h c -> p (h c)"), True, True, [("Btm", bi), xs2k], [pdsk])
                                Sf3 = Sf[d][:].rearrange("p (h c) -> p h c", h=HP)
                                TT(Sf3, Sf3, dcs[:, 2, bi, :].unsqueeze(2).to_broadcast([128, HP, 64]), ALU.mult,
                                   [("Sf", d), dcsk], [("Sf", d)], eng=OFF_ENG)
                                TT(Sf[d][:], Sf[d][:], pds[:, 0:WP], ALU.add, [("Sf", d), pdsk], [("Sf", d)])
                                P.op("act", (lambda e, o=Sb[d][:], i=Sf[d][:]: e.copy(out=o, in_=i)), [("Sf", d)], [("Sb", d)], cost=400.0)
                            if ns_out is not None:
                                for a in range(WP // 128):
                                    pt, ptk = PSH()
                                    TR(pt[:, 0:128], Sf[d][:, a * 128:(a + 1) * 128], ident[:], [("Sf", d), "ident"], [ptk])
                                    sg, sgk = stg_r.next()
                                    CP(sg[:], pt[:, 0:128], [ptk], [sgk])
                                    DMA(ns_out[s_, l, d, q * WP + a * 128:q * WP + (a + 1) * 128, :], sg[:], [sgk], (), final=True)

            P.barrier()
            if debug and l == 0:
                DMA(dbg[f"d_yA_{nm}"], yT[:, :, 0:Ttok], allk, (), final=True)
            def ssd_norm(sn):
                sq_r = Rot("sqn", [128, 8, 512], BF16, 1, sn)
                rs_r = Rot("rsn", [128, 512], F32, 2, sn)
                for t in range(NT):
                    sq, sqk = sq_r.next()
                    rs, rsk = rs_r.next()
                    tl = slice(t * 512, (t + 1) * 512)
                    ACT(sq[:], yT[:, 0:8, tl], AF.Square, ytile(range(8), t), [sqk])
                    stats_rs(lambda k: (sq[:, k, :], [sqk]), 8, rs, rsk, [], eps=4.0 * EPS)
                    for k in range(8):
                        STT(yT[:, k, tl], yT[:, k, tl], sng[:, l, k:k + 1], rs[:], ALU.mult, ALU.mult,
                            ytile([k], t) + ["sng", rsk], ytile([k], t))

            pad = 15 * stride
            with ExitStack() as sb_:
                ssd_norm(sb_)
                if debug and l == 0:
                    DMA(dbg[f"d_yB_{nm}"], yT[:, :, 0:Ttok], allk, (), final=True)
                wga = T([128, 8, 1024], BF16, "wga", sb_)
                wgb = T([128, 8, 1024], BF16, "wgb", sb_)
                for j in range(8):
                    win_load(wga[:, :, j * 128:(j + 1) * 128], I_GA + j * 128, 128, ("wga", j))
                    win_load(wgb[:, :, j * 128:(j + 1) * 128], I_GB + j * 128, 128, ("wgb", j))
                hc_r = Rot("hc", [128, nseq, L + 2 * pad], BF16, 2, sb_)
                d31_r = Rot("d31", [128, 31, 128], BF16, 2, sb_)
                sig_r = Rot("sig", [128, 512], F32, 2, sb_)
                accd_r = Rot("accd", [128, 512], F32, 2, sb_)
                accp_r = Rot("accp", [128, 512], F32, 2, sb_)
                accbd_r = Rot("accbd", [128, 512], BF16, 2, sb_)
                accbp_r = Rot("accbp", [128, 512], BF16, 2, sb_)
                for r in hc_r.t:
                    MEMSET(r[:], 0.0, [("hc", hc_r.t.index(r))])
                for j in range(8):
                    hc, hck = hc_r.next()
                    dg, dgk = d31_r.next()
                    TT(dg[:], ident[:].unsqueeze(1).to_broadcast([128, 31, 128]),
                       cw31[:, l, j, :].unsqueeze(2).to_broadcast([128, 31, 128]), ALU.mult, ["ident", "cw31"], [dgk])
                    for t in range(NT):
                        pa, pak = PS()
                        pb, pbk = PS()
                        for kc in range(8):
                            MM(pa[:], wga[:, kc, j * 128:(j + 1) * 128], hT[:, kc, t * 512:(t + 1) * 512], kc == 0, kc == 7, [("wga", j), hk(t)], [pak])
                        for kc in range(8):
                            MM(pb[:], wgb[:, kc, j * 128:(j + 1) * 128], hT[:, kc, t * 512:(t + 1) * 512], kc == 0, kc == 7, [("wgb", j), hk(t)], [pbk])
                        sig, sigk = sig_r.next()
                        ACT(sig[:], pb[:], AF.Sigmoid, [pbk], [sigk])
                        if L >= 512:
                            s_, off, n_, c0 = segs(t)[0]
                            TT(hc[:, s_, pad + off:pad + off + 512], pa[:], sig[:], ALU.mult, [pak, sigk], [hck])
                        else:
                            n = 512 // L
                            TT(hc[:, t * n:(t + 1) * n, pad:pad + L], pa[:].rearrange("p (s x) -> p s x", s=n),
                               sig[:].rearrange("p (s x) -> p s x", s=n), ALU.mult, [pak, sigk], [hck])
                    for t in range(NT):
                        pc, pck = PS()
                        for (s_, off, n_, c0) in segs(t):
                            taps = [k for k in range(31) if off + (k - 15) * stride + n_ > 0 and off + (k - 15) * stride < L]
                            win = lambda k: hc[:, s_, pad + off + (k - 15) * stride:pad + off + (k - 15) * stride + n_]
                            wk_ = lambda k: cw31[:, l, j, k:k + 1]
                            extra = []
                            rest = list(taps)
                            for eng_, ntap, acc_r, accb_r in (("dve", CONV_ND, accd_r, accbd_r), ("pool", CONV_NP, accp_r, accbp_r)):
                                if ntap == 0 or len(rest) - ntap < 4:
                                    continue
                                mine, rest = rest[:ntap], rest[ntap:]
                                acc, acck = acc_r.next()
                                accb, accbk = accb_r.next()
                                c_ = (120.0 + n_ / 0.96) if eng_ == "dve" else (200.0 + n_ / 0.55)
                                for ii, k in enumerate(mine):
                                    last_ = ii == len(mine) - 1
                                    dst, dstk = (accb, accbk) if last_ else (acc, acck)
                                    if ii == 0:
                                        P.op(eng_, (lambda e, o=dst[:, 0:n_], i0=win(k), sc=wk_(k): e.tensor_scalar(
                                            out=o, in0=i0, scalar1=sc, scalar2=None, op0=ALU.mult)), [hck, "cw31"], [dstk], cost=c_)
                                    else:
                                        P.op(eng_, (lambda e, o=dst[:, 0:n_], i0=win(k), sc=wk_(k), i1=acc[:, 0:n_]: e.scalar_tensor_tensor(
                                            out=o, in0=i0, scalar=sc, in1=i1, op0=ALU.mult, op1=ALU.add)), [hck, "cw31", acck], [dstk], cost=c_)
                                extra.append((accb, accbk))
                            nmm = len(rest) + len(extra)
                            im = 0
                            for k in rest:
                                MM(pc[:, c0:c0 + n_], dg[:, k, :], win(k), im == 0, im == nmm - 1, [hck, dgk], [pck])
                                im += 1
                            for accb, accbk in extra:
                                MM(pc[:, c0:c0 + n_], identb[:], accb[:, 0:n_], im == 0, im == nmm - 1, ["identb", accbk], [pck])
                                im += 1
                        ACT(yT[:, 8 + j, t * 512:(t + 1) * 512], pc[:], AF.Identity, [pck, "cb31"], ytile([8 + j], t),
                            bias=cb31[:, l, j:j + 1])
            P.barrier()
            with ExitStack() as sb_:
                wgs = T([128, 8, 1024], BF16, "wgs", sb_)
                for j in range(8):
                    win_load(wgs[:, :, j * 128:(j + 1) * 128], I_GS + j * 128, 128, ("wgs", j))
                sq_r = Rot("sqc", [128, 8, 512], BF16, 2, sb_)
                mean_r = Rot("mean", [128, 512], F32, 2, sb_)
                rs_r = Rot("rsc", [128, 512], F32, 2, sb_)
                tmp_r = Rot("tmpc", [128, 512], F32, 2, sb_)
                s1_r = Rot("s1c", [128, 512], F32, 2, sb_)
                for t in range(NT):
                    tl = slice(t * 512, (t + 1) * 512)
                    sq, sqk = sq_r.next()
                    mean, meank = mean_r.next()
                    rs, rsk = rs_r.next()
                    p1, p1k = PS()
                    for k in range(8):
                        MM(p1[:], onesb[:], yT[:, 8 + k, tl], k == 0, k == 7, ["onesb"] + ytile([8 + k], t), [p1k])
                    ACT(sq[:], yT[:, 8:16, tl], AF.Square, ytile(range(8, 16), t), [sqk])
                    p2, p2k = PS()
                    for k in range(8):
                        MM(p2[:], onesb[:], sq[:, k, :], k == 0, k == 7, ["onesb", sqk], [p2k])
                    TS(mean[:], p1[:], 1.0 / 1024.0, ALU.mult, [p1k], [meank])
                    tmp, tmpk = tmp_r.next()
                    TT(tmp[:], mean[:], mean[:], ALU.mult, [meank], [tmpk])
                    STT(tmp[:], p2[:], 1.0 / 1024.0, tmp[:], ALU.mult, ALU.subtract, [p2k, tmpk], [tmpk])
                    ACT(rs[:], tmp[:], AF.Ln, [tmpk], [rsk], bias=EPS)
                    ACT(rs[:], rs[:], AF.Exp, [rsk], [rsk], scale=-0.5)
                    for j in range(8):
                        tmp, tmpk = tmp_r.next()
                        s1, s1k = s1_r.next()
                        TT(tmp[:], yT[:, 8 + j, tl], mean[:], ALU.subtract, ytile([8 + j], t) + [meank], [tmpk])
                        TT(tmp[:], tmp[:], rs[:], ALU.mult, [tmpk, rsk], [tmpk])
                        ACT(s1[:], tmp[:], AF.Silu, [tmpk, "lng", "lnb"], [s1k], bias=lnb[:, l, j:j + 1], scale=lng[:, l, j:j + 1])
                        pg, pgk = PS()
                        for kc in range(8):
                            MM(pg[:], wgs[:, kc, j * 128:(j + 1) * 128], hT[:, kc, tl], kc == 0, kc == 7, [("wgs", j), hk(t)], [pgk])
                        ACT(tmp[:], pg[:], AF.Silu, [pgk], [tmpk])
                        TT(yT[:, 8 + j, tl], s1[:], tmp[:], ALU.mult, [s1k, tmpk], ytile([8 + j], t))

            P.barrier()
            if debug and l == 0:
                DMA(dbg[f"d_yC_{nm}"], yT[:, :, 0:Ttok], allk, (), final=True)
            with ExitStack() as sc:
                wo = T([128, 16, 1024], BF16, "wo", sc)
                for fo in range(8):
                    DMA(wo[:, :, fo * 128:(fo + 1) * 128], wout_d[l].rearrange("(kc p) c -> p kc c", p=128)[:, :, fo * 128:(fo + 1) * 128],
                        (), [("wo", fo)], eng="pool")
                osb_r = Rot("osb", [128, 8, 512], F32, 2, sc)
                sq_r = Rot("sqo", [128, 8, 512], BF16, 1, sc)
                rs_r = Rot("rso", [128, 512], F32, 2, sc)
                xt_r = Rot("xto", [128, 8, 512], F32, 1, sc)
                for t in range(NT):
                    tl = slice(t * 512, (t + 1) * 512)
                    osb, osbk = osb_r.next()
                    sq, sqk = sq_r.next()
                    rs, rsk = rs_r.next()
                    xt, xtk = xt_r.next()
                    DMA(xt[:], xsrc_v[:, :, tl], [("xd", id(x_src), t)], [xtk])
                    for fo in range(8):
                        po, pok = PS()
                        for kc in range(16):
                            MM(po[:], wo[:, kc, fo * 128:(fo + 1) * 128], yT[:, kc, tl], kc == 0, kc == 15,
                               [("wo", fo)] + ytile([kc], t), [pok])
                        P.op("act", (lambda e, o=osb[:, fo, :], i=po[:]: e.copy(out=o, in_=i)), [pok], [(osbk, fo)], cost=590.0)
                        ACT(sq[:, fo, :], po[:], AF.Square, [pok], [(sqk, fo)])
                    stats_rs(lambda k: (sq[:, k, :], [(sqk, k)]), 8, rs, rsk, [])
                    for fo in range(8):
                        TT(osb[:, fo, :], osb[:, fo, :], rs[:], ALU.mult, [(osbk, fo), rsk], [(osbk, fo)])
                        STT(osb[:, fo, :], osb[:, fo, :], modG[:, l, fo, wsel:wsel + 1], xt[:, fo, :], ALU.mult, ALU.add,
                            [(osbk, fo), "modG", xtk], [(osbk, fo)])
                    allosb = [(osbk, k) for k in range(8)]
                    DMA(xdst_v[:, :, tl], osb[:], allosb, [("xd", id(x_dst), t)], final=final_out)
                    if fuse_next:
                        allsq = [(sqk, k) for k in range(8)]
                        ACT(sq[:], osb[:], AF.Square, allosb, allsq)
                        rs2, rs2k = rs_r.next()
                        stats_rs(lambda k: (sq[:, k, :], [(sqk, k)]), 8, rs2, rs2k, [])
                        TT(xt[:], osb[:], rs2[:].unsqueeze(1).to_broadcast([128, 8, 512]), ALU.mult, allosb + [rs2k], [xtk])
                        for kc in range(8):
                            ACT(hT[:, kc, tl], xt[:, kc, :], AF.Identity, [xtk, "modA", "modB"], [hk(t)],
                                bias=modB[:, l + 1, kc, wsel:wsel + 1], scale=modA[:, l + 1, kc, wsel:wsel + 1],
                                group=("hTn", l, t, nseq))
            P.barrier()

        for nm_ in ("P", "S"):
            for l in range(DEPTH):
                last = (l == DEPTH - 1)
                if only is not None and (l, nm_) not in only:
                    continue
                if nm_ == "P":
                    run_block(l, xp_d if l == 0 else x1p_d, yp_d if last else x1p_d, 2, 256, 1, 0, None, ns_d, last or debug,
                              l == 0 or debug, (not last) and not debug)
                else:
                    run_block(l, xs_d if l == 0 else x1s_d, ys_d if last else x1s_d, 1, 2048, 64, 1, h0_d, None, last or debug,
                              l == 0 or debug, (not last) and not debug)
        P.emit()
        n_ins = len(P.ins)
    return nc, n_ins


_CACHE = {}


def _fm(v):
    v = np.asarray(v, np.float32)
    lead = v.shape[:-1]
    nchunk = v.shape[-1] // 128
    r = v.reshape(lead + (nchunk, 128))
    return np.ascontiguousarray(np.moveaxis(r, -1, 0))


def kernel(x_prompt, x_sample, state_ssd, c, c_ctx, w_mod, b_mod, g_pre, g_post, w_in,
           ssd_conv_w, ssd_conv_b, ssd_a_log, ssd_dt_bias, ssd_d, ssd_norm_g,
           conf_conv_w, conf_conv_b, conf_ln_g, conf_ln_b, w_out):
    f = lambda a: np.ascontiguousarray(np.asarray(a, np.float32))
    x_prompt, x_sample, state_ssd = f(x_prompt), f(x_sample), f(state_ssd)
    if "nc" not in _CACHE:
        _CACHE["nc"] = build_program()[0]
    nc = _CACHE["nc"]
    rep = lambda a: np.ascontiguousarray(np.broadcast_to(f(a).reshape(1, DEPTH, -1), (128, DEPTH, f(a).reshape(DEPTH, -1).shape[1])))
    shared = {
        "w_mod": f(w_mod), "b_mod": _fm(b_mod), "g_pre": _fm(g_pre), "g_post": _fm(g_post), "w_in": f(w_in),
        "cw5": np.ascontiguousarray(np.transpose(f(ssd_conv_w).reshape(DEPTH, 5, 12, 128), (3, 0, 2, 1))),
        "cb5": _fm(ssd_conv_b), "cb5row": f(ssd_conv_b).reshape(1, DEPTH * 1536),
        "alog": rep(ssd_a_log), "dtb": rep(ssd_dt_bias), "dsk": rep(ssd_d), "sng": _fm(ssd_norm_g),
        "cw31": np.ascontiguousarray(np.transpose(f(conf_conv_w).reshape(DEPTH, 31, 8, 128), (3, 0, 2, 1))),
        "cb31": _fm(conf_conv_b), "lng": _fm(conf_ln_g), "lnb": _fm(conf_ln_b), "w_out": f(w_out),
    }
    in_maps = []
    for core in range(NCORES):
        b = core // 4
        m = dict(shared)
        m["xp"] = np.ascontiguousarray(x_prompt[2 * core:2 * core + 2].reshape(512, D).T)
        m["xs"] = np.ascontiguousarray(x_sample[b].T)
        m["h0"] = np.ascontiguousarray(state_ssd[b].reshape(DEPTH, 2, 1024, 128))
        cv = np.stack([f(c_ctx), f(c)[b]], axis=-1)
        m["cvec"] = np.ascontiguousarray(np.transpose(cv.reshape(8, 128, 2), (1, 0, 2)))
        in_maps.append(m)
    res = run_bass_kernel_spmd(nc, in_maps, core_ids=list(range(NCORES)))
    r = res.results
    y_prompt = np.stack([r[core]["yp"].T.reshape(2, 256, D) for core in range(NCORES)], 0).reshape(16, 256, D)
    y_sample = np.stack([r[0]["ys"].T, r[4]["ys"].T], 0)
    new_state = np.concatenate([r[core]["ns"] for core in range(NCORES)], 0).reshape(16, DEPTH, 2, 16, 64, 128)
    return (np.ascontiguousarray(y_prompt, dtype=np.float32), np.ascontiguousarray(y_sample, dtype=np.float32),
            np.ascontiguousarray(new_state, dtype=np.float32))
```

```python
import numpy as np
from contextlib import ExitStack
import concourse.bass as bass
import concourse.mybir as mybir
from concourse.bass_utils import run_bass_kernel_spmd

F32 = mybir.dt.float32
BF16 = mybir.dt.bfloat16
AF = mybir.ActivationFunctionType
ALU = mybir.AluOpType

D = 1024
DEPTH = 2
NCORES = 8
EPS = 1e-6
I_Z, I_X, I_B, I_C, I_DT, I_GA, I_GB, I_GS = 0, 1024, 2048, 2304, 2560, 2592, 3616, 4640
IN_COLS = 5664
HP = 4
WP = HP * 64
TMAX = 2048
NPS = 7
CONV_ND = 8
CONV_NP = 0
NROT = 3
OFF_ENG = "pool"


class Prog:
    SEM_LIMIT = 4000
    WINDOW = 128
    SEM_LAT = 260.0

    def __init__(self, nc, stack, same_engine_sync=True, schedule=True):
        self.nc = nc
        self.stack = stack
        self.engs = {"pe": nc.tensor, "act": nc.scalar, "dve": nc.vector, "pool": nc.gpsimd, "sp": nc.sync}
        self.ins = []
        self.last_w = {}
        self.readers = {}
        self.same_engine_sync = same_engine_sync
        self.schedule = schedule
        self.n_dma_sems = {"sp": 16, "pool": 8, "act": 4, "dve": 4, "pe": 4}
        self.out_dmas = []
        self.w_rdeps = {}
        self.phase = 0

    def barrier(self):
        self.phase += 1

    def op(self, eng, fn, reads=(), writes=(), dma=False, final=False, cost=300.0, lat=0.0, group=None):
        deps = set()
        for r in reads:
            if r in self.last_w:
                deps |= set(self.last_w[r][1])
        i = len(self.ins)
        for w in writes:
            same = False
            if w in self.last_w:
                gid, members = self.last_w[w]
                same = group is not None and gid == group
                if not same:
                    deps |= set(members)
            if same:
                deps |= self.w_rdeps.get(w, set())
            else:
                rd = set(self.readers.get(w, set()))
                deps |= rd
                self.w_rdeps[w] = (set(self.last_w[w][1]) if w in self.last_w else set()) | rd
        deps.discard(i)
        self.ins.append(dict(eng=eng, fn=fn, deps=deps, dma=dma, cost=cost, lat=lat, phase=self.phase))
        for r in reads:
            self.readers.setdefault(r, set()).add(i)
        for w in writes:
            if w in self.last_w and group is not None and self.last_w[w][0] == group:
                self.last_w[w][1].append(i)
            else:
                self.last_w[w] = (group, [i])
                self.readers[w] = set()
        if final:
            self.out_dmas.append(i)
        return i

    def _order(self):
        ins = self.ins
        n = len(ins)
        per_eng = {e: [] for e in self.engs}
        for i, it in enumerate(ins):
            per_eng[it["eng"]].append(i)
        if not self.schedule:
            return per_eng
        users = [[] for _ in range(n)]
        nun = [0] * n
        for i, it in enumerate(ins):
            nun[i] = len(it["deps"])
            for d in it["deps"]:
                users[d].append(i)
        blev = [0.0] * n
        for i in range(n - 1, -1, -1):
            it = ins[i]
            m = 0.0
            for u in users[i]:
                if ins[u]["phase"] == it["phase"] and blev[u] > m:
                    m = blev[u]
            blev[i] = it["cost"] + it["lat"] + m
        rdy = [0.0] * n
        self.t_start = [0.0] * n
        self.t_fin = [0.0] * n
        nphase = self.phase + 1
        left = [0] * nphase
        for it in ins:
            left[it["phase"]] += 1
        cur = 0
        while cur < nphase and left[cur] == 0:
            cur += 1
        phase_t = 0.0
        tmax = 0.0
        eng_free = {e: 0.0 for e in self.engs}
        pend = {e: list(v) for e, v in per_eng.items()}
        order = {e: [] for e in self.engs}
        remaining = n
        while remaining:
            best = None
            for e, lst in pend.items():
                cand = None
                ef = eng_free[e]
                for i in lst[:self.WINDOW]:
                    it = ins[i]
                    if it["phase"] != cur:
                        break
                    if nun[i]:
                        continue
                    stt = max(rdy[i], ef, phase_t)
                    key = (stt, -blev[i]) if stt > ef + 1e-9 else (ef, -blev[i])
                    if cand is None or key < cand[2]:
                        cand = (key[0], i, key)
                if cand is not None and (best is None or cand[0] < best[0] - 1e-9 or
                                         (abs(cand[0] - best[0]) <= 1e-9 and cand[1] < best[1])):
                    best = (cand[0], cand[1], e)
            assert best is not None, "scheduler stuck"
            stt, i, e = best
            it = ins[i]
            eng_free[e] = stt + it["cost"]
            f = stt + it["cost"] + it["lat"]
            self.t_start[i] = stt
            self.t_fin[i] = f
            tmax = max(tmax, f)
            for u in users[i]:
                nun[u] -= 1
                fl = f if (ins[u]["eng"] == e and e == "pe" and not it["dma"]) else f + self.SEM_LAT
                if fl > rdy[u]:
                    rdy[u] = fl
            pend[e].remove(i)
            order[e].append(i)
            remaining -= 1
            left[cur] -= 1
            if left[cur] == 0:
                while cur < nphase and left[cur] == 0:
                    cur += 1
                phase_t = tmax + 200.0
        self.sim_time = tmax
        return order

    def emit(self):
        nc = self.nc
        ins = self.ins
        n = len(ins)
        order = self._order()
        pos = [0] * n
        for e, lst in order.items():
            for k, i in enumerate(lst):
                pos[i] = k
        last_before = {}
        for e, lst in order.items():
            cuts = {}
            for k, i in enumerate(lst):
                cuts.setdefault(ins[i]["phase"], k)
            last_before[e] = (lst, cuts)
        for e, lst in order.items():
            seen = -1
            for i in lst:
                p = ins[i]["phase"]
                if p == seen:
                    continue
                seen = p
                if p == 0:
                    continue
                extra = set()
                for e2, (lst2, cuts2) in last_before.items():
                    ks = [k for ph, k in cuts2.items() if ph >= p]
                    endk = min(ks) if ks else len(lst2)
                    if endk == 0:
                        continue
                    extra.add(lst2[endk - 1])
                    nd = self.n_dma_sems[e2]
                    cnt = 0
                    for k in range(endk - 1, -1, -1):
                        if ins[lst2[k]]["dma"]:
                            extra.add(lst2[k])
                            cnt += 1
                            if cnt >= nd:
                                break
                extra.discard(i)
                ins[i]["deps"] = set(ins[i]["deps"]) | extra
        pruned = [None] * n
        for i, it in enumerate(ins):
            e = it["eng"]
            keep = {}
            dmas = []
            for d in it["deps"]:
                p = ins[d]
                if p["dma"]:
                    dmas.append(d)
                    continue
                if p["eng"] == e and (e == "pe" or not self.same_engine_sync):
                    continue
                pe_ = p["eng"]
                if pe_ not in keep or pos[d] > pos[keep[pe_]]:
                    keep[pe_] = d
            pruned[i] = list(keep.values()) + dmas
        needed = [False] * n
        for i in range(n):
            for d in pruned[i]:
                needed[d] = True
        for i in self.out_dmas:
            needed[i] = True
        sem_of = [None] * n
        dma_prev = [None] * n
        for e, lst in order.items():
            nd = self.n_dma_sems[e]
            dsems = None
            dcnt = None
            rr = 0
            cur = None
            ccnt = 0
            k = 0
            for i in lst:
                it = ins[i]
                if it["dma"]:
                    if dsems is None:
                        dsems = [self.stack.enter_context(nc.semaphore(f"dq_{e}_{j}")) for j in range(nd)]
                        dcnt = [0] * nd
                    j = rr
                    rr = (rr + 1) % nd
                    if dcnt[j] > 0:
                        dma_prev[i] = (dsems[j], dcnt[j])
                    dcnt[j] += 16
                    sem_of[i] = (dsems[j], dcnt[j])
                elif needed[i]:
                    if cur is None or ccnt >= self.SEM_LIMIT:
                        cur = self.stack.enter_context(nc.semaphore(f"s_{e}_{k}"))
                        k += 1
                        ccnt = 0
                    ccnt += 1
                    sem_of[i] = (cur, ccnt)
        for e, lst in order.items():
            eng = self.engs[e]
            waited = {}

            def do_wait(sem, cnt):
                key = id(sem)
                if waited.get(key, 0) >= cnt:
                    return
                eng.wait_ge(sem, cnt)
                waited[key] = cnt

            for i in lst:
                it = ins[i]
                ws = [sem_of[d] for d in pruned[i]]
                ws.sort(key=lambda sc: -sc[1])
                for sem, c in ws:
                    do_wait(sem, c)
                if it["dma"] and dma_prev[i] is not None:
                    do_wait(*dma_prev[i])
                inst = it["fn"](eng)
                if sem_of[i] is not None:
                    inst.then_inc(sem_of[i][0], 16 if it["dma"] else 1)
            if e == "sp":
                for i in self.out_dmas:
                    do_wait(*sem_of[i])


def build_program(debug=False, only=None):
    nc = bass.Bass("TRN2", target_bir_lowering=False)
    dt_in = lambda name, shape: nc.dram_tensor(name, shape, F32, kind="ExternalInput").ap()
    dt_out = lambda name, shape: nc.dram_tensor(name, shape, F32, kind="ExternalOutput").ap()
    xp_d = dt_in("xp", [D, 512])
    xs_d = dt_in("xs", [D, 2048])
    h0_d = dt_in("h0", [DEPTH, 2, 1024, 128])
    cvec_d = dt_in("cvec", [128, 8, 2])
    wmod_d = dt_in("w_mod", [DEPTH, D, 3 * D])
    bmod_d = dt_in("b_mod", [128, DEPTH, 24])
    gpre_d = dt_in("g_pre", [128, DEPTH, 8])
    gpost_d = dt_in("g_post", [128, DEPTH, 8])
    win_d = dt_in("w_in", [DEPTH, D, IN_COLS])
    cw5_d = dt_in("cw5", [128, DEPTH, 12, 5])
    cb5_d = dt_in("cb5", [128, DEPTH, 12])
    cb5row_d = dt_in("cb5row", [1, DEPTH * 1536])
    alog_d = dt_in("alog", [128, DEPTH, 32])
    dtb_d = dt_in("dtb", [128, DEPTH, 32])
    dsk_d = dt_in("dsk", [128, DEPTH, 16])
    sng_d = dt_in("sng", [128, DEPTH, 8])
    cw31_d = dt_in("cw31", [128, DEPTH, 8, 31])
    cb31_d = dt_in("cb31", [128, DEPTH, 8])
    lng_d = dt_in("lng", [128, DEPTH, 8])
    lnb_d = dt_in("lnb", [128, DEPTH, 8])
    wout_d = dt_in("w_out", [DEPTH, 2 * D, D])
    yp_d = dt_out("yp", [D, 512])
    ys_d = dt_out("ys", [D, 2048])
    ns_d = dt_out("ns", [2, DEPTH, 2, 1024, 128])
    dbg = {}
    if debug:
        dbg["d_modA"] = dt_out("d_modA", [128, DEPTH, 8, 2])
        dbg["d_modB"] = dt_out("d_modB", [128, DEPTH, 8, 2])
        dbg["d_modG"] = dt_out("d_modG", [128, DEPTH, 8, 2])
        for nm, T_ in (("P", 512), ("S", 2048)):
            dbg[f"d_hT_{nm}"] = nc.dram_tensor(f"d_hT_{nm}", [128, 8, T_], BF16, kind="ExternalOutput").ap()
            dbg[f"d_yA_{nm}"] = nc.dram_tensor(f"d_yA_{nm}", [128, 16, T_], BF16, kind="ExternalOutput").ap()
            dbg[f"d_yB_{nm}"] = nc.dram_tensor(f"d_yB_{nm}", [128, 16, T_], BF16, kind="ExternalOutput").ap()
            dbg[f"d_yC_{nm}"] = nc.dram_tensor(f"d_yC_{nm}", [128, 16, T_], BF16, kind="ExternalOutput").ap()
    x1p_d = nc.dram_tensor("x1p", [D, 512], F32, kind="ExternalOutput" if debug else "Internal").ap()
    x1s_d = nc.dram_tensor("x1s", [D, 2048], F32, kind="ExternalOutput" if debug else "Internal").ap()

    with ExitStack() as st:
        P = Prog(nc, st)
        cnt = [0]

        def T(shape, dt, name=None, stack=None):
            cnt[0] += 1
            return (stack or st).enter_context(nc.sbuf_tensor(f"sb{cnt[0]}_{name or 't'}", shape, dt))

        def nfree(ap):
            r = 1
            for d in ap.shape[1:]:
                r *= d
            return r

        def DMA(out, in_, reads=(), writes=(), eng="sp", final=False):
            nbytes = nfree(out) * out.shape[0] * 4
            P.op(eng, lambda e: e.dma_start(out=out, in_=in_), reads, writes, dma=True, final=final,
                 cost=(150.0 if eng == "sp" else 1200.0), lat=2000.0 + nbytes / 120.0)

        def MM(out, lhsT, rhs, start, stop, reads, writes):
            passes = 4 if lhsT.dtype == F32 else 1
            P.op("pe", lambda e: e.matmul(out, lhsT=lhsT, rhs=rhs, start=start, stop=stop), reads, writes,
                 cost=30.0 + passes * max(nfree(rhs), 64) / 2.4, lat=120.0)

        def TR(out, in_, ident, reads, writes):
            P.op("pe", lambda e: e.transpose(out=out, in_=in_, identity=ident), reads, writes,
                 cost=(4 if in_.dtype == F32 else 1) * 60.0 + 30.0, lat=120.0)

        def ACT(out, in_, func, reads, writes, bias=None, scale=None, group=None):
            kw = {}
            if bias is not None:
                kw["bias"] = bias
            if scale is not None:
                kw["scale"] = scale
            P.op("act", lambda e: e.activation(out=out, in_=in_, func=func, **kw), reads, writes, cost=220.0 + nfree(out) / 1.4,
                 group=group)

        def TT(out, in0, in1, op, reads, writes, eng="dve"):
            c = 120.0 + nfree(out) / 0.96 if eng != "pool" else 200.0 + nfree(out) / 0.55
            P.op(eng, lambda e: e.tensor_tensor(out=out, in0=in0, in1=in1, op=op), reads, writes, cost=c)

        def TS(out, in0, s1, op0, reads, writes, s2=None, op1=None, eng="dve"):
            if op1 is None:
                P.op(eng, lambda e: e.tensor_scalar(out=out, in0=in0, scalar1=s1, scalar2=None, op0=op0), reads, writes,
                     cost=120.0 + nfree(out) / 0.96)
            else:
                P.op(eng, lambda e: e.tensor_scalar(out=out, in0=in0, scalar1=s1, scalar2=s2, op0=op0, op1=op1), reads, writes,
                     cost=120.0 + nfree(out) / 0.96)

        def STT(out, in0, scalar, in1, op0, op1, reads, writes, eng="dve"):
            P.op(eng, lambda e: e.scalar_tensor_tensor(out=out, in0=in0, scalar=scalar, in1=in1, op0=op0, op1=op1), reads, writes,
                 cost=120.0 + nfree(out) / 0.96)

        def CP(out, in_, reads, writes, eng="dve"):
            P.op(eng, lambda e: e.tensor_copy(out=out, in_=in_), reads, writes, cost=120.0 + nfree(out) / 0.96)

        def RECIP(out, in_, reads, writes):
            P.op("dve", lambda e: e.reciprocal(out=out, in_=in_), reads, writes, cost=120.0 + nfree(out) * 6.5)

        def MEMSET(ap, val, writes, eng="pool"):
            P.op(eng, lambda e: e.memset(ap, val), (), writes, cost=150.0 + nfree(ap) / 1.0)

        class Rot:
            def __init__(self, name, shape, dt, n, stack):
                self.t = [T(shape, dt, f"{name}{i}", stack) for i in range(n)]
                self.name = name
                self.i = 0

            def next(self):
                k = self.i % len(self.t)
                self.i += 1
                return self.t[k], (self.name, k)

        ps_t = [st.enter_context(nc.psum_tensor(f"ps{i}", [128, 512], F32)) for i in range(NPS)]
        psb_t = st.enter_context(nc.psum_tensor("psb", [128, 1024], BF16))
        ps_i = [0]

        def PS():
            k = ps_i[0] % NPS
            ps_i[0] += 1
            return ps_t[k], ("ps", k)

        PSH = PS

        ident = T([128, 128], F32, "ident")
        identb = T([128, 128], BF16, "identb")
        onesb = T([128, 128], BF16, "onesb")
        onesf = T([128, 128], F32, "onesf")
        Uf = T([128, 128], F32, "Uf")
        SLf = T([128, 128], F32, "SLf")
        Ub = T([128, 128], F32, "Ub")
        SLb = T([128, 128], F32, "SLb")
        MEMSET(onesf[:], 1.0, ["onesf"])
        MEMSET(onesb[:], 1.0, ["onesb"])

        def SEL(t, key, cm, pat, op):
            MEMSET(t[:], 1.0, [key])
            P.op("pool", lambda e: e.affine_select(out=t[:], in_=t[:], pattern=[[pat, 128]], compare_op=op,
                                                   fill=0.0, base=0, channel_multiplier=cm), [key], [key])
        SEL(Uf, "Uf", -1, 1, ALU.is_ge)
        SEL(SLf, "SLf", 1, -1, ALU.is_gt)
        SEL(Ub, "Ub", 1, -1, ALU.is_ge)
        SEL(SLb, "SLb", -1, 1, ALU.is_gt)
        MEMSET(ident[:], 0.0, ["ident"])
        P.op("pool", lambda e: e.affine_select(out=ident[:], in_=ident[:], pattern=[[-1, 128]], compare_op=ALU.not_equal,
                                               fill=1.0, base=0, channel_multiplier=1), ["ident"], ["ident"])
        CP(identb[:], ident[:], ["ident"], ["identb"])

        def LOADP(dram, shape, name):
            t = T(shape, F32, name)
            DMA(t[:], dram, (), [name])
            return t
        cvec = LOADP(cvec_d, [128, 8, 2], "cvec")
        bmod = LOADP(bmod_d, [128, DEPTH, 24], "bmod")
        gpre = LOADP(gpre_d, [128, DEPTH, 8], "gpre")
        gpost = LOADP(gpost_d, [128, DEPTH, 8], "gpost")
        cw5 = LOADP(cw5_d, [128, DEPTH, 12, 5], "cw5")
        cb5 = LOADP(cb5_d, [128, DEPTH, 12], "cb5")
        alog = LOADP(alog_d, [128, DEPTH, 32], "alog")
        dtb = LOADP(dtb_d, [128, DEPTH, 32], "dtb")
        dsk = LOADP(dsk_d, [128, DEPTH, 16], "dsk")
        sng = LOADP(sng_d, [128, DEPTH, 8], "sng")
        cw31 = LOADP(cw31_d, [128, DEPTH, 8, 31], "cw31")
        cb31 = LOADP(cb31_d, [128, DEPTH, 8], "cb31")
        lng = LOADP(lng_d, [128, DEPTH, 8], "lng")
        lnb = LOADP(lnb_d, [128, DEPTH, 8], "lnb")
        cb5row = T([1, DEPTH * 1536], BF16, "cb5row")
        for l_ in range(DEPTH):
            DMA(cb5row[:, l_ * 1536:(l_ + 1) * 1536], cb5row_d[:, l_ * 1536:(l_ + 1) * 1536], (), ["cb5row"], eng="pool")
        aneg = T([128, DEPTH, 32], F32, "aneg")
        ACT(aneg[:], alog[:], AF.Exp, ["alog"], ["aneg"])
        TS(aneg[:], aneg[:], -1.0, ALU.mult, ["aneg"], ["aneg"])

        silc = T([128, 8, 2], F32, "silc")
        ACT(silc[:], cvec[:], AF.Silu, ["cvec"], ["silc"])
        modA = T([128, DEPTH, 8, 2], F32, "modA")
        modB = T([128, DEPTH, 8, 2], F32, "modB")
        modG = T([128, DEPTH, 8, 2], F32, "modG")
        hT = T([128, 8, TMAX], BF16, "hT")
        yT = T([128, 16, TMAX], BF16, "yT")
        ms = ExitStack()
        st.callback(ms.close)
        if True:
            wm = Rot("wm", [128, 8, 512], F32, 2, ms)
            modsb = T([128, 24, 2], F32, "modsb", ms)
            modrow = T([2, 3 * D], F32, "modrow", ms)
            for l in range(DEPTH):
                for cb in range(6):
                    wt, wk = wm.next()
                    DMA(wt[:], wmod_d[l].rearrange("(kc p) c -> p kc c", p=128)[:, :, cb * 512:(cb + 1) * 512], (), [wk])
                    pr, prk = PS()
                    for kc in range(8):
                        MM(pr[0:2, :], silc[:, kc, :], wt[:, kc, :], kc == 0, kc == 7, [wk, "silc"], [prk])
                    CP(modrow[:, cb * 512:(cb + 1) * 512], pr[0:2, :], [prk], [("modrow", cb)])
                pm, pmk = PSH()
                for f in range(24):
                    TR(pm[:, f * 2:f * 2 + 2], modrow[0:2, f * 128:(f + 1) * 128], ident[0:2, 0:2], [("modrow", f // 4), "ident"], [pmk])
                TT(modsb[:], pm[:, 0:48].rearrange("p (f w) -> p f w", w=2),
                   bmod[:, l, :].unsqueeze(2).to_broadcast([128, 24, 2]), ALU.add, [pmk, "bmod"], ["modsb"])
                TS(modA[:, l], modsb[:, 8:16, :], 1.0, ALU.add, ["modsb"], ["modA"])
                TT(modA[:, l], modA[:, l], gpre[:, l, :].unsqueeze(2).to_broadcast([128, 8, 2]), ALU.mult, ["modA", "gpre"], ["modA"])
                CP(modB[:, l], modsb[:, 0:8, :], ["modsb"], ["modB"])
                TT(modG[:, l], modsb[:, 16:24, :], gpost[:, l, :].unsqueeze(2).to_broadcast([128, 8, 2]), ALU.mult,
                   ["modsb", "gpost"], ["modG"])

        if debug:
            DMA(dbg["d_modA"], modA[:], ["modA"], (), final=True)
            DMA(dbg["d_modB"], modB[:], ["modB"], (), final=True)
            DMA(dbg["d_modG"], modG[:], ["modG"], (), final=True)

        def stats_rs(src_sq_fn, nk, rs, rsk, extra_reads, eps=EPS):
            pst, pstk = PS()
            for k in range(nk):
                ap, rd = src_sq_fn(k)
                MM(pst[:], onesb[:], ap, k == 0, k == nk - 1, ["onesb"] + rd, [pstk])
            ACT(rs[:], pst[:], AF.Ln, [pstk], [rsk], bias=eps, scale=1.0 / 1024.0)
            ACT(rs[:], rs[:], AF.Exp, [rsk], [rsk], scale=-0.5)

        ms_holder = [ms]

        def run_block(l, x_src, x_dst, nseq, L, stride, wsel, h0, ns_out, final_out, do_front, fuse_next):
            Ttok = nseq * L
            NT = Ttok // 512
            nch = L // 128
            nblk = nseq * nch
            xsrc_v = x_src.rearrange("(kc p) t -> p kc t", p=128)
            xdst_v = x_dst.rearrange("(kc p) t -> p kc t", p=128)
            hk = lambda t: ("hT", t)
            yk = lambda k, b: ("yT", k, b)
            ytile = lambda ks, t: [yk(k, b) for k in ks for b in range(4 * t, 4 * t + 4)]

            def segs(t):
                if L >= 512:
                    per = L // 512
                    return [(t // per, (t % per) * 512, 512, 0)]
                n = 512 // L
                return [(t * n + i, 0, L, i * L) for i in range(n)]

            def win_load(dst, c0, w, key):
                DMA(dst, win_d[l].rearrange("(kc p) c -> p kc c", p=128)[:, :, c0:c0 + w], (), [key], eng="pool")

            with ExitStack() as s0:
                xt_r = Rot("xt", [128, 8, 512], F32, 2, s0)
                sq_r = Rot("sq0", [128, 8, 512], BF16, 2, s0)
                rs_r = Rot("rs0", [128, 512], F32, 2, s0)
                for t in range(NT if do_front else 0):
                    xt, xtk = xt_r.next()
                    sq, sqk = sq_r.next()
                    rs, rsk = rs_r.next()
                    DMA(xt[:], xsrc_v[:, :, t * 512:(t + 1) * 512], [("xd", id(x_src), t)], [xtk])
                    ACT(sq[:], xt[:], AF.Square, [xtk], [sqk])
                    stats_rs(lambda k: (sq[:, k, :], [sqk]), 8, rs, rsk, [])
                    TT(xt[:], xt[:], rs[:].unsqueeze(1).to_broadcast([128, 8, 512]), ALU.mult, [xtk, rsk], [xtk])
                    for kc in range(8):
                        ACT(hT[:, kc, t * 512:(t + 1) * 512], xt[:, kc, :], AF.Identity, [xtk, "modA", "modB"], [hk(t)],
                            bias=modB[:, l, kc, wsel:wsel + 1], scale=modA[:, l, kc, wsel:wsel + 1], group=("hTf", l, t, nseq))

            if ms_holder:
                ms_holder.pop().close()
            P.barrier()
            nm = "P" if nseq == 2 else "S"
            allk = [yk(k, b) for k in range(16) for b in range(nblk)]
            if debug and l == 0:
                DMA(dbg[f"d_hT_{nm}"], hT[:, :, 0:Ttok], [hk(t) for t in range(NT)], (), final=True)
            with ExitStack() as sa:
                wx = T([128, 8, WP], BF16, "wx", sa)
                wB = T([128, 8, 128], BF16, "wB", sa)
                wC = T([128, 8, 128], BF16, "wC", sa)
                wz = T([128, 8, WP], BF16, "wz", sa)
                wdt = T([128, 8, 32], BF16, "wdt", sa)
                upad_r = Rot("upad", [128, nseq, L + 4], BF16, 2, sa)
                diag5_r = Rot("diag5", [128, 5, 128], BF16, 2, sa)
                xg = T([128, nblk, WP], BF16, "xg", sa)
                Btm = T([128, nblk, 128], BF16, "Btm", sa)
                Bfm = T([128, Ttok], BF16, "Bfm", sa)
                Cfm = T([128, Ttok], BF16, "Cfm", sa)
                dt_all = T([128, nblk, 32], F32, "dt_all", sa)
                la_all = T([128, nblk, 32], F32, "la_all", sa)
                v_all = T([128, nblk, 32], F32, "v_all", sa)
                cum_sb = [T([128, nblk, HP], F32, f"cum{d}", sa) for d in range(2)]
                cum_hi = [T([128, nblk, HP], BF16, f"cumhi{d}", sa) for d in range(2)]
                cum_lo = [T([128, nblk, HP], BF16, f"cumlo{d}", sa) for d in range(2)]
                decs_all = [T([128, 3, nblk, HP], F32, f"decs{d}", sa) for d in range(2)]
                diagD = T([128, HP, 128], BF16, "diagD", sa)
                ypark = T([128, nblk, WP], F32, "ypark", sa)
                Sf = [T([128, WP], F32, f"Sf{d}", sa) for d in range(2)]
                Sb = [T([128, WP], BF16, f"Sb{d}", sa) for d in range(2)]
                cbm_r = Rot("cbm", [128, 128], BF16, NROT, sa)
                xdt_r = Rot("xdt", [128, HP, 64], BF16, NROT, sa)
                xs2_r = Rot("xs2", [128, HP, 64], BF16, NROT, sa)
                Lh_r = Rot("Lh", [128, HP, 128], BF16, NROT, sa)
                La_r = Rot("La", [128, HP, 128], F32, 2, sa)
                Mh_r = Rot("Mh", [128, HP, 128], BF16, NROT, sa)
                t1_r = Rot("t1", [128, HP, 64], F32, NROT, sa)
                zs_r = Rot("zs", [128, WP], F32, 2, sa)
                yg_r = Rot("yg", [128, WP], BF16, 2, sa)
                stg_r = Rot("stg", [128, 128], F32, 2, sa)
                for r in upad_r.t:
                    MEMSET(r[:], 0.0, [("upad", upad_r.t.index(r))])

                win_load(wdt[:], I_DT, 32, "wdt")
                for bi in range(nblk):
                    t = bi // 4
                    pd, pdk = PSH()
                    for kc in range(8):
                        MM(pd[:, 0:32], hT[:, kc, bi * 128:(bi + 1) * 128], wdt[:, kc, :], kc == 0, kc == 7, [hk(t), "wdt"], [pdk])
                    TT(v_all[:, bi, :], pd[:, 0:32], dtb[:, l, :], ALU.add, [pdk, "dtb"], ["v_all"])
                TS(dt_all[:], v_all[:], 30.0, ALU.min, ["v_all"], ["dt_all"])
                ACT(dt_all[:], dt_all[:], AF.Exp, ["dt_all"], ["dt_all"])
                ACT(dt_all[:], dt_all[:], AF.Ln, ["dt_all"], ["dt_all"], bias=1.0)
                TT(dt_all[:], dt_all[:], v_all[:], ALU.max, ["dt_all", "v_all"], ["dt_all"])
                TT(la_all[:], dt_all[:], aneg[:, l, :].unsqueeze(1).to_broadcast([128, nblk, 32]), ALU.mult, ["dt_all", "aneg"], ["la_all"])
                for q in range(16 // HP):
                    g = (q * HP) // 8
                    nb4 = nblk * HP
                    win_load(wx[:], I_X + q * WP, WP, "wx")
                    if (q * HP) % 8 == 0:
                        win_load(wB[:], I_B + g * 128, 128, "wB")
                        win_load(wC[:], I_C + g * 128, 128, "wC")
                    win_load(wz[:], I_Z + q * WP, WP, "wz")
                    nxc = WP // 128
                    chunks = [("x", a, q * nxc + a, wx, a * 128, "wx") for a in range(nxc)]
                    if (q * HP) % 8 == 0:
                        chunks += [("B", 0, 8 + g, wB, 0, "wB"), ("C", 0, 10 + g, wC, 0, "wC")]
                    for kind, a, cidx, wt, wc0, wk in chunks:
                        upad, upk = upad_r.next()
                        dg, dgk = diag5_r.next()
                        TT(dg[:], ident[:].unsqueeze(1).to_broadcast([128, 5, 128]),
                           cw5[:, l, cidx, :].unsqueeze(2).to_broadcast([128, 5, 128]), ALU.mult, ["ident", "cw5"], [dgk])
                        for t in range(NT):
                            pu, puk = PS()
                            for kc in range(8):
                                MM(pu[:], wt[:, kc, wc0:wc0 + 128], hT[:, kc, t * 512:(t + 1) * 512], kc == 0, kc == 7,
                                   [wk, hk(t)], [puk])
                            if L >= 512:
                                s_, off, n_, c0 = segs(t)[0]
                                P.op("act", (lambda e, o=upad[:, s_, 2 + off:2 + off + 512], i=pu[:]: e.copy(out=o, in_=i)),
                                     [puk], [upk], cost=590.0)
                            else:
                                n = 512 // L
                                P.op("act", (lambda e, o=upad[:, t * n:(t + 1) * n, 2:2 + L],
                                             i=pu[:].rearrange("p (s x) -> p s x", s=n): e.copy(out=o, in_=i)), [puk], [upk], cost=590.0)
                        if kind in ("x", "B"):
                            for s_ in range(nseq):
                                for j in range(nch):
                                    bi = s_ * nch + j
                                    pc, pck = PSH()
                                    for k in range(5):
                                        MM(pc[:, 0:128], upad[:, s_, j * 128 + k:j * 128 + k + 128], dg[:, k, :], k == 0, False,
                                           [upk, dgk], [pck])
                                    MM(pc[:, 0:128], onesb[0:1, 0:128], cb5row[0:1, l * 1536 + cidx * 128:l * 1536 + (cidx + 1) * 128],
                                       False, True, ["onesb", "cb5row"], [pck])
                                    if kind == "x":
                                        ACT(xg[:, bi, a * 128:(a + 1) * 128], pc[:, 0:128], AF.Silu, [pck], [("xg", bi)], group=("xg", l, nseq, q, bi))
                                    else:
                                        ACT(Btm[:, bi, :], pc[:, 0:128], AF.Silu, [pck], [("Btm", bi)])
                        if kind in ("B", "C"):
                            dstT, dkey = (Bfm, "Bfm") if kind == "B" else (Cfm, "Cfm")
                            for t in range(NT):
                                pc, pck = PS()
                                for (s_, off, n_, c0) in segs(t):
                                    for k in range(5):
                                        MM(pc[:, c0:c0 + n_], dg[:, k, :], upad[:, s_, off + k:off + k + n_], k == 0, k == 4,
                                           [upk, dgk], [pck])
                                ACT(dstT[:, t * 512:(t + 1) * 512], pc[:], AF.Silu, [pck, "cb5"], [(dkey, t)],
                                    bias=cb5[:, l, cidx:cidx + 1])
                    TT(diagD[:], ident[:].unsqueeze(1).to_broadcast([128, HP, 128]),
                       dsk[:, l, q * HP:(q + 1) * HP].unsqueeze(2).to_broadcast([128, HP, 128]), ALU.mult, ["ident", "dsk"], ["diagD"])
                    for d in (1, 0):
                        Uin, UinK = (Uf, "Uf") if d == 0 else (Ub, "Ub")
                        SLo, SLoK = (SLf, "SLf") if d == 0 else (SLb, "SLb")
                        pdc, pdck = PSH()
                        la_d = la_all[:, :, d * 16 + q * HP:d * 16 + (q + 1) * HP]
                        for ci, (mt, mk) in enumerate(((Uin, UinK), (SLo, SLoK), (onesf, "onesf"))):
                            MM(pdc[:, ci * nb4:(ci + 1) * nb4].rearrange("p (b h) -> p b h", h=HP), mt[:], la_d, True, True,
                               [mk, "la_all"], [pdck])
                        dcs = decs_all[d]
                        dcsk = ("decs", d)
                        ACT(dcs[:].rearrange("p c b h -> p (c b h)"), pdc[:, 0:3 * nb4], AF.Exp, [pdck], [dcsk])
                        cum = cum_sb[d]
                        cumk = ("cum", d)
                        CP(cum[:].rearrange("p b h -> p (b h)"), pdc[:, 0:nb4], [pdck, dcsk], [cumk])
                        chi, clo = cum_hi[d], cum_lo[d]
                        CP(chi[:], cum[:], [cumk], [("chi", d)])
                        TT(clo[:], cum[:], chi[:], ALU.subtract, [cumk, ("chi", d)], [("clo", d)])
                        TT(cum[:], chi[:], clo[:], ALU.add, [("chi", d), ("clo", d), cumk], [cumk])
                        for s_ in range(nseq):
                            if h0 is None:
                                MEMSET(Sf[d][:], 0.0, [("Sf", d)], eng="dve")
                                MEMSET(Sb[d][:], 0.0, [("Sb", d)], eng="dve")
                            else:
                                for a in range(WP // 128):
                                    sg, sgk = stg_r.next()
                                    DMA(sg[:], h0[l, d, q * WP + a * 128:q * WP + (a + 1) * 128, :], (), [sgk])
                                    pt, ptk = PSH()
                                    TR(pt[:, 0:128], sg[:], ident[:], [sgk, "ident"], [ptk])
                                    CP(Sf[d][:, a * 128:(a + 1) * 128], pt[:, 0:128], [ptk], [("Sf", d)])
                                CP(Sb[d][:], Sf[d][:], [("Sf", d)], [("Sb", d)])
                            order = range(nch) if d == 0 else range(nch - 1, -1, -1)
                            for j in order:
                                bi = s_ * nch + j
                                t = bi // 4
                                tok = slice(bi * 128, (bi + 1) * 128)
                                la_b = la_all[:, bi, d * 16 + q * HP:d * 16 + (q + 1) * HP]
                                dt_b = dt_all[:, bi, d * 16 + q * HP:d * 16 + (q + 1) * HP]
                                pcb, pcbk = PSH()
                                MM(pcb[:, 0:128], Bfm[:, tok], Cfm[:, tok], True, True, [("Bfm", t), ("Cfm", t)], [pcbk])
                                cbm, cbmk = cbm_r.next()
                                TT(cbm[:], pcb[:, 0:128], Uin[:], ALU.mult, [pcbk, UinK], [cbmk])
                                xdt, xdtk = xdt_r.next()
                                xs2, xs2k = xs2_r.next()
                                xg_b = xg[:, bi, :].rearrange("p (h c) -> p h c", h=HP)
                                TT(xdt[:], xg_b, dt_b.unsqueeze(2).to_broadcast([128, HP, 64]), ALU.mult, [("xg", bi), "dt_all"], [xdtk], eng=OFF_ENG)
                                TT(xs2[:], xdt[:], dcs[:, 1, bi, :].unsqueeze(2).to_broadcast([128, HP, 64]), ALU.mult,
                                   [xdtk, dcsk], [xs2k], eng=OFF_ENG)
                                parg, pargk = PS()
                                for h in range(HP):
                                    po_ = parg[:, h * 128:(h + 1) * 128]
                                    MM(po_, chi[:, bi, h:h + 1].to_broadcast([128, 128]), identb[:], True, False, [("chi", d), "identb"], [pargk])
                                    MM(po_, clo[:, bi, h:h + 1].to_broadcast([128, 128]), identb[:], False, True, [("clo", d), "identb"], [pargk])
                                La, Lak = La_r.next()
                                for h in range(HP):
                                    ACT(La[:, h, :], parg[:, h * 128:(h + 1) * 128], AF.Relu, [pargk, cumk], [Lak],
                                        bias=cum[:, bi, h:h + 1], scale=-1.0, group=("relu", l, nseq, q, d, bi))
                                Lh, Lhk = Lh_r.next()
                                ACT(Lh[:].rearrange("p h c -> p (h c)"), La[:].rearrange("p h c -> p (h c)"), AF.Exp, [Lak], [Lhk], scale=-1.0)
                                Mh, Mhk = Mh_r.next()
                                TT(Mh[:], Lh[:], cbm[:].unsqueeze(1).to_broadcast([128, HP, 128]), ALU.mult, [Lhk, cbmk], [Mhk])
                                py, pyk = PSH()
                                for h in range(HP):
                                    MM(py[:, h * 64:(h + 1) * 64], Mh[:, h, :], xdt[:, h, :], True, d == 1, [Mhk, xdtk], [pyk])
                                    if d == 0:
                                        MM(py[:, h * 64:(h + 1) * 64], diagD[:, h, :], xg[:, bi, h * 64:(h + 1) * 64], False, True,
                                           ["diagD", ("xg", bi)], [pyk])
                                po, pok = PSH()
                                MM(po[:, 0:WP], Cfm[:, tok], Sb[d][:], True, True, [("Cfm", t), ("Sb", d)], [pok])
                                t1, t1k = t1_r.next()
                                TT(t1[:], po[:, 0:WP].rearrange("p (h c) -> p h c", h=HP),
                                   dcs[:, 0, bi, :].unsqueeze(2).to_broadcast([128, HP, 64]), ALU.mult, [pok, dcsk], [t1k])
                                t1f = t1[:].rearrange("p h c -> p (h c)")
                                if d == 1:
                                    TT(ypark[:, bi, :], t1f, py[:, 0:WP], ALU.add, [t1k, pyk], [("ypark", bi)])
                                else:
                                    TT(t1f, t1f, py[:, 0:WP], ALU.add, [t1k, pyk], [t1k])
                                    TT(t1f, t1f, ypark[:, bi, :], ALU.add, [t1k, ("ypark", bi)], [t1k], eng=OFF_ENG)
                                    yg, ygk = yg_r.next()
                                    zs, zsk = zs_r.next()
                                    pz, pzk = PSH()
                                    for kc in range(8):
                                        MM(pz[:, 0:WP], hT[:, kc, tok], wz[:, kc, :], kc == 0, kc == 7, [hk(t), "wz"], [pzk])
                                    ACT(zs[:], pz[:, 0:WP], AF.Tanh, [pzk], [zsk], scale=0.5)
                                    STT(zs[:], zs[:], 1.0, pz[:, 0:WP], ALU.add, ALU.mult, [zsk, pzk], [zsk])
                                    TT(yg[:], t1f, zs[:], ALU.mult, [t1k, zsk], [ygk])
                                    for a in range(WP // 128):
                                        TR(psb_t[:, a * 128:(a + 1) * 128], yg[:, a * 128:(a + 1) * 128], identb[:], [ygk, "identb"], ["psb"])
                                    kc0 = q * (WP // 128)
                                    P.op("act", (lambda e, o=yT[:, kc0:kc0 + WP // 128, tok],
                                                 i=psb_t[:, 0:WP].rearrange("p (a c) -> p a c", c=128): e.copy(out=o, in_=i)),
                                         ["psb"], [yk(kc0 + a, bi) for a in range(WP // 128)], cost=400.0)
                                pds, pdsk = PSH()
                                MM(pds[:, 0:WP], Btm[:, bi, :], xs2[:].rearrange("p h c -> p (h c)"), True, True, [("Btm", bi), xs2k], [pdsk])
                                Sf3 = Sf[d][:].rearrange("p (h c) -> p h c", h=HP)
                                TT(Sf3, Sf3, dcs[:, 2, bi, :].unsqueeze(2).to_broadcast([128, HP, 64]), ALU.mult,
                                   [("Sf", d), dcsk], [("Sf", d)], eng=OFF_ENG)
                                TT(Sf[d][:], Sf[d][:], pds[:, 0:WP], ALU.add, [("Sf", d), pdsk], [("Sf", d)])
                                P.op("act", (lambda e, o=Sb[d][:], i=Sf[d][:]: e.copy(out=o, in_=i)), [("Sf", d)], [("Sb", d)], cost=400.0)
                            if ns_out is not None:
                                for a in range(WP // 128):
                                    pt, ptk = PSH()
                                    TR(pt[:, 0:128], Sf[d][:, a * 128:(a + 1) * 128], ident[:], [("Sf", d), "ident"], [ptk])
                                    sg, sgk = stg_r.next()
                                    CP(sg[:], pt[:, 0:128], [ptk], [sgk])
                                    DMA(ns_out[s_, l, d, q * WP + a * 128:q * WP + (a + 1) * 128, :], sg[:], [sgk], (), final=True)

            P.barrier()
            if debug and l == 0:
                DMA(dbg[f"d_yA_{nm}"], yT[:, :, 0:Ttok], allk, (), final=True)
            def ssd_norm(sn):
                sq_r = Rot("sqn", [128, 8, 512], BF16, 1, sn)
                rs_r = Rot("rsn", [128, 512], F32, 2, sn)
                for t in range(NT):
                    sq, sqk = sq_r.next()
                    rs, rsk = rs_r.next()
                    tl = slice(t * 512, (t + 1) * 512)
                    ACT(sq[:], yT[:, 0:8, tl], AF.Square, ytile(range(8), t), [sqk])
                    stats_rs(lambda k: (sq[:, k, :], [sqk]), 8, rs, rsk, [], eps=4.0 * EPS)
                    for k in range(8):
                        STT(yT[:, k, tl], yT[:, k, tl], sng[:, l, k:k + 1], rs[:], ALU.mult, ALU.mult,
                            ytile([k], t) + ["sng", rsk], ytile([k], t))

            pad = 15 * stride
            with ExitStack() as sb_:
                ssd_norm(sb_)
                if debug and l == 0:
                    DMA(dbg[f"d_yB_{nm}"], yT[:, :, 0:Ttok], allk, (), final=True)
                wga = T([128, 8, 1024], BF16, "wga", sb_)
                wgb = T([128, 8, 1024], BF16, "wgb", sb_)
                for j in range(8):
                    win_load(wga[:, :, j * 128:(j + 1) * 128], I_GA + j * 128, 128, ("wga", j))
                    win_load(wgb[:, :, j * 128:(j + 1) * 128], I_GB + j * 128, 128, ("wgb", j))
                hc_r = Rot("hc", [128, nseq, L + 2 * pad], BF16, 2, sb_)
                d31_r = Rot("d31", [128, 31, 128], BF16, 2, sb_)
                sig_r = Rot("sig", [128, 512], F32, 2, sb_)
                accd_r = Rot("accd", [128, 512], F32, 2, sb_)
                accp_r = Rot("accp", [128, 512], F32, 2, sb_)
                accbd_r = Rot("accbd", [128, 512], BF16, 2, sb_)
                accbp_r = Rot("accbp", [128, 512], BF16, 2, sb_)
                for r in hc_r.t:
                    MEMSET(r[:], 0.0, [("hc", hc_r.t.index(r))])
                for j in range(8):
                    hc, hck = hc_r.next()
                    dg, dgk = d31_r.next()
                    TT(dg[:], ident[:].unsqueeze(1).to_broadcast([128, 31, 128]),
                       cw31[:, l, j, :].unsqueeze(2).to_broadcast([128, 31, 128]), ALU.mult, ["ident", "cw31"], [dgk])
                    for t in range(NT):
                        pa, pak = PS()
                        pb, pbk = PS()
                        for kc in range(8):
                            MM(pa[:], wga[:, kc, j * 128:(j + 1) * 128], hT[:, kc, t * 512:(t + 1) * 512], kc == 0, kc == 7, [("wga", j), hk(t)], [pak])
                        for kc in range(8):
                            MM(pb[:], wgb[:, kc, j * 128:(j + 1) * 128], hT[:, kc, t * 512:(t + 1) * 512], kc == 0, kc == 7, [("wgb", j), hk(t)], [pbk])
                        sig, sigk = sig_r.next()
                        ACT(sig[:], pb[:], AF.Sigmoid, [pbk], [sigk])
                        if L >= 512:
                            s_, off, n_, c0 = segs(t)[0]
                            TT(hc[:, s_, pad + off:pad + off + 512], pa[:], sig[:], ALU.mult, [pak, sigk], [hck])
                        else:
                            n = 512 // L
                            TT(hc[:, t * n:(t + 1) * n, pad:pad + L], pa[:].rearrange("p (s x) -> p s x", s=n),
                               sig[:].rearrange("p (s x) -> p s x", s=n), ALU.mult, [pak, sigk], [hck])
                    for t in range(NT):
                        pc, pck = PS()
                        for (s_, off, n_, c0) in segs(t):
                            taps = [k for k in range(31) if off + (k - 15) * stride + n_ > 0 and off + (k - 15) * stride < L]
                            win = lambda k: hc[:, s_, pad + off + (k - 15) * stride:pad + off + (k - 15) * stride + n_]
                            wk_ = lambda k: cw31[:, l, j, k:k + 1]
                            extra = []
                            rest = list(taps)
                            for eng_, ntap, acc_r, accb_r in (("dve", CONV_ND, accd_r, accbd_r), ("pool", CONV_NP, accp_r, accbp_r)):
                                if ntap == 0 or len(rest) - ntap < 4:
                                    continue
                                mine, rest = rest[:ntap], rest[ntap:]
                                acc, acck = acc_r.next()
                                accb, accbk = accb_r.next()
                                c_ = (120.0 + n_ / 0.96) if eng_ == "dve" else (200.0 + n_ / 0.55)
                                for ii, k in enumerate(mine):
                                    last_ = ii == len(mine) - 1
                                    dst, dstk = (accb, accbk) if last_ else (acc, acck)
                                    if ii == 0:
                                        P.op(eng_, (lambda e, o=dst[:, 0:n_], i0=win(k), sc=wk_(k): e.tensor_scalar(
                                            out=o, in0=i0, scalar1=sc, scalar2=None, op0=ALU.mult)), [hck, "cw31"], [dstk], cost=c_)
                                    else:
                                        P.op(eng_, (lambda e, o=dst[:, 0:n_], i0=win(k), sc=wk_(k), i1=acc[:, 0:n_]: e.scalar_tensor_tensor(
                                            out=o, in0=i0, scalar=sc, in1=i1, op0=ALU.mult, op1=ALU.add)), [hck, "cw31", acck], [dstk], cost=c_)
                                extra.append((accb, accbk))
                            nmm = len(rest) + len(extra)
                            im = 0
                            for k in rest:
                                MM(pc[:, c0:c0 + n_], dg[:, k, :], win(k), im == 0, im == nmm - 1, [hck, dgk], [pck])
                                im += 1
                            for accb, accbk in extra:
                                MM(pc[:, c0:c0 + n_], identb[:], accb[:, 0:n_], im == 0, im == nmm - 1, ["identb", accbk], [pck])
                                im += 1
                        ACT(yT[:, 8 + j, t * 512:(t + 1) * 512], pc[:], AF.Identity, [pck, "cb31"], ytile([8 + j], t),
                            bias=cb31[:, l, j:j + 1])
            P.barrier()
            with ExitStack() as sb_:
                wgs = T([128, 8, 1024], BF16, "wgs", sb_)
                for j in range(8):
                    win_load(wgs[:, :, j * 128:(j + 1) * 128], I_GS + j * 128, 128, ("wgs", j))
                sq_r = Rot("sqc", [128, 8, 512], BF16, 2, sb_)
                mean_r = Rot("mean", [128, 512], F32, 2, sb_)
                rs_r = Rot("rsc", [128, 512], F32, 2, sb_)
                tmp_r = Rot("tmpc", [128, 512], F32, 2, sb_)
                s1_r = Rot("s1c", [128, 512], F32, 2, sb_)
                for t in range(NT):
                    tl = slice(t * 512, (t + 1) * 512)
                    sq, sqk = sq_r.next()
                    mean, meank = mean_r.next()
                    rs, rsk = rs_r.next()
                    p1, p1k = PS()
                    for k in range(8):
                        MM(p1[:], onesb[:], yT[:, 8 + k, tl], k == 0, k == 7, ["onesb"] + ytile([8 + k], t), [p1k])
                    ACT(sq[:], yT[:, 8:16, tl], AF.Square, ytile(range(8, 16), t), [sqk])
                    p2, p2k = PS()
                    for k in range(8):
                        MM(p2[:], onesb[:], sq[:, k, :], k == 0, k == 7, ["onesb", sqk], [p2k])
                    TS(mean[:], p1[:], 1.0 / 1024.0, ALU.mult, [p1k], [meank])
                    tmp, tmpk = tmp_r.next()
                    TT(tmp[:], mean[:], mean[:], ALU.mult, [meank], [tmpk])
                    STT(tmp[:], p2[:], 1.0 / 1024.0, tmp[:], ALU.mult, ALU.subtract, [p2k, tmpk], [tmpk])
                    ACT(rs[:], tmp[:], AF.Ln, [tmpk], [rsk], bias=EPS)
                    ACT(rs[:], rs[:], AF.Exp, [rsk], [rsk], scale=-0.5)
                    for j in range(8):
                        tmp, tmpk = tmp_r.next()
                        s1, s1k = s1_r.next()
                        TT(tmp[:], yT[:, 8 + j, tl], mean[:], ALU.subtract, ytile([8 + j], t) + [meank], [tmpk])
                        TT(tmp[:], tmp[:], rs[:], ALU.mult, [tmpk, rsk], [tmpk])
                        ACT(s1[:], tmp[:], AF.Silu, [tmpk, "lng", "lnb"], [s1k], bias=lnb[:, l, j:j + 1], scale=lng[:, l, j:j + 1])
                        pg, pgk = PS()
                        for kc in range(8):
                            MM(pg[:], wgs[:, kc, j * 128:(j + 1) * 128], hT[:, kc, tl], kc == 0, kc == 7, [("wgs", j), hk(t)], [pgk])
                        ACT(tmp[:], pg[:], AF.Silu, [pgk], [tmpk])
                        TT(yT[:, 8 + j, tl], s1[:], tmp[:], ALU.mult, [s1k, tmpk], ytile([8 + j], t))

            P.barrier()
            if debug and l == 0:
                DMA(dbg[f"d_yC_{nm}"], yT[:, :, 0:Ttok], allk, (), final=True)
            with ExitStack() as sc:
                wo = T([128, 16, 1024], BF16, "wo", sc)
                for fo in range(8):
                    DMA(wo[:, :, fo * 128:(fo + 1) * 128], wout_d[l].rearrange("(kc p) c -> p kc c", p=128)[:, :, fo * 128:(fo + 1) * 128],
                        (), [("wo", fo)], eng="pool")
                osb_r = Rot("osb", [128, 8, 512], F32, 2, sc)
                sq_r = Rot("sqo", [128, 8, 512], BF16, 1, sc)
                rs_r = Rot("rso", [128, 512], F32, 2, sc)
                xt_r = Rot("xto", [128, 8, 512], F32, 1, sc)
                for t in range(NT):
                    tl = slice(t * 512, (t + 1) * 512)
                    osb, osbk = osb_r.next()
                    sq, sqk = sq_r.next()
                    rs, rsk = rs_r.next()
                    xt, xtk = xt_r.next()
                    DMA(xt[:], xsrc_v[:, :, tl], [("xd", id(x_src), t)], [xtk])
                    for fo in range(8):
                        po, pok = PS()
                        for kc in range(16):
                            MM(po[:], wo[:, kc, fo * 128:(fo + 1) * 128], yT[:, kc, tl], kc == 0, kc == 15,
                               [("wo", fo)] + ytile([kc], t), [pok])
                        P.op("act", (lambda e, o=osb[:, fo, :], i=po[:]: e.copy(out=o, in_=i)), [pok], [(osbk, fo)], cost=590.0)
                        ACT(sq[:, fo, :], po[:], AF.Square, [pok], [(sqk, fo)])
                    stats_rs(lambda k: (sq[:, k, :], [(sqk, k)]), 8, rs, rsk, [])
                    for fo in range(8):
                        TT(osb[:, fo, :], osb[:, fo, :], rs[:], ALU.mult, [(osbk, fo), rsk], [(osbk, fo)])
                        STT(osb[:, fo, :], osb[:, fo, :], modG[:, l, fo, wsel:wsel + 1], xt[:, fo, :], ALU.mult, ALU.add,
                            [(osbk, fo), "modG", xtk], [(osbk, fo)])
                    allosb = [(osbk, k) for k in range(8)]
                    DMA(xdst_v[:, :, tl], osb[:], allosb, [("xd", id(x_dst), t)], final=final_out)
                    if fuse_next:
                        allsq = [(sqk, k) for k in range(8)]
                        ACT(sq[:], osb[:], AF.Square, allosb, allsq)
                        rs2, rs2k = rs_r.next()
                        stats_rs(lambda k: (sq[:, k, :], [(sqk, k)]), 8, rs2, rs2k, [])
                        TT(xt[:], osb[:], rs2[:].unsqueeze(1).to_broadcast([128, 8, 512]), ALU.mult, allosb + [rs2k], [xtk])
                        for kc in range(8):
                            ACT(hT[:, kc, tl], xt[:, kc, :], AF.Identity, [xtk, "modA", "modB"], [hk(t)],
                                bias=modB[:, l + 1, kc, wsel:wsel + 1], scale=modA[:, l + 1, kc, wsel:wsel + 1],
                                group=("hTn", l, t, nseq))
            P.barrier()

        for nm_ in ("P", "S"):
            for l in range(DEPTH):
                last = (l == DEPTH - 1)
                if only is not None and (l, nm_) not in only:
                    continue
                if nm_ == "P":
                    run_block(l, xp_d if l == 0 else x1p_d, yp_d if last else x1p_d, 2, 256, 1, 0, None, ns_d, last or debug,
                              l == 0 or debug, (not last) and not debug)
                else:
                    run_block(l, xs_d if l == 0 else x1s_d, ys_d if last else x1s_d, 1, 2048, 64, 1, h0_d, None, last or debug,
                              l == 0 or debug, (not last) and not debug)
        P.emit()
        n_ins = len(P.ins)
    return nc, n_ins


_CACHE = {}


def _fm(v):
    v = np.asarray(v, np.float32)
    lead = v.shape[:-1]
    nchunk = v.shape[-1] // 128
    r = v.reshape(lead + (nchunk, 128))
    return np.ascontiguousarray(np.moveaxis(r, -1, 0))


def kernel(x_prompt, x_sample, state_ssd, c, c_ctx, w_mod, b_mod, g_pre, g_post, w_in,
           ssd_conv_w, ssd_conv_b, ssd_a_log, ssd_dt_bias, ssd_d, ssd_norm_g,
           conf_conv_w, conf_conv_b, conf_ln_g, conf_ln_b, w_out):
    f = lambda a: np.ascontiguousarray(np.asarray(a, np.float32))
    x_prompt, x_sample, state_ssd = f(x_prompt), f(x_sample), f(state_ssd)
    if "nc" not in _CACHE:
        _CACHE["nc"] = build_program()[0]
    nc = _CACHE["nc"]
    rep = lambda a: np.ascontiguousarray(np.broadcast_to(f(a).reshape(1, DEPTH, -1), (128, DEPTH, f(a).reshape(DEPTH, -1).shape[1])))
    shared = {
        "w_mod": f(w_mod), "b_mod": _fm(b_mod), "g_pre": _fm(g_pre), "g_post": _fm(g_post), "w_in": f(w_in),
        "cw5": np.ascontiguousarray(np.transpose(f(ssd_conv_w).reshape(DEPTH, 5, 12, 128), (3, 0, 2, 1))),
        "cb5": _fm(ssd_conv_b), "cb5row": f(ssd_conv_b).reshape(1, DEPTH * 1536),
        "alog": rep(ssd_a_log), "dtb": rep(ssd_dt_bias), "dsk": rep(ssd_d), "sng": _fm(ssd_norm_g),
        "cw31": np.ascontiguousarray(np.transpose(f(conf_conv_w).reshape(DEPTH, 31, 8, 128), (3, 0, 2, 1))),
        "cb31": _fm(conf_conv_b), "lng": _fm(conf_ln_g), "lnb": _fm(conf_ln_b), "w_out": f(w_out),
    }
    in_maps = []
    for core in range(NCORES):
        b = core // 4
        m = dict(shared)
        m["xp"] = np.ascontiguousarray(x_prompt[2 * core:2 * core + 2].reshape(512, D).T)
        m["xs"] = np.ascontiguousarray(x_sample[b].T)
        m["h0"] = np.ascontiguousarray(state_ssd[b].reshape(DEPTH, 2, 1024, 128))
        cv = np.stack([f(c_ctx), f(c)[b]], axis=-1)
        m["cvec"] = np.ascontiguousarray(np.transpose(cv.reshape(8, 128, 2), (1, 0, 2)))
        in_maps.append(m)
    res = run_bass_kernel_spmd(nc, in_maps, core_ids=list(range(NCORES)))
    r = res.results
    y_prompt = np.stack([r[core]["yp"].T.reshape(2, 256, D) for core in range(NCORES)], 0).reshape(16, 256, D)
    y_sample = np.stack([r[0]["ys"].T, r[4]["ys"].T], 0)
    new_state = np.concatenate([r[core]["ns"] for core in range(NCORES)], 0).reshape(16, DEPTH, 2, 16, 64, 128)
    return (np.ascontiguousarray(y_prompt, dtype=np.float32), np.ascontiguousarray(y_sample, dtype=np.float32),
            np.ascontiguousarray(new_state, dtype=np.float32))
```

```python
import numpy as np
from contextlib import ExitStack
import concourse.bass as bass
import concourse.mybir as mybir
from concourse.bass_utils import run_bass_kernel_spmd

F32 = mybir.dt.float32
BF16 = mybir.dt.bfloat16
AF = mybir.ActivationFunctionType
ALU = mybir.AluOpType

D = 1024
DEPTH = 2
NCORES = 8
EPS = 1e-6
I_Z, I_X, I_B, I_C, I_DT, I_GA, I_GB, I_GS = 0, 1024, 2048, 2304, 2560, 2592, 3616, 4640
IN_COLS = 5664
HP = 4
WP = HP * 64
TMAX = 2048
NPS = 7
CONV_ND = 8
CONV_NP = 0
NROT = 3
OFF_ENG = "pool"


class Prog:
    SEM_LIMIT = 4000
    WINDOW = 128
    SEM_LAT = 400.0

    def __init__(self, nc, stack, same_engine_sync=True, schedule=True):
        self.nc = nc
        self.stack = stack
        self.engs = {"pe": nc.tensor, "act": nc.scalar, "dve": nc.vector, "pool": nc.gpsimd, "sp": nc.sync}
        self.ins = []
        self.last_w = {}
        self.readers = {}
        self.same_engine_sync = same_engine_sync
        self.schedule = schedule
        self.n_dma_sems = {"sp": 16, "pool": 8, "act": 4, "dve": 4, "pe": 4}
        self.out_dmas = []
        self.w_rdeps = {}
        self.phase = 0

    def barrier(self):
        self.phase += 1

    def op(self, eng, fn, reads=(), writes=(), dma=False, final=False, cost=300.0, lat=0.0, group=None):
        deps = set()
        for r in reads:
            if r in self.last_w:
                deps |= set(self.last_w[r][1])
        i = len(self.ins)
        for w in writes:
            same = False
            if w in self.last_w:
                gid, members = self.last_w[w]
                same = group is not None and gid == group
                if not same:
                    deps |= set(members)
            if same:
                deps |= self.w_rdeps.get(w, set())
            else:
                rd = set(self.readers.get(w, set()))
                deps |= rd
                self.w_rdeps[w] = (set(self.last_w[w][1]) if w in self.last_w else set()) | rd
        deps.discard(i)
        self.ins.append(dict(eng=eng, fn=fn, deps=deps, dma=dma, cost=cost, lat=lat, phase=self.phase))
        for r in reads:
            self.readers.setdefault(r, set()).add(i)
        for w in writes:
            if w in self.last_w and group is not None and self.last_w[w][0] == group:
                self.last_w[w][1].append(i)
            else:
                self.last_w[w] = (group, [i])
                self.readers[w] = set()
        if final:
            self.out_dmas.append(i)
        return i

    def _order(self):
        ins = self.ins
        n = len(ins)
        per_eng = {e: [] for e in self.engs}
        for i, it in enumerate(ins):
            per_eng[it["eng"]].append(i)
        if not self.schedule:
            return per_eng
        users = [[] for _ in range(n)]
        nun = [0] * n
        for i, it in enumerate(ins):
            nun[i] = len(it["deps"])
            for d in it["deps"]:
                users[d].append(i)
        blev = [0.0] * n
        for i in range(n - 1, -1, -1):
            it = ins[i]
            m = 0.0
            for u in users[i]:
                if ins[u]["phase"] == it["phase"] and blev[u] > m:
                    m = blev[u]
            blev[i] = it["cost"] + it["lat"] + m
        rdy = [0.0] * n
        self.t_start = [0.0] * n
        self.t_fin = [0.0] * n
        nphase = self.phase + 1
        left = [0] * nphase
        for it in ins:
            left[it["phase"]] += 1
        cur = 0
        while cur < nphase and left[cur] == 0:
            cur += 1
        phase_t = 0.0
        tmax = 0.0
        eng_free = {e: 0.0 for e in self.engs}
        pend = {e: list(v) for e, v in per_eng.items()}
        order = {e: [] for e in self.engs}
        remaining = n
        while remaining:
            best = None
            for e, lst in pend.items():
                cand = None
                ef = eng_free[e]
                for i in lst[:self.WINDOW]:
                    it = ins[i]
                    if it["phase"] != cur:
                        break
                    if nun[i]:
                        continue
                    stt = max(rdy[i], ef, phase_t)
                    key = (stt, -blev[i]) if stt > ef + 1e-9 else (ef, -blev[i])
                    if cand is None or key < cand[2]:
                        cand = (key[0], i, key)
                if cand is not None and (best is None or cand[0] < best[0] - 1e-9 or
                                         (abs(cand[0] - best[0]) <= 1e-9 and cand[1] < best[1])):
                    best = (cand[0], cand[1], e)
            assert best is not None, "scheduler stuck"
            stt, i, e = best
            it = ins[i]
            eng_free[e] = stt + it["cost"]
            f = stt + it["cost"] + it["lat"]
            self.t_start[i] = stt
            self.t_fin[i] = f
            tmax = max(tmax, f)
            for u in users[i]:
                nun[u] -= 1
                fl = f if (ins[u]["eng"] == e and e == "pe" and not it["dma"]) else f + self.SEM_LAT
                if fl > rdy[u]:
                    rdy[u] = fl
            pend[e].remove(i)
            order[e].append(i)
            remaining -= 1
            left[cur] -= 1
            if left[cur] == 0:
                while cur < nphase and left[cur] == 0:
                    cur += 1
                phase_t = tmax + 200.0
        self.sim_time = tmax
        return order

    def emit(self):
        nc = self.nc
        ins = self.ins
        n = len(ins)
        order = self._order()
        pos = [0] * n
        for e, lst in order.items():
            for k, i in enumerate(lst):
                pos[i] = k
        last_before = {}
        for e, lst in order.items():
            cuts = {}
            for k, i in enumerate(lst):
                cuts.setdefault(ins[i]["phase"], k)
            last_before[e] = (lst, cuts)
        for e, lst in order.items():
            seen = -1
            for i in lst:
                p = ins[i]["phase"]
                if p == seen:
                    continue
                seen = p
                if p == 0:
                    continue
                extra = set()
                for e2, (lst2, cuts2) in last_before.items():
                    ks = [k for ph, k in cuts2.items() if ph >= p]
                    endk = min(ks) if ks else len(lst2)
                    if endk == 0:
                        continue
                    extra.add(lst2[endk - 1])
                    nd = self.n_dma_sems[e2]
                    cnt = 0
                    for k in range(endk - 1, -1, -1):
                        if ins[lst2[k]]["dma"]:
                            extra.add(lst2[k])
                            cnt += 1
                            if cnt >= nd:
                                break
                extra.discard(i)
                ins[i]["deps"] = set(ins[i]["deps"]) | extra
        pruned = [None] * n
        for i, it in enumerate(ins):
            e = it["eng"]
            keep = {}
            dmas = []
            for d in it["deps"]:
                p = ins[d]
                if p["dma"]:
                    dmas.append(d)
                    continue
                if p["eng"] == e and (e == "pe" or not self.same_engine_sync):
                    continue
                pe_ = p["eng"]
                if pe_ not in keep or pos[d] > pos[keep[pe_]]:
                    keep[pe_] = d
            pruned[i] = list(keep.values()) + dmas
        needed = [False] * n
        for i in range(n):
            for d in pruned[i]:
                needed[d] = True
        for i in self.out_dmas:
            needed[i] = True
        sem_of = [None] * n
        dma_prev = [None] * n
        for e, lst in order.items():
            nd = self.n_dma_sems[e]
            dsems = None
            dcnt = None
            rr = 0
            cur = None
            ccnt = 0
            k = 0
            for i in lst:
                it = ins[i]
                if it["dma"]:
                    if dsems is None:
                        dsems = [self.stack.enter_context(nc.semaphore(f"dq_{e}_{j}")) for j in range(nd)]
                        dcnt = [0] * nd
                    j = rr
                    rr = (rr + 1) % nd
                    if dcnt[j] > 0:
                        dma_prev[i] = (dsems[j], dcnt[j])
                    dcnt[j] += 16
                    sem_of[i] = (dsems[j], dcnt[j])
                elif needed[i]:
                    if cur is None or ccnt >= self.SEM_LIMIT:
                        cur = self.stack.enter_context(nc.semaphore(f"s_{e}_{k}"))
                        k += 1
                        ccnt = 0
                    ccnt += 1
                    sem_of[i] = (cur, ccnt)
        for e, lst in order.items():
            eng = self.engs[e]
            waited = {}

            def do_wait(sem, cnt):
                key = id(sem)
                if waited.get(key, 0) >= cnt:
                    return
                eng.wait_ge(sem, cnt)
                waited[key] = cnt

            for i in lst:
                it = ins[i]
                ws = [sem_of[d] for d in pruned[i]]
                ws.sort(key=lambda sc: -sc[1])
                for sem, c in ws:
                    do_wait(sem, c)
                if it["dma"] and dma_prev[i] is not None:
                    do_wait(*dma_prev[i])
                inst = it["fn"](eng)
                if sem_of[i] is not None:
                    inst.then_inc(sem_of[i][0], 16 if it["dma"] else 1)
            if e == "sp":
                for i in self.out_dmas:
                    do_wait(*sem_of[i])


def build_program(debug=False, only=None):
    nc = bass.Bass("TRN2", target_bir_lowering=False)
    dt_in = lambda name, shape: nc.dram_tensor(name, shape, F32, kind="ExternalInput").ap()
    dt_out = lambda name, shape: nc.dram_tensor(name, shape, F32, kind="ExternalOutput").ap()
    xp_d = dt_in("xp", [D, 512])
    xs_d = dt_in("xs", [D, 2048])
    h0_d = dt_in("h0", [DEPTH, 2, 1024, 128])
    cvec_d = dt_in("cvec", [128, 8, 2])
    wmod_d = dt_in("w_mod", [DEPTH, D, 3 * D])
    bmod_d = dt_in("b_mod", [128, DEPTH, 24])
    gpre_d = dt_in("g_pre", [128, DEPTH, 8])
    gpost_d = dt_in("g_post", [128, DEPTH, 8])
    win_d = dt_in("w_in", [DEPTH, D, IN_COLS])
    cw5_d = dt_in("cw5", [128, DEPTH, 12, 5])
    cb5_d = dt_in("cb5", [128, DEPTH, 12])
    cb5row_d = dt_in("cb5row", [1, DEPTH * 1536])
    alog_d = dt_in("alog", [128, DEPTH, 32])
    dtb_d = dt_in("dtb", [128, DEPTH, 32])
    dsk_d = dt_in("dsk", [128, DEPTH, 16])
    sng_d = dt_in("sng", [128, DEPTH, 8])
    cw31_d = dt_in("cw31", [128, DEPTH, 8, 31])
    cb31_d = dt_in("cb31", [128, DEPTH, 8])
    lng_d = dt_in("lng", [128, DEPTH, 8])
    lnb_d = dt_in("lnb", [128, DEPTH, 8])
    wout_d = dt_in("w_out", [DEPTH, 2 * D, D])
    yp_d = dt_out("yp", [D, 512])
    ys_d = dt_out("ys", [D, 2048])
    ns_d = dt_out("ns", [2, DEPTH, 2, 1024, 128])
    dbg = {}
    if debug:
        dbg["d_modA"] = dt_out("d_modA", [128, DEPTH, 8, 2])
        dbg["d_modB"] = dt_out("d_modB", [128, DEPTH, 8, 2])
        dbg["d_modG"] = dt_out("d_modG", [128, DEPTH, 8, 2])
        for nm, T_ in (("P", 512), ("S", 2048)):
            dbg[f"d_hT_{nm}"] = nc.dram_tensor(f"d_hT_{nm}", [128, 8, T_], BF16, kind="ExternalOutput").ap()
            dbg[f"d_yA_{nm}"] = nc.dram_tensor(f"d_yA_{nm}", [128, 16, T_], BF16, kind="ExternalOutput").ap()
            dbg[f"d_yB_{nm}"] = nc.dram_tensor(f"d_yB_{nm}", [128, 16, T_], BF16, kind="ExternalOutput").ap()
            dbg[f"d_yC_{nm}"] = nc.dram_tensor(f"d_yC_{nm}", [128, 16, T_], BF16, kind="ExternalOutput").ap()
    x1p_d = nc.dram_tensor("x1p", [D, 512], F32, kind="ExternalOutput" if debug else "Internal").ap()
    x1s_d = nc.dram_tensor("x1s", [D, 2048], F32, kind="ExternalOutput" if debug else "Internal").ap()

    with ExitStack() as st:
        P = Prog(nc, st)
        cnt = [0]

        def T(shape, dt, name=None, stack=None):
            cnt[0] += 1
            return (stack or st).enter_context(nc.sbuf_tensor(f"sb{cnt[0]}_{name or 't'}", shape, dt))

        def nfree(ap):
            r = 1
            for d in ap.shape[1:]:
                r *= d
            return r

        def DMA(out, in_, reads=(), writes=(), eng="sp", final=False):
            nbytes = nfree(out) * out.shape[0] * 4
            P.op(eng, lambda e: e.dma_start(out=out, in_=in_), reads, writes, dma=True, final=final,
                 cost=(150.0 if eng == "sp" else 1200.0), lat=2000.0 + nbytes / 120.0)

        def MM(out, lhsT, rhs, start, stop, reads, writes):
            passes = 4 if lhsT.dtype == F32 else 1
            P.op("pe", lambda e: e.matmul(out, lhsT=lhsT, rhs=rhs, start=start, stop=stop), reads, writes,
                 cost=30.0 + passes * max(nfree(rhs), 64) / 2.4, lat=120.0)

        def TR(out, in_, ident, reads, writes):
            P.op("pe", lambda e: e.transpose(out=out, in_=in_, identity=ident), reads, writes,
                 cost=(4 if in_.dtype == F32 else 1) * 60.0 + 30.0, lat=120.0)

        def ACT(out, in_, func, reads, writes, bias=None, scale=None, group=None):
            kw = {}
            if bias is not None:
                kw["bias"] = bias
            if scale is not None:
                kw["scale"] = scale
            P.op("act", lambda e: e.activation(out=out, in_=in_, func=func, **kw), reads, writes, cost=220.0 + nfree(out) / 1.4,
                 group=group)

        def TT(out, in0, in1, op, reads, writes, eng="dve"):
            c = 120.0 + nfree(out) / 0.96 if eng != "pool" else 200.0 + nfree(out) / 0.55
            P.op(eng, lambda e: e.tensor_tensor(out=out, in0=in0, in1=in1, op=op), reads, writes, cost=c)

        def TS(out, in0, s1, op0, reads, writes, s2=None, op1=None, eng="dve"):
            if op1 is None:
                P.op(eng, lambda e: e.tensor_scalar(out=out, in0=in0, scalar1=s1, scalar2=None, op0=op0), reads, writes,
                     cost=120.0 + nfree(out) / 0.96)
            else:
                P.op(eng, lambda e: e.tensor_scalar(out=out, in0=in0, scalar1=s1, scalar2=s2, op0=op0, op1=op1), reads, writes,
                     cost=120.0 + nfree(out) / 0.96)

        def STT(out, in0, scalar, in1, op0, op1, reads, writes, eng="dve"):
            P.op(eng, lambda e: e.scalar_tensor_tensor(out=out, in0=in0, scalar=scalar, in1=in1, op0=op0, op1=op1), reads, writes,
                 cost=120.0 + nfree(out) / 0.96)

        def CP(out, in_, reads, writes, eng="dve"):
            P.op(eng, lambda e: e.tensor_copy(out=out, in_=in_), reads, writes, cost=120.0 + nfree(out) / 0.96)

        def RECIP(out, in_, reads, writes):
            P.op("dve", lambda e: e.reciprocal(out=out, in_=in_), reads, writes, cost=120.0 + nfree(out) * 6.5)

        def MEMSET(ap, val, writes, eng="pool"):
            P.op(eng, lambda e: e.memset(ap, val), (), writes, cost=150.0 + nfree(ap) / 1.0)

        class Rot:
            def __init__(self, name, shape, dt, n, stack):
                self.t = [T(shape, dt, f"{name}{i}", stack) for i in range(n)]
                self.name = name
                self.i = 0

            def next(self):
                k = self.i % len(self.t)
                self.i += 1
                return self.t[k], (self.name, k)

        ps_t = [st.enter_context(nc.psum_tensor(f"ps{i}", [128, 512], F32)) for i in range(NPS)]
        psb_t = st.enter_context(nc.psum_tensor("psb", [128, 1024], BF16))
        ps_i = [0]

        def PS():
            k = ps_i[0] % NPS
            ps_i[0] += 1
            return ps_t[k], ("ps", k)

        PSH = PS

        ident = T([128, 128], F32, "ident")
        identb = T([128, 128], BF16, "identb")
        onesb = T([128, 128], BF16, "onesb")
        onesf = T([128, 128], F32, "onesf")
        Uf = T([128, 128], F32, "Uf")
        SLf = T([128, 128], F32, "SLf")
        Ub = T([128, 128], F32, "Ub")
        SLb = T([128, 128], F32, "SLb")
        MEMSET(onesf[:], 1.0, ["onesf"])
        MEMSET(onesb[:], 1.0, ["onesb"])

        def SEL(t, key, cm, pat, op):
            MEMSET(t[:], 1.0, [key])
            P.op("pool", lambda e: e.affine_select(out=t[:], in_=t[:], pattern=[[pat, 128]], compare_op=op,
                                                   fill=0.0, base=0, channel_multiplier=cm), [key], [key])
        SEL(Uf, "Uf", -1, 1, ALU.is_ge)
        SEL(SLf, "SLf", 1, -1, ALU.is_gt)
        SEL(Ub, "Ub", 1, -1, ALU.is_ge)
        SEL(SLb, "SLb", -1, 1, ALU.is_gt)
        MEMSET(ident[:], 0.0, ["ident"])
        P.op("pool", lambda e: e.affine_select(out=ident[:], in_=ident[:], pattern=[[-1, 128]], compare_op=ALU.not_equal,
                                               fill=1.0, base=0, channel_multiplier=1), ["ident"], ["ident"])
        CP(identb[:], ident[:], ["ident"], ["identb"])

        def LOADP(dram, shape, name):
            t = T(shape, F32, name)
            DMA(t[:], dram, (), [name])
            return t
        cvec = LOADP(cvec_d, [128, 8, 2], "cvec")
        bmod = LOADP(bmod_d, [128, DEPTH, 24], "bmod")
        gpre = LOADP(gpre_d, [128, DEPTH, 8], "gpre")
        gpost = LOADP(gpost_d, [128, DEPTH, 8], "gpost")
        cw5 = LOADP(cw5_d, [128, DEPTH, 12, 5], "cw5")
        cb5 = LOADP(cb5_d, [128, DEPTH, 12], "cb5")
        alog = LOADP(alog_d, [128, DEPTH, 32], "alog")
        dtb = LOADP(dtb_d, [128, DEPTH, 32], "dtb")
        dsk = LOADP(dsk_d, [128, DEPTH, 16], "dsk")
        sng = LOADP(sng_d, [128, DEPTH, 8], "sng")
        cw31 = LOADP(cw31_d, [128, DEPTH, 8, 31], "cw31")
        cb31 = LOADP(cb31_d, [128, DEPTH, 8], "cb31")
        lng = LOADP(lng_d, [128, DEPTH, 8], "lng")
        lnb = LOADP(lnb_d, [128, DEPTH, 8], "lnb")
        cb5row = T([1, DEPTH * 1536], BF16, "cb5row")
        for l_ in range(DEPTH):
            DMA(cb5row[:, l_ * 1536:(l_ + 1) * 1536], cb5row_d[:, l_ * 1536:(l_ + 1) * 1536], (), ["cb5row"], eng="pool")
        aneg = T([128, DEPTH, 32], F32, "aneg")
        ACT(aneg[:], alog[:], AF.Exp, ["alog"], ["aneg"])
        TS(aneg[:], aneg[:], -1.0, ALU.mult, ["aneg"], ["aneg"])

        silc = T([128, 8, 2], F32, "silc")
        ACT(silc[:], cvec[:], AF.Silu, ["cvec"], ["silc"])
        modA = T([128, DEPTH, 8, 2], F32, "modA")
        modB = T([128, DEPTH, 8, 2], F32, "modB")
        modG = T([128, DEPTH, 8, 2], F32, "modG")
        hT = T([128, 8, TMAX], BF16, "hT")
        yT = T([128, 16, TMAX], BF16, "yT")
        ms = ExitStack()
        st.callback(ms.close)
        if True:
            wm = Rot("wm", [128, 8, 512], F32, 2, ms)
            modsb = T([128, 24, 2], F32, "modsb", ms)
            modrow = T([2, 3 * D], F32, "modrow", ms)
            for l in range(DEPTH):
                for cb in range(6):
                    wt, wk = wm.next()
                    DMA(wt[:], wmod_d[l].rearrange("(kc p) c -> p kc c", p=128)[:, :, cb * 512:(cb + 1) * 512], (), [wk])
                    pr, prk = PS()
                    for kc in range(8):
                        MM(pr[0:2, :], silc[:, kc, :], wt[:, kc, :], kc == 0, kc == 7, [wk, "silc"], [prk])
                    CP(modrow[:, cb * 512:(cb + 1) * 512], pr[0:2, :], [prk], [("modrow", cb)])
                pm, pmk = PSH()
                for f in range(24):
                    TR(pm[:, f * 2:f * 2 + 2], modrow[0:2, f * 128:(f + 1) * 128], ident[0:2, 0:2], [("modrow", f // 4), "ident"], [pmk])
                TT(modsb[:], pm[:, 0:48].rearrange("p (f w) -> p f w", w=2),
                   bmod[:, l, :].unsqueeze(2).to_broadcast([128, 24, 2]), ALU.add, [pmk, "bmod"], ["modsb"])
                TS(modA[:, l], modsb[:, 8:16, :], 1.0, ALU.add, ["modsb"], ["modA"])
                TT(modA[:, l], modA[:, l], gpre[:, l, :].unsqueeze(2).to_broadcast([128, 8, 2]), ALU.mult, ["modA", "gpre"], ["modA"])
                CP(modB[:, l], modsb[:, 0:8, :], ["modsb"], ["modB"])
                TT(modG[:, l], modsb[:, 16:24, :], gpost[:, l, :].unsqueeze(2).to_broadcast([128, 8, 2]), ALU.mult,
                   ["modsb", "gpost"], ["modG"])

        if debug:
            DMA(dbg["d_modA"], modA[:], ["modA"], (), final=True)
            DMA(dbg["d_modB"], modB[:], ["modB"], (), final=True)
            DMA(dbg["d_modG"], modG[:], ["modG"], (), final=True)

        def stats_rs(src_sq_fn, nk, rs, rsk, extra_reads, eps=EPS):
            pst, pstk = PS()
            for k in range(nk):
                ap, rd = src_sq_fn(k)
                MM(pst[:], onesb[:], ap, k == 0, k == nk - 1, ["onesb"] + rd, [pstk])
            ACT(rs[:], pst[:], AF.Ln, [pstk], [rsk], bias=eps, scale=1.0 / 1024.0)
            ACT(rs[:], rs[:], AF.Exp, [rsk], [rsk], scale=-0.5)

        ms_holder = [ms]

        def run_block(l, x_src, x_dst, nseq, L, stride, wsel, h0, ns_out, final_out, do_front, fuse_next):
            Ttok = nseq * L
            NT = Ttok // 512
            nch = L // 128
            nblk = nseq * nch
            xsrc_v = x_src.rearrange("(kc p) t -> p kc t", p=128)
            xdst_v = x_dst.rearrange("(kc p) t -> p kc t", p=128)
            hk = lambda t: ("hT", t)
            yk = lambda k, b: ("yT", k, b)
            ytile = lambda ks, t: [yk(k, b) for k in ks for b in range(4 * t, 4 * t + 4)]

            def segs(t):
                if L >= 512:
                    per = L // 512
                    return [(t // per, (t % per) * 512, 512, 0)]
                n = 512 // L
                return [(t * n + i, 0, L, i * L) for i in range(n)]

            def win_load(dst, c0, w, key):
                DMA(dst, win_d[l].rearrange("(kc p) c -> p kc c", p=128)[:, :, c0:c0 + w], (), [key], eng="pool")

            with ExitStack() as s0:
                xt_r = Rot("xt", [128, 8, 512], F32, 2, s0)
                sq_r = Rot("sq0", [128, 8, 512], BF16, 2, s0)
                rs_r = Rot("rs0", [128, 512], F32, 2, s0)
                for t in range(NT if do_front else 0):
                    xt, xtk = xt_r.next()
                    sq, sqk = sq_r.next()
                    rs, rsk = rs_r.next()
                    DMA(xt[:], xsrc_v[:, :, t * 512:(t + 1) * 512], [("xd", id(x_src), t)], [xtk])
                    ACT(sq[:], xt[:], AF.Square, [xtk], [sqk])
                    stats_rs(lambda k: (sq[:, k, :], [sqk]), 8, rs, rsk, [])
                    TT(xt[:], xt[:], rs[:].unsqueeze(1).to_broadcast([128, 8, 512]), ALU.mult, [xtk, rsk], [xtk])
                    for kc in range(8):
                        ACT(hT[:, kc, t * 512:(t + 1) * 512], xt[:, kc, :], AF.Identity, [xtk, "modA", "modB"], [hk(t)],
                            bias=modB[:, l, kc, wsel:wsel + 1], scale=modA[:, l, kc, wsel:wsel + 1], group=("hTf", l, t, nseq))

            if ms_holder:
                ms_holder.pop().close()
            P.barrier()
            nm = "P" if nseq == 2 else "S"
            allk = [yk(k, b) for k in range(16) for b in range(nblk)]
            if debug and l == 0:
                DMA(dbg[f"d_hT_{nm}"], hT[:, :, 0:Ttok], [hk(t) for t in range(NT)], (), final=True)
            with ExitStack() as sa:
                wx = T([128, 8, WP], BF16, "wx", sa)
                wB = T([128, 8, 128], BF16, "wB", sa)
                wC = T([128, 8, 128], BF16, "wC", sa)
                wz = T([128, 8, WP], BF16, "wz", sa)
                wdt = T([128, 8, 32], BF16, "wdt", sa)
                upad_r = Rot("upad", [128, nseq, L + 4], BF16, 2, sa)
                diag5_r = Rot("diag5", [128, 5, 128], BF16, 2, sa)
                xg = T([128, nblk, WP], BF16, "xg", sa)
                Btm = T([128, nblk, 128], BF16, "Btm", sa)
                Bfm = T([128, Ttok], BF16, "Bfm", sa)
                Cfm = T([128, Ttok], BF16, "Cfm", sa)
                dt_all = T([128, nblk, 32], F32, "dt_all", sa)
                la_all = T([128, nblk, 32], F32, "la_all", sa)
                v_all = T([128, nblk, 32], F32, "v_all", sa)
                cum_sb = [T([128, nblk, HP], F32, f"cum{d}", sa) for d in range(2)]
                cum_hi = [T([128, nblk, HP], BF16, f"cumhi{d}", sa) for d in range(2)]
                cum_lo = [T([128, nblk, HP], BF16, f"cumlo{d}", sa) for d in range(2)]
                decs_all = [T([128, 3, nblk, HP], F32, f"decs{d}", sa) for d in range(2)]
                diagD = T([128, HP, 128], BF16, "diagD", sa)
                ypark = T([128, nblk, WP], F32, "ypark", sa)
                Sf = [T([128, WP], F32, f"Sf{d}", sa) for d in range(2)]
                Sb = [T([128, WP], BF16, f"Sb{d}", sa) for d in range(2)]
                cbm_r = Rot("cbm", [128, 128], BF16, NROT, sa)
                xdt_r = Rot("xdt", [128, HP, 64], BF16, NROT, sa)
                xs2_r = Rot("xs2", [128, HP, 64], BF16, NROT, sa)
                Lh_r = Rot("Lh", [128, HP, 128], BF16, NROT, sa)
                La_r = Rot("La", [128, HP, 128], F32, 2, sa)
                Mh_r = Rot("Mh", [128, HP, 128], BF16, NROT, sa)
                t1_r = Rot("t1", [128, HP, 64], F32, NROT, sa)
                zs_r = Rot("zs", [128, WP], F32, 2, sa)
                yg_r = Rot("yg", [128, WP], BF16, 2, sa)
                stg_r = Rot("stg", [128, 128], F32, 2, sa)
                for r in upad_r.t:
                    MEMSET(r[:], 0.0, [("upad", upad_r.t.index(r))])

                win_load(wdt[:], I_DT, 32, "wdt")
                for bi in range(nblk):
                    t = bi // 4
                    pd, pdk = PSH()
                    for kc in range(8):
                        MM(pd[:, 0:32], hT[:, kc, bi * 128:(bi + 1) * 128], wdt[:, kc, :], kc == 0, kc == 7, [hk(t), "wdt"], [pdk])
                    TT(v_all[:, bi, :], pd[:, 0:32], dtb[:, l, :], ALU.add, [pdk, "dtb"], ["v_all"])
                TS(dt_all[:], v_all[:], 30.0, ALU.min, ["v_all"], ["dt_all"])
                ACT(dt_all[:], dt_all[:], AF.Exp, ["dt_all"], ["dt_all"])
                ACT(dt_all[:], dt_all[:], AF.Ln, ["dt_all"], ["dt_all"], bias=1.0)
                TT(dt_all[:], dt_all[:], v_all[:], ALU.max, ["dt_all", "v_all"], ["dt_all"])
                TT(la_all[:], dt_all[:], aneg[:, l, :].unsqueeze(1).to_broadcast([128, nblk, 32]), ALU.mult, ["dt_all", "aneg"], ["la_all"])
                for q in range(16 // HP):
                    g = (q * HP) // 8
                    nb4 = nblk * HP
                    win_load(wx[:], I_X + q * WP, WP, "wx")
                    if (q * HP) % 8 == 0:
                        win_load(wB[:], I_B + g * 128, 128, "wB")
                        win_load(wC[:], I_C + g * 128, 128, "wC")
                    win_load(wz[:], I_Z + q * WP, WP, "wz")
                    nxc = WP // 128
                    chunks = [("x", a, q * nxc + a, wx, a * 128, "wx") for a in range(nxc)]
                    if (q * HP) % 8 == 0:
                        chunks += [("B", 0, 8 + g, wB, 0, "wB"), ("C", 0, 10 + g, wC, 0, "wC")]
                    for kind, a, cidx, wt, wc0, wk in chunks:
                        upad, upk = upad_r.next()
                        dg, dgk = diag5_r.next()
                        TT(dg[:], ident[:].unsqueeze(1).to_broadcast([128, 5, 128]),
                           cw5[:, l, cidx, :].unsqueeze(2).to_broadcast([128, 5, 128]), ALU.mult, ["ident", "cw5"], [dgk])
                        for t in range(NT):
                            pu, puk = PS()
                            for kc in range(8):
                                MM(pu[:], wt[:, kc, wc0:wc0 + 128], hT[:, kc, t * 512:(t + 1) * 512], kc == 0, kc == 7,
                                   [wk, hk(t)], [puk])
                            if L >= 512:
                                s_, off, n_, c0 = segs(t)[0]
                                P.op("act", (lambda e, o=upad[:, s_, 2 + off:2 + off + 512], i=pu[:]: e.copy(out=o, in_=i)),
                                     [puk], [upk], cost=590.0)
                            else:
                                n = 512 // L
                                P.op("act", (lambda e, o=upad[:, t * n:(t + 1) * n, 2:2 + L],
                                             i=pu[:].rearrange("p (s x) -> p s x", s=n): e.copy(out=o, in_=i)), [puk], [upk], cost=590.0)
                        if kind in ("x", "B"):
                            for s_ in range(nseq):
                                for j in range(nch):
                                    bi = s_ * nch + j
                                    pc, pck = PSH()
                                    for k in range(5):
                                        MM(pc[:, 0:128], upad[:, s_, j * 128 + k:j * 128 + k + 128], dg[:, k, :], k == 0, False,
                                           [upk, dgk], [pck])
                                    MM(pc[:, 0:128], onesb[0:1, 0:128], cb5row[0:1, l * 1536 + cidx * 128:l * 1536 + (cidx + 1) * 128],
                                       False, True, ["onesb", "cb5row"], [pck])
                                    if kind == "x":
                                        ACT(xg[:, bi, a * 128:(a + 1) * 128], pc[:, 0:128], AF.Silu, [pck], [("xg", bi)], group=("xg", l, nseq, q, bi))
                                    else:
                                        ACT(Btm[:, bi, :], pc[:, 0:128], AF.Silu, [pck], [("Btm", bi)])
                        if kind in ("B", "C"):
                            dstT, dkey = (Bfm, "Bfm") if kind == "B" else (Cfm, "Cfm")
                            for t in range(NT):
                                pc, pck = PS()
                                for (s_, off, n_, c0) in segs(t):
                                    for k in range(5):
                                        MM(pc[:, c0:c0 + n_], dg[:, k, :], upad[:, s_, off + k:off + k + n_], k == 0, k == 4,
                                           [upk, dgk], [pck])
                                ACT(dstT[:, t * 512:(t + 1) * 512], pc[:], AF.Silu, [pck, "cb5"], [(dkey, t)],
                                    bias=cb5[:, l, cidx:cidx + 1])
                    TT(diagD[:], ident[:].unsqueeze(1).to_broadcast([128, HP, 128]),
                       dsk[:, l, q * HP:(q + 1) * HP].unsqueeze(2).to_broadcast([128, HP, 128]), ALU.mult, ["ident", "dsk"], ["diagD"])
                    for d in (1, 0):
                        Uin, UinK = (Uf, "Uf") if d == 0 else (Ub, "Ub")
                        SLo, SLoK = (SLf, "SLf") if d == 0 else (SLb, "SLb")
                        pdc, pdck = PSH()
                        la_d = la_all[:, :, d * 16 + q * HP:d * 16 + (q + 1) * HP]
                        for ci, (mt, mk) in enumerate(((Uin, UinK), (SLo, SLoK), (onesf, "onesf"))):
                            MM(pdc[:, ci * nb4:(ci + 1) * nb4].rearrange("p (b h) -> p b h", h=HP), mt[:], la_d, True, True,
                               [mk, "la_all"], [pdck])
                        dcs = decs_all[d]
                        dcsk = ("decs", d)
                        ACT(dcs[:].rearrange("p c b h -> p (c b h)"), pdc[:, 0:3 * nb4], AF.Exp, [pdck], [dcsk])
                        cum = cum_sb[d]
                        cumk = ("cum", d)
                        CP(cum[:].rearrange("p b h -> p (b h)"), pdc[:, 0:nb4], [pdck, dcsk], [cumk])
                        chi, clo = cum_hi[d], cum_lo[d]
                        CP(chi[:], cum[:], [cumk], [("chi", d)])
                        TT(clo[:], cum[:], chi[:], ALU.subtract, [cumk, ("chi", d)], [("clo", d)])
                        TT(cum[:], chi[:], clo[:], ALU.add, [("chi", d), ("clo", d), cumk], [cumk])
                        for s_ in range(nseq):
                            if h0 is None:
                                MEMSET(Sf[d][:], 0.0, [("Sf", d)], eng="dve")
                                MEMSET(Sb[d][:], 0.0, [("Sb", d)], eng="dve")
                            else:
                                for a in range(WP // 128):
                                    sg, sgk = stg_r.next()
                                    DMA(sg[:], h0[l, d, q * WP + a * 128:q * WP + (a + 1) * 128, :], (), [sgk])
                                    pt, ptk = PSH()
                                    TR(pt[:, 0:128], sg[:], ident[:], [sgk, "ident"], [ptk])
                                    CP(Sf[d][:, a * 128:(a + 1) * 128], pt[:, 0:128], [ptk], [("Sf", d)])
                                CP(Sb[d][:], Sf[d][:], [("Sf", d)], [("Sb", d)])
                            order = range(nch) if d == 0 else range(nch - 1, -1, -1)
                            for j in order:
                                bi = s_ * nch + j
                                t = bi // 4
                                tok = slice(bi * 128, (bi + 1) * 128)
                                la_b = la_all[:, bi, d * 16 + q * HP:d * 16 + (q + 1) * HP]
                                dt_b = dt_all[:, bi, d * 16 + q * HP:d * 16 + (q + 1) * HP]
                                pcb, pcbk = PSH()
                                MM(pcb[:, 0:128], Bfm[:, tok], Cfm[:, tok], True, True, [("Bfm", t), ("Cfm", t)], [pcbk])
                                cbm, cbmk = cbm_r.next()
                                TT(cbm[:], pcb[:, 0:128], Uin[:], ALU.mult, [pcbk, UinK], [cbmk])
                                xdt, xdtk = xdt_r.next()
                                xs2, xs2k = xs2_r.next()
                                xg_b = xg[:, bi, :].rearrange("p (h c) -> p h c", h=HP)
                                TT(xdt[:], xg_b, dt_b.unsqueeze(2).to_broadcast([128, HP, 64]), ALU.mult, [("xg", bi), "dt_all"], [xdtk], eng=OFF_ENG)
                                TT(xs2[:], xdt[:], dcs[:, 1, bi, :].unsqueeze(2).to_broadcast([128, HP, 64]), ALU.mult,
                                   [xdtk, dcsk], [xs2k], eng=OFF_ENG)
                                parg, pargk = PS()
                                for h in range(HP):
                                    po_ = parg[:, h * 128:(h + 1) * 128]
                                    MM(po_, chi[:, bi, h:h + 1].to_broadcast([128, 128]), identb[:], True, False, [("chi", d), "identb"], [pargk])
                                    MM(po_, clo[:, bi, h:h + 1].to_broadcast([128, 128]), identb[:], False, True, [("clo", d), "identb"], [pargk])
                                La, Lak = La_r.next()
                                for h in range(HP):
                                    ACT(La[:, h, :], parg[:, h * 128:(h + 1) * 128], AF.Relu, [pargk, cumk], [Lak],
                                        bias=cum[:, bi, h:h + 1], scale=-1.0, group=("relu", l, nseq, q, d, bi))
                                Lh, Lhk = Lh_r.next()
                                ACT(Lh[:].rearrange("p h c -> p (h c)"), La[:].rearrange("p h c -> p (h c)"), AF.Exp, [Lak], [Lhk], scale=-1.0)
                                Mh, Mhk = Mh_r.next()
                                TT(Mh[:], Lh[:], cbm[:].unsqueeze(1).to_broadcast([128, HP, 128]), ALU.mult, [Lhk, cbmk], [Mhk])
                                py, pyk = PSH()
                                for h in range(HP):
                                    MM(py[:, h * 64:(h + 1) * 64], Mh[:, h, :], xdt[:, h, :], True, d == 1, [Mhk, xdtk], [pyk])
                                    if d == 0:
                                        MM(py[:, h * 64:(h + 1) * 64], diagD[:, h, :], xg[:, bi, h * 64:(h + 1) * 64], False, True,
                                           ["diagD", ("xg", bi)], [pyk])
                                po, pok = PSH()
                                MM(po[:, 0:WP], Cfm[:, tok], Sb[d][:], True, True, [("Cfm", t), ("Sb", d)], [pok])
                                t1, t1k = t1_r.next()
                                TT(t1[:], po[:, 0:WP].rearrange("p (h c) -> p h c", h=HP),
                                   dcs[:, 0, bi, :].unsqueeze(2).to_broadcast([128, HP, 64]), ALU.mult, [pok, dcsk], [t1k])
                                t1f = t1[:].rearrange("p h c -> p (h c)")
                                if d == 1:
                                    TT(ypark[:, bi, :], t1f, py[:, 0:WP], ALU.add, [t1k, pyk], [("ypark", bi)])
                                else:
                                    TT(t1f, t1f, py[:, 0:WP], ALU.add, [t1k, pyk], [t1k])
                                    TT(t1f, t1f, ypark[:, bi, :], ALU.add, [t1k, ("ypark", bi)], [t1k], eng=OFF_ENG)
                                    yg, ygk = yg_r.next()
                                    zs, zsk = zs_r.next()
                                    pz, pzk = PSH()
                                    for kc in range(8):
                                        MM(pz[:, 0:WP], hT[:, kc, tok], wz[:, kc, :], kc == 0, kc == 7, [hk(t), "wz"], [pzk])
                                    ACT(zs[:], pz[:, 0:WP], AF.Tanh, [pzk], [zsk], scale=0.5)
                                    STT(zs[:], zs[:], 1.0, pz[:, 0:WP], ALU.add, ALU.mult, [zsk, pzk], [zsk])
                                    TT(yg[:], t1f, zs[:], ALU.mult, [t1k, zsk], [ygk])
                                    for a in range(WP // 128):
                                        TR(psb_t[:, a * 128:(a + 1) * 128], yg[:, a * 128:(a + 1) * 128], identb[:], [ygk, "identb"], ["psb"])
                                    kc0 = q * (WP // 128)
                                    P.op("act", (lambda e, o=yT[:, kc0:kc0 + WP // 128, tok],
                                                 i=psb_t[:, 0:WP].rearrange("p (a c) -> p a c", c=128): e.copy(out=o, in_=i)),
                                         ["psb"], [yk(kc0 + a, bi) for a in range(WP // 128)], cost=400.0)
                                pds, pdsk = PSH()
                                MM(pds[:, 0:WP], Btm[:, bi, :], xs2[:].rearrange("p h c -> p (h c)"), True, True, [("Btm", bi), xs2k], [pdsk])
                                Sf3 = Sf[d][:].rearrange("p (h c) -> p h c", h=HP)
                                TT(Sf3, Sf3, dcs[:, 2, bi, :].unsqueeze(2).to_broadcast([128, HP, 64]), ALU.mult,
                                   [("Sf", d), dcsk], [("Sf", d)], eng=OFF_ENG)
                                TT(Sf[d][:], Sf[d][:], pds[:, 0:WP], ALU.add, [("Sf", d), pdsk], [("Sf", d)])
                                P.op("act", (lambda e, o=Sb[d][:], i=Sf[d][:]: e.copy(out=o, in_=i)), [("Sf", d)], [("Sb", d)], cost=400.0)
                            if ns_out is not None:
                                for a in range(WP // 128):
                                    pt, ptk = PSH()
                                    TR(pt[:, 0:128], Sf[d][:, a * 128:(a + 1) * 128], ident[:], [("Sf", d), "ident"], [ptk])
                                    sg, sgk = stg_r.next()
                                    CP(sg[:], pt[:, 0:128], [ptk], [sgk])
                                    DMA(ns_out[s_, l, d, q * WP + a * 128:q * WP + (a + 1) * 128, :], sg[:], [sgk], (), final=True)

            P.barrier()
            if debug and l == 0:
                DMA(dbg[f"d_yA_{nm}"], yT[:, :, 0:Ttok], allk, (), final=True)
            def ssd_norm(sn):
                sq_r = Rot("sqn", [128, 8, 512], BF16, 1, sn)
                rs_r = Rot("rsn", [128, 512], F32, 2, sn)
                for t in range(NT):
                    sq, sqk = sq_r.next()
                    rs, rsk = rs_r.next()
                    tl = slice(t * 512, (t + 1) * 512)
                    ACT(sq[:], yT[:, 0:8, tl], AF.Square, ytile(range(8), t), [sqk])
                    stats_rs(lambda k: (sq[:, k, :], [sqk]), 8, rs, rsk, [], eps=4.0 * EPS)
                    for k in range(8):
                        STT(yT[:, k, tl], yT[:, k, tl], sng[:, l, k:k + 1], rs[:], ALU.mult, ALU.mult,
                            ytile([k], t) + ["sng", rsk], ytile([k], t))

            pad = 15 * stride
            with ExitStack() as sb_:
                ssd_norm(sb_)
                if debug and l == 0:
                    DMA(dbg[f"d_yB_{nm}"], yT[:, :, 0:Ttok], allk, (), final=True)
                wga = T([128, 8, 1024], BF16, "wga", sb_)
                wgb = T([128, 8, 1024], BF16, "wgb", sb_)
                for j in range(8):
                    win_load(wga[:, :, j * 128:(j + 1) * 128], I_GA + j * 128, 128, ("wga", j))
                    win_load(wgb[:, :, j * 128:(j + 1) * 128], I_GB + j * 128, 128, ("wgb", j))
                hc_r = Rot("hc", [128, nseq, L + 2 * pad], BF16, 2, sb_)
                d31_r = Rot("d31", [128, 31, 128], BF16, 2, sb_)
                sig_r = Rot("sig", [128, 512], F32, 2, sb_)
                accd_r = Rot("accd", [128, 512], F32, 2, sb_)
                accp_r = Rot("accp", [128, 512], F32, 2, sb_)
                accbd_r = Rot("accbd", [128, 512], BF16, 2, sb_)
                accbp_r = Rot("accbp", [128, 512], BF16, 2, sb_)
                for r in hc_r.t:
                    MEMSET(r[:], 0.0, [("hc", hc_r.t.index(r))])
                for j in range(8):
                    hc, hck = hc_r.next()
                    dg, dgk = d31_r.next()
                    TT(dg[:], ident[:].unsqueeze(1).to_broadcast([128, 31, 128]),
                       cw31[:, l, j, :].unsqueeze(2).to_broadcast([128, 31, 128]), ALU.mult, ["ident", "cw31"], [dgk])
                    for t in range(NT):
                        pa, pak = PS()
                        pb, pbk = PS()
                        for kc in range(8):
                            MM(pa[:], wga[:, kc, j * 128:(j + 1) * 128], hT[:, kc, t * 512:(t + 1) * 512], kc == 0, kc == 7, [("wga", j), hk(t)], [pak])
                        for kc in range(8):
                            MM(pb[:], wgb[:, kc, j * 128:(j + 1) * 128], hT[:, kc, t * 512:(t + 1) * 512], kc == 0, kc == 7, [("wgb", j), hk(t)], [pbk])
                        sig, sigk = sig_r.next()
                        ACT(sig[:], pb[:], AF.Sigmoid, [pbk], [sigk])
                        if L >= 512:
                            s_, off, n_, c0 = segs(t)[0]
                            TT(hc[:, s_, pad + off:pad + off + 512], pa[:], sig[:], ALU.mult, [pak, sigk], [hck])
                        else:
                            n = 512 // L
                            TT(hc[:, t * n:(t + 1) * n, pad:pad + L], pa[:].rearrange("p (s x) -> p s x", s=n),
                               sig[:].rearrange("p (s x) -> p s x", s=n), ALU.mult, [pak, sigk], [hck])
                    for t in range(NT):
                        pc, pck = PS()
                        for (s_, off, n_, c0) in segs(t):
                            taps = [k for k in range(31) if off + (k - 15) * stride + n_ > 0 and off + (k - 15) * stride < L]
                            win = lambda k: hc[:, s_, pad + off + (k - 15) * stride:pad + off + (k - 15) * stride + n_]
                            wk_ = lambda k: cw31[:, l, j, k:k + 1]
                            extra = []
                            rest = list(taps)
                            for eng_, ntap, acc_r, accb_r in (("dve", CONV_ND, accd_r, accbd_r), ("pool", CONV_NP, accp_r, accbp_r)):
                                if ntap == 0 or len(rest) - ntap < 4:
                                    continue
                                mine, rest = rest[:ntap], rest[ntap:]
                                acc, acck = acc_r.next()
                                accb, accbk = accb_r.next()
                                c_ = (120.0 + n_ / 0.96) if eng_ == "dve" else (200.0 + n_ / 0.55)
                                for ii, k in enumerate(mine):
                                    last_ = ii == len(mine) - 1
                                    dst, dstk = (accb, accbk) if last_ else (acc, acck)
                                    if ii == 0:
                                        P.op(eng_, (lambda e, o=dst[:, 0:n_], i0=win(k), sc=wk_(k): e.tensor_scalar(
                                            out=o, in0=i0, scalar1=sc, scalar2=None, op0=ALU.mult)), [hck, "cw31"], [dstk], cost=c_)
                                    else:
                                        P.op(eng_, (lambda e, o=dst[:, 0:n_], i0=win(k), sc=wk_(k), i1=acc[:, 0:n_]: e.scalar_tensor_tensor(
                                            out=o, in0=i0, scalar=sc, in1=i1, op0=ALU.mult, op1=ALU.add)), [hck, "cw31", acck], [dstk], cost=c_)
                                extra.append((accb, accbk))
                            nmm = len(rest) + len(extra)
                            im = 0
                            for k in rest:
                                MM(pc[:, c0:c0 + n_], dg[:, k, :], win(k), im == 0, im == nmm - 1, [hck, dgk], [pck])
                                im += 1
                            for accb, accbk in extra:
                                MM(pc[:, c0:c0 + n_], identb[:], accb[:, 0:n_], im == 0, im == nmm - 1, ["identb", accbk], [pck])
                                im += 1
                        ACT(yT[:, 8 + j, t * 512:(t + 1) * 512], pc[:], AF.Identity, [pck, "cb31"], ytile([8 + j], t),
                            bias=cb31[:, l, j:j + 1])
            P.barrier()
            with ExitStack() as sb_:
                wgs = T([128, 8, 1024], BF16, "wgs", sb_)
                for j in range(8):
                    win_load(wgs[:, :, j * 128:(j + 1) * 128], I_GS + j * 128, 128, ("wgs", j))
                sq_r = Rot("sqc", [128, 8, 512], BF16, 2, sb_)
                mean_r = Rot("mean", [128, 512], F32, 2, sb_)
                rs_r = Rot("rsc", [128, 512], F32, 2, sb_)
                tmp_r = Rot("tmpc", [128, 512], F32, 2, sb_)
                s1_r = Rot("s1c", [128, 512], F32, 2, sb_)
                for t in range(NT):
                    tl = slice(t * 512, (t + 1) * 512)
                    sq, sqk = sq_r.next()
                    mean, meank = mean_r.next()
                    rs, rsk = rs_r.next()
                    p1, p1k = PS()
                    for k in range(8):
                        MM(p1[:], onesb[:], yT[:, 8 + k, tl], k == 0, k == 7, ["onesb"] + ytile([8 + k], t), [p1k])
                    ACT(sq[:], yT[:, 8:16, tl], AF.Square, ytile(range(8, 16), t), [sqk])
                    p2, p2k = PS()
                    for k in range(8):
                        MM(p2[:], onesb[:], sq[:, k, :], k == 0, k == 7, ["onesb", sqk], [p2k])
                    TS(mean[:], p1[:], 1.0 / 1024.0, ALU.mult, [p1k], [meank])
                    tmp, tmpk = tmp_r.next()
                    TT(tmp[:], mean[:], mean[:], ALU.mult, [meank], [tmpk])
                    STT(tmp[:], p2[:], 1.0 / 1024.0, tmp[:], ALU.mult, ALU.subtract, [p2k, tmpk], [tmpk])
                    ACT(rs[:], tmp[:], AF.Ln, [tmpk], [rsk], bias=EPS)
                    ACT(rs[:], rs[:], AF.Exp, [rsk], [rsk], scale=-0.5)
                    for j in range(8):
                        tmp, tmpk = tmp_r.next()
                        s1, s1k = s1_r.next()
                        TT(tmp[:], yT[:, 8 + j, tl], mean[:], ALU.subtract, ytile([8 + j], t) + [meank], [tmpk])
                        TT(tmp[:], tmp[:], rs[:], ALU.mult, [tmpk, rsk], [tmpk])
                        ACT(s1[:], tmp[:], AF.Silu, [tmpk, "lng", "lnb"], [s1k], bias=lnb[:, l, j:j + 1], scale=lng[:, l, j:j + 1])
                        pg, pgk = PS()
                        for kc in range(8):
                            MM(pg[:], wgs[:, kc, j * 128:(j + 1) * 128], hT[:, kc, tl], kc == 0, kc == 7, [("wgs", j), hk(t)], [pgk])
                        ACT(tmp[:], pg[:], AF.Silu, [pgk], [tmpk])
                        TT(yT[:, 8 + j, tl], s1[:], tmp[:], ALU.mult, [s1k, tmpk], ytile([8 + j], t))

            P.barrier()
            if debug and l == 0:
                DMA(dbg[f"d_yC_{nm}"], yT[:, :, 0:Ttok], allk, (), final=True)
            with ExitStack() as sc:
                wo = T([128, 16, 1024], BF16, "wo", sc)
                for fo in range(8):
                    DMA(wo[:, :, fo * 128:(fo + 1) * 128], wout_d[l].rearrange("(kc p) c -> p kc c", p=128)[:, :, fo * 128:(fo + 1) * 128],
                        (), [("wo", fo)], eng="pool")
                osb_r = Rot("osb", [128, 8, 512], F32, 2, sc)
                sq_r = Rot("sqo", [128, 8, 512], BF16, 1, sc)
                rs_r = Rot("rso", [128, 512], F32, 2, sc)
                xt_r = Rot("xto", [128, 8, 512], F32, 1, sc)
                for t in range(NT):
                    tl = slice(t * 512, (t + 1) * 512)
                    osb, osbk = osb_r.next()
                    sq, sqk = sq_r.next()
                    rs, rsk = rs_r.next()
                    xt, xtk = xt_r.next()
                    DMA(xt[:], xsrc_v[:, :, tl], [("xd", id(x_src), t)], [xtk])
                    for fo in range(8):
                        po, pok = PS()
                        for kc in range(16):
                            MM(po[:], wo[:, kc, fo * 128:(fo + 1) * 128], yT[:, kc, tl], kc == 0, kc == 15,
                               [("wo", fo)] + ytile([kc], t), [pok])
                        P.op("act", (lambda e, o=osb[:, fo, :], i=po[:]: e.copy(out=o, in_=i)), [pok], [(osbk, fo)], cost=590.0)
                        ACT(sq[:, fo, :], po[:], AF.Square, [pok], [(sqk, fo)])
                    stats_rs(lambda k: (sq[:, k, :], [(sqk, k)]), 8, rs, rsk, [])
                    for fo in range(8):
                        TT(osb[:, fo, :], osb[:, fo, :], rs[:], ALU.mult, [(osbk, fo), rsk], [(osbk, fo)])
                        STT(osb[:, fo, :], osb[:, fo, :], modG[:, l, fo, wsel:wsel + 1], xt[:, fo, :], ALU.mult, ALU.add,
                            [(osbk, fo), "modG", xtk], [(osbk, fo)])
                    allosb = [(osbk, k) for k in range(8)]
                    DMA(xdst_v[:, :, tl], osb[:], allosb, [("xd", id(x_dst), t)], final=final_out)
                    if fuse_next:
                        allsq = [(sqk, k) for k in range(8)]
                        ACT(sq[:], osb[:], AF.Square, allosb, allsq)
                        rs2, rs2k = rs_r.next()
                        stats_rs(lambda k: (sq[:, k, :], [(sqk, k)]), 8, rs2, rs2k, [])
                        TT(xt[:], osb[:], rs2[:].unsqueeze(1).to_broadcast([128, 8, 512]), ALU.mult, allosb + [rs2k], [xtk])
                        for kc in range(8):
                            ACT(hT[:, kc, tl], xt[:, kc, :], AF.Identity, [xtk, "modA", "modB"], [hk(t)],
                                bias=modB[:, l + 1, kc, wsel:wsel + 1], scale=modA[:, l + 1, kc, wsel:wsel + 1],
                                group=("hTn", l, t, nseq))
            P.barrier()

        for nm_ in ("P", "S"):
            for l in range(DEPTH):
                last = (l == DEPTH - 1)
                if only is not None and (l, nm_) not in only:
                    continue
                if nm_ == "P":
                    run_block(l, xp_d if l == 0 else x1p_d, yp_d if last else x1p_d, 2, 256, 1, 0, None, ns_d, last or debug,
                              l == 0 or debug, (not last) and not debug)
                else:
                    run_block(l, xs_d if l == 0 else x1s_d, ys_d if last else x1s_d, 1, 2048, 64, 1, h0_d, None, last or debug,
                              l == 0 or debug, (not last) and not debug)
        P.emit()
        n_ins = len(P.ins)
    return nc, n_ins


_CACHE = {}


def _fm(v):
    v = np.asarray(v, np.float32)
    lead = v.shape[:-1]
    nchunk = v.shape[-1] // 128
    r = v.reshape(lead + (nchunk, 128))
    return np.ascontiguousarray(np.moveaxis(r, -1, 0))


def kernel(x_prompt, x_sample, state_ssd, c, c_ctx, w_mod, b_mod, g_pre, g_post, w_in,
           ssd_conv_w, ssd_conv_b, ssd_a_log, ssd_dt_bias, ssd_d, ssd_norm_g,
           conf_conv_w, conf_conv_b, conf_ln_g, conf_ln_b, w_out):
    f = lambda a: np.ascontiguousarray(np.asarray(a, np.float32))
    x_prompt, x_sample, state_ssd = f(x_prompt), f(x_sample), f(state_ssd)
    if "nc" not in _CACHE:
        _CACHE["nc"] = build_program()[0]
    nc = _CACHE["nc"]
    rep = lambda a: np.ascontiguousarray(np.broadcast_to(f(a).reshape(1, DEPTH, -1), (128, DEPTH, f(a).reshape(DEPTH, -1).shape[1])))
    shared = {
        "w_mod": f(w_mod), "b_mod": _fm(b_mod), "g_pre": _fm(g_pre), "g_post": _fm(g_post), "w_in": f(w_in),
        "cw5": np.ascontiguousarray(np.transpose(f(ssd_conv_w).reshape(DEPTH, 5, 12, 128), (3, 0, 2, 1))),
        "cb5": _fm(ssd_conv_b), "cb5row": f(ssd_conv_b).reshape(1, DEPTH * 1536),
        "alog": rep(ssd_a_log), "dtb": rep(ssd_dt_bias), "dsk": rep(ssd_d), "sng": _fm(ssd_norm_g),
        "cw31": np.ascontiguousarray(np.transpose(f(conf_conv_w).reshape(DEPTH, 31, 8, 128), (3, 0, 2, 1))),
        "cb31": _fm(conf_conv_b), "lng": _fm(conf_ln_g), "lnb": _fm(conf_ln_b), "w_out": f(w_out),
    }
    in_maps = []
    for core in range(NCORES):
        b = core // 4
        m = dict(shared)
        m["xp"] = np.ascontiguousarray(x_prompt[2 * core:2 * core + 2].reshape(512, D).T)
        m["xs"] = np.ascontiguousarray(x_sample[b].T)
        m["h0"] = np.ascontiguousarray(state_ssd[b].reshape(DEPTH, 2, 1024, 128))
        cv = np.stack([f(c_ctx), f(c)[b]], axis=-1)
        m["cvec"] = np.ascontiguousarray(np.transpose(cv.reshape(8, 128, 2), (1, 0, 2)))
        in_maps.append(m)
    res = run_bass_kernel_spmd(nc, in_maps, core_ids=list(range(NCORES)))
    r = res.results
    y_prompt = np.stack([r[core]["yp"].T.reshape(2, 256, D) for core in range(NCORES)], 0).reshape(16, 256, D)
    y_sample = np.stack([r[0]["ys"].T, r[4]["ys"].T], 0)
    new_state = np.concatenate([r[core]["ns"] for core in range(NCORES)], 0).reshape(16, DEPTH, 2, 16, 64, 128)
    return (np.ascontiguousarray(y_prompt, dtype=np.float32), np.ascontiguousarray(y_sample, dtype=np.float32),
            np.ascontiguousarray(new_state, dtype=np.float32))
```

```python
import numpy as np
from contextlib import ExitStack
import concourse.bass as bass
import concourse.mybir as mybir
from concourse.bass_utils import run_bass_kernel_spmd

F32 = mybir.dt.float32
BF16 = mybir.dt.bfloat16
AF = mybir.ActivationFunctionType
ALU = mybir.AluOpType

D = 1024
DEPTH = 2
NCORES = 8
EPS = 1e-6
I_Z, I_X, I_B, I_C, I_DT, I_GA, I_GB, I_GS = 0, 1024, 2048, 2304, 2560, 2592, 3616, 4640
IN_COLS = 5664
HP = 4
WP = HP * 64
TMAX = 2048
NPS = 7
CONV_ND = 8
CONV_NP = 0
NROT = 3
OFF_ENG = "pool"


class Prog:
    SEM_LIMIT = 4000
    WINDOW = 128
    SEM_LAT = 700.0

    def __init__(self, nc, stack, same_engine_sync=True, schedule=True):
        self.nc = nc
        self.stack = stack
        self.engs = {"pe": nc.tensor, "act": nc.scalar, "dve": nc.vector, "pool": nc.gpsimd, "sp": nc.sync}
        self.ins = []
        self.last_w = {}
        self.readers = {}
        self.same_engine_sync = same_engine_sync
        self.schedule = schedule
        self.n_dma_sems = {"sp": 16, "pool": 8, "act": 4, "dve": 4, "pe": 4}
        self.out_dmas = []
        self.w_rdeps = {}
        self.phase = 0

    def barrier(self):
        self.phase += 1

    def op(self, eng, fn, reads=(), writes=(), dma=False, final=False, cost=300.0, lat=0.0, group=None):
        deps = set()
        for r in reads:
            if r in self.last_w:
                deps |= set(self.last_w[r][1])
        i = len(self.ins)
        for w in writes:
            same = False
            if w in self.last_w:
                gid, members = self.last_w[w]
                same = group is not None and gid == group
                if not same:
                    deps |= set(members)
            if same:
                deps |= self.w_rdeps.get(w, set())
            else:
                rd = set(self.readers.get(w, set()))
                deps |= rd
                self.w_rdeps[w] = (set(self.last_w[w][1]) if w in self.last_w else set()) | rd
        deps.discard(i)
        self.ins.append(dict(eng=eng, fn=fn, deps=deps, dma=dma, cost=cost, lat=lat, phase=self.phase))
        for r in reads:
            self.readers.setdefault(r, set()).add(i)
        for w in writes:
            if w in self.last_w and group is not None and self.last_w[w][0] == group:
                self.last_w[w][1].append(i)
            else:
                self.last_w[w] = (group, [i])
                self.readers[w] = set()
        if final:
            self.out_dmas.append(i)
        return i

    def _order(self):
        ins = self.ins
        n = len(ins)
        per_eng = {e: [] for e in self.engs}
        for i, it in enumerate(ins):
            per_eng[it["eng"]].append(i)
        if not self.schedule:
            return per_eng
        users = [[] for _ in range(n)]
        nun = [0] * n
        for i, it in enumerate(ins):
            nun[i] = len(it["deps"])
            for d in it["deps"]:
                users[d].append(i)
        blev = [0.0] * n
        for i in range(n - 1, -1, -1):
            it = ins[i]
            m = 0.0
            for u in users[i]:
                if ins[u]["phase"] == it["phase"] and blev[u] > m:
                    m = blev[u]
            blev[i] = it["cost"] + it["lat"] + m
        rdy = [0.0] * n
        self.t_start = [0.0] * n
        self.t_fin = [0.0] * n
        nphase = self.phase + 1
        left = [0] * nphase
        for it in ins:
            left[it["phase"]] += 1
        cur = 0
        while cur < nphase and left[cur] == 0:
            cur += 1
        phase_t = 0.0
        tmax = 0.0
        eng_free = {e: 0.0 for e in self.engs}
        pend = {e: list(v) for e, v in per_eng.items()}
        order = {e: [] for e in self.engs}
        remaining = n
        while remaining:
            best = None
            for e, lst in pend.items():
                cand = None
                ef = eng_free[e]
                for i in lst[:self.WINDOW]:
                    it = ins[i]
                    if it["phase"] != cur:
                        break
                    if nun[i]:
                        continue
                    stt = max(rdy[i], ef, phase_t)
                    key = (stt, -blev[i]) if stt > ef + 1e-9 else (ef, -blev[i])
                    if cand is None or key < cand[2]:
                        cand = (key[0], i, key)
                if cand is not None and (best is None or cand[0] < best[0] - 1e-9 or
                                         (abs(cand[0] - best[0]) <= 1e-9 and cand[1] < best[1])):
                    best = (cand[0], cand[1], e)
            assert best is not None, "scheduler stuck"
            stt, i, e = best
            it = ins[i]
            eng_free[e] = stt + it["cost"]
            f = stt + it["cost"] + it["lat"]
            self.t_start[i] = stt
            self.t_fin[i] = f
            tmax = max(tmax, f)
            for u in users[i]:
                nun[u] -= 1
                fl = f if (ins[u]["eng"] == e and e == "pe" and not it["dma"]) else f + self.SEM_LAT
                if fl > rdy[u]:
                    rdy[u] = fl
            pend[e].remove(i)
            order[e].append(i)
            remaining -= 1
            left[cur] -= 1
            if left[cur] == 0:
                while cur < nphase and left[cur] == 0:
                    cur += 1
                phase_t = tmax + 200.0
        self.sim_time = tmax
        return order

    def emit(self):
        nc = self.nc
        ins = self.ins
        n = len(ins)
        order = self._order()
        pos = [0] * n
        for e, lst in order.items():
            for k, i in enumerate(lst):
                pos[i] = k
        last_before = {}
        for e, lst in order.items():
            cuts = {}
            for k, i in enumerate(lst):
                cuts.setdefault(ins[i]["phase"], k)
            last_before[e] = (lst, cuts)
        for e, lst in order.items():
            seen = -1
            for i in lst:
                p = ins[i]["phase"]
                if p == seen:
                    continue
                seen = p
                if p == 0:
                    continue
                extra = set()
                for e2, (lst2, cuts2) in last_before.items():
                    ks = [k for ph, k in cuts2.items() if ph >= p]
                    endk = min(ks) if ks else len(lst2)
                    if endk == 0:
                        continue
                    extra.add(lst2[endk - 1])
                    nd = self.n_dma_sems[e2]
                    cnt = 0
                    for k in range(endk - 1, -1, -1):
                        if ins[lst2[k]]["dma"]:
                            extra.add(lst2[k])
                            cnt += 1
                            if cnt >= nd:
                                break
                extra.discard(i)
                ins[i]["deps"] = set(ins[i]["deps"]) | extra
        pruned = [None] * n
        for i, it in enumerate(ins):
            e = it["eng"]
            keep = {}
            dmas = []
            for d in it["deps"]:
                p = ins[d]
                if p["dma"]:
                    dmas.append(d)
                    continue
                if p["eng"] == e and (e == "pe" or not self.same_engine_sync):
                    continue
                pe_ = p["eng"]
                if pe_ not in keep or pos[d] > pos[keep[pe_]]:
                    keep[pe_] = d
            pruned[i] = list(keep.values()) + dmas
        needed = [False] * n
        for i in range(n):
            for d in pruned[i]:
                needed[d] = True
        for i in self.out_dmas:
            needed[i] = True
        sem_of = [None] * n
        dma_prev = [None] * n
        for e, lst in order.items():
            nd = self.n_dma_sems[e]
            dsems = None
            dcnt = None
            rr = 0
            cur = None
            ccnt = 0
            k = 0
            for i in lst:
                it = ins[i]
                if it["dma"]:
                    if dsems is None:
                        dsems = [self.stack.enter_context(nc.semaphore(f"dq_{e}_{j}")) for j in range(nd)]
                        dcnt = [0] * nd
                    j = rr
                    rr = (rr + 1) % nd
                    if dcnt[j] > 0:
                        dma_prev[i] = (dsems[j], dcnt[j])
                    dcnt[j] += 16
                    sem_of[i] = (dsems[j], dcnt[j])
                elif needed[i]:
                    if cur is None or ccnt >= self.SEM_LIMIT:
                        cur = self.stack.enter_context(nc.semaphore(f"s_{e}_{k}"))
                        k += 1
                        ccnt = 0
                    ccnt += 1
                    sem_of[i] = (cur, ccnt)
        for e, lst in order.items():
            eng = self.engs[e]
            waited = {}

            def do_wait(sem, cnt):
                key = id(sem)
                if waited.get(key, 0) >= cnt:
                    return
                eng.wait_ge(sem, cnt)
                waited[key] = cnt

            for i in lst:
                it = ins[i]
                ws = [sem_of[d] for d in pruned[i]]
                ws.sort(key=lambda sc: -sc[1])
                for sem, c in ws:
                    do_wait(sem, c)
                if it["dma"] and dma_prev[i] is not None:
                    do_wait(*dma_prev[i])
                inst = it["fn"](eng)
                if sem_of[i] is not None:
                    inst.then_inc(sem_of[i][0], 16 if it["dma"] else 1)
            if e == "sp":
                for i in self.out_dmas:
                    do_wait(*sem_of[i])


def build_program(debug=False, only=None):
    nc = bass.Bass("TRN2", target_bir_lowering=False)
    dt_in = lambda name, shape: nc.dram_tensor(name, shape, F32, kind="ExternalInput").ap()
    dt_out = lambda name, shape: nc.dram_tensor(name, shape, F32, kind="ExternalOutput").ap()
    xp_d = dt_in("xp", [D, 512])
    xs_d = dt_in("xs", [D, 2048])
    h0_d = dt_in("h0", [DEPTH, 2, 1024, 128])
    cvec_d = dt_in("cvec", [128, 8, 2])
    wmod_d = dt_in("w_mod", [DEPTH, D, 3 * D])
    bmod_d = dt_in("b_mod", [128, DEPTH, 24])
    gpre_d = dt_in("g_pre", [128, DEPTH, 8])
    gpost_d = dt_in("g_post", [128, DEPTH, 8])
    win_d = dt_in("w_in", [DEPTH, D, IN_COLS])
    cw5_d = dt_in("cw5", [128, DEPTH, 12, 5])
    cb5_d = dt_in("cb5", [128, DEPTH, 12])
    cb5row_d = dt_in("cb5row", [1, DEPTH * 1536])
    alog_d = dt_in("alog", [128, DEPTH, 32])
    dtb_d = dt_in("dtb", [128, DEPTH, 32])
    dsk_d = dt_in("dsk", [128, DEPTH, 16])
    sng_d = dt_in("sng", [128, DEPTH, 8])
    cw31_d = dt_in("cw31", [128, DEPTH, 8, 31])
    cb31_d = dt_in("cb31", [128, DEPTH, 8])
    lng_d = dt_in("lng", [128, DEPTH, 8])
    lnb_d = dt_in("lnb", [128, DEPTH, 8])
    wout_d = dt_in("w_out", [DEPTH, 2 * D, D])
    yp_d = dt_out("yp", [D, 512])
    ys_d = dt_out("ys", [D, 2048])
    ns_d = dt_out("ns", [2, DEPTH, 2, 1024, 128])
    dbg = {}
    if debug:
        dbg["d_modA"] = dt_out("d_modA", [128, DEPTH, 8, 2])
        dbg["d_modB"] = dt_out("d_modB", [128, DEPTH, 8, 2])
        dbg["d_modG"] = dt_out("d_modG", [128, DEPTH, 8, 2])
        for nm, T_ in (("P", 512), ("S", 2048)):
            dbg[f"d_hT_{nm}"] = nc.dram_tensor(f"d_hT_{nm}", [128, 8, T_], BF16, kind="ExternalOutput").ap()
            dbg[f"d_yA_{nm}"] = nc.dram_tensor(f"d_yA_{nm}", [128, 16, T_], BF16, kind="ExternalOutput").ap()
            dbg[f"d_yB_{nm}"] = nc.dram_tensor(f"d_yB_{nm}", [128, 16, T_], BF16, kind="ExternalOutput").ap()
            dbg[f"d_yC_{nm}"] = nc.dram_tensor(f"d_yC_{nm}", [128, 16, T_], BF16, kind="ExternalOutput").ap()
    x1p_d = nc.dram_tensor("x1p", [D, 512], F32, kind="ExternalOutput" if debug else "Internal").ap()
    x1s_d = nc.dram_tensor("x1s", [D, 2048], F32, kind="ExternalOutput" if debug else "Internal").ap()

    with ExitStack() as st:
        P = Prog(nc, st)
        cnt = [0]

        def T(shape, dt, name=None, stack=None):
            cnt[0] += 1
            return (stack or st).enter_context(nc.sbuf_tensor(f"sb{cnt[0]}_{name or 't'}", shape, dt))

        def nfree(ap):
            r = 1
            for d in ap.shape[1:]:
                r *= d
            return r

        def DMA(out, in_, reads=(), writes=(), eng="sp", final=False):
            nbytes = nfree(out) * out.shape[0] * 4
            P.op(eng, lambda e: e.dma_start(out=out, in_=in_), reads, writes, dma=True, final=final,
                 cost=(150.0 if eng == "sp" else 1200.0), lat=2000.0 + nbytes / 120.0)

        def MM(out, lhsT, rhs, start, stop, reads, writes):
            passes = 4 if lhsT.dtype == F32 else 1
            P.op("pe", lambda e: e.matmul(out, lhsT=lhsT, rhs=rhs, start=start, stop=stop), reads, writes,
                 cost=30.0 + passes * max(nfree(rhs), 64) / 2.4, lat=120.0)

        def TR(out, in_, ident, reads, writes):
            P.op("pe", lambda e: e.transpose(out=out, in_=in_, identity=ident), reads, writes,
                 cost=(4 if in_.dtype == F32 else 1) * 60.0 + 30.0, lat=120.0)

        def ACT(out, in_, func, reads, writes, bias=None, scale=None, group=None):
            kw = {}
            if bias is not None:
                kw["bias"] = bias
            if scale is not None:
                kw["scale"] = scale
            P.op("act", lambda e: e.activation(out=out, in_=in_, func=func, **kw), reads, writes, cost=220.0 + nfree(out) / 1.4,
                 group=group)

        def TT(out, in0, in1, op, reads, writes, eng="dve"):
            c = 120.0 + nfree(out) / 0.96 if eng != "pool" else 200.0 + nfree(out) / 0.55
            P.op(eng, lambda e: e.tensor_tensor(out=out, in0=in0, in1=in1, op=op), reads, writes, cost=c)

        def TS(out, in0, s1, op0, reads, writes, s2=None, op1=None, eng="dve"):
            if op1 is None:
                P.op(eng, lambda e: e.tensor_scalar(out=out, in0=in0, scalar1=s1, scalar2=None, op0=op0), reads, writes,
                     cost=120.0 + nfree(out) / 0.96)
            else:
                P.op(eng, lambda e: e.tensor_scalar(out=out, in0=in0, scalar1=s1, scalar2=s2, op0=op0, op1=op1), reads, writes,
                     cost=120.0 + nfree(out) / 0.96)

        def STT(out, in0, scalar, in1, op0, op1, reads, writes, eng="dve"):
            P.op(eng, lambda e: e.scalar_tensor_tensor(out=out, in0=in0, scalar=scalar, in1=in1, op0=op0, op1=op1), reads, writes,
                 cost=120.0 + nfree(out) / 0.96)

        def CP(out, in_, reads, writes, eng="dve"):
            P.op(eng, lambda e: e.tensor_copy(out=out, in_=in_), reads, writes, cost=120.0 + nfree(out) / 0.96)

        def RECIP(out, in_, reads, writes):
            P.op("dve", lambda e: e.reciprocal(out=out, in_=in_), reads, writes, cost=120.0 + nfree(out) * 6.5)

        def MEMSET(ap, val, writes, eng="pool"):
            P.op(eng, lambda e: e.memset(ap, val), (), writes, cost=150.0 + nfree(ap) / 1.0)

        class Rot:
            def __init__(self, name, shape, dt, n, stack):
                self.t = [T(shape, dt, f"{name}{i}", stack) for i in range(n)]
                self.name = name
                self.i = 0

            def next(self):
                k = self.i % len(self.t)
                self.i += 1
                return self.t[k], (self.name, k)

        ps_t = [st.enter_context(nc.psum_tensor(f"ps{i}", [128, 512], F32)) for i in range(NPS)]
        psb_t = st.enter_context(nc.psum_tensor("psb", [128, 1024], BF16))
        ps_i = [0]

        def PS():
            k = ps_i[0] % NPS
            ps_i[0] += 1
            return ps_t[k], ("ps", k)

        PSH = PS

        ident = T([128, 128], F32, "ident")
        identb = T([128, 128], BF16, "identb")
        onesb = T([128, 128], BF16, "onesb")
        onesf = T([128, 128], F32, "onesf")
        Uf = T([128, 128], F32, "Uf")
        SLf = T([128, 128], F32, "SLf")
        Ub = T([128, 128], F32, "Ub")
        SLb = T([128, 128], F32, "SLb")
        MEMSET(onesf[:], 1.0, ["onesf"])
        MEMSET(onesb[:], 1.0, ["onesb"])

        def SEL(t, key, cm, pat, op):
            MEMSET(t[:], 1.0, [key])
            P.op("pool", lambda e: e.affine_select(out=t[:], in_=t[:], pattern=[[pat, 128]], compare_op=op,
                                                   fill=0.0, base=0, channel_multiplier=cm), [key], [key])
        SEL(Uf, "Uf", -1, 1, ALU.is_ge)
        SEL(SLf, "SLf", 1, -1, ALU.is_gt)
        SEL(Ub, "Ub", 1, -1, ALU.is_ge)
        SEL(SLb, "SLb", -1, 1, ALU.is_gt)
        MEMSET(ident[:], 0.0, ["ident"])
        P.op("pool", lambda e: e.affine_select(out=ident[:], in_=ident[:], pattern=[[-1, 128]], compare_op=ALU.not_equal,
                                               fill=1.0, base=0, channel_multiplier=1), ["ident"], ["ident"])
        CP(identb[:], ident[:], ["ident"], ["identb"])

        def LOADP(dram, shape, name):
            t = T(shape, F32, name)
            DMA(t[:], dram, (), [name])
            return t
        cvec = LOADP(cvec_d, [128, 8, 2], "cvec")
        bmod = LOADP(bmod_d, [128, DEPTH, 24], "bmod")
        gpre = LOADP(gpre_d, [128, DEPTH, 8], "gpre")
        gpost = LOADP(gpost_d, [128, DEPTH, 8], "gpost")
        cw5 = LOADP(cw5_d, [128, DEPTH, 12, 5], "cw5")
        cb5 = LOADP(cb5_d, [128, DEPTH, 12], "cb5")
        alog = LOADP(alog_d, [128, DEPTH, 32], "alog")
        dtb = LOADP(dtb_d, [128, DEPTH, 32], "dtb")
        dsk = LOADP(dsk_d, [128, DEPTH, 16], "dsk")
        sng = LOADP(sng_d, [128, DEPTH, 8], "sng")
        cw31 = LOADP(cw31_d, [128, DEPTH, 8, 31], "cw31")
        cb31 = LOADP(cb31_d, [128, DEPTH, 8], "cb31")
        lng = LOADP(lng_d, [128, DEPTH, 8], "lng")
        lnb = LOADP(lnb_d, [128, DEPTH, 8], "lnb")
        cb5row = T([1, DEPTH * 1536], BF16, "cb5row")
        for l_ in range(DEPTH):
            DMA(cb5row[:, l_ * 1536:(l_ + 1) * 1536], cb5row_d[:, l_ * 1536:(l_ + 1) * 1536], (), ["cb5row"], eng="pool")
        aneg = T([128, DEPTH, 32], F32, "aneg")
        ACT(aneg[:], alog[:], AF.Exp, ["alog"], ["aneg"])
        TS(aneg[:], aneg[:], -1.0, ALU.mult, ["aneg"], ["aneg"])

        silc = T([128, 8, 2], F32, "silc")
        ACT(silc[:], cvec[:], AF.Silu, ["cvec"], ["silc"])
        modA = T([128, DEPTH, 8, 2], F32, "modA")
        modB = T([128, DEPTH, 8, 2], F32, "modB")
        modG = T([128, DEPTH, 8, 2], F32, "modG")
        hT = T([128, 8, TMAX], BF16, "hT")
        yT = T([128, 16, TMAX], BF16, "yT")
        ms = ExitStack()
        st.callback(ms.close)
        if True:
            wm = Rot("wm", [128, 8, 512], F32, 2, ms)
            modsb = T([128, 24, 2], F32, "modsb", ms)
            modrow = T([2, 3 * D], F32, "modrow", ms)
            for l in range(DEPTH):
                for cb in range(6):
                    wt, wk = wm.next()
                    DMA(wt[:], wmod_d[l].rearrange("(kc p) c -> p kc c", p=128)[:, :, cb * 512:(cb + 1) * 512], (), [wk])
                    pr, prk = PS()
                    for kc in range(8):
                        MM(pr[0:2, :], silc[:, kc, :], wt[:, kc, :], kc == 0, kc == 7, [wk, "silc"], [prk])
                    CP(modrow[:, cb * 512:(cb + 1) * 512], pr[0:2, :], [prk], [("modrow", cb)])
                pm, pmk = PSH()
                for f in range(24):
                    TR(pm[:, f * 2:f * 2 + 2], modrow[0:2, f * 128:(f + 1) * 128], ident[0:2, 0:2], [("modrow", f // 4), "ident"], [pmk])
                TT(modsb[:], pm[:, 0:48].rearrange("p (f w) -> p f w", w=2),
                   bmod[:, l, :].unsqueeze(2).to_broadcast([128, 24, 2]), ALU.add, [pmk, "bmod"], ["modsb"])
                TS(modA[:, l], modsb[:, 8:16, :], 1.0, ALU.add, ["modsb"], ["modA"])
                TT(modA[:, l], modA[:, l], gpre[:, l, :].unsqueeze(2).to_broadcast([128, 8, 2]), ALU.mult, ["modA", "gpre"], ["modA"])
                CP(modB[:, l], modsb[:, 0:8, :], ["modsb"], ["modB"])
                TT(modG[:, l], modsb[:, 16:24, :], gpost[:, l, :].unsqueeze(2).to_broadcast([128, 8, 2]), ALU.mult,
                   ["modsb", "gpost"], ["modG"])

        if debug:
            DMA(dbg["d_modA"], modA[:], ["modA"], (), final=True)
            DMA(dbg["d_modB"], modB[:], ["modB"], (), final=True)
            DMA(dbg["d_modG"], modG[:], ["modG"], (), final=True)

        def stats_rs(src_sq_fn, nk, rs, rsk, extra_reads, eps=EPS):
            pst, pstk = PS()
            for k in range(nk):
                ap, rd = src_sq_fn(k)
                MM(pst[:], onesb[:], ap, k == 0, k == nk - 1, ["onesb"] + rd, [pstk])
            ACT(rs[:], pst[:], AF.Ln, [pstk], [rsk], bias=eps, scale=1.0 / 1024.0)
            ACT(rs[:], rs[:], AF.Exp, [rsk], [rsk], scale=-0.5)

        ms_holder = [ms]

        def run_block(l, x_src, x_dst, nseq, L, stride, wsel, h0, ns_out, final_out, do_front, fuse_next):
            Ttok = nseq * L
            NT = Ttok // 512
            nch = L // 128
            nblk = nseq * nch
            xsrc_v = x_src.rearrange("(kc p) t -> p kc t", p=128)
            xdst_v = x_dst.rearrange("(kc p) t -> p kc t", p=128)
            hk = lambda t: ("hT", t)
            yk = lambda k, b: ("yT", k, b)
            ytile = lambda ks, t: [yk(k, b) for k in ks for b in range(4 * t, 4 * t + 4)]

            def segs(t):
                if L >= 512:
                    per = L // 512
                    return [(t // per, (t % per) * 512, 512, 0)]
                n = 512 // L
                return [(t * n + i, 0, L, i * L) for i in range(n)]

            def win_load(dst, c0, w, key):
                DMA(dst, win_d[l].rearrange("(kc p) c -> p kc c", p=128)[:, :, c0:c0 + w], (), [key], eng="pool")

            with ExitStack() as s0:
                xt_r = Rot("xt", [128, 8, 512], F32, 2, s0)
                sq_r = Rot("sq0", [128, 8, 512], BF16, 2, s0)
                rs_r = Rot("rs0", [128, 512], F32, 2, s0)
                for t in range(NT if do_front else 0):
                    xt, xtk = xt_r.next()
                    sq, sqk = sq_r.next()
                    rs, rsk = rs_r.next()
                    DMA(xt[:], xsrc_v[:, :, t * 512:(t + 1) * 512], [("xd", id(x_src), t)], [xtk])
                    ACT(sq[:], xt[:], AF.Square, [xtk], [sqk])
                    stats_rs(lambda k: (sq[:, k, :], [sqk]), 8, rs, rsk, [])
                    TT(xt[:], xt[:], rs[:].unsqueeze(1).to_broadcast([128, 8, 512]), ALU.mult, [xtk, rsk], [xtk])
                    for kc in range(8):
                        ACT(hT[:, kc, t * 512:(t + 1) * 512], xt[:, kc, :], AF.Identity, [xtk, "modA", "modB"], [hk(t)],
                            bias=modB[:, l, kc, wsel:wsel + 1], scale=modA[:, l, kc, wsel:wsel + 1], group=("hTf", l, t, nseq))

            if ms_holder:
                ms_holder.pop().close()
            P.barrier()
            nm = "P" if nseq == 2 else "S"
            allk = [yk(k, b) for k in range(16) for b in range(nblk)]
            if debug and l == 0:
                DMA(dbg[f"d_hT_{nm}"], hT[:, :, 0:Ttok], [hk(t) for t in range(NT)], (), final=True)
            with ExitStack() as sa:
                wx = T([128, 8, WP], BF16, "wx", sa)
                wB = T([128, 8, 128], BF16, "wB", sa)
                wC = T([128, 8, 128], BF16, "wC", sa)
                wz = T([128, 8, WP], BF16, "wz", sa)
                wdt = T([128, 8, 32], BF16, "wdt", sa)
                upad_r = Rot("upad", [128, nseq, L + 4], BF16, 2, sa)
                diag5_r = Rot("diag5", [128, 5, 128], BF16, 2, sa)
                xg = T([128, nblk, WP], BF16, "xg", sa)
                Btm = T([128, nblk, 128], BF16, "Btm", sa)
                Bfm = T([128, Ttok], BF16, "Bfm", sa)
                Cfm = T([128, Ttok], BF16, "Cfm", sa)
                dt_all = T([128, nblk, 32], F32, "dt_all", sa)
                la_all = T([128, nblk, 32], F32, "la_all", sa)
                v_all = T([128, nblk, 32], F32, "v_all", sa)
                cum_sb = [T([128, nblk, HP], F32, f"cum{d}", sa) for d in range(2)]
                cum_hi = [T([128, nblk, HP], BF16, f"cumhi{d}", sa) for d in range(2)]
                cum_lo = [T([128, nblk, HP], BF16, f"cumlo{d}", sa) for d in range(2)]
                decs_all = [T([128, 3, nblk, HP], F32, f"decs{d}", sa) for d in range(2)]
                diagD = T([128, HP, 128], BF16, "diagD", sa)
                ypark = T([128, nblk, WP], F32, "ypark", sa)
                Sf = [T([128, WP], F32, f"Sf{d}", sa) for d in range(2)]
                Sb = [T([128, WP], BF16, f"Sb{d}", sa) for d in range(2)]
                cbm_r = Rot("cbm", [128, 128], BF16, NROT, sa)
                xdt_r = Rot("xdt", [128, HP, 64], BF16, NROT, sa)
                xs2_r = Rot("xs2", [128, HP, 64], BF16, NROT, sa)
                Lh_r = Rot("Lh", [128, HP, 128], BF16, NROT, sa)
                La_r = Rot("La", [128, HP, 128], F32, 2, sa)
                Mh_r = Rot("Mh", [128, HP, 128], BF16, NROT, sa)
                t1_r = Rot("t1", [128, HP, 64], F32, NROT, sa)
                zs_r = Rot("zs", [128, WP], F32, 2, sa)
                yg_r = Rot("yg", [128, WP], BF16, 2, sa)
                stg_r = Rot("stg", [128, 128], F32, 2, sa)
                for r in upad_r.t:
                    MEMSET(r[:], 0.0, [("upad", upad_r.t.index(r))])

                win_load(wdt[:], I_DT, 32, "wdt")
                for bi in range(nblk):
                    t = bi // 4
                    pd, pdk = PSH()
                    for kc in range(8):
                        MM(pd[:, 0:32], hT[:, kc, bi * 128:(bi + 1) * 128], wdt[:, kc, :], kc == 0, kc == 7, [hk(t), "wdt"], [pdk])
                    TT(v_all[:, bi, :], pd[:, 0:32], dtb[:, l, :], ALU.add, [pdk, "dtb"], ["v_all"])
                TS(dt_all[:], v_all[:], 30.0, ALU.min, ["v_all"], ["dt_all"])
                ACT(dt_all[:], dt_all[:], AF.Exp, ["dt_all"], ["dt_all"])
                ACT(dt_all[:], dt_all[:], AF.Ln, ["dt_all"], ["dt_all"], bias=1.0)
                TT(dt_all[:], dt_all[:], v_all[:], ALU.max, ["dt_all", "v_all"], ["dt_all"])
                TT(la_all[:], dt_all[:], aneg[:, l, :].unsqueeze(1).to_broadcast([128, nblk, 32]), ALU.mult, ["dt_all", "aneg"], ["la_all"])
                for q in range(16 // HP):
                    g = (q * HP) // 8
                    nb4 = nblk * HP
                    win_load(wx[:], I_X + q * WP, WP, "wx")
                    if (q * HP) % 8 == 0:
                        win_load(wB[:], I_B + g * 128, 128, "wB")
                        win_load(wC[:], I_C + g * 128, 128, "wC")
                    win_load(wz[:], I_Z + q * WP, WP, "wz")
                    nxc = WP // 128
                    chunks = [("x", a, q * nxc + a, wx, a * 128, "wx") for a in range(nxc)]
                    if (q * HP) % 8 == 0:
                        chunks += [("B", 0, 8 + g, wB, 0, "wB"), ("C", 0, 10 + g, wC, 0, "wC")]
                    for kind, a, cidx, wt, wc0, wk in chunks:
                        upad, upk = upad_r.next()
                        dg, dgk = diag5_r.next()
                        TT(dg[:], ident[:].unsqueeze(1).to_broadcast([128, 5, 128]),
                           cw5[:, l, cidx, :].unsqueeze(2).to_broadcast([128, 5, 128]), ALU.mult, ["ident", "cw5"], [dgk])
                        for t in range(NT):
                            pu, puk = PS()
                            for kc in range(8):
                                MM(pu[:], wt[:, kc, wc0:wc0 + 128], hT[:, kc, t * 512:(t + 1) * 512], kc == 0, kc == 7,
                                   [wk, hk(t)], [puk])
                            if L >= 512:
                                s_, off, n_, c0 = segs(t)[0]
                                P.op("act", (lambda e, o=upad[:, s_, 2 + off:2 + off + 512], i=pu[:]: e.copy(out=o, in_=i)),
                                     [puk], [upk], cost=590.0)
                            else:
                                n = 512 // L
                                P.op("act", (lambda e, o=upad[:, t * n:(t + 1) * n, 2:2 + L],
                                             i=pu[:].rearrange("p (s x) -> p s x", s=n): e.copy(out=o, in_=i)), [puk], [upk], cost=590.0)
                        if kind in ("x", "B"):
                            for s_ in range(nseq):
                                for j in range(nch):
                                    bi = s_ * nch + j
                                    pc, pck = PSH()
                                    for k in range(5):
                                        MM(pc[:, 0:128], upad[:, s_, j * 128 + k:j * 128 + k + 128], dg[:, k, :], k == 0, False,
                                           [upk, dgk], [pck])
                                    MM(pc[:, 0:128], onesb[0:1, 0:128], cb5row[0:1, l * 1536 + cidx * 128:l * 1536 + (cidx + 1) * 128],
                                       False, True, ["onesb", "cb5row"], [pck])
                                    if kind == "x":
                                        ACT(xg[:, bi, a * 128:(a + 1) * 128], pc[:, 0:128], AF.Silu, [pck], [("xg", bi)], group=("xg", l, nseq, q, bi))
                                    else:
                                        ACT(Btm[:, bi, :], pc[:, 0:128], AF.Silu, [pck], [("Btm", bi)])
                        if kind in ("B", "C"):
                            dstT, dkey = (Bfm, "Bfm") if kind == "B" else (Cfm, "Cfm")
                            for t in range(NT):
                                pc, pck = PS()
                                for (s_, off, n_, c0) in segs(t):
                                    for k in range(5):
                                        MM(pc[:, c0:c0 + n_], dg[:, k, :], upad[:, s_, off + k:off + k + n_], k == 0, k == 4,
                                           [upk, dgk], [pck])
                                ACT(dstT[:, t * 512:(t + 1) * 512], pc[:], AF.Silu, [pck, "cb5"], [(dkey, t)],
                                    bias=cb5[:, l, cidx:cidx + 1])
                    TT(diagD[:], ident[:].unsqueeze(1).to_broadcast([128, HP, 128]),
                       dsk[:, l, q * HP:(q + 1) * HP].unsqueeze(2).to_broadcast([128, HP, 128]), ALU.mult, ["ident", "dsk"], ["diagD"])
                    for d in (1, 0):
                        Uin, UinK = (Uf, "Uf") if d == 0 else (Ub, "Ub")
                        SLo, SLoK = (SLf, "SLf") if d == 0 else (SLb, "SLb")
                        pdc, pdck = PSH()
                        la_d = la_all[:, :, d * 16 + q * HP:d * 16 + (q + 1) * HP]
                        for ci, (mt, mk) in enumerate(((Uin, UinK), (SLo, SLoK), (onesf, "onesf"))):
                            MM(pdc[:, ci * nb4:(ci + 1) * nb4].rearrange("p (b h) -> p b h", h=HP), mt[:], la_d, True, True,
                               [mk, "la_all"], [pdck])
                        dcs = decs_all[d]
                        dcsk = ("decs", d)
                        ACT(dcs[:].rearrange("p c b h -> p (c b h)"), pdc[:, 0:3 * nb4], AF.Exp, [pdck], [dcsk])
                        cum = cum_sb[d]
                        cumk = ("cum", d)
                        CP(cum[:].rearrange("p b h -> p (b h)"), pdc[:, 0:nb4], [pdck, dcsk], [cumk])
                        chi, clo = cum_hi[d], cum_lo[d]
                        CP(chi[:], cum[:], [cumk], [("chi", d)])
                        TT(clo[:], cum[:], chi[:], ALU.subtract, [cumk, ("chi", d)], [("clo", d)])
                        TT(cum[:], chi[:], clo[:], ALU.add, [("chi", d), ("clo", d), cumk], [cumk])
                        for s_ in range(nseq):
                            if h0 is None:
                                MEMSET(Sf[d][:], 0.0, [("Sf", d)], eng="dve")
                                MEMSET(Sb[d][:], 0.0, [("Sb", d)], eng="dve")
                            else:
                                for a in range(WP // 128):
                                    sg, sgk = stg_r.next()
                                    DMA(sg[:], h0[l, d, q * WP + a * 128:q * WP + (a + 1) * 128, :], (), [sgk])
                                    pt, ptk = PSH()
                                    TR(pt[:, 0:128], sg[:], ident[:], [sgk, "ident"], [ptk])
                                    CP(Sf[d][:, a * 128:(a + 1) * 128], pt[:, 0:128], [ptk], [("Sf", d)])
                                CP(Sb[d][:], Sf[d][:], [("Sf", d)], [("Sb", d)])
                            order = range(nch) if d == 0 else range(nch - 1, -1, -1)
                            for j in order:
                                bi = s_ * nch + j
                                t = bi // 4
                                tok = slice(bi * 128, (bi + 1) * 128)
                                la_b = la_all[:, bi, d * 16 + q * HP:d * 16 + (q + 1) * HP]
                                dt_b = dt_all[:, bi, d * 16 + q * HP:d * 16 + (q + 1) * HP]
                                pcb, pcbk = PSH()
                                MM(pcb[:, 0:128], Bfm[:, tok], Cfm[:, tok], True, True, [("Bfm", t), ("Cfm", t)], [pcbk])
                                cbm, cbmk = cbm_r.next()
                                TT(cbm[:], pcb[:, 0:128], Uin[:], ALU.mult, [pcbk, UinK], [cbmk])
                                xdt, xdtk = xdt_r.next()
                                xs2, xs2k = xs2_r.next()
                                xg_b = xg[:, bi, :].rearrange("p (h c) -> p h c", h=HP)
                                TT(xdt[:], xg_b, dt_b.unsqueeze(2).to_broadcast([128, HP, 64]), ALU.mult, [("xg", bi), "dt_all"], [xdtk], eng=OFF_ENG)
                                TT(xs2[:], xdt[:], dcs[:, 1, bi, :].unsqueeze(2).to_broadcast([128, HP, 64]), ALU.mult,
                                   [xdtk, dcsk], [xs2k], eng=OFF_ENG)
                                parg, pargk = PS()
                                for h in range(HP):
                                    po_ = parg[:, h * 128:(h + 1) * 128]
                                    MM(po_, chi[:, bi, h:h + 1].to_broadcast([128, 128]), identb[:], True, False, [("chi", d), "identb"], [pargk])
                                    MM(po_, clo[:, bi, h:h + 1].to_broadcast([128, 128]), identb[:], False, True, [("clo", d), "identb"], [pargk])
                                La, Lak = La_r.next()
                                for h in range(HP):
                                    ACT(La[:, h, :], parg[:, h * 128:(h + 1) * 128], AF.Relu, [pargk, cumk], [Lak],
                                        bias=cum[:, bi, h:h + 1], scale=-1.0, group=("relu", l, nseq, q, d, bi))
                                Lh, Lhk = Lh_r.next()
                                ACT(Lh[:].rearrange("p h c -> p (h c)"), La[:].rearrange("p h c -> p (h c)"), AF.Exp, [Lak], [Lhk], scale=-1.0)
                                Mh, Mhk = Mh_r.next()
                                TT(Mh[:], Lh[:], cbm[:].unsqueeze(1).to_broadcast([128, HP, 128]), ALU.mult, [Lhk, cbmk], [Mhk])
                                py, pyk = PSH()
                                for h in range(HP):
                                    MM(py[:, h * 64:(h + 1) * 64], Mh[:, h, :], xdt[:, h, :], True, d == 1, [Mhk, xdtk], [pyk])
                                    if d == 0:
                                        MM(py[:, h * 64:(h + 1) * 64], diagD[:, h, :], xg[:, bi, h * 64:(h + 1) * 64], False, True,
                                           ["diagD", ("xg", bi)], [pyk])
                                po, pok = PSH()
                                MM(po[:, 0:WP], Cfm[:, tok], Sb[d][:], True, True, [("Cfm", t), ("Sb", d)], [pok])
                                t1, t1k = t1_r.next()
                                TT(t1[:], po[:, 0:WP].rearrange("p (h c) -> p h c", h=HP),
                                   dcs[:, 0, bi, :].unsqueeze(2).to_broadcast([128, HP, 64]), ALU.mult, [pok, dcsk], [t1k])
                                t1f = t1[:].rearrange("p h c -> p (h c)")
                                if d == 1:
                                    TT(ypark[:, bi, :], t1f, py[:, 0:WP], ALU.add, [t1k, pyk], [("ypark", bi)])
                                else:
                                    TT(t1f, t1f, py[:, 0:WP], ALU.add, [t1k, pyk], [t1k])
                                    TT(t1f, t1f, ypark[:, bi, :], ALU.add, [t1k, ("ypark", bi)], [t1k], eng=OFF_ENG)
                                    yg, ygk = yg_r.next()
                                    zs, zsk = zs_r.next()
                                    pz, pzk = PSH()
                                    for kc in range(8):
                                        MM(pz[:, 0:WP], hT[:, kc, tok], wz[:, kc, :], kc == 0, kc == 7, [hk(t), "wz"], [pzk])
                                    ACT(zs[:], pz[:, 0:WP], AF.Tanh, [pzk], [zsk], scale=0.5)
                                    STT(zs[:], zs[:], 1.0, pz[:, 0:WP], ALU.add, ALU.mult, [zsk, pzk], [zsk])
                                    TT(yg[:], t1f, zs[:], ALU.mult, [t1k, zsk], [ygk])
                                    for a in range(WP // 128):
                                        TR(psb_t[:, a * 128:(a + 1) * 128], yg[:, a * 128:(a + 1) * 128], identb[:], [ygk, "identb"], ["psb"])
                                    kc0 = q * (WP // 128)
                                    P.op("act", (lambda e, o=yT[:, kc0:kc0 + WP // 128, tok],
                                                 i=psb_t[:, 0:WP].rearrange("p (a c) -> p a c", c=128): e.copy(out=o, in_=i)),
                                         ["psb"], [yk(kc0 + a, bi) for a in range(WP // 128)], cost=400.0)
                                pds, pdsk = PSH()
                                MM(pds[:, 0:WP], Btm[:, bi, :], xs2[:].rearrange("p h c -> p (h c)"), True, True, [("Btm", bi), xs2k], [pdsk])
                                Sf3 = Sf[d][:].rearrange("p (h c) -> p h c", h=HP)
                                TT(Sf3, Sf3, dcs[:, 2, bi, :].unsqueeze(2).to_broadcast([128, HP, 64]), ALU.mult,
                                   [("Sf", d), dcsk], [("Sf", d)], eng=OFF_ENG)
                                TT(Sf[d][:], Sf[d][:], pds[:, 0:WP], ALU.add, [("Sf", d), pdsk], [("Sf", d)])
                                P.op("act", (lambda e, o=Sb[d][:], i=Sf[d][:]: e.copy(out=o, in_=i)), [("Sf", d)], [("Sb", d)], cost=400.0)
                            if ns_out is not None:
                                for a in range(WP // 128):
                                    pt, ptk = PSH()
                                    TR(pt[:, 0:128], Sf[d][:, a * 128:(a + 1) * 128], ident[:], [("Sf", d), "ident"], [ptk])
                                    sg, sgk = stg_r.next()
                                    CP(sg[:], pt[:, 0:128], [ptk], [sgk])
                                    DMA(ns_out[s_, l, d, q * WP + a * 128:q * WP + (a + 1) * 128, :], sg[:], [sgk], (), final=True)

            P.barrier()
            if debug and l == 0:
                DMA(dbg[f"d_yA_{nm}"], yT[:, :, 0:Ttok], allk, (), final=True)
            def ssd_norm(sn):
                sq_r = Rot("sqn", [128, 8, 512], BF16, 1, sn)
                rs_r = Rot("rsn", [128, 512], F32, 2, sn)
                for t in range(NT):
                    sq, sqk = sq_r.next()
                    rs, rsk = rs_r.next()
                    tl = slice(t * 512, (t + 1) * 512)
                    ACT(sq[:], yT[:, 0:8, tl], AF.Square, ytile(range(8), t), [sqk])
                    stats_rs(lambda k: (sq[:, k, :], [sqk]), 8, rs, rsk, [], eps=4.0 * EPS)
                    for k in range(8):
                        STT(yT[:, k, tl], yT[:, k, tl], sng[:, l, k:k + 1], rs[:], ALU.mult, ALU.mult,
                            ytile([k], t) + ["sng", rsk], ytile([k], t))

            pad = 15 * stride
            with ExitStack() as sb_:
                ssd_norm(sb_)
                if debug and l == 0:
                    DMA(dbg[f"d_yB_{nm}"], yT[:, :, 0:Ttok], allk, (), final=True)
                wga = T([128, 8, 1024], BF16, "wga", sb_)
                wgb = T([128, 8, 1024], BF16, "wgb", sb_)
                for j in range(8):
                    win_load(wga[:, :, j * 128:(j + 1) * 128], I_GA + j * 128, 128, ("wga", j))
                    win_load(wgb[:, :, j * 128:(j + 1) * 128], I_GB + j * 128, 128, ("wgb", j))
                hc_r = Rot("hc", [128, nseq, L + 2 * pad], BF16, 2, sb_)
                d31_r = Rot("d31", [128, 31, 128], BF16, 2, sb_)
                sig_r = Rot("sig", [128, 512], F32, 2, sb_)
                accd_r = Rot("accd", [128, 512], F32, 2, sb_)
                accp_r = Rot("accp", [128, 512], F32, 2, sb_)
                accbd_r = Rot("accbd", [128, 512], BF16, 2, sb_)
                accbp_r = Rot("accbp", [128, 512], BF16, 2, sb_)
                for r in hc_r.t:
                    MEMSET(r[:], 0.0, [("hc", hc_r.t.index(r))])
                for j in range(8):
                    hc, hck = hc_r.next()
                    dg, dgk = d31_r.next()
                    TT(dg[:], ident[:].unsqueeze(1).to_broadcast([128, 31, 128]),
                       cw31[:, l, j, :].unsqueeze(2).to_broadcast([128, 31, 128]), ALU.mult, ["ident", "cw31"], [dgk])
                    for t in range(NT):
                        pa, pak = PS()
                        pb, pbk = PS()
                        for kc in range(8):
                            MM(pa[:], wga[:, kc, j * 128:(j + 1) * 128], hT[:, kc, t * 512:(t + 1) * 512], kc == 0, kc == 7, [("wga", j), hk(t)], [pak])
                        for kc in range(8):
                            MM(pb[:], wgb[:, kc, j * 128:(j + 1) * 128], hT[:, kc, t * 512:(t + 1) * 512], kc == 0, kc == 7, [("wgb", j), hk(t)], [pbk])
                        sig, sigk = sig_r.next()
                        ACT(sig[:], pb[:], AF.Sigmoid, [pbk], [sigk])
                        if L >= 512:
                            s_, off, n_, c0 = segs(t)[0]
                            TT(hc[:, s_, pad + off:pad + off + 512], pa[:], sig[:], ALU.mult, [pak, sigk], [hck])
                        else:
                            n = 512 // L
                            TT(hc[:, t * n:(t + 1) * n, pad:pad + L], pa[:].rearrange("p (s x) -> p s x", s=n),
                               sig[:].rearrange("p (s x) -> p s x", s=n), ALU.mult, [pak, sigk], [hck])
                    for t in range(NT):
                        pc, pck = PS()
                        for (s_, off, n_, c0) in segs(t):
                            taps = [k for k in range(31) if off + (k - 15) * stride + n_ > 0 and off + (k - 15) * stride < L]
                            win = lambda k: hc[:, s_, pad + off + (k - 15) * stride:pad + off + (k - 15) * stride + n_]
                            wk_ = lambda k: cw31[:, l, j, k:k + 1]
                            extra = []
                            rest = list(taps)
                            for eng_, ntap, acc_r, accb_r in (("dve", CONV_ND, accd_r, accbd_r), ("pool", CONV_NP, accp_r, accbp_r)):
                                if ntap == 0 or len(rest) - ntap < 4:
                                    continue
                                mine, rest = rest[:ntap], rest[ntap:]
                                acc, acck = acc_r.next()
                                accb, accbk = accb_r.next()
                                c_ = (120.0 + n_ / 0.96) if eng_ == "dve" else (200.0 + n_ / 0.55)
                                for ii, k in enumerate(mine):
                                    last_ = ii == len(mine) - 1
                                    dst, dstk = (accb, accbk) if last_ else (acc, acck)
                                    if ii == 0:
                                        P.op(eng_, (lambda e, o=dst[:, 0:n_], i0=win(k), sc=wk_(k): e.tensor_scalar(
                                            out=o, in0=i0, scalar1=sc, scalar2=None, op0=ALU.mult)), [hck, "cw31"], [dstk], cost=c_)
                                    else:
                                        P.op(eng_, (lambda e, o=dst[:, 0:n_], i0=win(k), sc=wk_(k), i1=acc[:, 0:n_]: e.scalar_tensor_tensor(
                                            out=o, in0=i0, scalar=sc, in1=i1, op0=ALU.mult, op1=ALU.add)), [hck, "cw31", acck], [dstk], cost=c_)
                                extra.append((accb, accbk))
                            nmm = len(rest) + len(extra)
                            im = 0
                            for k in rest:
                                MM(pc[:, c0:c0 + n_], dg[:, k, :], win(k), im == 0, im == nmm - 1, [hck, dgk], [pck])
                                im += 1
                            for accb, accbk in extra:
                                MM(pc[:, c0:c0 + n_], identb[:], accb[:, 0:n_], im == 0, im == nmm - 1, ["identb", accbk], [pck])
                                im += 1
                        ACT(yT[:, 8 + j, t * 512:(t + 1) * 512], pc[:], AF.Identity, [pck, "cb31"], ytile([8 + j], t),
                            bias=cb31[:, l, j:j + 1])
            P.barrier()
            with ExitStack() as sb_:
                wgs = T([128, 8, 1024], BF16, "wgs", sb_)
                for j in range(8):
                    win_load(wgs[:, :, j * 128:(j + 1) * 128], I_GS + j * 128, 128, ("wgs", j))
                sq_r = Rot("sqc", [128, 8, 512], BF16, 2, sb_)
                mean_r = Rot("mean", [128, 512], F32, 2, sb_)
                rs_r = Rot("rsc", [128, 512], F32, 2, sb_)
                tmp_r = Rot("tmpc", [128, 512], F32, 2, sb_)
                s1_r = Rot("s1c", [128, 512], F32, 2, sb_)
                for t in range(NT):
                    tl = slice(t * 512, (t + 1) * 512)
                    sq, sqk = sq_r.next()
                    mean, meank = mean_r.next()
                    rs, rsk = rs_r.next()
                    p1, p1k = PS()
                    for k in range(8):
                        MM(p1[:], onesb[:], yT[:, 8 + k, tl], k == 0, k == 7, ["onesb"] + ytile([8 + k], t), [p1k])
                    ACT(sq[:], yT[:, 8:16, tl], AF.Square, ytile(range(8, 16), t), [sqk])
                    p2, p2k = PS()
                    for k in range(8):
                        MM(p2[:], onesb[:], sq[:, k, :], k == 0, k == 7, ["onesb", sqk], [p2k])
                    TS(mean[:], p1[:], 1.0 / 1024.0, ALU.mult, [p1k], [meank])
                    tmp, tmpk = tmp_r.next()
                    TT(tmp[:], mean[:], mean[:], ALU.mult, [meank], [tmpk])
                    STT(tmp[:], p2[:], 1.0 / 1024.0, tmp[:], ALU.mult, ALU.subtract, [p2k, tmpk], [tmpk])
                    ACT(rs[:], tmp[:], AF.Ln, [tmpk], [rsk], bias=EPS)
                    ACT(rs[:], rs[:], AF.Exp, [rsk], [rsk], scale=-0.5)
                    for j in range(8):
                        tmp, tmpk = tmp_r.next()
                        s1, s1k = s1_r.next()
                        TT(tmp[:], yT[:, 8 + j, tl], mean[:], ALU.subtract, ytile([8 + j], t) + [meank], [tmpk])
                        TT(tmp[:], tmp[:], rs[:], ALU.mult, [tmpk, rsk], [tmpk])
                        ACT(s1[:], tmp[:], AF.Silu, [tmpk, "lng", "lnb"], [s1k], bias=lnb[:, l, j:j + 1], scale=lng[:, l, j:j + 1])
                        pg, pgk = PS()
                        for kc in range(8):
                            MM(pg[:], wgs[:, kc, j * 128:(j + 1) * 128], hT[:, kc, tl], kc == 0, kc == 7, [("wgs", j), hk(t)], [pgk])
                        ACT(tmp[:], pg[:], AF.Silu, [pgk], [tmpk])
                        TT(yT[:, 8 + j, tl], s1[:], tmp[:], ALU.mult, [s1k, tmpk], ytile([8 + j], t))

            P.barrier()
            if debug and l == 0:
                DMA(dbg[f"d_yC_{nm}"], yT[:, :, 0:Ttok], allk, (), final=True)
            with ExitStack() as sc:
                wo = T([128, 16, 1024], BF16, "wo", sc)
                for fo in range(8):
                    DMA(wo[:, :, fo * 128:(fo + 1) * 128], wout_d[l].rearrange("(kc p) c -> p kc c", p=128)[:, :, fo * 128:(fo + 1) * 128],
                        (), [("wo", fo)], eng="pool")
                osb_r = Rot("osb", [128, 8, 512], F32, 2, sc)
                sq_r = Rot("sqo", [128, 8, 512], BF16, 1, sc)
                rs_r = Rot("rso", [128, 512], F32, 2, sc)
                xt_r = Rot("xto", [128, 8, 512], F32, 1, sc)
                for t in range(NT):
                    tl = slice(t * 512, (t + 1) * 512)
                    osb, osbk = osb_r.next()
                    sq, sqk = sq_r.next()
                    rs, rsk = rs_r.next()
                    xt, xtk = xt_r.next()
                    DMA(xt[:], xsrc_v[:, :, tl], [("xd", id(x_src), t)], [xtk])
                    for fo in range(8):
                        po, pok = PS()
                        for kc in range(16):
                            MM(po[:], wo[:, kc, fo * 128:(fo + 1) * 128], yT[:, kc, tl], kc == 0, kc == 15,
                               [("wo", fo)] + ytile([kc], t), [pok])
                        P.op("act", (lambda e, o=osb[:, fo, :], i=po[:]: e.copy(out=o, in_=i)), [pok], [(osbk, fo)], cost=590.0)
                        ACT(sq[:, fo, :], po[:], AF.Square, [pok], [(sqk, fo)])
                    stats_rs(lambda k: (sq[:, k, :], [(sqk, k)]), 8, rs, rsk, [])
                    for fo in range(8):
                        TT(osb[:, fo, :], osb[:, fo, :], rs[:], ALU.mult, [(osbk, fo), rsk], [(osbk, fo)])
                        STT(osb[:, fo, :], osb[:, fo, :], modG[:, l, fo, wsel:wsel + 1], xt[:, fo, :], ALU.mult, ALU.add,
                            [(osbk, fo), "modG", xtk], [(osbk, fo)])
                    allosb = [(osbk, k) for k in range(8)]
                    DMA(xdst_v[:, :, tl], osb[:], allosb, [("xd", id(x_dst), t)], final=final_out)
                    if fuse_next:
                        allsq = [(sqk, k) for k in range(8)]
                        ACT(sq[:], osb[:], AF.Square, allosb, allsq)
                        rs2, rs2k = rs_r.next()
                        stats_rs(lambda k: (sq[:, k, :], [(sqk, k)]), 8, rs2, rs2k, [])
                        TT(xt[:], osb[:], rs2[:].unsqueeze(1).to_broadcast([128, 8, 512]), ALU.mult, allosb + [rs2k], [xtk])
                        for kc in range(8):
                            ACT(hT[:, kc, tl], xt[:, kc, :], AF.Identity, [xtk, "modA", "modB"], [hk(t)],
                                bias=modB[:, l + 1, kc, wsel:wsel + 1], scale=modA[:, l + 1, kc, wsel:wsel + 1],
                                group=("hTn", l, t, nseq))
            P.barrier()

        for nm_ in ("P", "S"):
            for l in range(DEPTH):
                last = (l == DEPTH - 1)
                if only is not None and (l, nm_) not in only:
                    continue
                if nm_ == "P":
                    run_block(l, xp_d if l == 0 else x1p_d, yp_d if last else x1p_d, 2, 256, 1, 0, None, ns_d, last or debug,
                              l == 0 or debug, (not last) and not debug)
                else:
                    run_block(l, xs_d if l == 0 else x1s_d, ys_d if last else x1s_d, 1, 2048, 64, 1, h0_d, None, last or debug,
                              l == 0 or debug, (not last) and not debug)
        P.emit()
        n_ins = len(P.ins)
    return nc, n_ins


_CACHE = {}


def _fm(v):
    v = np.asarray(v, np.float32)
    lead = v.shape[:-1]
    nchunk = v.shape[-1] // 128
    r = v.reshape(lead + (nchunk, 128))
    return np.ascontiguousarray(np.moveaxis(r, -1, 0))


def kernel(x_prompt, x_sample, state_ssd, c, c_ctx, w_mod, b_mod, g_pre, g_post, w_in,
           ssd_conv_w, ssd_conv_b, ssd_a_log, ssd_dt_bias, ssd_d, ssd_norm_g,
           conf_conv_w, conf_conv_b, conf_ln_g, conf_ln_b, w_out):
    f = lambda a: np.ascontiguousarray(np.asarray(a, np.float32))
    x_prompt, x_sample, state_ssd = f(x_prompt), f(x_sample), f(state_ssd)
    if "nc" not in _CACHE:
        _CACHE["nc"] = build_program()[0]
    nc = _CACHE["nc"]
    rep = lambda a: np.ascontiguousarray(np.broadcast_to(f(a).reshape(1, DEPTH, -1), (128, DEPTH, f(a).reshape(DEPTH, -1).shape[1])))
    shared = {
        "w_mod": f(w_mod), "b_mod": _fm(b_mod), "g_pre": _fm(g_pre), "g_post": _fm(g_post), "w_in": f(w_in),
        "cw5": np.ascontiguousarray(np.transpose(f(ssd_conv_w).reshape(DEPTH, 5, 12, 128), (3, 0, 2, 1))),
        "cb5": _fm(ssd_conv_b), "cb5row": f(ssd_conv_b).reshape(1, DEPTH * 1536),
        "alog": rep(ssd_a_log), "dtb": rep(ssd_dt_bias), "dsk": rep(ssd_d), "sng": _fm(ssd_norm_g),
        "cw31": np.ascontiguousarray(np.transpose(f(conf_conv_w).reshape(DEPTH, 31, 8, 128), (3, 0, 2, 1))),
        "cb31": _fm(conf_conv_b), "lng": _fm(conf_ln_g), "lnb": _fm(conf_ln_b), "w_out": f(w_out),
    }
    in_maps = []
    for core in range(NCORES):
        b = core // 4
        m = dict(shared)
        m["xp"] = np.ascontiguousarray(x_prompt[2 * core:2 * core + 2].reshape(512, D).T)
        m["xs"] = np.ascontiguousarray(x_sample[b].T)
        m["h0"] = np.ascontiguousarray(state_ssd[b].reshape(DEPTH, 2, 1024, 128))
        cv = np.stack([f(c_ctx), f(c)[b]], axis=-1)
        m["cvec"] = np.ascontiguousarray(np.transpose(cv.reshape(8, 128, 2), (1, 0, 2)))
        in_maps.append(m)
    res = run_bass_kernel_spmd(nc, in_maps, core_ids=list(range(NCORES)))
    r = res.results
    y_prompt = np.stack([r[core]["yp"].T.reshape(2, 256, D) for core in range(NCORES)], 0).reshape(16, 256, D)
    y_sample = np.stack([r[0]["ys"].T, r[4]["ys"].T], 0)
    new_state = np.concatenate([r[core]["ns"] for core in range(NCORES)], 0).reshape(16, DEPTH, 2, 16, 64, 128)
    return (np.ascontiguousarray(y_prompt, dtype=np.float32), np.ascontiguousarray(y_sample, dtype=np.float32),
            np.ascontiguousarray(new_state, dtype=np.float32))
```

```python
import numpy as np
from contextlib import ExitStack
import concourse.bass as bass
import concourse.mybir as mybir
from concourse.bass_utils import run_bass_kernel_spmd

F32 = mybir.dt.float32
BF16 = mybir.dt.bfloat16
AF = mybir.ActivationFunctionType
ALU = mybir.AluOpType

D = 1024
DEPTH = 2
NCORES = 8
EPS = 1e-6
I_Z, I_X, I_B, I_C, I_DT, I_GA, I_GB, I_GS = 0, 1024, 2048, 2304, 2560, 2592, 3616, 4640
IN_COLS = 5664
HP = 4
WP = HP * 64
TMAX = 2048
NPS = 7
CONV_ND = 8
CONV_NP = 0
NROT = 3
OFF_ENG = "pool"


class Prog:
    SEM_LIMIT = 4000
    WINDOW = 256
    SEM_LAT = 400.0

    def __init__(self, nc, stack, same_engine_sync=True, schedule=True):
        self.nc = nc
        self.stack = stack
        self.engs = {"pe": nc.tensor, "act": nc.scalar, "dve": nc.vector, "pool": nc.gpsimd, "sp": nc.sync}
        self.ins = []
        self.last_w = {}
        self.readers = {}
        self.same_engine_sync = same_engine_sync
        self.schedule = schedule
        self.n_dma_sems = {"sp": 16, "pool": 8, "act": 4, "dve": 4, "pe": 4}
        self.out_dmas = []
        self.w_rdeps = {}
        self.phase = 0

    def barrier(self):
        self.phase += 1

    def op(self, eng, fn, reads=(), writes=(), dma=False, final=False, cost=300.0, lat=0.0, group=None):
        deps = set()
        for r in reads:
            if r in self.last_w:
                deps |= set(self.last_w[r][1])
        i = len(self.ins)
        for w in writes:
            same = False
            if w in self.last_w:
                gid, members = self.last_w[w]
                same = group is not None and gid == group
                if not same:
                    deps |= set(members)
            if same:
                deps |= self.w_rdeps.get(w, set())
            else:
                rd = set(self.readers.get(w, set()))
                deps |= rd
                self.w_rdeps[w] = (set(self.last_w[w][1]) if w in self.last_w else set()) | rd
        deps.discard(i)
        self.ins.append(dict(eng=eng, fn=fn, deps=deps, dma=dma, cost=cost, lat=lat, phase=self.phase))
        for r in reads:
            self.readers.setdefault(r, set()).add(i)
        for w in writes:
            if w in self.last_w and group is not None and self.last_w[w][0] == group:
                self.last_w[w][1].append(i)
            else:
                self.last_w[w] = (group, [i])
                self.readers[w] = set()
        if final:
            self.out_dmas.append(i)
        return i

    def _order(self):
        ins = self.ins
        n = len(ins)
        per_eng = {e: [] for e in self.engs}
        for i, it in enumerate(ins):
            per_eng[it["eng"]].append(i)
        if not self.schedule:
            return per_eng
        users = [[] for _ in range(n)]
        nun = [0] * n
        for i, it in enumerate(ins):
            nun[i] = len(it["deps"])
            for d in it["deps"]:
                users[d].append(i)
        blev = [0.0] * n
        for i in range(n - 1, -1, -1):
            it = ins[i]
            m = 0.0
            for u in users[i]:
                if ins[u]["phase"] == it["phase"] and blev[u] > m:
                    m = blev[u]
            blev[i] = it["cost"] + it["lat"] + m
        rdy = [0.0] * n
        self.t_start = [0.0] * n
        self.t_fin = [0.0] * n
        nphase = self.phase + 1
        left = [0] * nphase
        for it in ins:
            left[it["phase"]] += 1
        cur = 0
        while cur < nphase and left[cur] == 0:
            cur += 1
        phase_t = 0.0
        tmax = 0.0
        eng_free = {e: 0.0 for e in self.engs}
        pend = {e: list(v) for e, v in per_eng.items()}
        order = {e: [] for e in self.engs}
        remaining = n
        while remaining:
            best = None
            for e, lst in pend.items():
                cand = None
                ef = eng_free[e]
                for i in lst[:self.WINDOW]:
                    it = ins[i]
                    if it["phase"] != cur:
                        break
                    if nun[i]:
                        continue
                    stt = max(rdy[i], ef, phase_t)
                    key = (stt, -blev[i]) if stt > ef + 1e-9 else (ef, -blev[i])
                    if cand is None or key < cand[2]:
                        cand = (key[0], i, key)
                if cand is not None and (best is None or cand[0] < best[0] - 1e-9 or
                                         (abs(cand[0] - best[0]) <= 1e-9 and cand[1] < best[1])):
                    best = (cand[0], cand[1], e)
            assert best is not None, "scheduler stuck"
            stt, i, e = best
            it = ins[i]
            eng_free[e] = stt + it["cost"]
            f = stt + it["cost"] + it["lat"]
            self.t_start[i] = stt
            self.t_fin[i] = f
            tmax = max(tmax, f)
            for u in users[i]:
                nun[u] -= 1
                fl = f if (ins[u]["eng"] == e and e == "pe" and not it["dma"]) else f + self.SEM_LAT
                if fl > rdy[u]:
                    rdy[u] = fl
            pend[e].remove(i)
            order[e].append(i)
            remaining -= 1
            left[cur] -= 1
            if left[cur] == 0:
                while cur < nphase and left[cur] == 0:
                    cur += 1
                phase_t = tmax + 200.0
        self.sim_time = tmax
        return order

    def emit(self):
        nc = self.nc
        ins = self.ins
        n = len(ins)
        order = self._order()
        pos = [0] * n
        for e, lst in order.items():
            for k, i in enumerate(lst):
                pos[i] = k
        last_before = {}
        for e, lst in order.items():
            cuts = {}
            for k, i in enumerate(lst):
                cuts.setdefault(ins[i]["phase"], k)
            last_before[e] = (lst, cuts)
        for e, lst in order.items():
            seen = -1
            for i in lst:
                p = ins[i]["phase"]
                if p == seen:
                    continue
                seen = p
                if p == 0:
                    continue
                extra = set()
                for e2, (lst2, cuts2) in last_before.items():
                    ks = [k for ph, k in cuts2.items() if ph >= p]
                    endk = min(ks) if ks else len(lst2)
                    if endk == 0:
                        continue
                    extra.add(lst2[endk - 1])
                    nd = self.n_dma_sems[e2]
                    cnt = 0
                    for k in range(endk - 1, -1, -1):
                        if ins[lst2[k]]["dma"]:
                            extra.add(lst2[k])
                            cnt += 1
                            if cnt >= nd:
                                break
                extra.discard(i)
                ins[i]["deps"] = set(ins[i]["deps"]) | extra
        pruned = [None] * n
        for i, it in enumerate(ins):
            e = it["eng"]
            keep = {}
            dmas = []
            for d in it["deps"]:
                p = ins[d]
                if p["dma"]:
                    dmas.append(d)
                    continue
                if p["eng"] == e and (e == "pe" or not self.same_engine_sync):
                    continue
                pe_ = p["eng"]
                if pe_ not in keep or pos[d] > pos[keep[pe_]]:
                    keep[pe_] = d
            pruned[i] = list(keep.values()) + dmas
        needed = [False] * n
        for i in range(n):
            for d in pruned[i]:
                needed[d] = True
        for i in self.out_dmas:
            needed[i] = True
        sem_of = [None] * n
        dma_prev = [None] * n
        for e, lst in order.items():
            nd = self.n_dma_sems[e]
            dsems = None
            dcnt = None
            rr = 0
            cur = None
            ccnt = 0
            k = 0
            for i in lst:
                it = ins[i]
                if it["dma"]:
                    if dsems is None:
                        dsems = [self.stack.enter_context(nc.semaphore(f"dq_{e}_{j}")) for j in range(nd)]
                        dcnt = [0] * nd
                    j = rr
                    rr = (rr + 1) % nd
                    if dcnt[j] > 0:
                        dma_prev[i] = (dsems[j], dcnt[j])
                    dcnt[j] += 16
                    sem_of[i] = (dsems[j], dcnt[j])
                elif needed[i]:
                    if cur is None or ccnt >= self.SEM_LIMIT:
                        cur = self.stack.enter_context(nc.semaphore(f"s_{e}_{k}"))
                        k += 1
                        ccnt = 0
                    ccnt += 1
                    sem_of[i] = (cur, ccnt)
        for e, lst in order.items():
            eng = self.engs[e]
            waited = {}

            def do_wait(sem, cnt):
                key = id(sem)
                if waited.get(key, 0) >= cnt:
                    return
                eng.wait_ge(sem, cnt)
                waited[key] = cnt

            for i in lst:
                it = ins[i]
                ws = [sem_of[d] for d in pruned[i]]
                ws.sort(key=lambda sc: -sc[1])
                for sem, c in ws:
                    do_wait(sem, c)
                if it["dma"] and dma_prev[i] is not None:
                    do_wait(*dma_prev[i])
                inst = it["fn"](eng)
                if sem_of[i] is not None:
                    inst.then_inc(sem_of[i][0], 16 if it["dma"] else 1)
            if e == "sp":
                for i in self.out_dmas:
                    do_wait(*sem_of[i])


def build_program(debug=False, only=None):
    nc = bass.Bass("TRN2", target_bir_lowering=False)
    dt_in = lambda name, shape: nc.dram_tensor(name, shape, F32, kind="ExternalInput").ap()
    dt_out = lambda name, shape: nc.dram_tensor(name, shape, F32, kind="ExternalOutput").ap()
    xp_d = dt_in("xp", [D, 512])
    xs_d = dt_in("xs", [D, 2048])
    h0_d = dt_in("h0", [DEPTH, 2, 1024, 128])
    cvec_d = dt_in("cvec", [128, 8, 2])
    wmod_d = dt_in("w_mod", [DEPTH, D, 3 * D])
    bmod_d = dt_in("b_mod", [128, DEPTH, 24])
    gpre_d = dt_in("g_pre", [128, DEPTH, 8])
    gpost_d = dt_in("g_post", [128, DEPTH, 8])
    win_d = dt_in("w_in", [DEPTH, D, IN_COLS])
    cw5_d = dt_in("cw5", [128, DEPTH, 12, 5])
    cb5_d = dt_in("cb5", [128, DEPTH, 12])
    cb5row_d = dt_in("cb5row", [1, DEPTH * 1536])
    alog_d = dt_in("alog", [128, DEPTH, 32])
    dtb_d = dt_in("dtb", [128, DEPTH, 32])
    dsk_d = dt_in("dsk", [128, DEPTH, 16])
    sng_d = dt_in("sng", [128, DEPTH, 8])
    cw31_d = dt_in("cw31", [128, DEPTH, 8, 31])
    cb31_d = dt_in("cb31", [128, DEPTH, 8])
    lng_d = dt_in("lng", [128, DEPTH, 8])
    lnb_d = dt_in("lnb", [128, DEPTH, 8])
    wout_d = dt_in("w_out", [DEPTH, 2 * D, D])
    yp_d = dt_out("yp", [D, 512])
    ys_d = dt_out("ys", [D, 2048])
    ns_d = dt_out("ns", [2, DEPTH, 2, 1024, 128])
    dbg = {}
    if debug:
        dbg["d_modA"] = dt_out("d_modA", [128, DEPTH, 8, 2])
        dbg["d_modB"] = dt_out("d_modB", [128, DEPTH, 8, 2])
        dbg["d_modG"] = dt_out("d_modG", [128, DEPTH, 8, 2])
        for nm, T_ in (("P", 512), ("S", 2048)):
            dbg[f"d_hT_{nm}"] = nc.dram_tensor(f"d_hT_{nm}", [128, 8, T_], BF16, kind="ExternalOutput").ap()
            dbg[f"d_yA_{nm}"] = nc.dram_tensor(f"d_yA_{nm}", [128, 16, T_], BF16, kind="ExternalOutput").ap()
            dbg[f"d_yB_{nm}"] = nc.dram_tensor(f"d_yB_{nm}", [128, 16, T_], BF16, kind="ExternalOutput").ap()
            dbg[f"d_yC_{nm}"] = nc.dram_tensor(f"d_yC_{nm}", [128, 16, T_], BF16, kind="ExternalOutput").ap()
    x1p_d = nc.dram_tensor("x1p", [D, 512], F32, kind="ExternalOutput" if debug else "Internal").ap()
    x1s_d = nc.dram_tensor("x1s", [D, 2048], F32, kind="ExternalOutput" if debug else "Internal").ap()

    with ExitStack() as st:
        P = Prog(nc, st)
        cnt = [0]

        def T(shape, dt, name=None, stack=None):
            cnt[0] += 1
            return (stack or st).enter_context(nc.sbuf_tensor(f"sb{cnt[0]}_{name or 't'}", shape, dt))

        def nfree(ap):
            r = 1
            for d in ap.shape[1:]:
                r *= d
            return r

        def DMA(out, in_, reads=(), writes=(), eng="sp", final=False):
            nbytes = nfree(out) * out.shape[0] * 4
            P.op(eng, lambda e: e.dma_start(out=out, in_=in_), reads, writes, dma=True, final=final,
                 cost=(150.0 if eng == "sp" else 1200.0), lat=2000.0 + nbytes / 120.0)

        def MM(out, lhsT, rhs, start, stop, reads, writes):
            passes = 4 if lhsT.dtype == F32 else 1
            P.op("pe", lambda e: e.matmul(out, lhsT=lhsT, rhs=rhs, start=start, stop=stop), reads, writes,
                 cost=30.0 + passes * max(nfree(rhs), 64) / 2.4, lat=120.0)

        def TR(out, in_, ident, reads, writes):
            P.op("pe", lambda e: e.transpose(out=out, in_=in_, identity=ident), reads, writes,
                 cost=(4 if in_.dtype == F32 else 1) * 60.0 + 30.0, lat=120.0)

        def ACT(out, in_, func, reads, writes, bias=None, scale=None, group=None):
            kw = {}
            if bias is not None:
                kw["bias"] = bias
            if scale is not None:
                kw["scale"] = scale
            P.op("act", lambda e: e.activation(out=out, in_=in_, func=func, **kw), reads, writes, cost=220.0 + nfree(out) / 1.4,
                 group=group)

        def TT(out, in0, in1, op, reads, writes, eng="dve"):
            c = 120.0 + nfree(out) / 0.96 if eng != "pool" else 200.0 + nfree(out) / 0.55
            P.op(eng, lambda e: e.tensor_tensor(out=out, in0=in0, in1=in1, op=op), reads, writes, cost=c)

        def TS(out, in0, s1, op0, reads, writes, s2=None, op1=None, eng="dve"):
            if op1 is None:
                P.op(eng, lambda e: e.tensor_scalar(out=out, in0=in0, scalar1=s1, scalar2=None, op0=op0), reads, writes,
                     cost=120.0 + nfree(out) / 0.96)
            else:
                P.op(eng, lambda e: e.tensor_scalar(out=out, in0=in0, scalar1=s1, scalar2=s2, op0=op0, op1=op1), reads, writes,
                     cost=120.0 + nfree(out) / 0.96)

        def STT(out, in0, scalar, in1, op0, op1, reads, writes, eng="dve"):
            P.op(eng, lambda e: e.scalar_tensor_tensor(out=out, in0=in0, scalar=scalar, in1=in1, op0=op0, op1=op1), reads, writes,
                 cost=120.0 + nfree(out) / 0.96)

        def CP(out, in_, reads, writes, eng="dve"):
            P.op(eng, lambda e: e.tensor_copy(out=out, in_=in_), reads, writes, cost=120.0 + nfree(out) / 0.96)

        def RECIP(out, in_, reads, writes):
            P.op("dve", lambda e: e.reciprocal(out=out, in_=in_), reads, writes, cost=120.0 + nfree(out) * 6.5)

        def MEMSET(ap, val, writes, eng="pool"):
            P.op(eng, lambda e: e.memset(ap, val), (), writes, cost=150.0 + nfree(ap) / 1.0)

        class Rot:
            def __init__(self, name, shape, dt, n, stack):
                self.t = [T(shape, dt, f"{name}{i}", stack) for i in range(n)]
                self.name = name
                self.i = 0

            def next(self):
                k = self.i % len(self.t)
                self.i += 1
                return self.t[k], (self.name, k)

        ps_t = [st.enter_context(nc.psum_tensor(f"ps{i}", [128, 512], F32)) for i in range(NPS)]
        psb_t = st.enter_context(nc.psum_tensor("psb", [128, 1024], BF16))
        ps_i = [0]

        def PS():
            k = ps_i[0] % NPS
            ps_i[0] += 1
            return ps_t[k], ("ps", k)

        PSH = PS

        ident = T([128, 128], F32, "ident")
        identb = T([128, 128], BF16, "identb")
        onesb = T([128, 128], BF16, "onesb")
        onesf = T([128, 128], F32, "onesf")
        Uf = T([128, 128], F32, "Uf")
        SLf = T([128, 128], F32, "SLf")
        Ub = T([128, 128], F32, "Ub")
        SLb = T([128, 128], F32, "SLb")
        MEMSET(onesf[:], 1.0, ["onesf"])
        MEMSET(onesb[:], 1.0, ["onesb"])

        def SEL(t, key, cm, pat, op):
            MEMSET(t[:], 1.0, [key])
            P.op("pool", lambda e: e.affine_select(out=t[:], in_=t[:], pattern=[[pat, 128]], compare_op=op,
                                                   fill=0.0, base=0, channel_multiplier=cm), [key], [key])
        SEL(Uf, "Uf", -1, 1, ALU.is_ge)
        SEL(SLf, "SLf", 1, -1, ALU.is_gt)
        SEL(Ub, "Ub", 1, -1, ALU.is_ge)
        SEL(SLb, "SLb", -1, 1, ALU.is_gt)
        MEMSET(ident[:], 0.0, ["ident"])
        P.op("pool", lambda e: e.affine_select(out=ident[:], in_=ident[:], pattern=[[-1, 128]], compare_op=ALU.not_equal,
                                               fill=1.0, base=0, channel_multiplier=1), ["ident"], ["ident"])
        CP(identb[:], ident[:], ["ident"], ["identb"])

        def LOADP(dram, shape, name):
            t = T(shape, F32, name)
            DMA(t[:], dram, (), [name])
            return t
        cvec = LOADP(cvec_d, [128, 8, 2], "cvec")
        bmod = LOADP(bmod_d, [128, DEPTH, 24], "bmod")
        gpre = LOADP(gpre_d, [128, DEPTH, 8], "gpre")
        gpost = LOADP(gpost_d, [128, DEPTH, 8], "gpost")
        cw5 = LOADP(cw5_d, [128, DEPTH, 12, 5], "cw5")
        cb5 = LOADP(cb5_d, [128, DEPTH, 12], "cb5")
        alog = LOADP(alog_d, [128, DEPTH, 32], "alog")
        dtb = LOADP(dtb_d, [128, DEPTH, 32], "dtb")
        dsk = LOADP(dsk_d, [128, DEPTH, 16], "dsk")
        sng = LOADP(sng_d, [128, DEPTH, 8], "sng")
        cw31 = LOADP(cw31_d, [128, DEPTH, 8, 31], "cw31")
        cb31 = LOADP(cb31_d, [128, DEPTH, 8], "cb31")
        lng = LOADP(lng_d, [128, DEPTH, 8], "lng")
        lnb = LOADP(lnb_d, [128, DEPTH, 8], "lnb")
        cb5row = T([1, DEPTH * 1536], BF16, "cb5row")
        for l_ in range(DEPTH):
            DMA(cb5row[:, l_ * 1536:(l_ + 1) * 1536], cb5row_d[:, l_ * 1536:(l_ + 1) * 1536], (), ["cb5row"], eng="pool")
        aneg = T([128, DEPTH, 32], F32, "aneg")
        ACT(aneg[:], alog[:], AF.Exp, ["alog"], ["aneg"])
        TS(aneg[:], aneg[:], -1.0, ALU.mult, ["aneg"], ["aneg"])

        silc = T([128, 8, 2], F32, "silc")
        ACT(silc[:], cvec[:], AF.Silu, ["cvec"], ["silc"])
        modA = T([128, DEPTH, 8, 2], F32, "modA")
        modB = T([128, DEPTH, 8, 2], F32, "modB")
        modG = T([128, DEPTH, 8, 2], F32, "modG")
        hT = T([128, 8, TMAX], BF16, "hT")
        yT = T([128, 16, TMAX], BF16, "yT")
        ms = ExitStack()
        st.callback(ms.close)
        if True:
            wm = Rot("wm", [128, 8, 512], F32, 2, ms)
            modsb = T([128, 24, 2], F32, "modsb", ms)
            modrow = T([2, 3 * D], F32, "modrow", ms)
            for l in range(DEPTH):
                for cb in range(6):
                    wt, wk = wm.next()
                    DMA(wt[:], wmod_d[l].rearrange("(kc p) c -> p kc c", p=128)[:, :, cb * 512:(cb + 1) * 512], (), [wk])
                    pr, prk = PS()
                    for kc in range(8):
                        MM(pr[0:2, :], silc[:, kc, :], wt[:, kc, :], kc == 0, kc == 7, [wk, "silc"], [prk])
                    CP(modrow[:, cb * 512:(cb + 1) * 512], pr[0:2, :], [prk], [("modrow", cb)])
                pm, pmk = PSH()
                for f in range(24):
                    TR(pm[:, f * 2:f * 2 + 2], modrow[0:2, f * 128:(f + 1) * 128], ident[0:2, 0:2], [("modrow", f // 4), "ident"], [pmk])
                TT(modsb[:], pm[:, 0:48].rearrange("p (f w) -> p f w", w=2),
                   bmod[:, l, :].unsqueeze(2).to_broadcast([128, 24, 2]), ALU.add, [pmk, "bmod"], ["modsb"])
                TS(modA[:, l], modsb[:, 8:16, :], 1.0, ALU.add, ["modsb"], ["modA"])
                TT(modA[:, l], modA[:, l], gpre[:, l, :].unsqueeze(2).to_broadcast([128, 8, 2]), ALU.mult, ["modA", "gpre"], ["modA"])
                CP(modB[:, l], modsb[:, 0:8, :], ["modsb"], ["modB"])
                TT(modG[:, l], modsb[:, 16:24, :], gpost[:, l, :].unsqueeze(2).to_broadcast([128, 8, 2]), ALU.mult,
                   ["modsb", "gpost"], ["modG"])

        if debug:
            DMA(dbg["d_modA"], modA[:], ["modA"], (), final=True)
            DMA(dbg["d_modB"], modB[:], ["modB"], (), final=True)
            DMA(dbg["d_modG"], modG[:], ["modG"], (), final=True)

        def stats_rs(src_sq_fn, nk, rs, rsk, extra_reads, eps=EPS):
            pst, pstk = PS()
            for k in range(nk):
                ap, rd = src_sq_fn(k)
                MM(pst[:], onesb[:], ap, k == 0, k == nk - 1, ["onesb"] + rd, [pstk])
            ACT(rs[:], pst[:], AF.Ln, [pstk], [rsk], bias=eps, scale=1.0 / 1024.0)
            ACT(rs[:], rs[:], AF.Exp, [rsk], [rsk], scale=-0.5)

        ms_holder = [ms]

        def run_block(l, x_src, x_dst, nseq, L, stride, wsel, h0, ns_out, final_out, do_front, fuse_next):
            Ttok = nseq * L
            NT = Ttok // 512
            nch = L // 128
            nblk = nseq * nch
            xsrc_v = x_src.rearrange("(kc p) t -> p kc t", p=128)
            xdst_v = x_dst.rearrange("(kc p) t -> p kc t", p=128)
            hk = lambda t: ("hT", t)
            yk = lambda k, b: ("yT", k, b)
            ytile = lambda ks, t: [yk(k, b) for k in ks for b in range(4 * t, 4 * t + 4)]

            def segs(t):
                if L >= 512:
                    per = L // 512
                    return [(t // per, (t % per) * 512, 512, 0)]
                n = 512 // L
                return [(t * n + i, 0, L, i * L) for i in range(n)]

            def win_load(dst, c0, w, key):
                DMA(dst, win_d[l].rearrange("(kc p) c -> p kc c", p=128)[:, :, c0:c0 + w], (), [key], eng="pool")

            with ExitStack() as s0:
                xt_r = Rot("xt", [128, 8, 512], F32, 2, s0)
                sq_r = Rot("sq0", [128, 8, 512], BF16, 2, s0)
                rs_r = Rot("rs0", [128, 512], F32, 2, s0)
                for t in range(NT if do_front else 0):
                    xt, xtk = xt_r.next()
                    sq, sqk = sq_r.next()
                    rs, rsk = rs_r.next()
                    DMA(xt[:], xsrc_v[:, :, t * 512:(t + 1) * 512], [("xd", id(x_src), t)], [xtk])
                    ACT(sq[:], xt[:], AF.Square, [xtk], [sqk])
                    stats_rs(lambda k: (sq[:, k, :], [sqk]), 8, rs, rsk, [])
                    TT(xt[:], xt[:], rs[:].unsqueeze(1).to_broadcast([128, 8, 512]), ALU.mult, [xtk, rsk], [xtk])
                    for kc in range(8):
                        ACT(hT[:, kc, t * 512:(t + 1) * 512], xt[:, kc, :], AF.Identity, [xtk, "modA", "modB"], [hk(t)],
                            bias=modB[:, l, kc, wsel:wsel + 1], scale=modA[:, l, kc, wsel:wsel + 1], group=("hTf", l, t, nseq))

            if ms_holder:
                ms_holder.pop().close()
            P.barrier()
            nm = "P" if nseq == 2 else "S"
            allk = [yk(k, b) for k in range(16) for b in range(nblk)]
            if debug and l == 0:
                DMA(dbg[f"d_hT_{nm}"], hT[:, :, 0:Ttok], [hk(t) for t in range(NT)], (), final=True)
            with ExitStack() as sa:
                wx = T([128, 8, WP], BF16, "wx", sa)
                wB = T([128, 8, 128], BF16, "wB", sa)
                wC = T([128, 8, 128], BF16, "wC", sa)
                wz = T([128, 8, WP], BF16, "wz", sa)
                wdt = T([128, 8, 32], BF16, "wdt", sa)
                upad_r = Rot("upad", [128, nseq, L + 4], BF16, 2, sa)
                diag5_r = Rot("diag5", [128, 5, 128], BF16, 2, sa)
                xg = T([128, nblk, WP], BF16, "xg", sa)
                Btm = T([128, nblk, 128], BF16, "Btm", sa)
                Bfm = T([128, Ttok], BF16, "Bfm", sa)
                Cfm = T([128, Ttok], BF16, "Cfm", sa)
                dt_all = T([128, nblk, 32], F32, "dt_all", sa)
                la_all = T([128, nblk, 32], F32, "la_all", sa)
                v_all = T([128, nblk, 32], F32, "v_all", sa)
                cum_sb = [T([128, nblk, HP], F32, f"cum{d}", sa) for d in range(2)]
                cum_hi = [T([128, nblk, HP], BF16, f"cumhi{d}", sa) for d in range(2)]
                cum_lo = [T([128, nblk, HP], BF16, f"cumlo{d}", sa) for d in range(2)]
                decs_all = [T([128, 3, nblk, HP], F32, f"decs{d}", sa) for d in range(2)]
                diagD = T([128, HP, 128], BF16, "diagD", sa)
                ypark = T([128, nblk, WP], F32, "ypark", sa)
                Sf = [T([128, WP], F32, f"Sf{d}", sa) for d in range(2)]
                Sb = [T([128, WP], BF16, f"Sb{d}", sa) for d in range(2)]
                cbm_r = Rot("cbm", [128, 128], BF16, NROT, sa)
                xdt_r = Rot("xdt", [128, HP, 64], BF16, NROT, sa)
                xs2_r = Rot("xs2", [128, HP, 64], BF16, NROT, sa)
                Lh_r = Rot("Lh", [128, HP, 128], BF16, NROT, sa)
                La_r = Rot("La", [128, HP, 128], F32, 2, sa)
                Mh_r = Rot("Mh", [128, HP, 128], BF16, NROT, sa)
                t1_r = Rot("t1", [128, HP, 64], F32, NROT, sa)
                zs_r = Rot("zs", [128, WP], F32, 2, sa)
                yg_r = Rot("yg", [128, WP], BF16, 2, sa)
                stg_r = Rot("stg", [128, 128], F32, 2, sa)
                for r in upad_r.t:
                    MEMSET(r[:], 0.0, [("upad", upad_r.t.index(r))])

                win_load(wdt[:], I_DT, 32, "wdt")
                for bi in range(nblk):
                    t = bi // 4
                    pd, pdk = PSH()
                    for kc in range(8):
                        MM(pd[:, 0:32], hT[:, kc, bi * 128:(bi + 1) * 128], wdt[:, kc, :], kc == 0, kc == 7, [hk(t), "wdt"], [pdk])
                    TT(v_all[:, bi, :], pd[:, 0:32], dtb[:, l, :], ALU.add, [pdk, "dtb"], ["v_all"])
                TS(dt_all[:], v_all[:], 30.0, ALU.min, ["v_all"], ["dt_all"])
                ACT(dt_all[:], dt_all[:], AF.Exp, ["dt_all"], ["dt_all"])
                ACT(dt_all[:], dt_all[:], AF.Ln, ["dt_all"], ["dt_all"], bias=1.0)
                TT(dt_all[:], dt_all[:], v_all[:], ALU.max, ["dt_all", "v_all"], ["dt_all"])
                TT(la_all[:], dt_all[:], aneg[:, l, :].unsqueeze(1).to_broadcast([128, nblk, 32]), ALU.mult, ["dt_all", "aneg"], ["la_all"])
                for q in range(16 // HP):
                    g = (q * HP) // 8
                    nb4 = nblk * HP
                    win_load(wx[:], I_X + q * WP, WP, "wx")
                    if (q * HP) % 8 == 0:
                        win_load(wB[:], I_B + g * 128, 128, "wB")
                        win_load(wC[:], I_C + g * 128, 128, "wC")
                    win_load(wz[:], I_Z + q * WP, WP, "wz")
                    nxc = WP // 128
                    chunks = [("x", a, q * nxc + a, wx, a * 128, "wx") for a in range(nxc)]
                    if (q * HP) % 8 == 0:
                        chunks += [("B", 0, 8 + g, wB, 0, "wB"), ("C", 0, 10 + g, wC, 0, "wC")]
                    for kind, a, cidx, wt, wc0, wk in chunks:
                        upad, upk = upad_r.next()
                        dg, dgk = diag5_r.next()
                        TT(dg[:], ident[:].unsqueeze(1).to_broadcast([128, 5, 128]),
                           cw5[:, l, cidx, :].unsqueeze(2).to_broadcast([128, 5, 128]), ALU.mult, ["ident", "cw5"], [dgk])
                        for t in range(NT):
                            pu, puk = PS()
                            for kc in range(8):
                                MM(pu[:], wt[:, kc, wc0:wc0 + 128], hT[:, kc, t * 512:(t + 1) * 512], kc == 0, kc == 7,
                                   [wk, hk(t)], [puk])
                            if L >= 512:
                                s_, off, n_, c0 = segs(t)[0]
                                P.op("act", (lambda e, o=upad[:, s_, 2 + off:2 + off + 512], i=pu[:]: e.copy(out=o, in_=i)),
                                     [puk], [upk], cost=590.0)
                            else:
                                n = 512 // L
                                P.op("act", (lambda e, o=upad[:, t * n:(t + 1) * n, 2:2 + L],
                                             i=pu[:].rearrange("p (s x) -> p s x", s=n): e.copy(out=o, in_=i)), [puk], [upk], cost=590.0)
                        if kind in ("x", "B"):
                            for s_ in range(nseq):
                                for j in range(nch):
                                    bi = s_ * nch + j
                                    pc, pck = PSH()
                                    for k in range(5):
                                        MM(pc[:, 0:128], upad[:, s_, j * 128 + k:j * 128 + k + 128], dg[:, k, :], k == 0, False,
                                           [upk, dgk], [pck])
                                    MM(pc[:, 0:128], onesb[0:1, 0:128], cb5row[0:1, l * 1536 + cidx * 128:l * 1536 + (cidx + 1) * 128],
                                       False, True, ["onesb", "cb5row"], [pck])
                                    if kind == "x":
                                        ACT(xg[:, bi, a * 128:(a + 1) * 128], pc[:, 0:128], AF.Silu, [pck], [("xg", bi)], group=("xg", l, nseq, q, bi))
                                    else:
                                        ACT(Btm[:, bi, :], pc[:, 0:128], AF.Silu, [pck], [("Btm", bi)])
                        if kind in ("B", "C"):
                            dstT, dkey = (Bfm, "Bfm") if kind == "B" else (Cfm, "Cfm")
                            for t in range(NT):
                                pc, pck = PS()
                                for (s_, off, n_, c0) in segs(t):
                                    for k in range(5):
                                        MM(pc[:, c0:c0 + n_], dg[:, k, :], upad[:, s_, off + k:off + k + n_], k == 0, k == 4,
                                           [upk, dgk], [pck])
                                ACT(dstT[:, t * 512:(t + 1) * 512], pc[:], AF.Silu, [pck, "cb5"], [(dkey, t)],
                                    bias=cb5[:, l, cidx:cidx + 1])
                    TT(diagD[:], ident[:].unsqueeze(1).to_broadcast([128, HP, 128]),
                       dsk[:, l, q * HP:(q + 1) * HP].unsqueeze(2).to_broadcast([128, HP, 128]), ALU.mult, ["ident", "dsk"], ["diagD"])
                    for d in (1, 0):
                        Uin, UinK = (Uf, "Uf") if d == 0 else (Ub, "Ub")
                        SLo, SLoK = (SLf, "SLf") if d == 0 else (SLb, "SLb")
                        pdc, pdck = PSH()
                        la_d = la_all[:, :, d * 16 + q * HP:d * 16 + (q + 1) * HP]
                        for ci, (mt, mk) in enumerate(((Uin, UinK), (SLo, SLoK), (onesf, "onesf"))):
                            MM(pdc[:, ci * nb4:(ci + 1) * nb4].rearrange("p (b h) -> p b h", h=HP), mt[:], la_d, True, True,
                               [mk, "la_all"], [pdck])
                        dcs = decs_all[d]
                        dcsk = ("decs", d)
                        ACT(dcs[:].rearrange("p c b h -> p (c b h)"), pdc[:, 0:3 * nb4], AF.Exp, [pdck], [dcsk])
                        cum = cum_sb[d]
                        cumk = ("cum", d)
                        CP(cum[:].rearrange("p b h -> p (b h)"), pdc[:, 0:nb4], [pdck, dcsk], [cumk])
                        chi, clo = cum_hi[d], cum_lo[d]
                        CP(chi[:], cum[:], [cumk], [("chi", d)])
                        TT(clo[:], cum[:], chi[:], ALU.subtract, [cumk, ("chi", d)], [("clo", d)])
                        TT(cum[:], chi[:], clo[:], ALU.add, [("chi", d), ("clo", d), cumk], [cumk])
                        for s_ in range(nseq):
                            if h0 is None:
                                MEMSET(Sf[d][:], 0.0, [("Sf", d)], eng="dve")
                                MEMSET(Sb[d][:], 0.0, [("Sb", d)], eng="dve")
                            else:
                                for a in range(WP // 128):
                                    sg, sgk = stg_r.next()
                                    DMA(sg[:], h0[l, d, q * WP + a * 128:q * WP + (a + 1) * 128, :], (), [sgk])
                                    pt, ptk = PSH()
                                    TR(pt[:, 0:128], sg[:], ident[:], [sgk, "ident"], [ptk])
                                    CP(Sf[d][:, a * 128:(a + 1) * 128], pt[:, 0:128], [ptk], [("Sf", d)])
                                CP(Sb[d][:], Sf[d][:], [("Sf", d)], [("Sb", d)])
                            order = range(nch) if d == 0 else range(nch - 1, -1, -1)
                            for j in order:
                                bi = s_ * nch + j
                                t = bi // 4
                                tok = slice(bi * 128, (bi + 1) * 128)
                                la_b = la_all[:, bi, d * 16 + q * HP:d * 16 + (q + 1) * HP]
                                dt_b = dt_all[:, bi, d * 16 + q * HP:d * 16 + (q + 1) * HP]
                                pcb, pcbk = PSH()
                                MM(pcb[:, 0:128], Bfm[:, tok], Cfm[:, tok], True, True, [("Bfm", t), ("Cfm", t)], [pcbk])
                                cbm, cbmk = cbm_r.next()
                                TT(cbm[:], pcb[:, 0:128], Uin[:], ALU.mult, [pcbk, UinK], [cbmk])
                                xdt, xdtk = xdt_r.next()
                                xs2, xs2k = xs2_r.next()
                                xg_b = xg[:, bi, :].rearrange("p (h c) -> p h c", h=HP)
                                TT(xdt[:], xg_b, dt_b.unsqueeze(2).to_broadcast([128, HP, 64]), ALU.mult, [("xg", bi), "dt_all"], [xdtk], eng=OFF_ENG)
                                TT(xs2[:], xdt[:], dcs[:, 1, bi, :].unsqueeze(2).to_broadcast([128, HP, 64]), ALU.mult,
                                   [xdtk, dcsk], [xs2k], eng=OFF_ENG)
                                parg, pargk = PS()
                                for h in range(HP):
                                    po_ = parg[:, h * 128:(h + 1) * 128]
                                    MM(po_, chi[:, bi, h:h + 1].to_broadcast([128, 128]), identb[:], True, False, [("chi", d), "identb"], [pargk])
                                    MM(po_, clo[:, bi, h:h + 1].to_broadcast([128, 128]), identb[:], False, True, [("clo", d), "identb"], [pargk])
                                La, Lak = La_r.next()
                                for h in range(HP):
                                    ACT(La[:, h, :], parg[:, h * 128:(h + 1) * 128], AF.Relu, [pargk, cumk], [Lak],
                                        bias=cum[:, bi, h:h + 1], scale=-1.0, group=("relu", l, nseq, q, d, bi))
                                Lh, Lhk = Lh_r.next()
                                ACT(Lh[:].rearrange("p h c -> p (h c)"), La[:].rearrange("p h c -> p (h c)"), AF.Exp, [Lak], [Lhk], scale=-1.0)
                                Mh, Mhk = Mh_r.next()
                                TT(Mh[:], Lh[:], cbm[:].unsqueeze(1).to_broadcast([128, HP, 128]), ALU.mult, [Lhk, cbmk], [Mhk])
                                py, pyk = PSH()
                                for h in range(HP):
                                    MM(py[:, h * 64:(h + 1) * 64], Mh[:, h, :], xdt[:, h, :], True, d == 1, [Mhk, xdtk], [pyk])
                                    if d == 0:
                                        MM(py[:, h * 64:(h + 1) * 64], diagD[:, h, :], xg[:, bi, h * 64:(h + 1) * 64], False, True,
                                           ["diagD", ("xg", bi)], [pyk])
                                po, pok = PSH()
                                MM(po[:, 0:WP], Cfm[:, tok], Sb[d][:], True, True, [("Cfm", t), ("Sb", d)], [pok])
                                t1, t1k = t1_r.next()
                                TT(t1[:], po[:, 0:WP].rearrange("p (h c) -> p h c", h=HP),
                                   dcs[:, 0, bi, :].unsqueeze(2).to_broadcast([128, HP, 64]), ALU.mult, [pok, dcsk], [t1k])
                                t1f = t1[:].rearrange("p h c -> p (h c)")
                                if d == 1:
                                    TT(ypark[:, bi, :], t1f, py[:, 0:WP], ALU.add, [t1k, pyk], [("ypark", bi)])
                                else:
                                    TT(t1f, t1f, py[:, 0:WP], ALU.add, [t1k, pyk], [t1k])
                                    TT(t1f, t1f, ypark[:, bi, :], ALU.add, [t1k, ("ypark", bi)], [t1k], eng=OFF_ENG)
                                    yg, ygk = yg_r.next()
                                    zs, zsk = zs_r.next()
                                    pz, pzk = PSH()
                                    for kc in range(8):
                                        MM(pz[:, 0:WP], hT[:, kc, tok], wz[:, kc, :], kc == 0, kc == 7, [hk(t), "wz"], [pzk])
                                    ACT(zs[:], pz[:, 0:WP], AF.Tanh, [pzk], [zsk], scale=0.5)
                                    STT(zs[:], zs[:], 1.0, pz[:, 0:WP], ALU.add, ALU.mult, [zsk, pzk], [zsk])
                                    TT(yg[:], t1f, zs[:], ALU.mult, [t1k, zsk], [ygk])
                                    for a in range(WP // 128):
                                        TR(psb_t[:, a * 128:(a + 1) * 128], yg[:, a * 128:(a + 1) * 128], identb[:], [ygk, "identb"], ["psb"])
                                    kc0 = q * (WP // 128)
                                    P.op("act", (lambda e, o=yT[:, kc0:kc0 + WP // 128, tok],
                                                 i=psb_t[:, 0:WP].rearrange("p (a c) -> p a c", c=128): e.copy(out=o, in_=i)),
                                         ["psb"], [yk(kc0 + a, bi) for a in range(WP // 128)], cost=400.0)
                                pds, pdsk = PSH()
                                MM(pds[:, 0:WP], Btm[:, bi, :], xs2[:].rearrange("p h c -> p (h c)"), True, True, [("Btm", bi), xs2k], [pdsk])
                                Sf3 = Sf[d][:].rearrange("p (h c) -> p h c", h=HP)
                                TT(Sf3, Sf3, dcs[:, 2, bi, :].unsqueeze(2).to_broadcast([128, HP, 64]), ALU.mult,
                                   [("Sf", d), dcsk], [("Sf", d)], eng=OFF_ENG)
                                TT(Sf[d][:], Sf[d][:], pds[:, 0:WP], ALU.add, [("Sf", d), pdsk], [("Sf", d)])
                                P.op("act", (lambda e, o=Sb[d][:], i=Sf[d][:]: e.copy(out=o, in_=i)), [("Sf", d)], [("Sb", d)], cost=400.0)
                            if ns_out is not None:
                                for a in range(WP // 128):
                                    pt, ptk = PSH()
                                    TR(pt[:, 0:128], Sf[d][:, a * 128:(a + 1) * 128], ident[:], [("Sf", d), "ident"], [ptk])
                                    sg, sgk = stg_r.next()
                                    CP(sg[:], pt[:, 0:128], [ptk], [sgk])
                                    DMA(ns_out[s_, l, d, q * WP + a * 128:q * WP + (a + 1) * 128, :], sg[:], [sgk], (), final=True)

            P.barrier()
            if debug and l == 0:
                DMA(dbg[f"d_yA_{nm}"], yT[:, :, 0:Ttok], allk, (), final=True)
            def ssd_norm(sn):
                sq_r = Rot("sqn", [128, 8, 512], BF16, 1, sn)
                rs_r = Rot("rsn", [128, 512], F32, 2, sn)
                for t in range(NT):
                    sq, sqk = sq_r.next()
                    rs, rsk = rs_r.next()
                    tl = slice(t * 512, (t + 1) * 512)
                    ACT(sq[:], yT[:, 0:8, tl], AF.Square, ytile(range(8), t), [sqk])
                    stats_rs(lambda k: (sq[:, k, :], [sqk]), 8, rs, rsk, [], eps=4.0 * EPS)
                    for k in range(8):
                        STT(yT[:, k, tl], yT[:, k, tl], sng[:, l, k:k + 1], rs[:], ALU.mult, ALU.mult,
                            ytile([k], t) + ["sng", rsk], ytile([k], t))

            pad = 15 * stride
            with ExitStack() as sb_:
                ssd_norm(sb_)
                if debug and l == 0:
                    DMA(dbg[f"d_yB_{nm}"], yT[:, :, 0:Ttok], allk, (), final=True)
                wga = T([128, 8, 1024], BF16, "wga", sb_)
                wgb = T([128, 8, 1024], BF16, "wgb", sb_)
                for j in range(8):
                    win_load(wga[:, :, j * 128:(j + 1) * 128], I_GA + j * 128, 128, ("wga", j))
                    win_load(wgb[:, :, j * 128:(j + 1) * 128], I_GB + j * 128, 128, ("wgb", j))
                hc_r = Rot("hc", [128, nseq, L + 2 * pad], BF16, 2, sb_)
                d31_r = Rot("d31", [128, 31, 128], BF16, 2, sb_)
                sig_r = Rot("sig", [128, 512], F32, 2, sb_)
                accd_r = Rot("accd", [128, 512], F32, 2, sb_)
                accp_r = Rot("accp", [128, 512], F32, 2, sb_)
                accbd_r = Rot("accbd", [128, 512], BF16, 2, sb_)
                accbp_r = Rot("accbp", [128, 512], BF16, 2, sb_)
                for r in hc_r.t:
                    MEMSET(r[:], 0.0, [("hc", hc_r.t.index(r))])
                for j in range(8):
                    hc, hck = hc_r.next()
                    dg, dgk = d31_r.next()
                    TT(dg[:], ident[:].unsqueeze(1).to_broadcast([128, 31, 128]),
                       cw31[:, l, j, :].unsqueeze(2).to_broadcast([128, 31, 128]), ALU.mult, ["ident", "cw31"], [dgk])
                    for t in range(NT):
                        pa, pak = PS()
                        pb, pbk = PS()
                        for kc in range(8):
                            MM(pa[:], wga[:, kc, j * 128:(j + 1) * 128], hT[:, kc, t * 512:(t + 1) * 512], kc == 0, kc == 7, [("wga", j), hk(t)], [pak])
                        for kc in range(8):
                            MM(pb[:], wgb[:, kc, j * 128:(j + 1) * 128], hT[:, kc, t * 512:(t + 1) * 512], kc == 0, kc == 7, [("wgb", j), hk(t)], [pbk])
                        sig, sigk = sig_r.next()
                        ACT(sig[:], pb[:], AF.Sigmoid, [pbk], [sigk])
                        if L >= 512:
                            s_, off, n_, c0 = segs(t)[0]
                            TT(hc[:, s_, pad + off:pad + off + 512], pa[:], sig[:], ALU.mult, [pak, sigk], [hck])
                        else:
                            n = 512 // L
                            TT(hc[:, t * n:(t + 1) * n, pad:pad + L], pa[:].rearrange("p (s x) -> p s x", s=n),
                               sig[:].rearrange("p (s x) -> p s x", s=n), ALU.mult, [pak, sigk], [hck])
                    for t in range(NT):
                        pc, pck = PS()
                        for (s_, off, n_, c0) in segs(t):
                            taps = [k for k in range(31) if off + (k - 15) * stride + n_ > 0 and off + (k - 15) * stride < L]
                            win = lambda k: hc[:, s_, pad + off + (k - 15) * stride:pad + off + (k - 15) * stride + n_]
                            wk_ = lambda k: cw31[:, l, j, k:k + 1]
                            extra = []
                            rest = list(taps)
                            for eng_, ntap, acc_r, accb_r in (("dve", CONV_ND, accd_r, accbd_r), ("pool", CONV_NP, accp_r, accbp_r)):
                                if ntap == 0 or len(rest) - ntap < 4:
                                    continue
                                mine, rest = rest[:ntap], rest[ntap:]
                                acc, acck = acc_r.next()
                                accb, accbk = accb_r.next()
                                c_ = (120.0 + n_ / 0.96) if eng_ == "dve" else (200.0 + n_ / 0.55)
                                for ii, k in enumerate(mine):
                                    last_ = ii == len(mine) - 1
                                    dst, dstk = (accb, accbk) if last_ else (acc, acck)
                                    if ii == 0:
                                        P.op(eng_, (lambda e, o=dst[:, 0:n_], i0=win(k), sc=wk_(k): e.tensor_scalar(
                                            out=o, in0=i0, scalar1=sc, scalar2=None, op0=ALU.mult)), [hck, "cw31"], [dstk], cost=c_)
                                    else:
                                        P.op(eng_, (lambda e, o=dst[:, 0:n_], i0=win(k), sc=wk_(k), i1=acc[:, 0:n_]: e.scalar_tensor_tensor(
                                            out=o, in0=i0, scalar=sc, in1=i1, op0=ALU.mult, op1=ALU.add)), [hck, "cw31", acck], [dstk], cost=c_)
                                extra.append((accb, accbk))
                            nmm = len(rest) + len(extra)
                            im = 0
                            for k in rest:
                                MM(pc[:, c0:c0 + n_], dg[:, k, :], win(k), im == 0, im == nmm - 1, [hck, dgk], [pck])
                                im += 1
                            for accb, accbk in extra:
                                MM(pc[:, c0:c0 + n_], identb[:], accb[:, 0:n_], im == 0, im == nmm - 1, ["identb", accbk], [pck])
                                im += 1
                        ACT(yT[:, 8 + j, t * 512:(t + 1) * 512], pc[:], AF.Identity, [pck, "cb31"], ytile([8 + j], t),
                            bias=cb31[:, l, j:j + 1])
            P.barrier()
            with ExitStack() as sb_:
                wgs = T([128, 8, 1024], BF16, "wgs", sb_)
                for j in range(8):
                    win_load(wgs[:, :, j * 128:(j + 1) * 128], I_GS + j * 128, 128, ("wgs", j))
                sq_r = Rot("sqc", [128, 8, 512], BF16, 2, sb_)
                mean_r = Rot("mean", [128, 512], F32, 2, sb_)
                rs_r = Rot("rsc", [128, 512], F32, 2, sb_)
                tmp_r = Rot("tmpc", [128, 512], F32, 2, sb_)
                s1_r = Rot("s1c", [128, 512], F32, 2, sb_)
                for t in range(NT):
                    tl = slice(t * 512, (t + 1) * 512)
                    sq, sqk = sq_r.next()
                    mean, meank = mean_r.next()
                    rs, rsk = rs_r.next()
                    p1, p1k = PS()
                    for k in range(8):
                        MM(p1[:], onesb[:], yT[:, 8 + k, tl], k == 0, k == 7, ["onesb"] + ytile([8 + k], t), [p1k])
                    ACT(sq[:], yT[:, 8:16, tl], AF.Square, ytile(range(8, 16), t), [sqk])
                    p2, p2k = PS()
                    for k in range(8):
                        MM(p2[:], onesb[:], sq[:, k, :], k == 0, k == 7, ["onesb", sqk], [p2k])
                    TS(mean[:], p1[:], 1.0 / 1024.0, ALU.mult, [p1k], [meank])
                    tmp, tmpk = tmp_r.next()
                    TT(tmp[:], mean[:], mean[:], ALU.mult, [meank], [tmpk])
                    STT(tmp[:], p2[:], 1.0 / 1024.0, tmp[:], ALU.mult, ALU.subtract, [p2k, tmpk], [tmpk])
                    ACT(rs[:], tmp[:], AF.Ln, [tmpk], [rsk], bias=EPS)
                    ACT(rs[:], rs[:], AF.Exp, [rsk], [rsk], scale=-0.5)
                    for j in range(8):
                        tmp, tmpk = tmp_r.next()
                        s1, s1k = s1_r.next()
                        TT(tmp[:], yT[:, 8 + j, tl], mean[:], ALU.subtract, ytile([8 + j], t) + [meank], [tmpk])
                        TT(tmp[:], tmp[:], rs[:], ALU.mult, [tmpk, rsk], [tmpk])
                        ACT(s1[:], tmp[:], AF.Silu, [tmpk, "lng", "lnb"], [s1k], bias=lnb[:, l, j:j + 1], scale=lng[:, l, j:j + 1])
                        pg, pgk = PS()
                        for kc in range(8):
                            MM(pg[:], wgs[:, kc, j * 128:(j + 1) * 128], hT[:, kc, tl], kc == 0, kc == 7, [("wgs", j), hk(t)], [pgk])
                        ACT(tmp[:], pg[:], AF.Silu, [pgk], [tmpk])
                        TT(yT[:, 8 + j, tl], s1[:], tmp[:], ALU.mult, [s1k, tmpk], ytile([8 + j], t))

            P.barrier()
            if debug and l == 0:
                DMA(dbg[f"d_yC_{nm}"], yT[:, :, 0:Ttok], allk, (), final=True)
            with ExitStack() as sc:
                wo = T([128, 16, 1024], BF16, "wo", sc)
                for fo in range(8):
                    DMA(wo[:, :, fo * 128:(fo + 1) * 128], wout_d[l].rearrange("(kc p) c -> p kc c", p=128)[:, :, fo * 128:(fo + 1) * 128],
                        (), [("wo", fo)], eng="pool")
                osb_r = Rot("osb", [128, 8, 512], F32, 2, sc)
                sq_r = Rot("sqo", [128, 8, 512], BF16, 1, sc)
                rs_r = Rot("rso", [128, 512], F32, 2, sc)
                xt_r = Rot("xto", [128, 8, 512], F32, 1, sc)
                for t in range(NT):
                    tl = slice(t * 512, (t + 1) * 512)
                    osb, osbk = osb_r.next()
                    sq, sqk = sq_r.next()
                    rs, rsk = rs_r.next()
                    xt, xtk = xt_r.next()
                    DMA(xt[:], xsrc_v[:, :, tl], [("xd", id(x_src), t)], [xtk])
                    for fo in range(8):
                        po, pok = PS()
                        for kc in range(16):
                            MM(po[:], wo[:, kc, fo * 128:(fo + 1) * 128], yT[:, kc, tl], kc == 0, kc == 15,
                               [("wo", fo)] + ytile([kc], t), [pok])
                        P.op("act", (lambda e, o=osb[:, fo, :], i=po[:]: e.copy(out=o, in_=i)), [pok], [(osbk, fo)], cost=590.0)
                        ACT(sq[:, fo, :], po[:], AF.Square, [pok], [(sqk, fo)])
                    stats_rs(lambda k: (sq[:, k, :], [(sqk, k)]), 8, rs, rsk, [])
                    for fo in range(8):
                        TT(osb[:, fo, :], osb[:, fo, :], rs[:], ALU.mult, [(osbk, fo), rsk], [(osbk, fo)])
                        STT(osb[:, fo, :], osb[:, fo, :], modG[:, l, fo, wsel:wsel + 1], xt[:, fo, :], ALU.mult, ALU.add,
                            [(osbk, fo), "modG", xtk], [(osbk, fo)])
                    allosb = [(osbk, k) for k in range(8)]
                    DMA(xdst_v[:, :, tl], osb[:], allosb, [("xd", id(x_dst), t)], final=final_out)
                    if fuse_next:
                        allsq = [(sqk, k) for k in range(8)]
                        ACT(sq[:], osb[:], AF.Square, allosb, allsq)
                        rs2, rs2k = rs_r.next()
                        stats_rs(lambda k: (sq[:, k, :], [(sqk, k)]), 8, rs2, rs2k, [])
                        TT(xt[:], osb[:], rs2[:].unsqueeze(1).to_broadcast([128, 8, 512]), ALU.mult, allosb + [rs2k], [xtk])
                        for kc in range(8):
                            ACT(hT[:, kc, tl], xt[:, kc, :], AF.Identity, [xtk, "modA", "modB"], [hk(t)],
                                bias=modB[:, l + 1, kc, wsel:wsel + 1], scale=modA[:, l + 1, kc, wsel:wsel + 1],
                                group=("hTn", l, t, nseq))
            P.barrier()

        for nm_ in ("P", "S"):
            for l in range(DEPTH):
                last = (l == DEPTH - 1)
                if only is not None and (l, nm_) not in only:
                    continue
                if nm_ == "P":
                    run_block(l, xp_d if l == 0 else x1p_d, yp_d if last else x1p_d, 2, 256, 1, 0, None, ns_d, last or debug,
                              l == 0 or debug, (not last) and not debug)
                else:
                    run_block(l, xs_d if l == 0 else x1s_d, ys_d if last else x1s_d, 1, 2048, 64, 1, h0_d, None, last or debug,
                              l == 0 or debug, (not last) and not debug)
        P.emit()
        n_ins = len(P.ins)
    return nc, n_ins


_CACHE = {}


def _fm(v):
    v = np.asarray(v, np.float32)
    lead = v.shape[:-1]
    nchunk = v.shape[-1] // 128
    r = v.reshape(lead + (nchunk, 128))
    return np.ascontiguousarray(np.moveaxis(r, -1, 0))


def kernel(x_prompt, x_sample, state_ssd, c, c_ctx, w_mod, b_mod, g_pre, g_post, w_in,
           ssd_conv_w, ssd_conv_b, ssd_a_log, ssd_dt_bias, ssd_d, ssd_norm_g,
           conf_conv_w, conf_conv_b, conf_ln_g, conf_ln_b, w_out):
    f = lambda a: np.ascontiguousarray(np.asarray(a, np.float32))
    x_prompt, x_sample, state_ssd = f(x_prompt), f(x_sample), f(state_ssd)
    if "nc" not in _CACHE:
        _CACHE["nc"] = build_program()[0]
    nc = _CACHE["nc"]
    rep = lambda a: np.ascontiguousarray(np.broadcast_to(f(a).reshape(1, DEPTH, -1), (128, DEPTH, f(a).reshape(DEPTH, -1).shape[1])))
    shared = {
        "w_mod": f(w_mod), "b_mod": _fm(b_mod), "g_pre": _fm(g_pre), "g_post": _fm(g_post), "w_in": f(w_in),
        "cw5": np.ascontiguousarray(np.transpose(f(ssd_conv_w).reshape(DEPTH, 5, 12, 128), (3, 0, 2, 1))),
        "cb5": _fm(ssd_conv_b), "cb5row": f(ssd_conv_b).reshape(1, DEPTH * 1536),
        "alog": rep(ssd_a_log), "dtb": rep(ssd_dt_bias), "dsk": rep(ssd_d), "sng": _fm(ssd_norm_g),
        "cw31": np.ascontiguousarray(np.transpose(f(conf_conv_w).reshape(DEPTH, 31, 8, 128), (3, 0, 2, 1))),
        "cb31": _fm(conf_conv_b), "lng": _fm(conf_ln_g), "lnb": _fm(conf_ln_b), "w_out": f(w_out),
    }
    in_maps = []
    for core in range(NCORES):
        b = core // 4
        m = dict(shared)
        m["xp"] = np.ascontiguousarray(x_prompt[2 * core:2 * core + 2].reshape(512, D).T)
        m["xs"] = np.ascontiguousarray(x_sample[b].T)
        m["h0"] = np.ascontiguousarray(state_ssd[b].reshape(DEPTH, 2, 1024, 128))
        cv = np.stack([f(c_ctx), f(c)[b]], axis=-1)
        m["cvec"] = np.ascontiguousarray(np.transpose(cv.reshape(8, 128, 2), (1, 0, 2)))
        in_maps.append(m)
    res = run_bass_kernel_spmd(nc, in_maps, core_ids=list(range(NCORES)))
    r = res.results
    y_prompt = np.stack([r[core]["yp"].T.reshape(2, 256, D) for core in range(NCORES)], 0).reshape(16, 256, D)
    y_sample = np.stack([r[0]["ys"].T, r[4]["ys"].T], 0)
    new_state = np.concatenate([r[core]["ns"] for core in range(NCORES)], 0).reshape(16, DEPTH, 2, 16, 64, 128)
    return (np.ascontiguousarray(y_prompt, dtype=np.float32), np.ascontiguousarray(y_sample, dtype=np.float32),
            np.ascontiguousarray(new_state, dtype=np.float32))
```

```python
import numpy as np
from contextlib import ExitStack
import concourse.bass as bass
import concourse.mybir as mybir
from concourse.bass_utils import run_bass_kernel_spmd

F32 = mybir.dt.float32
BF16 = mybir.dt.bfloat16
AF = mybir.ActivationFunctionType
ALU = mybir.AluOpType

D = 1024
DEPTH = 2
NCORES = 8
EPS = 1e-6
I_Z, I_X, I_B, I_C, I_DT, I_GA, I_GB, I_GS = 0, 1024, 2048, 2304, 2560, 2592, 3616, 4640
IN_COLS = 5664
HP = 4
WP = HP * 64
TMAX = 2048
NPS = 7
CONV_ND = 8
CONV_NP = 0
NROT = 3
OFF_ENG = "pool"


class Prog:
    SEM_LIMIT = 4000
    WINDOW = 128
    SEM_LAT = 400.0

    def __init__(self, nc, stack, same_engine_sync=True, schedule=True):
        self.nc = nc
        self.stack = stack
        self.engs = {"pe": nc.tensor, "act": nc.scalar, "dve": nc.vector, "pool": nc.gpsimd, "sp": nc.sync}
        self.ins = []
        self.last_w = {}
        self.readers = {}
        self.same_engine_sync = same_engine_sync
        self.schedule = schedule
        self.n_dma_sems = {"sp": 16, "pool": 8, "act": 4, "dve": 4, "pe": 4}
        self.out_dmas = []
        self.w_rdeps = {}
        self.phase = 0

    def barrier(self):
        self.phase += 1

    def op(self, eng, fn, reads=(), writes=(), dma=False, final=False, cost=300.0, lat=0.0, group=None):
        deps = set()
        for r in reads:
            if r in self.last_w:
                deps |= set(self.last_w[r][1])
        i = len(self.ins)
        for w in writes:
            same = False
            if w in self.last_w:
                gid, members = self.last_w[w]
                same = group is not None and gid == group
                if not same:
                    deps |= set(members)
            if same:
                deps |= self.w_rdeps.get(w, set())
            else:
                rd = set(self.readers.get(w, set()))
                deps |= rd
                self.w_rdeps[w] = (set(self.last_w[w][1]) if w in self.last_w else set()) | rd
        deps.discard(i)
        self.ins.append(dict(eng=eng, fn=fn, deps=deps, dma=dma, cost=cost, lat=lat, phase=self.phase))
        for r in reads:
            self.readers.setdefault(r, set()).add(i)
        for w in writes:
            if w in self.last_w and group is not None and self.last_w[w][0] == group:
                self.last_w[w][1].append(i)
            else:
                self.last_w[w] = (group, [i])
                self.readers[w] = set()
        if final:
            self.out_dmas.append(i)
        return i

    def _order(self):
        ins = self.ins
        n = len(ins)
        per_eng = {e: [] for e in self.engs}
        for i, it in enumerate(ins):
            per_eng[it["eng"]].append(i)
        if not self.schedule:
            return per_eng
        users = [[] for _ in range(n)]
        nun = [0] * n
        for i, it in enumerate(ins):
            nun[i] = len(it["deps"])
            for d in it["deps"]:
                users[d].append(i)
        blev = [0.0] * n
        for i in range(n - 1, -1, -1):
            it = ins[i]
            m = 0.0
            for u in users[i]:
                if ins[u]["phase"] == it["phase"] and blev[u] > m:
                    m = blev[u]
            blev[i] = it["cost"] + it["lat"] + m
        rdy = [0.0] * n
        self.t_start = [0.0] * n
        self.t_fin = [0.0] * n
        nphase = self.phase + 1
        left = [0] * nphase
        for it in ins:
            left[it["phase"]] += 1
        cur = 0
        while cur < nphase and left[cur] == 0:
            cur += 1
        phase_t = 0.0
        tmax = 0.0
        eng_free = {e: 0.0 for e in self.engs}
        pend = {e: list(v) for e, v in per_eng.items()}
        order = {e: [] for e in self.engs}
        remaining = n
        while remaining:
            best = None
            for e, lst in pend.items():
                cand = None
                ef = eng_free[e]
                for i in lst[:self.WINDOW]:
                    it = ins[i]
                    if it["phase"] != cur:
                        break
                    if nun[i]:
                        continue
                    stt = max(rdy[i], ef, phase_t)
                    key = (stt, -blev[i]) if stt > ef + 1e-9 else (ef, -blev[i])
                    if cand is None or key < cand[2]:
                        cand = (key[0], i, key)
                if cand is not None and (best is None or cand[0] < best[0] - 1e-9 or
                                         (abs(cand[0] - best[0]) <= 1e-9 and cand[1] < best[1])):
                    best = (cand[0], cand[1], e)
            assert best is not None, "scheduler stuck"
            stt, i, e = best
            it = ins[i]
            eng_free[e] = stt + it["cost"]
            f = stt + it["cost"] + it["lat"]
            self.t_start[i] = stt
            self.t_fin[i] = f
            tmax = max(tmax, f)
            for u in users[i]:
                nun[u] -= 1
                fl = f if (ins[u]["eng"] == e and e == "pe" and not it["dma"]) else f + self.SEM_LAT
                if fl > rdy[u]:
                    rdy[u] = fl
            pend[e].remove(i)
            order[e].append(i)
            remaining -= 1
            left[cur] -= 1
            if left[cur] == 0:
                while cur < nphase and left[cur] == 0:
                    cur += 1
                phase_t = tmax + 200.0
        self.sim_time = tmax
        return order

    def emit(self):
        nc = self.nc
        ins = self.ins
        n = len(ins)
        order = self._order()
        pos = [0] * n
        for e, lst in order.items():
            for k, i in enumerate(lst):
                pos[i] = k
        last_before = {}
        for e, lst in order.items():
            cuts = {}
            for k, i in enumerate(lst):
                cuts.setdefault(ins[i]["phase"], k)
            last_before[e] = (lst, cuts)
        for e, lst in order.items():
            seen = -1
            for i in lst:
                p = ins[i]["phase"]
                if p == seen:
                    continue
                seen = p
                if p == 0:
                    continue
                extra = set()
                for e2, (lst2, cuts2) in last_before.items():
                    ks = [k for ph, k in cuts2.items() if ph >= p]
                    endk = min(ks) if ks else len(lst2)
                    if endk == 0:
                        continue
                    extra.add(lst2[endk - 1])
                    nd = self.n_dma_sems[e2]
                    cnt = 0
                    for k in range(endk - 1, -1, -1):
                        if ins[lst2[k]]["dma"]:
                            extra.add(lst2[k])
                            cnt += 1
                            if cnt >= nd:
                                break
                extra.discard(i)
                ins[i]["deps"] = set(ins[i]["deps"]) | extra
        pruned = [None] * n
        for i, it in enumerate(ins):
            e = it["eng"]
            keep = {}
            dmas = []
            for d in it["deps"]:
                p = ins[d]
                if p["dma"]:
                    dmas.append(d)
                    continue
                if p["eng"] == e and (e == "pe" or not self.same_engine_sync):
                    continue
                pe_ = p["eng"]
                if pe_ not in keep or pos[d] > pos[keep[pe_]]:
                    keep[pe_] = d
            pruned[i] = list(keep.values()) + dmas
        needed = [False] * n
        for i in range(n):
            for d in pruned[i]:
                needed[d] = True
        for i in self.out_dmas:
            needed[i] = True
        sem_of = [None] * n
        dma_prev = [None] * n
        for e, lst in order.items():
            nd = self.n_dma_sems[e]
            dsems = None
            dcnt = None
            rr = 0
            cur = None
            ccnt = 0
            k = 0
            for i in lst:
                it = ins[i]
                if it["dma"]:
                    if dsems is None:
                        dsems = [self.stack.enter_context(nc.semaphore(f"dq_{e}_{j}")) for j in range(nd)]
                        dcnt = [0] * nd
                    j = rr
                    rr = (rr + 1) % nd
                    if dcnt[j] > 0:
                        dma_prev[i] = (dsems[j], dcnt[j])
                    dcnt[j] += 16
                    sem_of[i] = (dsems[j], dcnt[j])
                elif needed[i]:
                    if cur is None or ccnt >= self.SEM_LIMIT:
                        cur = self.stack.enter_context(nc.semaphore(f"s_{e}_{k}"))
                        k += 1
                        ccnt = 0
                    ccnt += 1
                    sem_of[i] = (cur, ccnt)
        for e, lst in order.items():
            eng = self.engs[e]
            waited = {}

            def do_wait(sem, cnt):
                key = id(sem)
                if waited.get(key, 0) >= cnt:
                    return
                eng.wait_ge(sem, cnt)
                waited[key] = cnt

            for i in lst:
                it = ins[i]
                ws = [sem_of[d] for d in pruned[i]]
                ws.sort(key=lambda sc: -sc[1])
                for sem, c in ws:
                    do_wait(sem, c)
                if it["dma"] and dma_prev[i] is not None:
                    do_wait(*dma_prev[i])
                inst = it["fn"](eng)
                if sem_of[i] is not None:
                    inst.then_inc(sem_of[i][0], 16 if it["dma"] else 1)
            if e == "sp":
                for i in self.out_dmas:
                    do_wait(*sem_of[i])


def build_program(debug=False, only=None):
    nc = bass.Bass("TRN2", target_bir_lowering=False)
    dt_in = lambda name, shape: nc.dram_tensor(name, shape, F32, kind="ExternalInput").ap()
    dt_out = lambda name, shape: nc.dram_tensor(name, shape, F32, kind="ExternalOutput").ap()
    xp_d = dt_in("xp", [D, 512])
    xs_d = dt_in("xs", [D, 2048])
    h0_d = dt_in("h0", [DEPTH, 2, 1024, 128])
    cvec_d = dt_in("cvec", [128, 8, 2])
    wmod_d = dt_in("w_mod", [DEPTH, D, 3 * D])
    bmod_d = dt_in("b_mod", [128, DEPTH, 24])
    gpre_d = dt_in("g_pre", [128, DEPTH, 8])
    gpost_d = dt_in("g_post", [128, DEPTH, 8])
    win_d = dt_in("w_in", [DEPTH, D, IN_COLS])
    cw5_d = dt_in("cw5", [128, DEPTH, 12, 5])
    cb5_d = dt_in("cb5", [128, DEPTH, 12])
    cb5row_d = dt_in("cb5row", [1, DEPTH * 1536])
    alog_d = dt_in("alog", [128, DEPTH, 32])
    dtb_d = dt_in("dtb", [128, DEPTH, 32])
    dsk_d = dt_in("dsk", [128, DEPTH, 16])
    sng_d = dt_in("sng", [128, DEPTH, 8])
    cw31_d = dt_in("cw31", [128, DEPTH, 8, 31])
    cb31_d = dt_in("cb31", [128, DEPTH, 8])
    lng_d = dt_in("lng", [128, DEPTH, 8])
    lnb_d = dt_in("lnb", [128, DEPTH, 8])
    wout_d = dt_in("w_out", [DEPTH, 2 * D, D])
    yp_d = dt_out("yp", [D, 512])
    ys_d = dt_out("ys", [D, 2048])
    ns_d = dt_out("ns", [2, DEPTH, 2, 1024, 128])
    dbg = {}
    if debug:
        dbg["d_modA"] = dt_out("d_modA", [128, DEPTH, 8, 2])
        dbg["d_modB"] = dt_out("d_modB", [128, DEPTH, 8, 2])
        dbg["d_modG"] = dt_out("d_modG", [128, DEPTH, 8, 2])
        for nm, T_ in (("P", 512), ("S", 2048)):
            dbg[f"d_hT_{nm}"] = nc.dram_tensor(f"d_hT_{nm}", [128, 8, T_], BF16, kind="ExternalOutput").ap()
            dbg[f"d_yA_{nm}"] = nc.dram_tensor(f"d_yA_{nm}", [128, 16, T_], BF16, kind="ExternalOutput").ap()
            dbg[f"d_yB_{nm}"] = nc.dram_tensor(f"d_yB_{nm}", [128, 16, T_], BF16, kind="ExternalOutput").ap()
            dbg[f"d_yC_{nm}"] = nc.dram_tensor(f"d_yC_{nm}", [128, 16, T_], BF16, kind="ExternalOutput").ap()
    x1p_d = nc.dram_tensor("x1p", [D, 512], F32, kind="ExternalOutput" if debug else "Internal").ap()
    x1s_d = nc.dram_tensor("x1s", [D, 2048], F32, kind="ExternalOutput" if debug else "Internal").ap()

    with ExitStack() as st:
        P = Prog(nc, st)
        cnt = [0]

        def T(shape, dt, name=None, stack=None):
            cnt[0] += 1
            return (stack or st).enter_context(nc.sbuf_tensor(f"sb{cnt[0]}_{name or 't'}", shape, dt))

        def nfree(ap):
            r = 1
            for d in ap.shape[1:]:
                r *= d
            return r

        def DMA(out, in_, reads=(), writes=(), eng="sp", final=False):
            nbytes = nfree(out) * out.shape[0] * 4
            P.op(eng, lambda e: e.dma_start(out=out, in_=in_), reads, writes, dma=True, final=final,
                 cost=(150.0 if eng == "sp" else 1200.0), lat=2000.0 + nbytes / 120.0)

        def MM(out, lhsT, rhs, start, stop, reads, writes):
            passes = 4 if lhsT.dtype == F32 else 1
            P.op("pe", lambda e: e.matmul(out, lhsT=lhsT, rhs=rhs, start=start, stop=stop), reads, writes,
                 cost=30.0 + passes * max(nfree(rhs), 64) / 2.4, lat=120.0)

        def TR(out, in_, ident, reads, writes):
            P.op("pe", lambda e: e.transpose(out=out, in_=in_, identity=ident), reads, writes,
                 cost=(4 if in_.dtype == F32 else 1) * 60.0 + 30.0, lat=120.0)

        def ACT(out, in_, func, reads, writes, bias=None, scale=None, group=None):
            kw = {}
            if bias is not None:
                kw["bias"] = bias
            if scale is not None:
                kw["scale"] = scale
            P.op("act", lambda e: e.activation(out=out, in_=in_, func=func, **kw), reads, writes, cost=220.0 + nfree(out) / 1.4,
                 group=group)

        def TT(out, in0, in1, op, reads, writes, eng="dve"):
            c = 120.0 + nfree(out) / 0.96 if eng != "pool" else 200.0 + nfree(out) / 0.55
            P.op(eng, lambda e: e.tensor_tensor(out=out, in0=in0, in1=in1, op=op), reads, writes, cost=c)

        def TS(out, in0, s1, op0, reads, writes, s2=None, op1=None, eng="dve"):
            if op1 is None:
                P.op(eng, lambda e: e.tensor_scalar(out=out, in0=in0, scalar1=s1, scalar2=None, op0=op0), reads, writes,
                     cost=120.0 + nfree(out) / 0.96)
            else:
                P.op(eng, lambda e: e.tensor_scalar(out=out, in0=in0, scalar1=s1, scalar2=s2, op0=op0, op1=op1), reads, writes,
                     cost=120.0 + nfree(out) / 0.96)

        def STT(out, in0, scalar, in1, op0, op1, reads, writes, eng="dve"):
            P.op(eng, lambda e: e.scalar_tensor_tensor(out=out, in0=in0, scalar=scalar, in1=in1, op0=op0, op1=op1), reads, writes,
                 cost=120.0 + nfree(out) / 0.96)

        def CP(out, in_, reads, writes, eng="dve"):
            P.op(eng, lambda e: e.tensor_copy(out=out, in_=in_), reads, writes, cost=120.0 + nfree(out) / 0.96)

        def RECIP(out, in_, reads, writes):
            P.op("dve", lambda e: e.reciprocal(out=out, in_=in_), reads, writes, cost=120.0 + nfree(out) * 6.5)

        def MEMSET(ap, val, writes, eng="pool"):
            P.op(eng, lambda e: e.memset(ap, val), (), writes, cost=150.0 + nfree(ap) / 1.0)

        class Rot:
            def __init__(self, name, shape, dt, n, stack):
                self.t = [T(shape, dt, f"{name}{i}", stack) for i in range(n)]
                self.name = name
                self.i = 0

            def next(self):
                k = self.i % len(self.t)
                self.i += 1
                return self.t[k], (self.name, k)

        ps_t = [st.enter_context(nc.psum_tensor(f"ps{i}", [128, 512], F32)) for i in range(NPS)]
        psb_t = st.enter_context(nc.psum_tensor("psb", [128, 1024], BF16))
        ps_i = [0]

        def PS():
            k = ps_i[0] % NPS
            ps_i[0] += 1
            return ps_t[k], ("ps", k)

        PSH = PS

        ident = T([128, 128], F32, "ident")
        identb = T([128, 128], BF16, "identb")
        onesb = T([128, 128], BF16, "onesb")
        onesf = T([128, 128], F32, "onesf")
        Uf = T([128, 128], F32, "Uf")
        SLf = T([128, 128], F32, "SLf")
        Ub = T([128, 128], F32, "Ub")
        SLb = T([128, 128], F32, "SLb")
        MEMSET(onesf[:], 1.0, ["onesf"])
        MEMSET(onesb[:], 1.0, ["onesb"])

        def SEL(t, key, cm, pat, op):
            MEMSET(t[:], 1.0, [key])
            P.op("pool", lambda e: e.affine_select(out=t[:], in_=t[:], pattern=[[pat, 128]], compare_op=op,
                                                   fill=0.0, base=0, channel_multiplier=cm), [key], [key])
        SEL(Uf, "Uf", -1, 1, ALU.is_ge)
        SEL(SLf, "SLf", 1, -1, ALU.is_gt)
        SEL(Ub, "Ub", 1, -1, ALU.is_ge)
        SEL(SLb, "SLb", -1, 1, ALU.is_gt)
        MEMSET(ident[:], 0.0, ["ident"])
        P.op("pool", lambda e: e.affine_select(out=ident[:], in_=ident[:], pattern=[[-1, 128]], compare_op=ALU.not_equal,
                                               fill=1.0, base=0, channel_multiplier=1), ["ident"], ["ident"])
        CP(identb[:], ident[:], ["ident"], ["identb"])

        def LOADP(dram, shape, name):
            t = T(shape, F32, name)
            DMA(t[:], dram, (), [name])
            return t
        cvec = LOADP(cvec_d, [128, 8, 2], "cvec")
        bmod = LOADP(bmod_d, [128, DEPTH, 24], "bmod")
        gpre = LOADP(gpre_d, [128, DEPTH, 8], "gpre")
        gpost = LOADP(gpost_d, [128, DEPTH, 8], "gpost")
        cw5 = LOADP(cw5_d, [128, DEPTH, 12, 5], "cw5")
        cb5 = LOADP(cb5_d, [128, DEPTH, 12], "cb5")
        alog = LOADP(alog_d, [128, DEPTH, 32], "alog")
        dtb = LOADP(dtb_d, [128, DEPTH, 32], "dtb")
        dsk = LOADP(dsk_d, [128, DEPTH, 16], "dsk")
        sng = LOADP(sng_d, [128, DEPTH, 8], "sng")
        cw31 = LOADP(cw31_d, [128, DEPTH, 8, 31], "cw31")
        cb31 = LOADP(cb31_d, [128, DEPTH, 8], "cb31")
        lng = LOADP(lng_d, [128, DEPTH, 8], "lng")
        lnb = LOADP(lnb_d, [128, DEPTH, 8], "lnb")
        cb5row = T([1, DEPTH * 1536], BF16, "cb5row")
        for l_ in range(DEPTH):
            DMA(cb5row[:, l_ * 1536:(l_ + 1) * 1536], cb5row_d[:, l_ * 1536:(l_ + 1) * 1536], (), ["cb5row"], eng="pool")
        aneg = T([128, DEPTH, 32], F32, "aneg")
        ACT(aneg[:], alog[:], AF.Exp, ["alog"], ["aneg"])
        TS(aneg[:], aneg[:], -1.0, ALU.mult, ["aneg"], ["aneg"])

        silc = T([128, 8, 2], F32, "silc")
        ACT(silc[:], cvec[:], AF.Silu, ["cvec"], ["silc"])
        modA = T([128, DEPTH, 8, 2], F32, "modA")
        modB = T([128, DEPTH, 8, 2], F32, "modB")
        modG = T([128, DEPTH, 8, 2], F32, "modG")
        hT = T([128, 8, TMAX], BF16, "hT")
        yT = T([128, 16, TMAX], BF16, "yT")
        ms = ExitStack()
        st.callback(ms.close)
        if True:
            wm = Rot("wm", [128, 8, 512], F32, 2, ms)
            modsb = T([128, 24, 2], F32, "modsb", ms)
            modrow = T([2, 3 * D], F32, "modrow", ms)
            for l in range(DEPTH):
                for cb in range(6):
                    wt, wk = wm.next()
                    DMA(wt[:], wmod_d[l].rearrange("(kc p) c -> p kc c", p=128)[:, :, cb * 512:(cb + 1) * 512], (), [wk])
                    pr, prk = PS()
                    for kc in range(8):
                        MM(pr[0:2, :], silc[:, kc, :], wt[:, kc, :], kc == 0, kc == 7, [wk, "silc"], [prk])
                    CP(modrow[:, cb * 512:(cb + 1) * 512], pr[0:2, :], [prk], [("modrow", cb)])
                pm, pmk = PSH()
                for f in range(24):
                    TR(pm[:, f * 2:f * 2 + 2], modrow[0:2, f * 128:(f + 1) * 128], ident[0:2, 0:2], [("modrow", f // 4), "ident"], [pmk])
                TT(modsb[:], pm[:, 0:48].rearrange("p (f w) -> p f w", w=2),
                   bmod[:, l, :].unsqueeze(2).to_broadcast([128, 24, 2]), ALU.add, [pmk, "bmod"], ["modsb"])
                TS(modA[:, l], modsb[:, 8:16, :], 1.0, ALU.add, ["modsb"], ["modA"])
                TT(modA[:, l], modA[:, l], gpre[:, l, :].unsqueeze(2).to_broadcast([128, 8, 2]), ALU.mult, ["modA", "gpre"], ["modA"])
                CP(modB[:, l], modsb[:, 0:8, :], ["modsb"], ["modB"])
                TT(modG[:, l], modsb[:, 16:24, :], gpost[:, l, :].unsqueeze(2).to_broadcast([128, 8, 2]), ALU.mult,
                   ["modsb", "gpost"], ["modG"])

        if debug:
            DMA(dbg["d_modA"], modA[:], ["modA"], (), final=True)
            DMA(dbg["d_modB"], modB[:], ["modB"], (), final=True)
            DMA(dbg["d_modG"], modG[:], ["modG"], (), final=True)

        def stats_rs(src_sq_fn, nk, rs, rsk, extra_reads, eps=EPS):
            pst, pstk = PS()
            for k in range(nk):
                ap, rd = src_sq_fn(k)
                MM(pst[:], onesb[:], ap, k == 0, k == nk - 1, ["onesb"] + rd, [pstk])
            ACT(rs[:], pst[:], AF.Ln, [pstk], [rsk], bias=eps, scale=1.0 / 1024.0)
            ACT(rs[:], rs[:], AF.Exp, [rsk], [rsk], scale=-0.5)

        ms_holder = [ms]

        def run_block(l, x_src, x_dst, nseq, L, stride, wsel, h0, ns_out, final_out, do_front, fuse_next):
            Ttok = nseq * L
            NT = Ttok // 512
            nch = L // 128
            nblk = nseq * nch
            xsrc_v = x_src.rearrange("(kc p) t -> p kc t", p=128)
            xdst_v = x_dst.rearrange("(kc p) t -> p kc t", p=128)
            hk = lambda t: ("hT", t)
            yk = lambda k, b: ("yT", k, b)
            ytile = lambda ks, t: [yk(k, b) for k in ks for b in range(4 * t, 4 * t + 4)]

            def segs(t):
                if L >= 512:
                    per = L // 512
                    return [(t // per, (t % per) * 512, 512, 0)]
                n = 512 // L
                return [(t * n + i, 0, L, i * L) for i in range(n)]

            def win_load(dst, c0, w, key):
                DMA(dst, win_d[l].rearrange("(kc p) c -> p kc c", p=128)[:, :, c0:c0 + w], (), [key], eng="pool")

            with ExitStack() as s0:
                xt_r = Rot("xt", [128, 8, 512], F32, 2, s0)
                sq_r = Rot("sq0", [128, 8, 512], BF16, 2, s0)
                rs_r = Rot("rs0", [128, 512], F32, 2, s0)
                for t in range(NT if do_front else 0):
                    xt, xtk = xt_r.next()
                    sq, sqk = sq_r.next()
                    rs, rsk = rs_r.next()
                    DMA(xt[:], xsrc_v[:, :, t * 512:(t + 1) * 512], [("xd", id(x_src), t)], [xtk])
                    ACT(sq[:], xt[:], AF.Square, [xtk], [sqk])
                    stats_rs(lambda k: (sq[:, k, :], [sqk]), 8, rs, rsk, [])
                    TT(xt[:], xt[:], rs[:].unsqueeze(1).to_broadcast([128, 8, 512]), ALU.mult, [xtk, rsk], [xtk])
                    for kc in range(8):
                        ACT(hT[:, kc, t * 512:(t + 1) * 512], xt[:, kc, :], AF.Identity, [xtk, "modA", "modB"], [hk(t)],
                            bias=modB[:, l, kc, wsel:wsel + 1], scale=modA[:, l, kc, wsel:wsel + 1], group=("hTf", l, t, nseq))

            if ms_holder:
                ms_holder.pop().close()
            P.barrier()
            nm = "P" if nseq == 2 else "S"
            allk = [yk(k, b) for k in range(16) for b in range(nblk)]
            if debug and l == 0:
                DMA(dbg[f"d_hT_{nm}"], hT[:, :, 0:Ttok], [hk(t) for t in range(NT)], (), final=True)
            with ExitStack() as sa:
                wx = T([128, 8, WP], BF16, "wx", sa)
                wB = T([128, 8, 128], BF16, "wB", sa)
                wC = T([128, 8, 128], BF16, "wC", sa)
                wz = T([128, 8, WP], BF16, "wz", sa)
                wdt = T([128, 8, 32], BF16, "wdt", sa)
                upad_r = Rot("upad", [128, nseq, L + 4], BF16, 2, sa)
                diag5_r = Rot("diag5", [128, 5, 128], BF16, 2, sa)
                xg = T([128, nblk, WP], BF16, "xg", sa)
                Btm = T([128, nblk, 128], BF16, "Btm", sa)
                Bfm = T([128, Ttok], BF16, "Bfm", sa)
                Cfm = T([128, Ttok], BF16, "Cfm", sa)
                dt_all = T([128, nblk, 32], F32, "dt_all", sa)
                la_all = T([128, nblk, 32], F32, "la_all", sa)
                v_all = T([128, nblk, 32], F32, "v_all", sa)
                cum_sb = [T([128, nblk, HP], F32, f"cum{d}", sa) for d in range(2)]
                cum_hi = [T([128, nblk, HP], BF16, f"cumhi{d}", sa) for d in range(2)]
                cum_lo = [T([128, nblk, HP], BF16, f"cumlo{d}", sa) for d in range(2)]
                decs_all = [T([128, 3, nblk, HP], F32, f"decs{d}", sa) for d in range(2)]
                diagD = T([128, HP, 128], BF16, "diagD", sa)
                ypark = T([128, nblk, WP], F32, "ypark", sa)
                Sf = [T([128, WP], F32, f"Sf{d}", sa) for d in range(2)]
                Sb = [T([128, WP], BF16, f"Sb{d}", sa) for d in range(2)]
                cbm_r = Rot("cbm", [128, 128], BF16, NROT + 1, sa)
                xdt_r = Rot("xdt", [128, HP, 64], BF16, NROT + 1, sa)
                xs2_r = Rot("xs2", [128, HP, 64], BF16, NROT + 1, sa)
                Lh_r = Rot("Lh", [128, HP, 128], BF16, NROT + 1, sa)
                La_r = Rot("La", [128, HP, 128], F32, 2, sa)
                Mh_r = Rot("Mh", [128, HP, 128], BF16, NROT + 1, sa)
                t1_r = Rot("t1", [128, HP, 64], F32, NROT, sa)
                zs_r = Rot("zs", [128, WP], F32, 2, sa)
                yg_r = Rot("yg", [128, WP], BF16, 2, sa)
                stg_r = Rot("stg", [128, 128], F32, 2, sa)
                for r in upad_r.t:
                    MEMSET(r[:], 0.0, [("upad", upad_r.t.index(r))])

                win_load(wdt[:], I_DT, 32, "wdt")
                for bi in range(nblk):
                    t = bi // 4
                    pd, pdk = PSH()
                    for kc in range(8):
                        MM(pd[:, 0:32], hT[:, kc, bi * 128:(bi + 1) * 128], wdt[:, kc, :], kc == 0, kc == 7, [hk(t), "wdt"], [pdk])
                    TT(v_all[:, bi, :], pd[:, 0:32], dtb[:, l, :], ALU.add, [pdk, "dtb"], ["v_all"])
                TS(dt_all[:], v_all[:], 30.0, ALU.min, ["v_all"], ["dt_all"])
                ACT(dt_all[:], dt_all[:], AF.Exp, ["dt_all"], ["dt_all"])
                ACT(dt_all[:], dt_all[:], AF.Ln, ["dt_all"], ["dt_all"], bias=1.0)
                TT(dt_all[:], dt_all[:], v_all[:], ALU.max, ["dt_all", "v_all"], ["dt_all"])
                TT(la_all[:], dt_all[:], aneg[:, l, :].unsqueeze(1).to_broadcast([128, nblk, 32]), ALU.mult, ["dt_all", "aneg"], ["la_all"])
                for q in range(16 // HP):
                    g = (q * HP) // 8
                    nb4 = nblk * HP
                    win_load(wx[:], I_X + q * WP, WP, "wx")
                    if (q * HP) % 8 == 0:
                        win_load(wB[:], I_B + g * 128, 128, "wB")
                        win_load(wC[:], I_C + g * 128, 128, "wC")
                    win_load(wz[:], I_Z + q * WP, WP, "wz")
                    nxc = WP // 128
                    chunks = [("x", a, q * nxc + a, wx, a * 128, "wx") for a in range(nxc)]
                    if (q * HP) % 8 == 0:
                        chunks += [("B", 0, 8 + g, wB, 0, "wB"), ("C", 0, 10 + g, wC, 0, "wC")]
                    for kind, a, cidx, wt, wc0, wk in chunks:
                        upad, upk = upad_r.next()
                        dg, dgk = diag5_r.next()
                        TT(dg[:], ident[:].unsqueeze(1).to_broadcast([128, 5, 128]),
                           cw5[:, l, cidx, :].unsqueeze(2).to_broadcast([128, 5, 128]), ALU.mult, ["ident", "cw5"], [dgk])
                        for t in range(NT):
                            pu, puk = PS()
                            for kc in range(8):
                                MM(pu[:], wt[:, kc, wc0:wc0 + 128], hT[:, kc, t * 512:(t + 1) * 512], kc == 0, kc == 7,
                                   [wk, hk(t)], [puk])
                            if L >= 512:
                                s_, off, n_, c0 = segs(t)[0]
                                P.op("act", (lambda e, o=upad[:, s_, 2 + off:2 + off + 512], i=pu[:]: e.copy(out=o, in_=i)),
                                     [puk], [upk], cost=590.0)
                            else:
                                n = 512 // L
                                P.op("act", (lambda e, o=upad[:, t * n:(t + 1) * n, 2:2 + L],
                                             i=pu[:].rearrange("p (s x) -> p s x", s=n): e.copy(out=o, in_=i)), [puk], [upk], cost=590.0)
                        if kind in ("x", "B"):
                            for s_ in range(nseq):
                                for j in range(nch):
                                    bi = s_ * nch + j
                                    pc, pck = PSH()
                                    for k in range(5):
                                        MM(pc[:, 0:128], upad[:, s_, j * 128 + k:j * 128 + k + 128], dg[:, k, :], k == 0, False,
                                           [upk, dgk], [pck])
                                    MM(pc[:, 0:128], onesb[0:1, 0:128], cb5row[0:1, l * 1536 + cidx * 128:l * 1536 + (cidx + 1) * 128],
                                       False, True, ["onesb", "cb5row"], [pck])
                                    if kind == "x":
                                        ACT(xg[:, bi, a * 128:(a + 1) * 128], pc[:, 0:128], AF.Silu, [pck], [("xg", bi)], group=("xg", l, nseq, q, bi))
                                    else:
                                        ACT(Btm[:, bi, :], pc[:, 0:128], AF.Silu, [pck], [("Btm", bi)])
                        if kind in ("B", "C"):
                            dstT, dkey = (Bfm, "Bfm") if kind == "B" else (Cfm, "Cfm")
                            for t in range(NT):
                                pc, pck = PS()
                                for (s_, off, n_, c0) in segs(t):
                                    for k in range(5):
                                        MM(pc[:, c0:c0 + n_], dg[:, k, :], upad[:, s_, off + k:off + k + n_], k == 0, k == 4,
                                           [upk, dgk], [pck])
                                ACT(dstT[:, t * 512:(t + 1) * 512], pc[:], AF.Silu, [pck, "cb5"], [(dkey, t)],
                                    bias=cb5[:, l, cidx:cidx + 1])
                    TT(diagD[:], ident[:].unsqueeze(1).to_broadcast([128, HP, 128]),
                       dsk[:, l, q * HP:(q + 1) * HP].unsqueeze(2).to_broadcast([128, HP, 128]), ALU.mult, ["ident", "dsk"], ["diagD"])
                    for d in (1, 0):
                        Uin, UinK = (Uf, "Uf") if d == 0 else (Ub, "Ub")
                        SLo, SLoK = (SLf, "SLf") if d == 0 else (SLb, "SLb")
                        pdc, pdck = PSH()
                        la_d = la_all[:, :, d * 16 + q * HP:d * 16 + (q + 1) * HP]
                        for ci, (mt, mk) in enumerate(((Uin, UinK), (SLo, SLoK), (onesf, "onesf"))):
                            MM(pdc[:, ci * nb4:(ci + 1) * nb4].rearrange("p (b h) -> p b h", h=HP), mt[:], la_d, True, True,
                               [mk, "la_all"], [pdck])
                        dcs = decs_all[d]
                        dcsk = ("decs", d)
                        ACT(dcs[:].rearrange("p c b h -> p (c b h)"), pdc[:, 0:3 * nb4], AF.Exp, [pdck], [dcsk])
                        cum = cum_sb[d]
                        cumk = ("cum", d)
                        CP(cum[:].rearrange("p b h -> p (b h)"), pdc[:, 0:nb4], [pdck, dcsk], [cumk])
                        chi, clo = cum_hi[d], cum_lo[d]
                        CP(chi[:], cum[:], [cumk], [("chi", d)])
                        TT(clo[:], cum[:], chi[:], ALU.subtract, [cumk, ("chi", d)], [("clo", d)])
                        TT(cum[:], chi[:], clo[:], ALU.add, [("chi", d), ("clo", d), cumk], [cumk])
                        for s_ in range(nseq):
                            if h0 is None:
                                MEMSET(Sf[d][:], 0.0, [("Sf", d)], eng="dve")
                                MEMSET(Sb[d][:], 0.0, [("Sb", d)], eng="dve")
                            else:
                                for a in range(WP // 128):
                                    sg, sgk = stg_r.next()
                                    DMA(sg[:], h0[l, d, q * WP + a * 128:q * WP + (a + 1) * 128, :], (), [sgk])
                                    pt, ptk = PSH()
                                    TR(pt[:, 0:128], sg[:], ident[:], [sgk, "ident"], [ptk])
                                    CP(Sf[d][:, a * 128:(a + 1) * 128], pt[:, 0:128], [ptk], [("Sf", d)])
                                CP(Sb[d][:], Sf[d][:], [("Sf", d)], [("Sb", d)])
                            order = range(nch) if d == 0 else range(nch - 1, -1, -1)
                            for j in order:
                                bi = s_ * nch + j
                                t = bi // 4
                                tok = slice(bi * 128, (bi + 1) * 128)
                                la_b = la_all[:, bi, d * 16 + q * HP:d * 16 + (q + 1) * HP]
                                dt_b = dt_all[:, bi, d * 16 + q * HP:d * 16 + (q + 1) * HP]
                                pcb, pcbk = PSH()
                                MM(pcb[:, 0:128], Bfm[:, tok], Cfm[:, tok], True, True, [("Bfm", t), ("Cfm", t)], [pcbk])
                                cbm, cbmk = cbm_r.next()
                                TT(cbm[:], pcb[:, 0:128], Uin[:], ALU.mult, [pcbk, UinK], [cbmk])
                                xdt, xdtk = xdt_r.next()
                                xs2, xs2k = xs2_r.next()
                                xg_b = xg[:, bi, :].rearrange("p (h c) -> p h c", h=HP)
                                TT(xdt[:], xg_b, dt_b.unsqueeze(2).to_broadcast([128, HP, 64]), ALU.mult, [("xg", bi), "dt_all"], [xdtk], eng=OFF_ENG)
                                TT(xs2[:], xdt[:], dcs[:, 1, bi, :].unsqueeze(2).to_broadcast([128, HP, 64]), ALU.mult,
                                   [xdtk, dcsk], [xs2k], eng=OFF_ENG)
                                parg, pargk = PS()
                                for h in range(HP):
                                    po_ = parg[:, h * 128:(h + 1) * 128]
                                    MM(po_, chi[:, bi, h:h + 1].to_broadcast([128, 128]), identb[:], True, False, [("chi", d), "identb"], [pargk])
                                    MM(po_, clo[:, bi, h:h + 1].to_broadcast([128, 128]), identb[:], False, True, [("clo", d), "identb"], [pargk])
                                La, Lak = La_r.next()
                                for h in range(HP):
                                    ACT(La[:, h, :], parg[:, h * 128:(h + 1) * 128], AF.Relu, [pargk, cumk], [Lak],
                                        bias=cum[:, bi, h:h + 1], scale=-1.0, group=("relu", l, nseq, q, d, bi))
                                Lh, Lhk = Lh_r.next()
                                ACT(Lh[:].rearrange("p h c -> p (h c)"), La[:].rearrange("p h c -> p (h c)"), AF.Exp, [Lak], [Lhk], scale=-1.0)
                                Mh, Mhk = Mh_r.next()
                                TT(Mh[:], Lh[:], cbm[:].unsqueeze(1).to_broadcast([128, HP, 128]), ALU.mult, [Lhk, cbmk], [Mhk])
                                py, pyk = PSH()
                                for h in range(HP):
                                    MM(py[:, h * 64:(h + 1) * 64], Mh[:, h, :], xdt[:, h, :], True, d == 1, [Mhk, xdtk], [pyk])
                                    if d == 0:
                                        MM(py[:, h * 64:(h + 1) * 64], diagD[:, h, :], xg[:, bi, h * 64:(h + 1) * 64], False, True,
                                           ["diagD", ("xg", bi)], [pyk])
                                po, pok = PSH()
                                MM(po[:, 0:WP], Cfm[:, tok], Sb[d][:], True, True, [("Cfm", t), ("Sb", d)], [pok])
                                t1, t1k = t1_r.next()
                                TT(t1[:], po[:, 0:WP].rearrange("p (h c) -> p h c", h=HP),
                                   dcs[:, 0, bi, :].unsqueeze(2).to_broadcast([128, HP, 64]), ALU.mult, [pok, dcsk], [t1k])
                                t1f = t1[:].rearrange("p h c -> p (h c)")
                                if d == 1:
                                    TT(ypark[:, bi, :], t1f, py[:, 0:WP], ALU.add, [t1k, pyk], [("ypark", bi)])
                                else:
                                    TT(t1f, t1f, py[:, 0:WP], ALU.add, [t1k, pyk], [t1k])
                                    TT(t1f, t1f, ypark[:, bi, :], ALU.add, [t1k, ("ypark", bi)], [t1k], eng=OFF_ENG)
                                    yg, ygk = yg_r.next()
                                    zs, zsk = zs_r.next()
                                    pz, pzk = PSH()
                                    for kc in range(8):
                                        MM(pz[:, 0:WP], hT[:, kc, tok], wz[:, kc, :], kc == 0, kc == 7, [hk(t), "wz"], [pzk])
                                    ACT(zs[:], pz[:, 0:WP], AF.Tanh, [pzk], [zsk], scale=0.5)
                                    STT(zs[:], zs[:], 1.0, pz[:, 0:WP], ALU.add, ALU.mult, [zsk, pzk], [zsk])
                                    TT(yg[:], t1f, zs[:], ALU.mult, [t1k, zsk], [ygk])
                                    for a in range(WP // 128):
                                        TR(psb_t[:, a * 128:(a + 1) * 128], yg[:, a * 128:(a + 1) * 128], identb[:], [ygk, "identb"], ["psb"])
                                    kc0 = q * (WP // 128)
                                    P.op("act", (lambda e, o=yT[:, kc0:kc0 + WP // 128, tok],
                                                 i=psb_t[:, 0:WP].rearrange("p (a c) -> p a c", c=128): e.copy(out=o, in_=i)),
                                         ["psb"], [yk(kc0 + a, bi) for a in range(WP // 128)], cost=400.0)
                                pds, pdsk = PSH()
                                MM(pds[:, 0:WP], Btm[:, bi, :], xs2[:].rearrange("p h c -> p (h c)"), True, True, [("Btm", bi), xs2k], [pdsk])
                                Sf3 = Sf[d][:].rearrange("p (h c) -> p h c", h=HP)
                                TT(Sf3, Sf3, dcs[:, 2, bi, :].unsqueeze(2).to_broadcast([128, HP, 64]), ALU.mult,
                                   [("Sf", d), dcsk], [("Sf", d)], eng=OFF_ENG)
                                TT(Sf[d][:], Sf[d][:], pds[:, 0:WP], ALU.add, [("Sf", d), pdsk], [("Sf", d)])
                                P.op("act", (lambda e, o=Sb[d][:], i=Sf[d][:]: e.copy(out=o, in_=i)), [("Sf", d)], [("Sb", d)], cost=400.0)
                            if ns_out is not None:
                                for a in range(WP // 128):
                                    pt, ptk = PSH()
                                    TR(pt[:, 0:128], Sf[d][:, a * 128:(a + 1) * 128], ident[:], [("Sf", d), "ident"], [ptk])
                                    sg, sgk = stg_r.next()
                                    CP(sg[:], pt[:, 0:128], [ptk], [sgk])
                                    DMA(ns_out[s_, l, d, q * WP + a * 128:q * WP + (a + 1) * 128, :], sg[:], [sgk], (), final=True)

            P.barrier()
            if debug and l == 0:
                DMA(dbg[f"d_yA_{nm}"], yT[:, :, 0:Ttok], allk, (), final=True)
            def ssd_norm(sn):
                sq_r = Rot("sqn", [128, 8, 512], BF16, 1, sn)
                rs_r = Rot("rsn", [128, 512], F32, 2, sn)
                for t in range(NT):
                    sq, sqk = sq_r.next()
                    rs, rsk = rs_r.next()
                    tl = slice(t * 512, (t + 1) * 512)
                    ACT(sq[:], yT[:, 0:8, tl], AF.Square, ytile(range(8), t), [sqk])
                    stats_rs(lambda k: (sq[:, k, :], [sqk]), 8, rs, rsk, [], eps=4.0 * EPS)
                    for k in range(8):
                        STT(yT[:, k, tl], yT[:, k, tl], sng[:, l, k:k + 1], rs[:], ALU.mult, ALU.mult,
                            ytile([k], t) + ["sng", rsk], ytile([k], t))

            pad = 15 * stride
            with ExitStack() as sb_:
                ssd_norm(sb_)
                if debug and l == 0:
                    DMA(dbg[f"d_yB_{nm}"], yT[:, :, 0:Ttok], allk, (), final=True)
                wga = T([128, 8, 1024], BF16, "wga", sb_)
                wgb = T([128, 8, 1024], BF16, "wgb", sb_)
                for j in range(8):
                    win_load(wga[:, :, j * 128:(j + 1) * 128], I_GA + j * 128, 128, ("wga", j))
                    win_load(wgb[:, :, j * 128:(j + 1) * 128], I_GB + j * 128, 128, ("wgb", j))
                hc_r = Rot("hc", [128, nseq, L + 2 * pad], BF16, 2, sb_)
                d31_r = Rot("d31", [128, 31, 128], BF16, 2, sb_)
                sig_r = Rot("sig", [128, 512], F32, 2, sb_)
                accd_r = Rot("accd", [128, 512], F32, 2, sb_)
                accp_r = Rot("accp", [128, 512], F32, 2, sb_)
                accbd_r = Rot("accbd", [128, 512], BF16, 2, sb_)
                accbp_r = Rot("accbp", [128, 512], BF16, 2, sb_)
                for r in hc_r.t:
                    MEMSET(r[:], 0.0, [("hc", hc_r.t.index(r))])
                for j in range(8):
                    hc, hck = hc_r.next()
                    dg, dgk = d31_r.next()
                    TT(dg[:], ident[:].unsqueeze(1).to_broadcast([128, 31, 128]),
                       cw31[:, l, j, :].unsqueeze(2).to_broadcast([128, 31, 128]), ALU.mult, ["ident", "cw31"], [dgk])
                    for t in range(NT):
                        pa, pak = PS()
                        pb, pbk = PS()
                        for kc in range(8):
                            MM(pa[:], wga[:, kc, j * 128:(j + 1) * 128], hT[:, kc, t * 512:(t + 1) * 512], kc == 0, kc == 7, [("wga", j), hk(t)], [pak])
                        for kc in range(8):
                            MM(pb[:], wgb[:, kc, j * 128:(j + 1) * 128], hT[:, kc, t * 512:(t + 1) * 512], kc == 0, kc == 7, [("wgb", j), hk(t)], [pbk])
                        sig, sigk = sig_r.next()
                        ACT(sig[:], pb[:], AF.Sigmoid, [pbk], [sigk])
                        if L >= 512:
                            s_, off, n_, c0 = segs(t)[0]
                            TT(hc[:, s_, pad + off:pad + off + 512], pa[:], sig[:], ALU.mult, [pak, sigk], [hck])
                        else:
                            n = 512 // L
                            TT(hc[:, t * n:(t + 1) * n, pad:pad + L], pa[:].rearrange("p (s x) -> p s x", s=n),
                               sig[:].rearrange("p (s x) -> p s x", s=n), ALU.mult, [pak, sigk], [hck])
                    for t in range(NT):
                        pc, pck = PS()
                        for (s_, off, n_, c0) in segs(t):
                            taps = [k for k in range(31) if off + (k - 15) * stride + n_ > 0 and off + (k - 15) * stride < L]
                            win = lambda k: hc[:, s_, pad + off + (k - 15) * stride:pad + off + (k - 15) * stride + n_]
                            wk_ = lambda k: cw31[:, l, j, k:k + 1]
                            extra = []
                            rest = list(taps)
                            for eng_, ntap, acc_r, accb_r in (("dve", CONV_ND, accd_r, accbd_r), ("pool", CONV_NP, accp_r, accbp_r)):
                                if ntap == 0 or len(rest) - ntap < 4:
                                    continue
                                mine, rest = rest[:ntap], rest[ntap:]
                                acc, acck = acc_r.next()
                                accb, accbk = accb_r.next()
                                c_ = (120.0 + n_ / 0.96) if eng_ == "dve" else (200.0 + n_ / 0.55)
                                for ii, k in enumerate(mine):
                                    last_ = ii == len(mine) - 1
                                    dst, dstk = (accb, accbk) if last_ else (acc, acck)
                                    if ii == 0:
                                        P.op(eng_, (lambda e, o=dst[:, 0:n_], i0=win(k), sc=wk_(k): e.tensor_scalar(
                                            out=o, in0=i0, scalar1=sc, scalar2=None, op0=ALU.mult)), [hck, "cw31"], [dstk], cost=c_)
                                    else:
                                        P.op(eng_, (lambda e, o=dst[:, 0:n_], i0=win(k), sc=wk_(k), i1=acc[:, 0:n_]: e.scalar_tensor_tensor(
                                            out=o, in0=i0, scalar=sc, in1=i1, op0=ALU.mult, op1=ALU.add)), [hck, "cw31", acck], [dstk], cost=c_)
                                extra.append((accb, accbk))
                            nmm = len(rest) + len(extra)
                            im = 0
                            for k in rest:
                                MM(pc[:, c0:c0 + n_], dg[:, k, :], win(k), im == 0, im == nmm - 1, [hck, dgk], [pck])
                                im += 1
                            for accb, accbk in extra:
                                MM(pc[:, c0:c0 + n_], identb[:], accb[:, 0:n_], im == 0, im == nmm - 1, ["identb", accbk], [pck])
                                im += 1
                        ACT(yT[:, 8 + j, t * 512:(t + 1) * 512], pc[:], AF.Identity, [pck, "cb31"], ytile([8 + j], t),
                            bias=cb31[:, l, j:j + 1])
            P.barrier()
            with ExitStack() as sb_:
                wgs = T([128, 8, 1024], BF16, "wgs", sb_)
                for j in range(8):
                    win_load(wgs[:, :, j * 128:(j + 1) * 128], I_GS + j * 128, 128, ("wgs", j))
                sq_r = Rot("sqc", [128, 8, 512], BF16, 2, sb_)
                mean_r = Rot("mean", [128, 512], F32, 2, sb_)
                rs_r = Rot("rsc", [128, 512], F32, 2, sb_)
                tmp_r = Rot("tmpc", [128, 512], F32, 2, sb_)
                s1_r = Rot("s1c", [128, 512], F32, 2, sb_)
                for t in range(NT):
                    tl = slice(t * 512, (t + 1) * 512)
                    sq, sqk = sq_r.next()
                    mean, meank = mean_r.next()
                    rs, rsk = rs_r.next()
                    p1, p1k = PS()
                    for k in range(8):
                        MM(p1[:], onesb[:], yT[:, 8 + k, tl], k == 0, k == 7, ["onesb"] + ytile([8 + k], t), [p1k])
                    ACT(sq[:], yT[:, 8:16, tl], AF.Square, ytile(range(8, 16), t), [sqk])
                    p2, p2k = PS()
                    for k in range(8):
                        MM(p2[:], onesb[:], sq[:, k, :], k == 0, k == 7, ["onesb", sqk], [p2k])
                    TS(mean[:], p1[:], 1.0 / 1024.0, ALU.mult, [p1k], [meank])
                    tmp, tmpk = tmp_r.next()
                    TT(tmp[:], mean[:], mean[:], ALU.mult, [meank], [tmpk])
                    STT(tmp[:], p2[:], 1.0 / 1024.0, tmp[:], ALU.mult, ALU.subtract, [p2k, tmpk], [tmpk])
                    ACT(rs[:], tmp[:], AF.Ln, [tmpk], [rsk], bias=EPS)
                    ACT(rs[:], rs[:], AF.Exp, [rsk], [rsk], scale=-0.5)
                    for j in range(8):
                        tmp, tmpk = tmp_r.next()
                        s1, s1k = s1_r.next()
                        TT(tmp[:], yT[:, 8 + j, tl], mean[:], ALU.subtract, ytile([8 + j], t) + [meank], [tmpk])
                        TT(tmp[:], tmp[:], rs[:], ALU.mult, [tmpk, rsk], [tmpk])
                        ACT(s1[:], tmp[:], AF.Silu, [tmpk, "lng", "lnb"], [s1k], bias=lnb[:, l, j:j + 1], scale=lng[:, l, j:j + 1])
                        pg, pgk = PS()
                        for kc in range(8):
                            MM(pg[:], wgs[:, kc, j * 128:(j + 1) * 128], hT[:, kc, tl], kc == 0, kc == 7, [("wgs", j), hk(t)], [pgk])
                        ACT(tmp[:], pg[:], AF.Silu, [pgk], [tmpk])
                        TT(yT[:, 8 + j, tl], s1[:], tmp[:], ALU.mult, [s1k, tmpk], ytile([8 + j], t))

            P.barrier()
            if debug and l == 0:
                DMA(dbg[f"d_yC_{nm}"], yT[:, :, 0:Ttok], allk, (), final=True)
            with ExitStack() as sc:
                wo = T([128, 16, 1024], BF16, "wo", sc)
                for fo in range(8):
                    DMA(wo[:, :, fo * 128:(fo + 1) * 128], wout_d[l].rearrange("(kc p) c -> p kc c", p=128)[:, :, fo * 128:(fo + 1) * 128],
                        (), [("wo", fo)], eng="pool")
                osb_r = Rot("osb", [128, 8, 512], F32, 2, sc)
                sq_r = Rot("sqo", [128, 8, 512], BF16, 1, sc)
                rs_r = Rot("rso", [128, 512], F32, 2, sc)
                xt_r = Rot("xto", [128, 8, 512], F32, 1, sc)
                for t in range(NT):
                    tl = slice(t * 512, (t + 1) * 512)
                    osb, osbk = osb_r.next()
                    sq, sqk = sq_r.next()
                    rs, rsk = rs_r.next()
                    xt, xtk = xt_r.next()
                    DMA(xt[:], xsrc_v[:, :, tl], [("xd", id(x_src), t)], [xtk])
                    for fo in range(8):
                        po, pok = PS()
                        for kc in range(16):
                            MM(po[:], wo[:, kc, fo * 128:(fo + 1) * 128], yT[:, kc, tl], kc == 0, kc == 15,
                               [("wo", fo)] + ytile([kc], t), [pok])
                        P.op("act", (lambda e, o=osb[:, fo, :], i=po[:]: e.copy(out=o, in_=i)), [pok], [(osbk, fo)], cost=590.0)
                        ACT(sq[:, fo, :], po[:], AF.Square, [pok], [(sqk, fo)])
                    stats_rs(lambda k: (sq[:, k, :], [(sqk, k)]), 8, rs, rsk, [])
                    for fo in range(8):
                        TT(osb[:, fo, :], osb[:, fo, :], rs[:], ALU.mult, [(osbk, fo), rsk], [(osbk, fo)])
                        STT(osb[:, fo, :], osb[:, fo, :], modG[:, l, fo, wsel:wsel + 1], xt[:, fo, :], ALU.mult, ALU.add,
                            [(osbk, fo), "modG", xtk], [(osbk, fo)])
                    allosb = [(osbk, k) for k in range(8)]
                    DMA(xdst_v[:, :, tl], osb[:], allosb, [("xd", id(x_dst), t)], final=final_out)
                    if fuse_next:
                        allsq = [(sqk, k) for k in range(8)]
                        ACT(sq[:], osb[:], AF.Square, allosb, allsq)
                        rs2, rs2k = rs_r.next()
                        stats_rs(lambda k: (sq[:, k, :], [(sqk, k)]), 8, rs2, rs2k, [])
                        TT(xt[:], osb[:], rs2[:].unsqueeze(1).to_broadcast([128, 8, 512]), ALU.mult, allosb + [rs2k], [xtk])
                        for kc in range(8):
                            ACT(hT[:, kc, tl], xt[:, kc, :], AF.Identity, [xtk, "modA", "modB"], [hk(t)],
                                bias=modB[:, l + 1, kc, wsel:wsel + 1], scale=modA[:, l + 1, kc, wsel:wsel + 1],
                                group=("hTn", l, t, nseq))
            P.barrier()

        for nm_ in ("P", "S"):
            for l in range(DEPTH):
                last = (l == DEPTH - 1)
                if only is not None and (l, nm_) not in only:
                    continue
                if nm_ == "P":
                    run_block(l, xp_d if l == 0 else x1p_d, yp_d if last else x1p_d, 2, 256, 1, 0, None, ns_d, last or debug,
                              l == 0 or debug, (not last) and not debug)
                else:
                    run_block(l, xs_d if l == 0 else x1s_d, ys_d if last else x1s_d, 1, 2048, 64, 1, h0_d, None, last or debug,
                              l == 0 or debug, (not last) and not debug)
        P.emit()
        n_ins = len(P.ins)
    return nc, n_ins


_CACHE = {}


def _fm(v):
    v = np.asarray(v, np.float32)
    lead = v.shape[:-1]
    nchunk = v.shape[-1] // 128
    r = v.reshape(lead + (nchunk, 128))
    return np.ascontiguousarray(np.moveaxis(r, -1, 0))


def kernel(x_prompt, x_sample, state_ssd, c, c_ctx, w_mod, b_mod, g_pre, g_post, w_in,
           ssd_conv_w, ssd_conv_b, ssd_a_log, ssd_dt_bias, ssd_d, ssd_norm_g,
           conf_conv_w, conf_conv_b, conf_ln_g, conf_ln_b, w_out):
    f = lambda a: np.ascontiguousarray(np.asarray(a, np.float32))
    x_prompt, x_sample, state_ssd = f(x_prompt), f(x_sample), f(state_ssd)
    if "nc" not in _CACHE:
        _CACHE["nc"] = build_program()[0]
    nc = _CACHE["nc"]
    rep = lambda a: np.ascontiguousarray(np.broadcast_to(f(a).reshape(1, DEPTH, -1), (128, DEPTH, f(a).reshape(DEPTH, -1).shape[1])))
    shared = {
        "w_mod": f(w_mod), "b_mod": _fm(b_mod), "g_pre": _fm(g_pre), "g_post": _fm(g_post), "w_in": f(w_in),
        "cw5": np.ascontiguousarray(np.transpose(f(ssd_conv_w).reshape(DEPTH, 5, 12, 128), (3, 0, 2, 1))),
        "cb5": _fm(ssd_conv_b), "cb5row": f(ssd_conv_b).reshape(1, DEPTH * 1536),
        "alog": rep(ssd_a_log), "dtb": rep(ssd_dt_bias), "dsk": rep(ssd_d), "sng": _fm(ssd_norm_g),
        "cw31": np.ascontiguousarray(np.transpose(f(conf_conv_w).reshape(DEPTH, 31, 8, 128), (3, 0, 2, 1))),
        "cb31": _fm(conf_conv_b), "lng": _fm(conf_ln_g), "lnb": _fm(conf_ln_b), "w_out": f(w_out),
    }
    in_maps = []
    for core in range(NCORES):
        b = core // 4
        m = dict(shared)
        m["xp"] = np.ascontiguousarray(x_prompt[2 * core:2 * core + 2].reshape(512, D).T)
        m["xs"] = np.ascontiguousarray(x_sample[b].T)
        m["h0"] = np.ascontiguousarray(state_ssd[b].reshape(DEPTH, 2, 1024, 128))
        cv = np.stack([f(c_ctx), f(c)[b]], axis=-1)
        m["cvec"] = np.ascontiguousarray(np.transpose(cv.reshape(8, 128, 2), (1, 0, 2)))
        in_maps.append(m)
    res = run_bass_kernel_spmd(nc, in_maps, core_ids=list(range(NCORES)))
    r = res.results
    y_prompt = np.stack([r[core]["yp"].T.reshape(2, 256, D) for core in range(NCORES)], 0).reshape(16, 256, D)
    y_sample = np.stack([r[0]["ys"].T, r[4]["ys"].T], 0)
    new_state = np.concatenate([r[core]["ns"] for core in range(NCORES)], 0).reshape(16, DEPTH, 2, 16, 64, 128)
    return (np.ascontiguousarray(y_prompt, dtype=np.float32), np.ascontiguousarray(y_sample, dtype=np.float32),
            np.ascontiguousarray(new_state, dtype=np.float32))
```
